# Optimizing a Trainium2 kernel written in Bass

```python
import math
import jax, jax.numpy as jnp
from jax import lax
import numpy as np

D_MODEL = 1024
BATCH = 16
SEQ = 4096
DEPTH = 1

N_MEM = 256
R_HEAD = 64
R_HEADS = D_MODEL // R_HEAD
R_WIDTH = R_HEADS * R_HEAD
R_LORA_W = 64
R_LORA_A = 64
R_LORA_G = 160
LNX_EPS = 64e-5
N_HEADS = 16
N_KV = 4
N_HG = N_HEADS // N_KV
HEAD_DIM = 64
CMP_BLOCK = 32
CMP_STRIDE = 16
CMP_HIDDEN = 256
SEL_BLOCK = 64
SEL_TOPK = 16
WINDOW = 512
Q_BLOCK = 32
FORCE_SCORE = 1e4
REL_BUCKETS = 32
REL_MAX_DIST = 128
CA_HEADS = 4
CA_HEAD_DIM = D_MODEL // CA_HEADS
D_FF = 2816
CONV_W = 3
NORM_EPS = 1e-6
QK_EPS = 1e-6
RWKV_COLS = 3 * R_WIDTH + R_LORA_W + R_LORA_A + R_LORA_G
NSA_KV = N_KV * HEAD_DIM
NSA_COLS = N_HEADS * HEAD_DIM + 6 * NSA_KV + 3 * N_HEADS
GATE_COLS = 2 * D_MODEL
IN_COLS = RWKV_COLS + NSA_COLS + GATE_COLS

kernel_name = "hybrid_rwkv7_nsa_gated_block"


def rms_norm(x, g, eps=NORM_EPS):
    xf = x.astype(jnp.float32)
    y = xf * lax.rsqrt(jnp.mean(xf * xf, axis=-1, keepdims=True) + eps)
    return (y * g.astype(jnp.float32)).astype(x.dtype)


def masked_softmax(logits, mask):
    logits = jnp.where(mask, logits.astype(jnp.float32), -jnp.inf)
    m = jnp.max(logits, axis=-1, keepdims=True)
    m = jnp.where(jnp.isfinite(m), m, 0.0)
    e = jnp.exp(logits - m)
    s = jnp.sum(e, axis=-1, keepdims=True)
    return e / jnp.where(s > 0, s, 1.0)


def t5_bucket(dist):
    n = jnp.maximum(dist, 0)
    exact = REL_BUCKETS // 2
    nf = jnp.maximum(n, 1).astype(jnp.float32)
    large = exact + (jnp.log(nf / exact) / math.log(REL_MAX_DIST / exact) * (REL_BUCKETS - exact)).astype(jnp.int32)
    large = jnp.minimum(large, REL_BUCKETS - 1)
    return jnp.where(n < exact, n, large)


def group_bias(table, bucket):
    tg = table.astype(jnp.float32).reshape(REL_BUCKETS, N_KV, N_HG)
    gi = jnp.arange(N_KV).reshape(1, N_KV, 1, 1)
    return jnp.moveaxis(tg[bucket, gi], -1, -2)


def token_shift(z, mu):
    prev = jnp.pad(z[:, :-1], ((0, 0), (1, 0), (0, 0)))
    return z + (prev - z) * mu


def rwkv7_scan(r, w, k, v, a, b):
    B, S, H, N = r.shape
    xs = tuple(jnp.swapaxes(t.astype(jnp.float32), 0, 1) for t in (r, w, k, v, a, b))

    def step(state, inp):
        r_t, w_t, k_t, v_t, a_t, b_t = inp
        sa = jnp.einsum('bhvk,bhk->bhv', state, a_t)
        state = state * w_t[:, :, None, :] + sa[..., None] * b_t[:, :, None, :] + v_t[..., None] * k_t[:, :, None, :]
        return state, jnp.einsum('bhvk,bhk->bhv', state, r_t)

    s0 = jnp.zeros((B, H, N, N), jnp.float32)
    _, ys = lax.scan(step, s0, xs)
    return jnp.swapaxes(ys, 0, 1)


def rwkv7_mixer(z, mu, w0, w2, a0, a2, g2, k_k, k_a, r_k, lnx_w, lnx_b):
    B, S, _ = z.shape
    z = token_shift(z, mu)
    r, k, v, zw, za, zg = jnp.split(z, np.cumsum([R_WIDTH, R_WIDTH, R_WIDTH, R_LORA_W, R_LORA_A]).tolist(), axis=-1)
    w = -jax.nn.softplus(-(w0 + jnp.tanh(zw) @ w2)) - 0.5
    a = jax.nn.sigmoid(a0 + za @ a2)
    g = jax.nn.sigmoid(zg) @ g2
    hd = lambda t: t.reshape(B, S, R_HEADS, R_HEAD)
    kk = hd(k * k_k).astype(jnp.float32)
    kk = kk / jnp.maximum(jnp.sqrt(jnp.sum(kk * kk, axis=-1, keepdims=True)), 1e-12)
    k = k * (1 + (a - 1) * k_a)
    decay = jnp.exp(-jnp.exp(w.astype(jnp.float32)))
    a_h = hd(a).astype(jnp.float32)
    y = rwkv7_scan(hd(r), hd(decay), hd(k), hd(v), -kk, kk * a_h)
    mean = jnp.mean(y, axis=-1, keepdims=True)
    var = jnp.mean(jnp.square(y - mean), axis=-1, keepdims=True)
    y = ((y - mean) * lax.rsqrt(var + LNX_EPS)).reshape(B, S, R_WIDTH)
    y = y * lnx_w.astype(jnp.float32) + lnx_b.astype(jnp.float32)
    bonus = jnp.sum(hd(r) * hd(k) * r_k, axis=-1, keepdims=True) * hd(v)
    out = (y + bonus.reshape(B, S, R_WIDTH).astype(jnp.float32)) * g.astype(jnp.float32)
    return out.astype(z.dtype)


def nsa_mixer(q, kc, vc, ks, vs, kw, vw, gates, rel_table, q_gain, k_gain, pe_k, pe_v, c1k, c2k, c1v, c2v):
    B, S, _ = q.shape
    q = rms_norm(q.reshape(B, S, N_KV, N_HG, HEAD_DIM), q_gain, QK_EPS)
    kvh = lambda t: t.reshape(B, S, N_KV, HEAD_DIM)
    scale = HEAD_DIM ** -0.5
    n_cmp = (S - CMP_BLOCK) // CMP_STRIDE + 1
    cidx = jnp.arange(n_cmp)[:, None] * CMP_STRIDE + jnp.arange(CMP_BLOCK)[None, :]

    def compress(t, pe, w1, w2):
        blk = kvh(t)[:, cidx] + pe[:, None, :]
        hid = jax.nn.gelu(jnp.einsum('bnlgd,ldc->bngc', blk, w1.reshape(CMP_BLOCK, HEAD_DIM, CMP_HIDDEN)))
        return jnp.einsum('bngc,cd->bngd', hid, w2)

    k_cmp = rms_norm(compress(kc, pe_k, c1k, c2k), k_gain[0], QK_EPS)
    v_cmp = compress(vc, pe_v, c1v, c2v)
    cmp_end = jnp.arange(n_cmp) * CMP_STRIDE + CMP_BLOCK - 1
    n_blk = S // SEL_BLOCK
    n_top = min(SEL_TOPK, n_blk)
    ci = jnp.arange(n_cmp)[:, None] * CMP_STRIDE
    sj = jnp.arange(n_blk)[None, :] * SEL_BLOCK
    cmp_to_sel = ((ci <= sj + SEL_BLOCK - 1) & (ci + CMP_BLOCK - 1 >= sj)).astype(jnp.float32)
    kb = rms_norm(kvh(ks), k_gain[1], QK_EPS).reshape(B, n_blk, SEL_BLOCK, N_KV, HEAD_DIM).transpose(0, 3, 1, 2, 4)
    vb = kvh(vs).reshape(B, n_blk, SEL_BLOCK, N_KV, HEAD_DIM).transpose(0, 3, 1, 2, 4)
    pad = ((0, 0), (WINDOW, 0), (0, 0), (0, 0))
    k_win = jnp.pad(rms_norm(kvh(kw), k_gain[2], QK_EPS), pad)
    v_win = jnp.pad(kvh(vw), pad)
    g_all = jax.nn.sigmoid(gates.astype(jnp.float32)).reshape(B, S, N_KV, N_HG, 3)
    nq = S // Q_BLOCK
    q_blocks = jnp.swapaxes(q.reshape(B, nq, Q_BLOCK, N_KV, N_HG, HEAD_DIM), 0, 1)
    g_blocks = jnp.swapaxes(g_all.reshape(B, nq, Q_BLOCK, N_KV, N_HG, 3), 0, 1)
    starts = jnp.arange(nq, dtype=jnp.int32) * Q_BLOCK
    bi = jnp.arange(B)[:, None, None, None]
    gi = jnp.arange(N_KV)[None, :, None, None]
    blk_ids = jnp.arange(n_blk)

    def block(args):
        qb, gb, q0 = args
        t = q0 + jnp.arange(Q_BLOCK)
        s_c = jnp.einsum('bqghd,bngd->bgqhn', qb, k_cmp).astype(jnp.float32) * scale
        s_c = s_c + group_bias(rel_table, t5_bucket(t[:, None] - cmp_end[None, :])[None, None])
        p_c = masked_softmax(s_c, (cmp_end[None, :] <= t[:, None])[None, None, :, None, :])
        o_c = jnp.einsum('bgqhn,bngd->bqghd', p_c.astype(v_cmp.dtype), v_cmp)
        imp = jnp.einsum('bgqhn,nj->bgqj', p_c, cmp_to_sel)
        cur = t // SEL_BLOCK
        forced = (blk_ids[None, :] == 0) | (blk_ids[None, :] == cur[:, None]) | (blk_ids[None, :] == cur[:, None] - 1)
        valid = blk_ids[None, :] * SEL_BLOCK <= t[:, None]
        imp = jnp.where(valid, jnp.where(forced, FORCE_SCORE, imp), -jnp.inf)
        _, idx = lax.top_k(imp, n_top)
        k_g = kb[bi, gi, idx].reshape(B, N_KV, Q_BLOCK, n_top * SEL_BLOCK, HEAD_DIM)
        v_g = vb[bi, gi, idx].reshape(B, N_KV, Q_BLOCK, n_top * SEL_BLOCK, HEAD_DIM)
        kpos = (idx[..., None] * SEL_BLOCK + jnp.arange(SEL_BLOCK)).reshape(B, N_KV, Q_BLOCK, n_top * SEL_BLOCK)
        dist_s = t[None, None, :, None] - kpos
        s_s = jnp.einsum('bqghd,bgqkd->bgqhk', qb, k_g).astype(jnp.float32) * scale + group_bias(rel_table, t5_bucket(dist_s))
        p_s = masked_softmax(s_s, (dist_s >= 0)[:, :, :, None, :])
        o_s = jnp.einsum('bgqhk,bgqkd->bqghd', p_s.astype(v_g.dtype), v_g)
        kw_b = lax.dynamic_slice_in_dim(k_win, q0, WINDOW + Q_BLOCK, axis=1)
        vw_b = lax.dynamic_slice_in_dim(v_win, q0, WINDOW + Q_BLOCK, axis=1)
        wpos = q0 - WINDOW + jnp.arange(WINDOW + Q_BLOCK)
        dist_w = t[:, None] - wpos[None, :]
        m_w = (wpos[None, :] >= 0) & (dist_w >= 0) & (dist_w < WINDOW)
        s_w = jnp.einsum('bqghd,bkgd->bgqhk', qb, kw_b).astype(jnp.float32) * scale
        s_w = s_w + group_bias(rel_table, t5_bucket(dist_w)[None, None])
        p_w = masked_softmax(s_w, m_w[None, None, :, None, :])
        o_w = jnp.einsum('bgqhk,bkgd->bqghd', p_w.astype(vw_b.dtype), vw_b)
        out = gb[..., 0:1] * o_c + gb[..., 1:2] * o_s + gb[..., 2:3] * o_w
        return out.astype(qb.dtype)

    o = lax.map(block, (q_blocks, g_blocks, starts))
    return jnp.swapaxes(o, 0, 1).reshape(B, S, N_HEADS * HEAD_DIM)


def cross_attn(xn, mn, wq, wkv, q_gain, k_gain, wo):
    B, S, _ = xn.shape
    M = mn.shape[1]
    q = rms_norm((xn @ wq).reshape(B, S, CA_HEADS, CA_HEAD_DIM), q_gain, QK_EPS)
    kv = (mn @ wkv).reshape(B, M, 2, CA_HEADS, CA_HEAD_DIM)
    k = rms_norm(kv[:, :, 0], k_gain, QK_EPS)
    v = kv[:, :, 1]
    s = jnp.einsum('bshd,bmhd->bhsm', q, k).astype(jnp.float32) * (CA_HEAD_DIM ** -0.5)
    p = jax.nn.softmax(s, axis=-1).astype(v.dtype)
    o = jnp.einsum('bhsm,bmhd->bshd', p, v).reshape(B, S, CA_HEADS * CA_HEAD_DIM)
    return o @ wo


def conv_ffn(xn, w_up, conv_w, conv_b, w_down):
    a, b = jnp.split(xn @ w_up, 2, axis=-1)
    a = lax.conv_general_dilated(a, conv_w[:, None, :].astype(a.dtype), (1,), [(CONV_W - 1, 0)],
                                 dimension_numbers=('NWC', 'WIO', 'NWC'), feature_group_count=D_FF) + conv_b
    return (jax.nn.silu(a) * b) @ w_down


def setup_inputs(seed: int = 0) -> dict:
    key = jax.random.key(seed)
    ks = iter(jax.random.split(key, 64))
    L, D = DEPTH, D_MODEL
    nrm = lambda shape, scale: jax.random.normal(next(ks), shape, jnp.float32) * scale
    lin = lambda fi, fo: nrm((L, fi, fo), fi ** -0.5)
    gain = lambda *s: 1.0 + nrm((L,) + s, 0.02)
    return {
        "x": nrm((BATCH, SEQ, D), 1.0),
        "mem": nrm((BATCH, N_MEM, D), 1.0),
        "rel_bias": nrm((REL_BUCKETS, N_HEADS), 0.1),
        "norm_mix": gain(D),
        "w_in": lin(D, IN_COLS),
        "rwkv_mu": jax.random.uniform(next(ks), (L, RWKV_COLS), jnp.float32),
        "rwkv_w0": jax.random.uniform(next(ks), (L, R_WIDTH), jnp.float32, -6.0, 0.5),
        "rwkv_w2": nrm((L, R_LORA_W, R_WIDTH), 0.5 * R_LORA_W ** -0.5),
        "rwkv_a0": nrm((L, R_WIDTH), 0.1),
        "rwkv_a2": nrm((L, R_LORA_A, R_WIDTH), 0.5 * R_LORA_A ** -0.5),
        "rwkv_g2": lin(R_LORA_G, R_WIDTH),
        "rwkv_kk": 0.85 + nrm((L, R_WIDTH), 0.02),
        "rwkv_ka": 1.0 + nrm((L, R_WIDTH), 0.02),
        "rwkv_rk": nrm((L, R_HEADS, R_HEAD), 0.1),
        "rwkv_lnx_w": gain(R_WIDTH),
        "rwkv_lnx_b": nrm((L, R_WIDTH), 0.02),
        "nsa_q_gain": gain(HEAD_DIM),
        "nsa_k_gain": gain(3, HEAD_DIM),
        "cmp_pe_k": nrm((L, CMP_BLOCK, HEAD_DIM), 0.1),
        "cmp_pe_v": nrm((L, CMP_BLOCK, HEAD_DIM), 0.1),
        "cmp_w1_k": lin(CMP_BLOCK * HEAD_DIM, CMP_HIDDEN),
        "cmp_w2_k": lin(CMP_HIDDEN, HEAD_DIM),
        "cmp_w1_v": lin(CMP_BLOCK * HEAD_DIM, CMP_HIDDEN),
        "cmp_w2_v": lin(CMP_HIDDEN, HEAD_DIM),
        "w_branch_rwkv": lin(R_WIDTH, D),
        "w_branch_nsa": lin(N_HEADS * HEAD_DIM, D),
        "w_mix_out": lin(D, D),
        "norm_cross": gain(D),
        "norm_mem": gain(D),
        "ca_wq": lin(D, CA_HEADS * CA_HEAD_DIM),
        "ca_wkv": lin(D, 2 * CA_HEADS * CA_HEAD_DIM),
        "ca_q_gain": gain(CA_HEAD_DIM),
        "ca_k_gain": gain(CA_HEAD_DIM),
        "ca_wo": lin(CA_HEADS * CA_HEAD_DIM, D),
        "norm_ffn": gain(D),
        "ffn_up": lin(D, 2 * D_FF),
        "ffn_conv": nrm((L, CONV_W, D_FF), CONV_W ** -0.5),
        "ffn_conv_b": nrm((L, D_FF), 0.02),
        "ffn_down": lin(D_FF, D),
    }


def reference(x, mem, rel_bias, norm_mix, w_in, rwkv_mu, rwkv_w0, rwkv_w2, rwkv_a0, rwkv_a2, rwkv_g2,
              rwkv_kk, rwkv_ka, rwkv_rk, rwkv_lnx_w, rwkv_lnx_b, nsa_q_gain, nsa_k_gain, cmp_pe_k, cmp_pe_v,
              cmp_w1_k, cmp_w2_k, cmp_w1_v, cmp_w2_v, w_branch_rwkv, w_branch_nsa, w_mix_out,
              norm_cross, norm_mem, ca_wq, ca_wkv, ca_q_gain, ca_k_gain, ca_wo,
              norm_ffn, ffn_up, ffn_conv, ffn_conv_b, ffn_down):
    B, S, D = x.shape
    split_pts = np.cumsum([RWKV_COLS, N_HEADS * HEAD_DIM] + [NSA_KV] * 6 + [3 * N_HEADS]).tolist()
    h = x
    for l in range(DEPTH):
        xn = rms_norm(h, norm_mix[l])
        z = xn @ w_in[l]
        z_r, z_q, z_kc, z_vc, z_ks, z_vs, z_kw, z_vw, z_g, z_m = jnp.split(z, split_pts, axis=-1)
        y_r = rwkv7_mixer(z_r, rwkv_mu[l], rwkv_w0[l], rwkv_w2[l], rwkv_a0[l], rwkv_a2[l], rwkv_g2[l],
                          rwkv_kk[l], rwkv_ka[l], rwkv_rk[l], rwkv_lnx_w[l], rwkv_lnx_b[l])
        y_n = nsa_mixer(z_q, z_kc, z_vc, z_ks, z_vs, z_kw, z_vw, z_g, rel_bias, nsa_q_gain[l], nsa_k_gain[l],
                        cmp_pe_k[l], cmp_pe_v[l], cmp_w1_k[l], cmp_w2_k[l], cmp_w1_v[l], cmp_w2_v[l])
        gm = jax.nn.sigmoid(z_m.astype(jnp.float32)).reshape(B, S, 2, D)
        merged = gm[:, :, 0] * (y_r @ w_branch_rwkv[l]) + gm[:, :, 1] * (y_n @ w_branch_nsa[l])
        h = h + merged.astype(h.dtype) @ w_mix_out[l]
        h = h + cross_attn(rms_norm(h, norm_cross[l]), rms_norm(mem, norm_mem[l]), ca_wq[l], ca_wkv[l],
                           ca_q_gain[l], ca_k_gain[l], ca_wo[l])
        h = h + conv_ffn(rms_norm(h, norm_ffn[l]), ffn_up[l], ffn_conv[l], ffn_conv_b[l], ffn_down[l])
    return h
```

```python
import contextlib
import math
import numpy as np
import concourse.bass as bass
import concourse.mybir as mybir
from concourse.bass_utils import run_bass_kernel_spmd

F32 = mybir.dt.float32
BF16 = mybir.dt.bfloat16
AF = mybir.ActivationFunctionType
ALU = mybir.AluOpType
AX = mybir.AxisListType

NCORES = 8
NB = 2
T = 4096
D = 1024
NMEM = 256
DFF = 2816
IN_COLS = 8016
BIG = 30000.0
OFFC = 2176
LVEC = 7680
SCALE_NSA = 0.125


class Buf:
    __slots__ = ("w", "r", "name", "dj", "xr")

    def __init__(self, name="", dj=False):
        self.w = {}
        self.r = {}
        self.name = name
        self.dj = dj
        self.xr = False


class Tile:
    def __init__(self, t, name, dj=False):
        self.t = t
        self.b = Buf(name, dj)

    def __getitem__(self, k):
        return self.t[k]


class Ring:
    def __init__(self, tiles):
        self.tiles = tiles
        self.i = 0

    def next(self):
        t = self.tiles[self.i]
        self.i = (self.i + 1) % len(self.tiles)
        return t


class TK:
    EPOCH = 20000
    NDSEM = 10

    def __init__(self, nc, es):
        self.nc = nc
        self.es = es
        self.eng = {"pe": nc.tensor, "act": nc.scalar, "dve": nc.vector,
                    "pool": nc.gpsimd, "sp": nc.sync}
        self.cnt = {e: 0 for e in self.eng}
        self.esem = {e: [] for e in self.eng}
        self.seen = {e: {} for e in self.eng}
        self.dsem = {}
        self.dptr = {}
        self.nwait = 0
        self.fence = {}

    def _newsem(self, name):
        return self.es.enter_context(self.nc.semaphore(name))

    def _engsem(self, e, epoch):
        while len(self.esem[e]) <= epoch:
            self.esem[e].append(self._newsem(f"s_{e}_{len(self.esem[e])}"))
        return self.esem[e][epoch]

    def _wait(self, e, ts):
        sem, val, src = ts
        if src == "pe" and e == "pe":
            return
        k = id(sem)
        if self.seen[e].get(k, 0) >= val:
            return
        self.seen[e][k] = val
        self.eng[e].wait_ge(sem, val)
        self.nwait += 1

    def deps(self, e, reads, writes):
        for b in reads:
            for ts in b.w.values():
                self._wait(e, ts)
            if b.xr:
                for ts in b.r.values():
                    if ts[2] != e:
                        self._wait(e, ts)
        for b in writes:
            if not (b.dj and not b.r):
                for ts in b.w.values():
                    self._wait(e, ts)
            for ts in b.r.values():
                self._wait(e, ts)

    def mark(self, ts, reads, writes):
        k = id(ts[0])
        for b in reads:
            b.r[k] = ts
        for b in writes:
            if b.dj and not b.r:
                b.w[k] = ts
            else:
                b.w = {k: ts}
                b.r = {}

    def op(self, e, ins_fn, reads=(), writes=()):
        self.deps(e, reads, writes)
        n = self.cnt[e]
        sem = self._engsem(e, n // self.EPOCH)
        val = n % self.EPOCH + 1
        ins_fn().then_inc(sem, 1)
        self.cnt[e] = n + 1
        ts = (sem, val, e)
        self.mark(ts, reads, writes)
        return ts

    def dma(self, q, out_ap, in_ap, reads=(), writes=(), **kw):
        if q not in self.dsem:
            self.dsem[q] = [[self._newsem(f"d_{q}_{i}"), 0] for i in range(self.NDSEM)]
            self.dptr[q] = 0
        slot = self.dsem[q][self.dptr[q]]
        self.dptr[q] = (self.dptr[q] + 1) % self.NDSEM
        sem, issued = slot
        if issued:
            self._wait(q, (sem, 16 * issued, None))
        self.deps(q, reads, writes)
        self.eng[q].dma_start(out=out_ap, in_=in_ap, **kw).then_inc(sem, 16)
        slot[1] = issued + 1
        ts = (sem, 16 * (issued + 1), None)
        self.mark(ts, reads, writes)
        return ts

    def update_fence(self):
        f = {}
        for e in self.eng:
            n = self.cnt[e]
            if n:
                sem = self.esem[e][(n - 1) // self.EPOCH]
                f[id(sem)] = (sem, (n - 1) % self.EPOCH + 1, e)
        for q in self.dsem:
            for sem, issued in self.dsem[q]:
                if issued:
                    f[id(sem)] = (sem, 16 * issued, None)
        self.fence = f

    def drain(self):
        for q in self.dsem:
            for sem, issued in self.dsem[q]:
                if issued:
                    self._wait(q, (sem, 16 * issued, None))


def _t5_bucket_np(dist):
    n = np.maximum(dist, 0)
    nf = np.maximum(n, 1).astype(np.float64)
    large = 16 + (np.log(nf / 16) / math.log(128 / 16) * 16).astype(np.int64)
    large = np.minimum(large, 31)
    return np.where(n < 16, n, large)


def host_consts():
    c = {}
    c["c_ident"] = np.eye(128, dtype=np.float32)
    c["c_J"] = np.ascontiguousarray(np.eye(128, dtype=np.float32)[::-1])
    hb = np.arange(128) // 64
    bd = (hb[:, None] == hb[None, :]).astype(np.float32)
    c["c_bdones"] = bd
    c["c_ones"] = np.ones((128, 128), np.float32)
    s = np.arange(128) % 64
    strict = bd * (s[:, None] < s[None, :])
    incl = bd * (s[:, None] <= s[None, :])
    c["c_mask2"] = np.concatenate([strict, incl], axis=1).astype(np.float32)
    bd5 = np.zeros((128, 5, 2, 64), np.float32)
    for h in range(2):
        bd5[64 * h:64 * h + 64, :, h, :] = 1.0
    c["c_bdmask5"] = bd5.reshape(128, 640)
    seg = np.ones((128, 1024), np.float32)
    seg[:, ::64] = 0.0
    c["c_segmask"] = seg
    dist = np.arange(LVEC) - OFFC
    bk = _t5_bucket_np(dist)
    oh = np.zeros((33, LVEC), np.float32)
    oh[bk, np.arange(LVEC)] = 1.0
    ec = oh.copy()
    ec[32] = np.where(dist >= 0, 0.0, -BIG)
    ec[:32, dist < 0] = 0.0
    ew = oh.copy()
    ok = (dist >= 0) & (dist < 512)
    ew[32] = np.where(ok, 0.0, -BIG)
    ew[:32, ~ok] = 0.0
    c["c_e33c"] = ec
    c["c_e33w"] = ew
    t = np.arange(T)
    cur = t // 64
    blk = np.arange(64)
    forced = (blk[:, None] == 0) | (blk[:, None] == cur[None, :]) | (blk[:, None] == cur[None, :] - 1)
    c["c_forced"] = np.where(forced, 1e4, 0.0).astype(np.float32)
    ex = np.zeros((64, 32, 128), np.float32)
    for kt in range(32):
        for p in range(128):
            ex[2 * kt + p // 64, kt, p] = 1.0
    c["c_expand"] = ex.reshape(64, 32 * 128)
    ncmp = 255
    ci = np.arange(256)[:, None] * 16
    sj = np.arange(64)[None, :] * 64
    c2s = ((ci <= sj + 63) & (ci + 31 >= sj)).astype(np.float32)
    c2s[ncmp:] = 0.0
    c["c_c2s"] = c2s
    return c


CONST_SHAPES = {k: v.shape for k, v in host_consts().items()}

W_SPECS = [
    ("w_in", 1024, IN_COLS), ("rwkv_w2", 64, 1024), ("rwkv_a2", 64, 1024), ("rwkv_g2", 160, 1024),
    ("cmp_w1_k", 2048, 256), ("cmp_w2_k", 256, 64), ("cmp_w1_v", 2048, 256), ("cmp_w2_v", 256, 64),
    ("w_branch_rwkv", 1024, 1024), ("w_branch_nsa", 1024, 1024), ("w_mix_out", 1024, 1024),
    ("ca_wq", 1024, 1024), ("ca_wkv", 1024, 2048), ("ca_wo", 1024, 1024),
    ("ffn_up", 1024, 2 * DFF), ("ffn_down", DFF, 1024),
]
V_SPECS = [
    ("rel_bias", (32, 16)), ("norm_mix", (1, 1024)), ("rwkv_mu", (1, 3360)), ("rwkv_w0", (1, 1024)),
    ("rwkv_a0", (1, 1024)), ("rwkv_kk", (1, 1024)), ("rwkv_ka", (1, 1024)), ("rwkv_rk", (1, 1024)),
    ("rwkv_lnx_w", (1, 1024)), ("rwkv_lnx_b", (1, 1024)), ("nsa_q_gain", (1, 64)), ("nsa_k_gain", (3, 64)),
    ("cmp_pe_k", (1, 2048)), ("cmp_pe_v", (1, 2048)), ("norm_cross", (1, 1024)), ("norm_mem", (1, 1024)),
    ("ca_q_gain", (1, 256)), ("ca_k_gain", (1, 256)), ("norm_ffn", (1, 1024)),
    ("ffn_conv", (3, DFF)), ("ffn_conv_b", (1, DFF)),
]


class Prog:
    def __init__(self, upto="all", dbg=()):
        self.upto = upto
        self.dbg = set(dbg)
        nc = self.nc = bass.Bass("TRN2", target_bir_lowering=False)
        self.es = contextlib.ExitStack()
        self.tk = TK(nc, self.es)
        self.din = {}
        self.dbuf = {}

    def dram_in(self, name, shape):
        self.din[name] = self.nc.dram_tensor(name, list(shape), F32, kind="ExternalInput")
        self.dbuf[name] = Buf(name, dj=True)
        return self.din[name]

    def scratch(self, name, shape, dt):
        if name in self.din:
            return self.din[name]
        kind = "ExternalOutput" if name in self.dbg else "Internal"
        self.din[name] = self.nc.dram_tensor(name, list(shape), dt, kind=kind)
        self.dbuf[name] = Buf(name, dj=True)
        return self.din[name]

    def sb(self, es, name, shape, dt, dj=False):
        self.uid = getattr(self, "uid", 0) + 1
        name = f"{name}_{self.uid}"
        t = Tile(es.enter_context(self.nc.sbuf_tensor(name, list(shape), dt)), name, dj)
        t.b.r = dict(self.tk.fence)
        return t

    @contextlib.contextmanager
    def scope(self):
        with contextlib.ExitStack() as es:
            yield es
        self.tk.update_fence()

    def ring(self, es, name, n, shape, dt):
        return Ring([self.sb(es, f"{name}{i}", shape, dt) for i in range(n)])

    def psum(self):
        return self.psr.next()

    def dap(self, name, offset, ap):
        return bass.AP(tensor=self.din[name], offset=offset, ap=[list(x) for x in ap])

    def load_const(self, es, name, dt=F32, tmp_es=None):
        nc, tk = self.nc, self.tk
        shp = CONST_SHAPES[name]
        t32 = self.sb(es if dt == F32 else tmp_es, name + "_f", shp, F32)
        tk.dma("sp", t32[:], self.din[name].ap()[:, :], reads=[self.dbuf[name]], writes=[t32.b])
        if dt == F32:
            return t32
        t16 = self.sb(es, name + "_h", shp, BF16)
        tk.op("dve", lambda: nc.vector.tensor_copy(t16[:], t32[:]), reads=[t32.b], writes=[t16.b])
        return t16

    def bcast_vec(self, es, name, row, c0, n, tname):
        t = self.sb(es, tname, [128, n], F32)
        src = self.din[name].ap()[row:row + 1, c0:c0 + n].partition_broadcast(128)
        self.tk.dma("sp", t[:], src, reads=[self.dbuf[name]], writes=[t.b])
        return t

    def col_vec(self, es, name, row, c0, nchunk, tname, p=128):
        nc, tk = self.nc, self.tk
        t = self.sb(es, tname, [p, nchunk], F32)
        ncols = self.din[name].shape[1]
        with self.scope() as es2:
            raw = self.sb(es2, tname + "_raw", [nchunk, p], F32)
            tk.dma("sp", raw[:], self.dap(name, row * ncols + c0, [[p, nchunk], [1, p]]), reads=[self.dbuf[name]], writes=[raw.b])
            ps_ = self.psum()
            tk.op("pe", lambda: nc.tensor.transpose(ps_[:p, 0:nchunk], raw[:], self.ident[:nchunk, :nchunk]),
                  reads=[raw.b, self.ident.b], writes=[ps_.b])
            tk.op("dve", lambda: nc.vector.tensor_copy(t[:], ps_[:p, 0:nchunk]), reads=[ps_.b], writes=[t.b])
        return t

    def phase_w(self):
        nc, tk = self.nc, self.tk
        with self.scope() as es:
            st = self.ring(es, "wst", 3, [128, 2048], F32)
            sh = self.ring(es, "wsh", 3, [128, 2048], BF16)
            k = 0
            for name, R, C in W_SPECS:
                dst = self.scratch(name + "_bf", [R, C], BF16)
                src = self.din[name].ap()
                for r0 in range(0, R, 128):
                    rr = min(128, R - r0)
                    for c0 in range(0, C, 2048):
                        cc = min(2048, C - c0)
                        a = st.next()
                        h = sh.next()
                        tk.dma("sp", a[:rr, :cc], src[r0:r0 + rr, c0:c0 + cc], reads=[self.dbuf[name]], writes=[a.b])
                        e = ("dve", "pool", "act")[k % 3]
                        k += 1
                        if e == "act":
                            tk.op(e, lambda: nc.scalar.copy(h[:rr, :cc], a[:rr, :cc]), reads=[a.b], writes=[h.b])
                        elif e == "dve":
                            tk.op(e, lambda: nc.vector.tensor_copy(h[:rr, :cc], a[:rr, :cc]), reads=[a.b], writes=[h.b])
                        else:
                            tk.op(e, lambda: nc.gpsimd.tensor_copy(h[:rr, :cc], a[:rr, :cc]), reads=[a.b], writes=[h.b])
                        tk.dma("pool", dst.ap()[r0:r0 + rr, c0:c0 + cc], h[:rr, :cc], reads=[h.b],
                               writes=[self.dbuf[name + "_bf"]])

    def norm_T(self, src_name, src_row0, ntok, gname, dstT):
        nc, tk = self.nc, self.tk
        with self.scope() as es:
            gbc = self.bcast_vec(es, gname, 0, 0, D, "nt_g")
            xr = self.ring(es, "nt_x", 2, [128, D], F32)
            xs = self.ring(es, "nt_xs", 2, [128, D], F32)
            junk = self.sb(es, "nt_junk", [128, D], BF16)
            st = self.ring(es, "nt_st", 2, [128, 4], F32)
            src = self.din[src_name].ap()
            for i in range(ntok // 128):
                x = xr.next()
                s = st.next()
                y = xs.next()
                tk.dma("sp", x[:], src[src_row0 + i * 128: src_row0 + (i + 1) * 128, :],
                       reads=[self.dbuf[src_name]], writes=[x.b])
                tk.op("act", lambda: nc.scalar.activation(out=junk[:], in_=x[:], func=AF.Square, accum_out=s[:, 0:1]),
                      reads=[x.b], writes=[junk.b, s.b])
                tk.op("dve", lambda: nc.vector.tensor_scalar(s[:, 1:2], s[:, 0:1], 1.0 / D, 1e-6, ALU.mult, ALU.add),
                      reads=[s.b], writes=[s.b])
                tk.op("act", lambda: nc.scalar.sqrt(s[:, 2:3], s[:, 1:2]), reads=[s.b], writes=[s.b])
                tk.op("dve", lambda: nc.vector.reciprocal(s[:, 3:4], s[:, 2:3]), reads=[s.b], writes=[s.b])
                tk.op("dve", lambda: nc.vector.scalar_tensor_tensor(out=y[:], in0=x[:], scalar=s[:, 3:4], in1=gbc[:],
                                                                    op0=ALU.mult, op1=ALU.mult),
                      reads=[x.b, s.b, gbc.b], writes=[y.b])
                for half in range(2):
                    p = self.psum()
                    for j in range(4):
                        kc = half * 4 + j
                        tk.op("pe", lambda: nc.tensor.transpose(p[:, j * 128:(j + 1) * 128], y[:, kc * 128:(kc + 1) * 128],
                                                                self.ident[:]),
                              reads=[y.b, self.ident.b], writes=[p.b])
                    o = dstT[:, half * 4:half * 4 + 4, i * 128:(i + 1) * 128]
                    pin = p[:, :].rearrange("p (a b) -> p a b", a=4)
                    if half == 0:
                        tk.op("act", lambda: nc.scalar.copy(o, pin), reads=[p.b], writes=[dstT.b])
                    else:
                        tk.op("dve", lambda: nc.vector.tensor_copy(o, pin), reads=[p.b], writes=[dstT.b])

    def load_w(self, tile, wname, c0, ncols, kchunks=8, r0=0):
        C = self.din[wname].shape[1]
        src = self.dap(wname, r0 * C + c0, [[C, 128], [128 * C, kchunks], [1, ncols]])
        self.tk.dma("sp", tile[:, 0:kchunks, 0:ncols], src, reads=[self.dbuf[wname]], writes=[tile.b])

    def proj_fm(self, wname, c0, ncols_total, actT, ntok, epi, kchunks=8, wring=None):
        nc, tk = self.nc, self.tk
        nct = (ncols_total + 127) // 128
        for ci in range(nct):
            cc = min(128, ncols_total - ci * 128)
            w = wring.next()
            self.load_w(w, wname, c0 + ci * 128, cc, kchunks)
            for tt in range(ntok // 512):
                p = self.psum()
                for kc in range(kchunks):
                    tk.op("pe", lambda: nc.tensor.matmul(p[:cc, :], lhsT=w[:, kc, 0:cc],
                                                          rhs=actT[:, kc, tt * 512:(tt + 1) * 512],
                                                          start=(kc == 0), stop=(kc == kchunks - 1)),
                          reads=[w.b, actT.b], writes=[p.b])
                epi(p, ci, tt, cc)

    def phase_b(self, xT):
        nc, tk = self.nc, self.tk
        self.scratch("zr_fm", [3360, T], F32)
        self.scratch("q_fm", [1024, T], BF16)
        self.scratch("kcvc_fm", [512, T], BF16)
        self.scratch("ks_fm", [256, T], BF16)
        self.scratch("kw_fm", [256, T], BF16)
        self.scratch("vsw_tm", [T, 512], BF16)
        self.scratch("gates_fm", [48, T], F32)
        self.scratch("gm_fm", [2048, T], F32)
        with self.scope() as es:
            wring = self.ring(es, "pb_w", 2, [128, 8, 128], BF16)
            o32 = self.ring(es, "pb_o32", 3, [128, 512], F32)
            o16 = self.ring(es, "pb_o16", 3, [128, 512], BF16)
            sq = self.ring(es, "pb_sq", 2, [128, 512], F32)
            qg = self.sb(es, "pb_qg", [128, 4], F32)
            eps = self.sb(es, "pb_eps", [128, 1], F32)
            tk.op("pool", lambda: nc.gpsimd.memset(eps[:], 1e-6), writes=[eps.b])
            for h in range(2):
                tk.dma("sp", qg[64 * h:64 * h + 64, 0:1], self.dap("nsa_q_gain", 0, [[1, 64], [1, 1]]),
                       reads=[self.dbuf["nsa_q_gain"]], writes=[qg.b])
                for j in (1, 2):
                    tk.dma("sp", qg[64 * h:64 * h + 64, j + 1:j + 2], self.dap("nsa_k_gain", 64 * j, [[1, 64], [1, 1]]),
                           reads=[self.dbuf["nsa_k_gain"]], writes=[qg.b])
            cnt = [0]

            def store(dname, row0, t0, tile, rows):
                tk.dma("pool", self.din[dname].ap()[row0:row0 + rows, t0:t0 + 512], tile[:rows, :], reads=[tile.b],
                       writes=[self.dbuf[dname]])

            def epi_copy(dname, row_base, dt):
                def f(p, ci, tt, cc):
                    o = (o32 if dt == F32 else o16).next()
                    cnt[0] += 1
                    if cnt[0] % 2:
                        tk.op("act", lambda: nc.scalar.copy(o[:cc, :], p[:cc, :]), reads=[p.b], writes=[o.b])
                    else:
                        tk.op("dve", lambda: nc.vector.tensor_copy(o[:cc, :], p[:cc, :]), reads=[p.b], writes=[o.b])
                    store(dname, row_base + ci * 128, tt * 512, o, cc)
                return f

            def epi_sig(dname, row_base):
                def f(p, ci, tt, cc):
                    o = o32.next()
                    tk.op("act", lambda: nc.scalar.activation(out=o[:cc, :], in_=p[:cc, :], func=AF.Sigmoid),
                          reads=[p.b], writes=[o.b])
                    store(dname, row_base + ci * 128, tt * 512, o, cc)
                return f

            def epi_norm(dname, row_base, gcol, scale):
                def f(p, ci, tt, cc):
                    s = sq.next()
                    tk.op("act", lambda: nc.scalar.activation(out=s[:], in_=p[:], func=AF.Square), reads=[p.b], writes=[s.b])
                    p2 = self.psum()
                    tk.op("pe", lambda: nc.tensor.matmul(p2[:], lhsT=self.bdones[:], rhs=s[:], start=True, stop=True),
                          reads=[self.bdones.b, s.b], writes=[p2.b])
                    r = o32.next()
                    tk.op("dve", lambda: nc.vector.tensor_scalar(r[:], p2[:], 1.0 / 64, 1e-6, ALU.mult, ALU.add),
                          reads=[p2.b], writes=[r.b])
                    tk.op("act", lambda: nc.scalar.sqrt(r[:], r[:]), reads=[r.b], writes=[r.b])
                    tk.op("dve", lambda: nc.vector.reciprocal(r[:], r[:]), reads=[r.b], writes=[r.b])
                    tk.op("dve", lambda: nc.vector.tensor_tensor(out=r[:], in0=p[:], in1=r[:], op=ALU.mult),
                          reads=[p.b, r.b], writes=[r.b])
                    o = o16.next()
                    tk.op("dve", lambda: nc.vector.tensor_scalar(o[:], r[:], qg[:, gcol:gcol + 1], scale, ALU.mult, ALU.mult),
                          reads=[r.b, qg.b], writes=[o.b])
                    store(dname, row_base + ci * 128, tt * 512, o, cc)
                return f

            segs = [
                (0, 3360, epi_copy("zr_fm", 0, F32)),
                (3360, 1024, epi_norm("q_fm", 0, 0, SCALE_NSA)),
                (4384, 512, epi_copy("kcvc_fm", 0, BF16)),
                (4896, 256, epi_norm("ks_fm", 0, 2, 1.0)),
                (5408, 256, epi_norm("kw_fm", 0, 3, 1.0)),
                (5920, 48, epi_sig("gates_fm", 0)),
                (5968, 2048, epi_sig("gm_fm", 0)),
            ]
            for c0, n, epi in segs:
                self.proj_fm("w_in_bf", c0, n, xT, T, epi, wring=wring)
            wv = self.sb(es, "pb_wv", [128, 8, 512], BF16)
            self.load_w(wv, "w_in_bf", 5152, 256)
            C = IN_COLS
            tk.dma("sp", wv[:, :, 256:512], self.dap("w_in_bf", 5664, [[C, 128], [128 * C, 8], [1, 256]]),
                   reads=[self.dbuf["w_in_bf"]], writes=[wv.b])
            for i in range(T // 128):
                p = self.psum()
                for kc in range(8):
                    tk.op("pe", lambda: nc.tensor.matmul(p[:], lhsT=xT[:, kc, i * 128:(i + 1) * 128], rhs=wv[:, kc, :],
                                                          start=(kc == 0), stop=(kc == 7)), reads=[xT.b, wv.b], writes=[p.b])
                o = o16.next()
                tk.op("act", lambda: nc.scalar.copy(o[:], p[:]), reads=[p.b], writes=[o.b])
                tk.dma("pool", self.din["vsw_tm"].ap()[i * 128:(i + 1) * 128, :], o[:], reads=[o.b], writes=[self.dbuf["vsw_tm"]])

    def build(self):
        nc, tk = self.nc, self.tk
        self.dram_in("x", [NB * T, D])
        self.dram_in("mem", [NB * NMEM, D])
        for name, R, C in W_SPECS:
            self.dram_in(name, [R, C])
        for name, shp in V_SPECS:
            self.dram_in(name, shp)
        for name, shp in CONST_SHAPES.items():
            self.dram_in(name, shp)
        self.out = self.nc.dram_tensor("out", [NB * T, D], F32, kind="ExternalOutput")
        self.din["out"] = self.out
        self.dbuf["out"] = Buf("out", dj=True)
        es = self.es
        self.psr = Ring([Tile(es.enter_context(nc.psum_tensor(f"ps{i}", [128, 512], F32)), f"ps{i}") for i in range(8)])
        for t_ in self.psr.tiles:
            t_.b.xr = True
        self.ident = self.load_const(es, "c_ident")
        self.bdones = self.load_const(es, "c_bdones")
        self.phase_w()
        if self.upto == "w":
            return self.finish()
        self.nsa_bias()
        for bi in range(NB):
            self.seq(bi)
            if self.upto != "all":
                break
        return self.finish()

    def seq(self, bi):
        tk = self.tk
        with self.scope() as es1:
            xT = self.sb(es1, "xT", [128, 8, T], BF16, dj=True)
            self.norm_T("x", bi * T, T, "norm_mix", xT)
            if bi == 0 and "xT_dbg" in self.dbg:
                d = self.scratch("xT_dbg", [128, 8 * T], BF16)
                tk.dma("sp", d.ap()[:, :], xT[:, :, :].rearrange("p a b -> p (a b)"), reads=[xT.b], writes=[self.dbuf["xT_dbg"]])
            if self.upto == "a":
                return
            self.phase_b(xT)
        if self.upto == "b":
            return
        if not getattr(self, "skip_rwkv", False):
            self.phase_rwkv()
        if self.upto == "rwkv":
            return
        self.phase_nsa()
        if self.upto == "nsa":
            return
        self.phase_merge(bi)
        if self.upto == "merge":
            return
        self.phase_cross(bi)
        if self.upto == "cross":
            return
        self.phase_ffn(bi)

    def finish(self):
        self.tk.drain()
        self.es.close()
        return self.nc


def make_in_maps(inputs, cores=range(NCORES)):
    consts = host_consts()
    shared = {}
    for name, R, C in W_SPECS:
        shared[name] = np.ascontiguousarray(np.asarray(inputs[name], np.float32).reshape(R, C))
    for name, shp in V_SPECS:
        shared[name] = np.ascontiguousarray(np.asarray(inputs[name], np.float32).reshape(shp))
    shared.update(consts)
    x = np.asarray(inputs["x"], np.float32)
    mem = np.asarray(inputs["mem"], np.float32)
    maps = []
    for c in cores:
        m = dict(shared)
        m["x"] = np.ascontiguousarray(x[NB * c:NB * c + NB].reshape(NB * T, D))
        m["mem"] = np.ascontiguousarray(mem[NB * c:NB * c + NB].reshape(NB * NMEM, D))
        maps.append(m)
    return maps


def kernel(**inputs):
    prog = Prog()
    nc = prog.build()
    maps = make_in_maps(inputs)
    res = run_bass_kernel_spmd(nc, maps, core_ids=list(range(NCORES)))
    outs = [np.asarray(r["out"]).reshape(NB, T, D) for r in res.results]
    return np.concatenate(outs, axis=0).astype(np.float32)


def _rwkv(self):
    nc, tk = self.nc, self.tk
    TB = 512
    self.scratch("yr_fm", [1024, T], BF16)
    zr = self.din["zr_fm"].ap()
    zb = self.dbuf["zr_fm"]

    def shift_load(dst_ap, dst_buf, r0, nrows, t0, nt, mucol, X, dtile):
        if t0 == 0:
            tk.op("pool", lambda: nc.gpsimd.memset(X[:nrows, 0:1], 0.0), writes=[X.b])
            tk.dma("sp", X[:nrows, 1:nt + 1], zr[r0:r0 + nrows, 0:nt], reads=[zb], writes=[X.b])
        else:
            tk.dma("sp", X[:nrows, 0:nt + 1], zr[r0:r0 + nrows, t0 - 1:t0 + nt], reads=[zb], writes=[X.b])
        tk.op("pool", lambda: nc.gpsimd.tensor_tensor(out=dtile[:nrows, :nt], in0=X[:nrows, 0:nt], in1=X[:nrows, 1:nt + 1],
                                                      op=ALU.subtract), reads=[X.b], writes=[dtile.b])
        tk.op("dve", lambda: nc.vector.scalar_tensor_tensor(out=dst_ap, in0=dtile[:nrows, :nt], scalar=mucol,
                                                             in1=X[:nrows, 1:nt + 1], op0=ALU.mult, op1=ALU.add),
              reads=[dtile.b, X.b], writes=[dst_buf])

    with self.scope() as es:
        mask4 = self.sb(es, "rk_mask4", [128, 512], F32)
        for j in range(2):
            tk.dma("sp", mask4[:, j * 256:(j + 1) * 256], self.din["c_mask2"].ap()[:, :], reads=[self.dbuf["c_mask2"]], writes=[mask4.b])
        bdm5 = self.load_const(es, "c_bdmask5")
        segm = self.sb(es, "rk_seg", [128, TB], F32)
        tk.dma("sp", segm[:], self.din["c_segmask"].ap()[:, 0:TB], reads=[self.dbuf["c_segmask"]], writes=[segm.b])
        lw = self.sb(es, "rk_lw", [64, T], BF16)
        la = self.sb(es, "rk_la", [64, T], BF16)
        lg = self.sb(es, "rk_lg", [128, 2, T], BF16)
        w2 = self.sb(es, "rk_w2", [64, 1024], BF16)
        a2 = self.sb(es, "rk_a2", [64, 1024], BF16)
        g2 = self.sb(es, "rk_g2", [128, 2, 1024], BF16)
        tk.dma("sp", w2[:], self.din["rwkv_w2_bf"].ap()[:, :], reads=[self.dbuf["rwkv_w2_bf"]], writes=[w2.b])
        tk.dma("sp", a2[:], self.din["rwkv_a2_bf"].ap()[:, :], reads=[self.dbuf["rwkv_a2_bf"]], writes=[a2.b])
        tk.dma("sp", g2[:, 0, :], self.din["rwkv_g2_bf"].ap()[0:128, :], reads=[self.dbuf["rwkv_g2_bf"]], writes=[g2.b])
        tk.dma("sp", g2[0:32, 1, :], self.din["rwkv_g2_bf"].ap()[128:160, :], reads=[self.dbuf["rwkv_g2_bf"]], writes=[g2.b])
        pc = {}
        for nm in ("rwkv_w0", "rwkv_a0", "rwkv_kk", "rwkv_ka", "rwkv_rk", "rwkv_lnx_w", "rwkv_lnx_b"):
            pc[nm] = self.col_vec(es, nm, 0, 0, 8, "rk_" + nm)
        mu = self.col_vec(es, "rwkv_mu", 0, 0, 24, "rk_mu")
        omk = self.sb(es, "rk_omk", [128, 8], F32)
        tk.op("dve", lambda: nc.vector.tensor_scalar(omk[:], pc["rwkv_ka"][:], -1.0, 1.0, ALU.mult, ALU.add),
              reads=[pc["rwkv_ka"].b], writes=[omk.b])
        with self.scope() as es2:
            X = self.sb(es2, "rk_LX", [128, T + 1], F32)
            dt_ = self.sb(es2, "rk_Ld", [128, T], F32)
            zt = self.sb(es2, "rk_Lz", [128, T], F32)
            for (r0, nrows, kind) in ((3072, 64, "w"), (3136, 64, "a"), (3200, 128, "g0"), (3328, 32, "g1")):
                mucol = self.sb(es2, "rk_Lmu" + kind, [128, 1], F32)
                tk.dma("sp", mucol[:nrows, :], self.dap("rwkv_mu", r0, [[1, nrows], [1, 1]]), reads=[self.dbuf["rwkv_mu"]], writes=[mucol.b])
                shift_load(zt[:nrows, :], zt.b, r0, nrows, 0, T, mucol[:nrows, 0:1], X, dt_)
                if kind == "w":
                    tk.op("act", lambda: nc.scalar.activation(out=lw[:, :], in_=zt[:64, :], func=AF.Tanh), reads=[zt.b], writes=[lw.b])
                elif kind == "a":
                    tk.op("act", lambda: nc.scalar.copy(la[:, :], zt[:64, :]), reads=[zt.b], writes=[la.b])
                elif kind == "g0":
                    tk.op("act", lambda: nc.scalar.activation(out=lg[:, 0, :], in_=zt[:, :], func=AF.Sigmoid), reads=[zt.b], writes=[lg.b])
                else:
                    tk.op("act", lambda: nc.scalar.activation(out=lg[:32, 1, :], in_=zt[:32, :], func=AF.Sigmoid), reads=[zt.b], writes=[lg.b])
        LIM = getattr(self, "rk_lim", 99)
        if LIM <= 1:
            return
        f = lambda n: self.sb(es, n, [128, TB], F32)
        Xr = self.ring(es, "rk_X", 2, [128, TB + 1], F32)
        dtl = f("rk_d")
        rr, kp, logw, aa, gg, kkr, sq, kmod, kb, cum, cex, epv, eng, bonus, tmp = [f("rk_t%d" % i) for i in range(15)]
        einr = self.ring(es, "rk_ein", 2, [128, TB], F32)
        Q5r = self.ring(es, "rk_Q5", 2, [128, 5, TB], F32)
        yfm = f("rk_yfm")
        dd = f("rk_dd")
        ob = self.ring(es, "rk_ob", 2, [128, TB], BF16)
        BD5r = self.ring(es, "rk_BD5", 2, [128, 5, 2, 64], F32)
        GBKr = self.ring(es, "rk_GBK", 2, [128, 512], F32)
        NTr = self.ring(es, "rk_NT", 2, [128, 128], F32)
        MXr = self.ring(es, "rk_MX", 3, [128, 256], F32)
        MTr = self.ring(es, "rk_MT", 3, [128, 128], F32)
        TTr = self.ring(es, "rk_TT", 2, [128, 128], F32)
        TM3r = self.ring(es, "rk_TM3", 2, [128, 384], F32)
        RHr = self.ring(es, "rk_RH", 2, [128, 128], F32)
        Ur = self.ring(es, "rk_U", 2, [128, 128], F32)
        Sr = self.ring(es, "rk_S", 2, [128, 128], F32)
        SPr = self.ring(es, "rk_SP", 2, [128, 128], F32)
        ident, bdones = self.ident, self.bdones

        def mm(p_ap, pbuf, lhsT, lb, rhs, rb, start=True, stop=True):
            tk.op("pe", lambda: nc.tensor.matmul(p_ap, lhsT=lhsT, rhs=rhs, start=start, stop=stop), reads=[lb, rb], writes=[pbuf])

        for hp in range(8):
            c0 = 128 * hp
            S = Sr.next()
            tk.op("pool", lambda: nc.gpsimd.memset(S[:], 0.0), writes=[S.b])
            for tb in range(T // TB):
                t0 = tb * TB
                Q5 = Q5r.next()
                ein = einr.next()
                shift_load(rr[:, :], rr.b, c0, 128, t0, TB, mu[:, hp:hp + 1], Xr.next(), dtl)
                shift_load(kp[:, :], kp.b, 1024 + c0, 128, t0, TB, mu[:, 8 + hp:9 + hp], Xr.next(), dtl)
                shift_load(Q5[:, 4, :], Q5.b, 2048 + c0, 128, t0, TB, mu[:, 16 + hp:17 + hp], Xr.next(), dtl)
                p = self.psum()
                mm(p[:], p.b, w2[:, c0:c0 + 128], w2.b, lw[:, t0:t0 + TB], lw.b)
                tk.op("act", lambda: nc.scalar.activation(out=logw[:], in_=p[:], func=AF.Sigmoid, bias=pc["rwkv_w0"][:, hp:hp + 1]),
                      reads=[p.b, pc["rwkv_w0"].b], writes=[logw.b])
                tk.op("pool", lambda: nc.gpsimd.tensor_scalar_mul(logw[:], logw[:], -math.exp(-0.5)), reads=[logw.b], writes=[logw.b])
                p = self.psum()
                mm(p[:], p.b, a2[:, c0:c0 + 128], a2.b, la[:, t0:t0 + TB], la.b)
                tk.op("act", lambda: nc.scalar.activation(out=aa[:], in_=p[:], func=AF.Sigmoid, bias=pc["rwkv_a0"][:, hp:hp + 1]),
                      reads=[p.b, pc["rwkv_a0"].b], writes=[aa.b])
                p = self.psum()
                mm(p[:], p.b, g2[:, 0, c0:c0 + 128], g2.b, lg[:, 0, t0:t0 + TB], lg.b, True, False)
                mm(p[:], p.b, g2[:32, 1, c0:c0 + 128], g2.b, lg[:32, 1, t0:t0 + TB], lg.b, False, True)
                tk.op("act", lambda: nc.scalar.copy(gg[:], p[:]), reads=[p.b], writes=[gg.b])
                tk.op("dve", lambda: nc.vector.tensor_scalar_mul(kkr[:], kp[:], pc["rwkv_kk"][:, hp:hp + 1]),
                      reads=[kp.b, pc["rwkv_kk"].b], writes=[kkr.b])
                tk.op("act", lambda: nc.scalar.activation(out=sq[:], in_=kkr[:], func=AF.Square), reads=[kkr.b], writes=[sq.b])
                p = self.psum()
                mm(p[:], p.b, bdones[:], bdones.b, sq[:], sq.b)
                tk.op("act", lambda: nc.scalar.sqrt(tmp[:], p[:]), reads=[p.b], writes=[tmp.b])
                tk.op("dve", lambda: nc.vector.tensor_scalar_max(tmp[:], tmp[:], 1e-12), reads=[tmp.b], writes=[tmp.b])
                tk.op("dve", lambda: nc.vector.reciprocal(tmp[:], tmp[:]), reads=[tmp.b], writes=[tmp.b])
                tk.op("dve", lambda: nc.vector.tensor_tensor(out=kkr[:], in0=kkr[:], in1=tmp[:], op=ALU.mult), reads=[kkr.b, tmp.b], writes=[kkr.b])
                tk.op("dve", lambda: nc.vector.tensor_scalar(kmod[:], aa[:], pc["rwkv_ka"][:, hp:hp + 1], omk[:, hp:hp + 1], ALU.mult, ALU.add),
                      reads=[aa.b, pc["rwkv_ka"].b, omk.b], writes=[kmod.b])
                tk.op("pool", lambda: nc.gpsimd.tensor_tensor(out=kmod[:], in0=kmod[:], in1=kp[:], op=ALU.mult), reads=[kmod.b, kp.b], writes=[kmod.b])
                tk.op("pool", lambda: nc.gpsimd.tensor_tensor(out=kb[:], in0=kkr[:], in1=aa[:], op=ALU.mult), reads=[kkr.b, aa.b], writes=[kb.b])
                tk.op("dve", lambda: nc.vector.scalar_tensor_tensor(out=tmp[:], in0=rr[:], scalar=pc["rwkv_rk"][:, hp:hp + 1], in1=kmod[:],
                                                                    op0=ALU.mult, op1=ALU.mult), reads=[rr.b, kmod.b, pc["rwkv_rk"].b], writes=[tmp.b])
                p = self.psum()
                mm(p[:], p.b, bdones[:], bdones.b, tmp[:], tmp.b)
                tk.op("dve", lambda: nc.vector.tensor_tensor(out=bonus[:], in0=p[:], in1=Q5[:, 4, :], op=ALU.mult), reads=[p.b, Q5.b], writes=[bonus.b])
                tk.op("dve", lambda: nc.vector.tensor_tensor_scan(out=cum[:], data0=segm[:], data1=logw[:], initial=0.0, op0=ALU.mult, op1=ALU.add),
                      reads=[segm.b, logw.b], writes=[cum.b])
                tk.op("pool", lambda: nc.gpsimd.tensor_tensor(out=cex[:], in0=cum[:], in1=logw[:], op=ALU.subtract), reads=[cum.b, logw.b], writes=[cex.b])
                tk.op("act", lambda: nc.scalar.activation(out=epv[:], in_=cex[:], func=AF.Exp), reads=[cex.b], writes=[epv.b])
                tk.op("act", lambda: nc.scalar.activation(out=ein[:], in_=cum[:], func=AF.Exp), reads=[cum.b], writes=[ein.b])
                tk.op("act", lambda: nc.scalar.activation(out=eng[:], in_=cum[:], func=AF.Exp, scale=-1.0), reads=[cum.b], writes=[eng.b])
                tk.op("dve", lambda: nc.vector.scalar_tensor_tensor(out=Q5[:, 0, :], in0=kkr[:], scalar=-1.0, in1=epv[:], op0=ALU.mult, op1=ALU.mult),
                      reads=[kkr.b, epv.b], writes=[Q5.b])
                tk.op("pool", lambda: nc.gpsimd.tensor_tensor(out=Q5[:, 1, :], in0=rr[:], in1=ein[:], op=ALU.mult), reads=[rr.b, ein.b], writes=[Q5.b])
                tk.op("dve", lambda: nc.vector.tensor_tensor(out=Q5[:, 2, :], in0=kb[:], in1=eng[:], op=ALU.mult), reads=[kb.b, eng.b], writes=[Q5.b])
                tk.op("pool", lambda: nc.gpsimd.tensor_tensor(out=Q5[:, 3, :], in0=kmod[:], in1=eng[:], op=ALU.mult), reads=[kmod.b, eng.b], writes=[Q5.b])
                if LIM <= 2:
                    return
                for c in range(TB // 64):
                    cs = slice(c * 64, (c + 1) * 64)
                    BD5 = BD5r.next()
                    src = Q5[:, :, cs].unsqueeze(2).to_broadcast([128, 5, 2, 64])
                    tk.op("dve", lambda: nc.vector.tensor_tensor(out=BD5[:], in0=src, in1=bdm5[:, :].rearrange("p (a h b) -> p a h b", a=5, h=2),
                                                                 op=ALU.mult), reads=[Q5.b, bdm5.b], writes=[BD5.b])
                    bd = lambda j: BD5[:, j, :, :].rearrange("p h b -> p (h b)")
                    p = self.psum()
                    ar = BD5[:, 0:2, :, :].rearrange("p a h b -> p (a h b)")
                    mm(p[:, 0:256], p.b, bd(2), BD5.b, ar, BD5.b)
                    mm(p[:, 256:512], p.b, bd(3), BD5.b, ar, BD5.b)
                    GBK = GBKr.next()
                    tk.op("dve", lambda: nc.vector.tensor_tensor(out=GBK[:], in0=p[:], in1=mask4[:], op=ALU.mult), reads=[p.b, mask4.b], writes=[GBK.b])
                    p3 = self.psum()
                    for j in range(3):
                        tk.op("pe", lambda: nc.tensor.transpose(p3[:, j * 128:(j + 1) * 128], bd(2 + j), ident[:]), reads=[BD5.b, ident.b], writes=[p3.b])
                    TM3 = TM3r.next()
                    tk.op("act", lambda: nc.scalar.copy(TM3[:], p3[:, 0:384]), reads=[p3.b], writes=[TM3.b])
                    if LIM <= 3:
                        return
                    NT = NTr.next()
                    p = self.psum()
                    tk.op("pe", lambda: nc.tensor.transpose(p[:, 0:128], GBK[:, 0:128], ident[:]), reads=[GBK.b, ident.b], writes=[p.b])
                    tk.op("act", lambda: nc.scalar.copy(NT[:], p[:, 0:128]), reads=[p.b], writes=[NT.b])
                    if LIM <= 3.2:
                        return
                    MX = MXr.next()
                    MT = MTr.next()
                    tk.op("pool", lambda: nc.gpsimd.tensor_tensor(out=MX[:, 128:256], in0=ident[:], in1=GBK[:, 0:128], op=ALU.add),
                          reads=[ident.b, GBK.b], writes=[MX.b])
                    if LIM <= 3.4:
                        return
                    p = self.psum()
                    mm(p[:, 0:128], p.b, NT[:], NT.b, GBK[:, 0:128], GBK.b)
                    pt = self.psum()
                    mm(pt[:, 0:128], pt.b, GBK[:, 0:128], GBK.b, NT[:], NT.b)
                    tk.op("act", lambda: nc.scalar.copy(MX[:, 0:128], p[:, 0:128]), reads=[p.b], writes=[MX.b])
                    tk.op("dve", lambda: nc.vector.tensor_copy(MT[:], pt[:, 0:128]), reads=[pt.b], writes=[MT.b])
                    if LIM <= 3.6:
                        return
                    for j in range(2, 6):
                        pm = self.psum()
                        mm(pm[:, 0:128], pm.b, MT[:], MT.b, MX[:, 0:128], MX.b)
                        px = self.psum()
                        mm(px[:, 0:128], px.b, MT[:], MT.b, MX[:, 128:256], MX.b)
                        pt = self.psum()
                        mm(pt[:, 0:128], pt.b, MX[:, 0:128], MX.b, MT[:], MT.b)
                        MX2 = MXr.next()
                        MT2 = MTr.next()
                        tk.op("act", lambda: nc.scalar.copy(MX2[:, 0:128], pm[:, 0:128]), reads=[pm.b], writes=[MX2.b])
                        tk.op("dve", lambda: nc.vector.tensor_tensor(out=MX2[:, 128:256], in0=px[:, 0:128], in1=MX[:, 128:256], op=ALU.add),
                              reads=[px.b, MX.b], writes=[MX2.b])
                        tk.op("act", lambda: nc.scalar.copy(MT2[:], pt[:, 0:128]), reads=[pt.b], writes=[MT2.b])
                        MX, MT = MX2, MT2
                    p = self.psum()
                    mm(p[:, 0:128], p.b, MT[:], MT.b, MX[:, 128:256], MX.b)
                    TT = TTr.next()
                    tk.op("dve", lambda: nc.vector.tensor_tensor(out=TT[:], in0=p[:, 0:128], in1=MX[:, 128:256], op=ALU.add), reads=[p.b, MX.b], writes=[TT.b])
                    if LIM <= 4:
                        return
                    PCc = ein[:, c * 64 + 63:c * 64 + 64]
                    SP = SPr.next()
                    tk.op("act", lambda: nc.scalar.activation(out=SP[:], in_=S[:], func=AF.Identity, scale=PCc), reads=[S.b, ein.b], writes=[SP.b])
                    p = self.psum()
                    mm(p[:, 0:128], p.b, bd(0), BD5.b, S[:], S.b, True, False)
                    mm(p[:, 0:128], p.b, GBK[:, 256:384], GBK.b, TM3[:, 256:384], TM3.b, False, True)
                    RH = RHr.next()
                    tk.op("act", lambda: nc.scalar.copy(RH[:], p[:, 0:128]), reads=[p.b], writes=[RH.b])
                    p = self.psum()
                    mm(p[:, 0:128], p.b, TT[:], TT.b, RH[:], RH.b)
                    U = Ur.next()
                    tk.op("dve", lambda: nc.vector.tensor_copy(U[:], p[:, 0:128]), reads=[p.b], writes=[U.b])
                    p = self.psum()
                    mm(p[:, 0:128], p.b, S[:], S.b, bd(1), BD5.b, True, False)
                    mm(p[:, 0:128], p.b, U[:], U.b, GBK[:, 128:256], GBK.b, False, False)
                    mm(p[:, 0:128], p.b, TM3[:, 256:384], TM3.b, GBK[:, 384:512], GBK.b, False, True)
                    tk.op("dve", lambda: nc.vector.tensor_copy(yfm[0:64, cs], p[0:64, 0:64]), reads=[p.b], writes=[yfm.b])
                    tk.op("dve", lambda: nc.vector.tensor_copy(yfm[64:128, cs], p[64:128, 64:128]), reads=[p.b], writes=[yfm.b])
                    p = self.psum()
                    mm(p[:, 0:128], p.b, TM3[:, 0:128], TM3.b, U[:], U.b, True, False)
                    mm(p[:, 0:128], p.b, TM3[:, 128:256], TM3.b, TM3[:, 256:384], TM3.b, False, True)
                    S2 = Sr.next()
                    tk.op("dve", lambda: nc.vector.scalar_tensor_tensor(out=S2[:], in0=p[:, 0:128], scalar=PCc, in1=SP[:], op0=ALU.mult, op1=ALU.add),
                          reads=[p.b, ein.b, SP.b], writes=[S2.b])
                    S = S2
                if LIM <= 5:
                    return
                p = self.psum()
                mm(p[:], p.b, bdones[:], bdones.b, yfm[:], yfm.b)
                tk.op("dve", lambda: nc.vector.scalar_tensor_tensor(out=dd[:], in0=p[:], scalar=-1.0 / 64, in1=yfm[:], op0=ALU.mult, op1=ALU.add),
                      reads=[p.b, yfm.b], writes=[dd.b])
                tk.op("act", lambda: nc.scalar.activation(out=sq[:], in_=dd[:], func=AF.Square), reads=[dd.b], writes=[sq.b])
                p = self.psum()
                mm(p[:], p.b, bdones[:], bdones.b, sq[:], sq.b)
                tk.op("dve", lambda: nc.vector.tensor_scalar(tmp[:], p[:], 1.0 / 64, 64e-5, ALU.mult, ALU.add), reads=[p.b], writes=[tmp.b])
                tk.op("act", lambda: nc.scalar.sqrt(tmp[:], tmp[:]), reads=[tmp.b], writes=[tmp.b])
                tk.op("dve", lambda: nc.vector.reciprocal(tmp[:], tmp[:]), reads=[tmp.b], writes=[tmp.b])
                tk.op("dve", lambda: nc.vector.tensor_tensor(out=dd[:], in0=dd[:], in1=tmp[:], op=ALU.mult), reads=[dd.b, tmp.b], writes=[dd.b])
                tk.op("act", lambda: nc.scalar.activation(out=dd[:], in_=dd[:], func=AF.Identity, bias=pc["rwkv_lnx_b"][:, hp:hp + 1],
                                                          scale=pc["rwkv_lnx_w"][:, hp:hp + 1]),
                      reads=[dd.b, pc["rwkv_lnx_b"].b, pc["rwkv_lnx_w"].b], writes=[dd.b])
                tk.op("pool", lambda: nc.gpsimd.tensor_tensor(out=dd[:], in0=dd[:], in1=bonus[:], op=ALU.add), reads=[dd.b, bonus.b], writes=[dd.b])
                o = ob.next()
                tk.op("dve", lambda: nc.vector.tensor_tensor(out=o[:], in0=dd[:], in1=gg[:], op=ALU.mult), reads=[dd.b, gg.b], writes=[o.b])
                tk.dma("pool", self.din["yr_fm"].ap()[c0:c0 + 128, t0:t0 + TB], o[:], reads=[o.b], writes=[self.dbuf["yr_fm"]])
                if LIM <= 6:
                    return


Prog.phase_rwkv = _rwkv


def _nsa_bias(self):
    nc, tk = self.nc, self.tk
    self.scratch("bvec_c", [16, LVEC], BF16)
    self.scratch("bvec_w", [16, LVEC], BF16)
    with self.scope() as es:
        tab = self.sb(es, "nb_tab", [33, 16], F32)
        tk.op("pool", lambda: nc.gpsimd.memset(tab[:], 1.0), writes=[tab.b])
        tk.dma("sp", tab[0:32, :], self.din["rel_bias"].ap()[:, :], reads=[self.dbuf["rel_bias"]], writes=[tab.b])
        e33 = self.sb(es, "nb_e33", [33, LVEC], F32)
        ob = self.ring(es, "nb_o", 2, [16, 512], BF16)
        for cname, dname in (("c_e33c", "bvec_c"), ("c_e33w", "bvec_w")):
            tk.dma("sp", e33[:], self.din[cname].ap()[:, :], reads=[self.dbuf[cname]], writes=[e33.b])
            for j in range(LVEC // 512):
                p = self.psum()
                tk.op("pe", lambda: nc.tensor.matmul(p[:16, :], lhsT=tab[:], rhs=e33[:, j * 512:(j + 1) * 512], start=True, stop=True),
                      reads=[tab.b, e33.b], writes=[p.b])
                o = ob.next()
                tk.op("act", lambda: nc.scalar.copy(o[:], p[:16, :]), reads=[p.b], writes=[o.b])
                tk.dma("pool", self.din[dname].ap()[:, j * 512:(j + 1) * 512], o[:], reads=[o.b], writes=[self.dbuf[dname]])


def _nsa(self):
    nc, tk = self.nc, self.tk
    self.scratch("yn_fm", [1024, T], BF16)
    gen = Ring(self.psr.tiles[0:4])
    accp = Ring(self.psr.tiles[4:8])

    def mm(p_ap, pbuf, lhsT, lb, rhs, rb, start=True, stop=True):
        tk.op("pe", lambda: nc.tensor.matmul(p_ap, lhsT=lhsT, rhs=rhs, start=start, stop=stop), reads=[lb, rb], writes=[pbuf])

    with self.scope() as es:
        Jb = self.load_const(es, "c_J", BF16, tmp_es=es)
        c2s_f = self.sb(es, "ns_c2sf", [128, 2, 64], F32)
        tk.dma("sp", c2s_f[:, :, :], self.dap("c_c2s", 0, [[64, 128], [128 * 64, 2], [1, 64]]), reads=[self.dbuf["c_c2s"]], writes=[c2s_f.b])
        c2s = self.sb(es, "ns_c2s", [128, 2, 64], BF16)
        tk.op("dve", lambda: nc.vector.tensor_copy(c2s[:], c2s_f[:]), reads=[c2s_f.b], writes=[c2s.b])
        exf = self.sb(es, "ns_exf", [64, 4096], F32)
        tk.dma("sp", exf[:], self.din["c_expand"].ap()[:, :], reads=[self.dbuf["c_expand"]], writes=[exf.b])
        exb = self.sb(es, "ns_exb", [64, 4096], BF16)
        tk.op("dve", lambda: nc.vector.tensor_scalar_mul(exb[:], exf[:], BIG), reads=[exf.b], writes=[exb.b])
        ones = self.sb(es, "ns_ones", [128, 64], BF16)
        tk.op("pool", lambda: nc.gpsimd.memset(ones[:], 1.0), writes=[ones.b])
        kgain = self.col_vec(es, "nsa_k_gain", 0, 0, 1, "ns_kg", p=64)
        ident, bdones = self.ident, self.bdones
        kcmpT = [self.sb(es, f"ns_kcT{g}", [64, 256], BF16) for g in range(4)]
        vcmp = [self.sb(es, f"ns_vc{g}", [128, 2, 64], BF16) for g in range(4)]
        with self.scope() as es2:
            kc2 = self.sb(es2, "ns_kc2", [128, T], BF16)
            w1t = self.sb(es2, "ns_w1", [128, 16, 256], BF16)
            w2t = self.sb(es2, "ns_w2", [128, 2, 64], BF16)
            hg_ = self.sb(es2, "ns_hg", [128, 2, 256], BF16)
            xx = self.sb(es2, "ns_x", [128, 256], F32)
            x2 = self.sb(es2, "ns_x2", [128, 256], F32)
            pvb = self.sb(es2, "ns_pvb", [128, 2], F32)
            t64 = self.sb(es2, "ns_t64", [64, 256], F32)
            t64b = self.sb(es2, "ns_t64b", [64, 256], F32)
            for kind in range(2):
                sfx = "_k" if kind == 0 else "_v"
                self.load_w(w1t, "cmp_w1" + sfx + "_bf", 0, 256, kchunks=16)
                self.load_w(w2t, "cmp_w2" + sfx + "_bf", 0, 64, kchunks=2)
                pe_f = self.col_vec(es2, "cmp_pe" + sfx, 0, 0, 16, "ns_pe" + sfx)
                pe_b = self.sb(es2, "ns_peb" + sfx, [128, 16], BF16)
                tk.op("dve", lambda: nc.vector.tensor_copy(pe_b[:], pe_f[:]), reads=[pe_f.b], writes=[pe_b.b])
                for ct in range(2):
                    p = gen.next()
                    for l2 in range(16):
                        mm(p[:, 0:1], p.b, w1t[:, l2, ct * 128:(ct + 1) * 128], w1t.b, pe_b[:, l2:l2 + 1], pe_b.b, l2 == 0, l2 == 15)
                    tk.op("dve", lambda: nc.vector.tensor_copy(pvb[:, ct:ct + 1], p[:, 0:1]), reads=[p.b], writes=[pvb.b])
                for g in range(4):
                    r0 = 256 * kind + 64 * g
                    tk.op("pool", lambda: nc.gpsimd.memset(kc2[64:128, T - 1:T], 0.0), writes=[kc2.b])
                    tk.dma("sp", kc2[0:64, :], self.din["kcvc_fm"].ap()[r0:r0 + 64, :], reads=[self.dbuf["kcvc_fm"]], writes=[kc2.b])
                    tk.dma("sp", kc2[64:128, 0:T - 1], self.din["kcvc_fm"].ap()[r0:r0 + 64, 1:T], reads=[self.dbuf["kcvc_fm"]], writes=[kc2.b])
                    tk.op("pool", lambda: nc.gpsimd.memset(hg_[:], 0.0), writes=[hg_.b])
                    for ct in range(2):
                        p = gen.next()
                        for l2 in range(16):
                            rhs = kc2[:, 2 * l2: 2 * l2 + 16 * 254 + 1: 16]
                            mm(p[:, 0:255], p.b, w1t[:, l2, ct * 128:(ct + 1) * 128], w1t.b, rhs, kc2.b, l2 == 0, l2 == 15)
                        tk.op("act", lambda: nc.scalar.activation(out=xx[:, 0:255], in_=p[:, 0:255], func=AF.Identity, bias=pvb[:, ct:ct + 1]),
                              reads=[p.b, pvb.b], writes=[xx.b])
                        tk.op("act", lambda: nc.scalar.activation(out=x2[:, 0:255], in_=xx[:, 0:255], func=AF.Square), reads=[xx.b], writes=[x2.b])
                        tk.op("dve", lambda: nc.vector.tensor_scalar(x2[:, 0:255], x2[:, 0:255], 0.044715, 1.0, ALU.mult, ALU.add), reads=[x2.b], writes=[x2.b])
                        tk.op("dve", lambda: nc.vector.tensor_tensor(out=x2[:, 0:255], in0=x2[:, 0:255], in1=xx[:, 0:255], op=ALU.mult), reads=[x2.b, xx.b], writes=[x2.b])
                        tk.op("act", lambda: nc.scalar.activation(out=x2[:, 0:255], in_=x2[:, 0:255], func=AF.Sigmoid, scale=1.5957691216057308),
                              reads=[x2.b], writes=[x2.b])
                        tk.op("dve", lambda: nc.vector.tensor_tensor(out=hg_[:, ct, 0:255], in0=x2[:, 0:255], in1=xx[:, 0:255], op=ALU.mult),
                              reads=[x2.b, xx.b], writes=[hg_.b])
                    if kind == 0:
                        p = gen.next()
                        for ct in range(2):
                            mm(p[0:64, 0:256], p.b, w2t[:, ct, :], w2t.b, hg_[:, ct, :], hg_.b, ct == 0, ct == 1)
                        tk.op("act", lambda: nc.scalar.activation(out=t64[:], in_=p[0:64, 0:256], func=AF.Square), reads=[p.b], writes=[t64.b])
                        p2 = gen.next()
                        mm(p2[0:64, 0:256], p2.b, bdones[0:64, 0:64], bdones.b, t64[:], t64.b)
                        tk.op("dve", lambda: nc.vector.tensor_scalar(t64[:], p2[0:64, 0:256], 1.0 / 64, 1e-6, ALU.mult, ALU.add), reads=[p2.b], writes=[t64.b])
                        tk.op("act", lambda: nc.scalar.sqrt(t64[:], t64[:]), reads=[t64.b], writes=[t64.b])
                        tk.op("dve", lambda: nc.vector.reciprocal(t64[:], t64[:]), reads=[t64.b], writes=[t64.b])
                        tk.op("dve", lambda: nc.vector.tensor_tensor(out=t64b[:], in0=p[0:64, 0:256], in1=t64[:], op=ALU.mult), reads=[p.b, t64.b], writes=[t64b.b])
                        tk.op("dve", lambda: nc.vector.tensor_scalar_mul(kcmpT[g][:], t64b[:], kgain[:, 0:1]), reads=[t64b.b, kgain.b], writes=[kcmpT[g].b])
                    else:
                        for nt in range(2):
                            p = gen.next()
                            for ct in range(2):
                                mm(p[:, 0:64], p.b, hg_[:, ct, nt * 128:(nt + 1) * 128], hg_.b, w2t[:, ct, :], w2t.b, ct == 0, ct == 1)
                            tk.op("act", lambda: nc.scalar.copy(vcmp[g][:, nt, :], p[:, 0:64]), reads=[p.b], writes=[vcmp[g].b])
        if getattr(self, "ns_lim", 99) <= 1:
            return
        ksT = self.sb(es, "ns_ksT", [64, T], BF16)
        kwT = self.sb(es, "ns_kwT", [64, T], BF16)
        Vs = self.sb(es, "ns_Vs", [128, 32, 64], BF16)
        Vw = self.sb(es, "ns_Vw", [128, 32, 64], BF16)
        qTr = self.ring(es, "ns_qT", 2, [64, 4, 512], BF16)
        gbr = self.ring(es, "ns_gb", 2, [64, 12, 512], F32)
        Hr = self.ring(es, "ns_H", 4, [128, 512], BF16)
        Er = self.ring(es, "ns_E", 4, [128, 512], BF16)
        Ec = self.sb(es, "ns_Ec", [128, 8, 512], BF16, dj=True)
        acc = self.sb(es, "ns_acc", [64, 4, 512], F32)
        impa = self.sb(es, "ns_impa", [64, 512], F32)
        frc = self.ring(es, "ns_frc", 2, [64, 512], F32)
        rdr = self.ring(es, "ns_rd", 3, [64, 512], F32)
        t1r = self.ring(es, "ns_t1", 3, [64, 512], F32)
        impq = self.sb(es, "ns_impq", [128, 4, 64], F32)
        selq = self.sb(es, "ns_selq", [128, 4, 64], F32)
        wk = self.sb(es, "ns_wk", [128, 64], F32)
        m8 = self.sb(es, "ns_m8", [128, 16], F32)
        selT = self.sb(es, "ns_selT", [64, 512], BF16)
        obr = self.ring(es, "ns_ob", 2, [64, 512], BF16)
        vsw = self.din["vsw_tm"]

        def hankel(vname, h, c, pstep):
            H = Hr.next()
            src = self.dap(vname, h * LVEC + c, [[pstep, 128], [1, 512]])
            tk.dma("sp", H[:], src, reads=[self.dbuf[vname]], writes=[H.b])
            return H

        def finish_branch(pn, pd, hg, gcol, gb, first):
            rd = rdr.next()
            tk.op("dve", lambda: nc.vector.tensor_scalar_max(rd[:], pd[0:64, :], 1e-30), reads=[pd.b], writes=[rd.b])
            tk.op("dve", lambda: nc.vector.reciprocal(rd[:], rd[:]), reads=[rd.b], writes=[rd.b])
            t1 = t1r.next()
            tk.op("dve", lambda: nc.vector.tensor_tensor(out=t1[:], in0=pn[0:64, :], in1=rd[:], op=ALU.mult), reads=[pn.b, rd.b], writes=[t1.b])
            if first:
                tk.op("pool", lambda: nc.gpsimd.tensor_tensor(out=acc[:, hg, :], in0=t1[:], in1=gb[:, gcol, :], op=ALU.mult),
                      reads=[t1.b, gb.b], writes=[acc.b])
            else:
                tk.op("pool", lambda: nc.gpsimd.tensor_tensor(out=t1[:], in0=t1[:], in1=gb[:, gcol, :], op=ALU.mult), reads=[t1.b, gb.b], writes=[t1.b])
                tk.op("pool", lambda: nc.gpsimd.tensor_tensor(out=acc[:, hg, :], in0=acc[:, hg, :], in1=t1[:], op=ALU.add), reads=[t1.b, acc.b], writes=[acc.b])
            return rd

        for g in range(4):
            tk.dma("sp", ksT[:], self.din["ks_fm"].ap()[64 * g:64 * g + 64, :], reads=[self.dbuf["ks_fm"]], writes=[ksT.b])
            tk.dma("sp", kwT[:], self.din["kw_fm"].ap()[64 * g:64 * g + 64, :], reads=[self.dbuf["kw_fm"]], writes=[kwT.b])
            for k8 in range(4):
                tk.dma("sp", Vs[:, 8 * k8:8 * k8 + 8, :], self.dap("vsw_tm", 64 * g + 8 * k8 * 128 * 512, [[512, 128], [128 * 512, 8], [1, 64]]),
                       reads=[self.dbuf["vsw_tm"]], writes=[Vs.b])
                tk.dma("sp", Vw[:, 8 * k8:8 * k8 + 8, :], self.dap("vsw_tm", 256 + 64 * g + 8 * k8 * 128 * 512, [[512, 128], [128 * 512, 8], [1, 64]]),
                       reads=[self.dbuf["vsw_tm"]], writes=[Vw.b])
            for qt in range(T // 512):
                t0 = qt * 512
                qT = qTr.next()
                tk.dma("sp", qT[:], self.dap("q_fm", 256 * g * T + t0, [[T, 64], [64 * T, 4], [1, 512]]), reads=[self.dbuf["q_fm"]], writes=[qT.b])
                gb = gbr.next()
                for j in range(12):
                    row = 12 * g + j
                    tk.dma("sp", gb[:, j, :], self.din["gates_fm"].ap()[row:row + 1, t0:t0 + 512].partition_broadcast(64),
                           reads=[self.dbuf["gates_fm"]], writes=[gb.b])
                fr = frc.next()
                tk.dma("sp", fr[:], self.din["c_forced"].ap()[:, t0:t0 + 512], reads=[self.dbuf["c_forced"]], writes=[fr.b])
                nnt = 2 if t0 >= 2048 else 1
                for hg in range(4):
                    h = 4 * g + hg
                    pn, pd, pi = accp.next(), accp.next(), accp.next()
                    for nt in range(nnt):
                        H = hankel("bvec_c", h, OFFC + t0 - 16 * 128 * nt - 2063, 16)
                        p = gen.next()
                        mm(p[:], p.b, kcmpT[g][:, nt * 128:(nt + 1) * 128], kcmpT[g].b, qT[:, hg, :], qT.b, True, False)
                        mm(p[:], p.b, Jb[:], Jb.b, H[:], H.b, False, True)
                        e_ap = Ec[:, hg * 2 + nt, :]
                        tk.op("act", lambda: nc.scalar.activation(out=e_ap, in_=p[:], func=AF.Exp), reads=[p.b], writes=[Ec.b])
                        mm(pn[0:64, :], pn.b, vcmp[g][:, nt, :], vcmp[g].b, e_ap, Ec.b, nt == 0, nt == nnt - 1)
                        mm(pd[0:64, :], pd.b, ones[:], ones.b, e_ap, Ec.b, nt == 0, nt == nnt - 1)
                        mm(pi[0:64, :], pi.b, c2s[:, nt, :], c2s.b, e_ap, Ec.b, nt == 0, nt == nnt - 1)
                    rd = finish_branch(pn, pd, hg, 3 * hg + 0, gb, True)
                    if hg == 0:
                        tk.op("dve", lambda: nc.vector.tensor_tensor(out=impa[:], in0=pi[0:64, :], in1=rd[:], op=ALU.mult), reads=[pi.b, rd.b], writes=[impa.b])
                    else:
                        t1 = t1r.next()
                        tk.op("dve", lambda: nc.vector.tensor_tensor(out=t1[:], in0=pi[0:64, :], in1=rd[:], op=ALU.mult), reads=[pi.b, rd.b], writes=[t1.b])
                        tk.op("pool", lambda: nc.gpsimd.tensor_tensor(out=impa[:], in0=impa[:], in1=t1[:], op=ALU.add), reads=[impa.b, t1.b], writes=[impa.b])
                tk.op("dve", lambda: nc.vector.tensor_tensor(out=impa[:], in0=impa[:], in1=fr[:], op=ALU.max), reads=[impa.b, fr.b], writes=[impa.b])
                p = gen.next()
                for s4 in range(4):
                    tk.op("pe", lambda: nc.tensor.transpose(p[:, s4 * 64:(s4 + 1) * 64], impa[:, s4 * 128:(s4 + 1) * 128], ident[0:64, 0:64]),
                          reads=[impa.b, ident.b], writes=[p.b])
                tk.op("act", lambda: nc.scalar.copy(impq[:], p[:, 0:256].rearrange("p (a b) -> p a b", a=4)), reads=[p.b], writes=[impq.b])
                for s4 in range(4):
                    tk.op("dve", lambda: nc.vector.max(out=m8[:, 0:8], in_=impq[:, s4, :]), reads=[impq.b], writes=[m8.b])
                    tk.op("dve", lambda: nc.vector.match_replace(out=wk[:], in_to_replace=m8[:, 0:8], in_values=impq[:, s4, :], imm_value=-1e30),
                          reads=[impq.b, m8.b], writes=[wk.b])
                    tk.op("dve", lambda: nc.vector.max(out=m8[:, 8:16], in_=wk[:]), reads=[wk.b], writes=[m8.b])
                    tk.op("dve", lambda: nc.vector.tensor_scalar(selq[:, s4, :], impq[:, s4, :], m8[:, 15:16], 1.0, ALU.is_ge, ALU.subtract),
                          reads=[impq.b, m8.b], writes=[selq.b])
                p = gen.next()
                for s4 in range(4):
                    tk.op("pe", lambda: nc.tensor.transpose(p[0:64, s4 * 128:(s4 + 1) * 128], selq[:, s4, :], ident[:]),
                          reads=[selq.b, ident.b], writes=[p.b])
                tk.op("act", lambda: nc.scalar.copy(selT[:], p[0:64, :]), reads=[p.b], writes=[selT.b])
                for hg in range(4):
                    h = 4 * g + hg
                    kts = list(range(0, (t0 + 511) // 128 + 1))
                    pn, pd = accp.next(), accp.next()
                    for i, kt in enumerate(kts):
                        H = hankel("bvec_c", h, OFFC + t0 - 128 * kt - 127, 1)
                        p = gen.next()
                        mm(p[:], p.b, ksT[:, kt * 128:(kt + 1) * 128], ksT.b, qT[:, hg, :], qT.b, True, False)
                        mm(p[:], p.b, Jb[:], Jb.b, H[:], H.b, False, False)
                        mm(p[:], p.b, exb[:, kt * 128:(kt + 1) * 128], exb.b, selT[:], selT.b, False, True)
                        E = Er.next()
                        tk.op("act", lambda: nc.scalar.activation(out=E[:], in_=p[:], func=AF.Exp), reads=[p.b], writes=[E.b])
                        mm(pn[0:64, :], pn.b, Vs[:, kt, :], Vs.b, E[:], E.b, i == 0, i == len(kts) - 1)
                        mm(pd[0:64, :], pd.b, ones[:], ones.b, E[:], E.b, i == 0, i == len(kts) - 1)
                    finish_branch(pn, pd, hg, 3 * hg + 1, gb, False)
                    kts = list(range(max(0, (t0 - 512) // 128), (t0 + 511) // 128 + 1))
                    pn, pd = accp.next(), accp.next()
                    for i, kt in enumerate(kts):
                        H = hankel("bvec_w", h, OFFC + t0 - 128 * kt - 127, 1)
                        p = gen.next()
                        mm(p[:], p.b, kwT[:, kt * 128:(kt + 1) * 128], kwT.b, qT[:, hg, :], qT.b, True, False)
                        mm(p[:], p.b, Jb[:], Jb.b, H[:], H.b, False, True)
                        E = Er.next()
                        tk.op("act", lambda: nc.scalar.activation(out=E[:], in_=p[:], func=AF.Exp), reads=[p.b], writes=[E.b])
                        mm(pn[0:64, :], pn.b, Vw[:, kt, :], Vw.b, E[:], E.b, i == 0, i == len(kts) - 1)
                        mm(pd[0:64, :], pd.b, ones[:], ones.b, E[:], E.b, i == 0, i == len(kts) - 1)
                    finish_branch(pn, pd, hg, 3 * hg + 2, gb, False)
                    o = obr.next()
                    tk.op("act", lambda: nc.scalar.copy(o[:], acc[:, hg, :]), reads=[acc.b], writes=[o.b])
                    tk.dma("pool", self.din["yn_fm"].ap()[64 * h:64 * h + 64, t0:t0 + 512], o[:], reads=[o.b], writes=[self.dbuf["yn_fm"]])
                if getattr(self, "ns_lim", 99) <= 2:
                    return


Prog.nsa_bias = _nsa_bias
Prog.phase_nsa = _nsa


def _proj_tm_res(self, actT, tok0, ntok, kchunks, w, res_name, res_row0, dst_name, dst_row0, es):
    nc, tk = self.nc, self.tk
    xr = self.ring(es, "pt_x", 2, [128, D], F32)
    orr = self.ring(es, "pt_o", 2, [128, D], F32)
    for i in range(ntok // 128):
        x = xr.next()
        o = orr.next()
        tk.dma("sp", x[:], self.din[res_name].ap()[res_row0 + i * 128:res_row0 + (i + 1) * 128, :], reads=[self.dbuf[res_name]], writes=[x.b])
        for half in range(2):
            p = self.psum()
            for kc in range(kchunks):
                tk.op("pe", lambda: nc.tensor.matmul(p[:], lhsT=actT[:, kc, tok0 + i * 128:tok0 + (i + 1) * 128], rhs=w[:, kc, half * 512:(half + 1) * 512],
                                                      start=(kc == 0), stop=(kc == kchunks - 1)), reads=[actT.b, w.b], writes=[p.b])
            tk.op("dve", lambda: nc.vector.tensor_tensor(out=o[:, half * 512:(half + 1) * 512], in0=p[:], in1=x[:, half * 512:(half + 1) * 512], op=ALU.add),
                  reads=[p.b, x.b], writes=[o.b])
        tk.dma("pool", self.din[dst_name].ap()[dst_row0 + i * 128:dst_row0 + (i + 1) * 128, :], o[:], reads=[o.b], writes=[self.dbuf[dst_name]])


def _merge(self, bi):
    nc, tk = self.nc, self.tk
    self.scratch("h1", [T, D], F32)
    with self.scope() as es:
        mT = self.sb(es, "mg_mT", [128, 8, T], BF16, dj=True)
        with self.scope() as es2:
            wr = self.sb(es2, "mg_wr", [128, 8, 1024], BF16)
            wn = self.sb(es2, "mg_wn", [128, 8, 1024], BF16)
            self.load_w(wr, "w_branch_rwkv_bf", 0, 1024)
            self.load_w(wn, "w_branch_nsa_bf", 0, 1024)
            yr = self.ring(es2, "mg_yr", 2, [128, 8, 512], BF16)
            yn = self.ring(es2, "mg_yn", 2, [128, 8, 512], BF16)
            gr = self.ring(es2, "mg_g", 4, [128, 512], F32)
            tr = self.ring(es2, "mg_t", 4, [128, 512], F32)
            for tt in range(T // 512):
                a, b = yr.next(), yn.next()
                tk.dma("sp", a[:], self.dap("yr_fm", tt * 512, [[T, 128], [128 * T, 8], [1, 512]]), reads=[self.dbuf["yr_fm"]], writes=[a.b])
                tk.dma("sp", b[:], self.dap("yn_fm", tt * 512, [[T, 128], [128 * T, 8], [1, 512]]), reads=[self.dbuf["yn_fm"]], writes=[b.b])
                for ci in range(8):
                    g0, g1 = gr.next(), gr.next()
                    tk.dma("sp", g0[:], self.din["gm_fm"].ap()[ci * 128:(ci + 1) * 128, tt * 512:(tt + 1) * 512], reads=[self.dbuf["gm_fm"]], writes=[g0.b])
                    tk.dma("sp", g1[:], self.din["gm_fm"].ap()[1024 + ci * 128:1024 + (ci + 1) * 128, tt * 512:(tt + 1) * 512], reads=[self.dbuf["gm_fm"]], writes=[g1.b])
                    pr, pn = self.psum(), self.psum()
                    for kc in range(8):
                        tk.op("pe", lambda: nc.tensor.matmul(pr[:], lhsT=wr[:, kc, ci * 128:(ci + 1) * 128], rhs=a[:, kc, :], start=(kc == 0), stop=(kc == 7)),
                              reads=[wr.b, a.b], writes=[pr.b])
                    for kc in range(8):
                        tk.op("pe", lambda: nc.tensor.matmul(pn[:], lhsT=wn[:, kc, ci * 128:(ci + 1) * 128], rhs=b[:, kc, :], start=(kc == 0), stop=(kc == 7)),
                              reads=[wn.b, b.b], writes=[pn.b])
                    t0_, t1_ = tr.next(), tr.next()
                    tk.op("dve", lambda: nc.vector.tensor_tensor(out=t0_[:], in0=pr[:], in1=g0[:], op=ALU.mult), reads=[pr.b, g0.b], writes=[t0_.b])
                    tk.op("dve", lambda: nc.vector.tensor_tensor(out=t1_[:], in0=pn[:], in1=g1[:], op=ALU.mult), reads=[pn.b, g1.b], writes=[t1_.b])
                    tk.op("pool", lambda: nc.gpsimd.tensor_tensor(out=mT[:, ci, tt * 512:(tt + 1) * 512], in0=t0_[:], in1=t1_[:], op=ALU.add),
                          reads=[t0_.b, t1_.b], writes=[mT.b])
        with self.scope() as es3:
            wm = self.sb(es3, "mg_wm", [128, 8, 1024], BF16)
            self.load_w(wm, "w_mix_out_bf", 0, 1024)
            self.proj_tm_res(mT, 0, T, 8, wm, "x", bi * T, "h1", 0, es3)


def _cross(self, bi):
    nc, tk = self.nc, self.tk
    self.scratch("h2", [T, D], F32)
    HT = 2048
    with self.scope() as es:
        wq = self.sb(es, "ca_wq", [128, 8, 1024], BF16)
        wo = self.sb(es, "ca_wo", [128, 8, 1024], BF16)
        self.load_w(wq, "ca_wq_bf", 0, 1024)
        self.load_w(wo, "ca_wo_bf", 0, 1024)
        kT = self.sb(es, "ca_kT", [128, 8, NMEM], BF16, dj=True)
        Vc = self.sb(es, "ca_V", [128, 2, 1024], BF16, dj=True)
        qgain = self.col_vec(es, "ca_q_gain", 0, 0, 2, "ca_qg")
        kgain = self.col_vec(es, "ca_k_gain", 0, 0, 2, "ca_kg")
        ones_f = self.load_const(es, "c_ones")
        ones_b = self.sb(es, "ca_1b", [128, 128], BF16)
        tk.op("dve", lambda: nc.vector.tensor_copy(ones_b[:], ones_f[:]), reads=[ones_f.b], writes=[ones_b.b])
        sqr = self.ring(es, "ca_sq", 2, [128, 2, 512], F32)
        rr = self.ring(es, "ca_r", 2, [128, 512], F32)
        tmpr = self.ring(es, "ca_tmp", 2, [128, 512], F32)
        qh = self.ring(es, "ca_qh", 2, [128, 2, 512], BF16)
        Er = self.ring(es, "ca_E", 2, [128, 2, 512], BF16)

        def qk_norm(p0, p1, n, gain, scale, out_aps, out_buf):
            s = sqr.next()
            tk.op("act", lambda: nc.scalar.activation(out=s[:, 0, 0:n], in_=p0[:, 0:n], func=AF.Square), reads=[p0.b], writes=[s.b])
            tk.op("act", lambda: nc.scalar.activation(out=s[:, 1, 0:n], in_=p1[:, 0:n], func=AF.Square), reads=[p1.b], writes=[s.b])
            p2 = self.psum()
            for j in range(2):
                tk.op("pe", lambda: nc.tensor.matmul(p2[:, 0:n], lhsT=ones_f[:], rhs=s[:, j, 0:n], start=(j == 0), stop=(j == 1)),
                      reads=[ones_f.b, s.b], writes=[p2.b])
            r = rr.next()
            tk.op("dve", lambda: nc.vector.tensor_scalar(r[:, 0:n], p2[:, 0:n], 1.0 / 256, 1e-6, ALU.mult, ALU.add), reads=[p2.b], writes=[r.b])
            tk.op("act", lambda: nc.scalar.sqrt(r[:, 0:n], r[:, 0:n]), reads=[r.b], writes=[r.b])
            tk.op("dve", lambda: nc.vector.reciprocal(r[:, 0:n], r[:, 0:n]), reads=[r.b], writes=[r.b])
            for j, pj in enumerate((p0, p1)):
                t = tmpr.next()
                tk.op("dve", lambda: nc.vector.tensor_tensor(out=t[:, 0:n], in0=pj[:, 0:n], in1=r[:, 0:n], op=ALU.mult), reads=[pj.b, r.b], writes=[t.b])
                tk.op("dve", lambda: nc.vector.tensor_scalar(out_aps[j], t[:, 0:n], gain[:, j:j + 1], scale, ALU.mult, ALU.mult),
                      reads=[t.b, gain.b], writes=[out_buf])

        with self.scope() as es2:
            mnT = self.sb(es2, "ca_mnT", [128, 8, NMEM], BF16, dj=True)
            self.norm_T("mem", bi * NMEM, NMEM, "norm_mem", mnT)
            wk = self.sb(es2, "ca_wk", [128, 8, 1024], BF16)
            wv = self.sb(es2, "ca_wv", [128, 8, 1024], BF16)
            self.load_w(wk, "ca_wkv_bf", 0, 1024)
            self.load_w(wv, "ca_wkv_bf", 1024, 1024)
            for h in range(4):
                ps_ = []
                for j in range(2):
                    p = self.psum()
                    ci = 2 * h + j
                    for kc in range(8):
                        tk.op("pe", lambda: nc.tensor.matmul(p[:, 0:NMEM], lhsT=wk[:, kc, ci * 128:(ci + 1) * 128], rhs=mnT[:, kc, :], start=(kc == 0), stop=(kc == 7)),
                              reads=[wk.b, mnT.b], writes=[p.b])
                    ps_.append(p)
                qk_norm(ps_[0], ps_[1], NMEM, kgain, 1.0, [kT[:, 2 * h, :], kT[:, 2 * h + 1, :]], kT.b)
            for mt in range(2):
                for half in range(2):
                    p = self.psum()
                    for kc in range(8):
                        tk.op("pe", lambda: nc.tensor.matmul(p[:], lhsT=mnT[:, kc, mt * 128:(mt + 1) * 128], rhs=wv[:, kc, half * 512:(half + 1) * 512],
                                                              start=(kc == 0), stop=(kc == 7)), reads=[mnT.b, wv.b], writes=[p.b])
                    tk.op("act", lambda: nc.scalar.copy(Vc[:, mt, half * 512:(half + 1) * 512], p[:]), reads=[p.b], writes=[Vc.b])
        for hf in range(T // HT):
            with self.scope() as es2:
                hnT = self.sb(es2, "ca_hnT", [128, 8, HT], BF16, dj=True)
                oT = self.sb(es2, "ca_oT", [128, 8, HT], BF16, dj=True)
                self.norm_T("h1", hf * HT, HT, "norm_cross", hnT)
                for h in range(4):
                    for tt in range(HT // 512):
                        ps_ = []
                        for j in range(2):
                            p = self.psum()
                            ci = 2 * h + j
                            for kc in range(8):
                                tk.op("pe", lambda: nc.tensor.matmul(p[:], lhsT=wq[:, kc, ci * 128:(ci + 1) * 128], rhs=hnT[:, kc, tt * 512:(tt + 1) * 512],
                                                                      start=(kc == 0), stop=(kc == 7)), reads=[wq.b, hnT.b], writes=[p.b])
                            ps_.append(p)
                        q = qh.next()
                        qk_norm(ps_[0], ps_[1], 512, qgain, 1.0 / 16, [q[:, 0, :], q[:, 1, :]], q.b)
                        E = Er.next()
                        for mt in range(2):
                            p = self.psum()
                            for j in range(2):
                                tk.op("pe", lambda: nc.tensor.matmul(p[:], lhsT=kT[:, 2 * h + j, mt * 128:(mt + 1) * 128], rhs=q[:, j, :], start=(j == 0), stop=(j == 1)),
                                      reads=[kT.b, q.b], writes=[p.b])
                            tk.op("act", lambda: nc.scalar.activation(out=E[:, mt, :], in_=p[:], func=AF.Exp), reads=[p.b], writes=[E.b])
                        pd = self.psum()
                        for mt in range(2):
                            tk.op("pe", lambda: nc.tensor.matmul(pd[:], lhsT=ones_b[:], rhs=E[:, mt, :], start=(mt == 0), stop=(mt == 1)),
                                  reads=[ones_b.b, E.b], writes=[pd.b])
                        r = rr.next()
                        tk.op("dve", lambda: nc.vector.reciprocal(r[:], pd[:]), reads=[pd.b], writes=[r.b])
                        for j in range(2):
                            pn = self.psum()
                            for mt in range(2):
                                tk.op("pe", lambda: nc.tensor.matmul(pn[:], lhsT=Vc[:, mt, h * 256 + j * 128:h * 256 + (j + 1) * 128], rhs=E[:, mt, :],
                                                                      start=(mt == 0), stop=(mt == 1)), reads=[Vc.b, E.b], writes=[pn.b])
                            tk.op("dve", lambda: nc.vector.tensor_tensor(out=oT[:, 2 * h + j, tt * 512:(tt + 1) * 512], in0=pn[:], in1=r[:], op=ALU.mult),
                                  reads=[pn.b, r.b], writes=[oT.b])
                self.proj_tm_res(oT, 0, HT, 8, wo, "h1", hf * HT, "h2", hf * HT, es2)


def _ffn(self, bi):
    nc, tk = self.nc, self.tk
    self.scratch("ff_fm", [DFF, T], BF16)
    NCT = DFF // 128
    with self.scope() as es:
        hnT = self.sb(es, "ff_hnT", [128, 8, T], BF16, dj=True)
        self.norm_T("h2", 0, T, "norm_ffn", hnT)
        cw = [self.col_vec(es, "ffn_conv", j, 0, NCT, f"ff_cw{j}") for j in range(3)]
        cb = self.col_vec(es, "ffn_conv_b", 0, 0, NCT, "ff_cb")
        wring = self.ring(es, "ff_w", 4, [128, 8, 128], BF16)
        at = self.sb(es, "ff_a", [128, T + 2], F32)
        bt = self.sb(es, "ff_b", [128, T], F32)
        acc = self.sb(es, "ff_acc", [128, T], F32)
        ob = self.ring(es, "ff_ob", 2, [128, T], BF16)
        tk.op("pool", lambda: nc.gpsimd.memset(at[:, 0:2], 0.0), writes=[at.b])
        for ci in range(NCT):
            wa, wb = wring.next(), wring.next()
            self.load_w(wa, "ffn_up_bf", ci * 128, 128)
            self.load_w(wb, "ffn_up_bf", DFF + ci * 128, 128)
            for tt in range(T // 512):
                pa, pb = self.psum(), self.psum()
                for kc in range(8):
                    tk.op("pe", lambda: nc.tensor.matmul(pa[:], lhsT=wa[:, kc, :], rhs=hnT[:, kc, tt * 512:(tt + 1) * 512], start=(kc == 0), stop=(kc == 7)),
                          reads=[wa.b, hnT.b], writes=[pa.b])
                for kc in range(8):
                    tk.op("pe", lambda: nc.tensor.matmul(pb[:], lhsT=wb[:, kc, :], rhs=hnT[:, kc, tt * 512:(tt + 1) * 512], start=(kc == 0), stop=(kc == 7)),
                          reads=[wb.b, hnT.b], writes=[pb.b])
                tk.op("act", lambda: nc.scalar.copy(at[:, 2 + tt * 512:2 + (tt + 1) * 512], pa[:]), reads=[pa.b], writes=[at.b])
                tk.op("dve", lambda: nc.vector.tensor_copy(bt[:, tt * 512:(tt + 1) * 512], pb[:]), reads=[pb.b], writes=[bt.b])
            tk.op("dve", lambda: nc.vector.tensor_scalar(acc[:], at[:, 2:T + 2], cw[2][:, ci:ci + 1], cb[:, ci:ci + 1], ALU.mult, ALU.add),
                  reads=[at.b, cw[2].b, cb.b], writes=[acc.b])
            tk.op("dve", lambda: nc.vector.scalar_tensor_tensor(out=acc[:], in0=at[:, 1:T + 1], scalar=cw[1][:, ci:ci + 1], in1=acc[:], op0=ALU.mult, op1=ALU.add),
                  reads=[at.b, cw[1].b, acc.b], writes=[acc.b])
            tk.op("dve", lambda: nc.vector.scalar_tensor_tensor(out=acc[:], in0=at[:, 0:T], scalar=cw[0][:, ci:ci + 1], in1=acc[:], op0=ALU.mult, op1=ALU.add),
                  reads=[at.b, cw[0].b, acc.b], writes=[acc.b])
            tk.op("act", lambda: nc.scalar.activation(out=acc[:], in_=acc[:], func=AF.Silu), reads=[acc.b], writes=[acc.b])
            o = ob.next()
            tk.op("pool", lambda: nc.gpsimd.tensor_tensor(out=o[:], in0=acc[:], in1=bt[:], op=ALU.mult), reads=[acc.b, bt.b], writes=[o.b])
            tk.dma("pool", self.din["ff_fm"].ap()[ci * 128:(ci + 1) * 128, :], o[:], reads=[o.b], writes=[self.dbuf["ff_fm"]])
    with self.scope() as es:
        wd = self.sb(es, "ff_wd", [128, NCT, 1024], BF16)
        self.load_w(wd, "ffn_down_bf", 0, 1024, kchunks=NCT)
        TBK = 1024
        for blk in range(T // TBK):
            with self.scope() as es2:
                fT = self.sb(es2, "ff_fT", [128, NCT, TBK], BF16)
                tk.dma("sp", fT[:], self.dap("ff_fm", blk * TBK, [[T, 128], [128 * T, NCT], [1, TBK]]), reads=[self.dbuf["ff_fm"]], writes=[fT.b])
                self.proj_tm_res(fT, 0, TBK, NCT, wd, "h2", blk * TBK, "out", bi * T + blk * TBK, es2)


Prog.proj_tm_res = _proj_tm_res
Prog.phase_merge = _merge
Prog.phase_cross = _cross
Prog.phase_ffn = _ffn
```

```python
import contextlib
import math
import numpy as np
import concourse.bass as bass
import concourse.mybir as mybir
from concourse.bass_utils import run_bass_kernel_spmd

F32 = mybir.dt.float32
BF16 = mybir.dt.bfloat16
AF = mybir.ActivationFunctionType
ALU = mybir.AluOpType
AX = mybir.AxisListType

NCORES = 8
NB = 2
T = 4096
D = 1024
NMEM = 256
DFF = 2816
IN_COLS = 8016
BIG = 30000.0
OFFC = 2176
LVEC = 7680
SCALE_NSA = 0.125


class Buf:
    __slots__ = ("w", "r", "name", "dj", "xr")

    def __init__(self, name="", dj=False):
        self.w = {}
        self.r = {}
        self.name = name
        self.dj = dj
        self.xr = False


class Tile:
    def __init__(self, t, name, dj=False):
        self.t = t
        self.b = Buf(name, dj)

    def __getitem__(self, k):
        return self.t[k]


class Ring:
    def __init__(self, tiles):
        self.tiles = tiles
        self.i = 0

    def next(self):
        t = self.tiles[self.i]
        self.i = (self.i + 1) % len(self.tiles)
        return t


class TK:
    EPOCH = 20000
    NDSEM = 10

    def __init__(self, nc, es):
        self.nc = nc
        self.es = es
        self.eng = {"pe": nc.tensor, "act": nc.scalar, "dve": nc.vector,
                    "pool": nc.gpsimd, "sp": nc.sync}
        self.cnt = {e: 0 for e in self.eng}
        self.esem = {e: [] for e in self.eng}
        self.seen = {e: {} for e in self.eng}
        self.dsem = {}
        self.dptr = {}
        self.nwait = 0
        self.fence = {}

    def _newsem(self, name):
        return self.es.enter_context(self.nc.semaphore(name))

    def _engsem(self, e, epoch):
        while len(self.esem[e]) <= epoch:
            self.esem[e].append(self._newsem(f"s_{e}_{len(self.esem[e])}"))
        return self.esem[e][epoch]

    def _wait(self, e, ts):
        sem, val, src = ts
        if src == "pe" and e == "pe":
            return
        k = id(sem)
        if self.seen[e].get(k, 0) >= val:
            return
        self.seen[e][k] = val
        self.eng[e].wait_ge(sem, val)
        self.nwait += 1

    def deps(self, e, reads, writes):
        for b in reads:
            for ts in b.w.values():
                self._wait(e, ts)
            if b.xr:
                for ts in b.r.values():
                    if ts[2] != e:
                        self._wait(e, ts)
        for b in writes:
            if not (b.dj and not b.r):
                for ts in b.w.values():
                    self._wait(e, ts)
            for ts in b.r.values():
                self._wait(e, ts)

    def mark(self, ts, reads, writes):
        k = id(ts[0])
        for b in reads:
            b.r[k] = ts
        for b in writes:
            if b.dj and not b.r:
                b.w[k] = ts
            else:
                b.w = {k: ts}
                b.r = {}

    def op(self, e, ins_fn, reads=(), writes=()):
        self.deps(e, reads, writes)
        n = self.cnt[e]
        sem = self._engsem(e, n // self.EPOCH)
        val = n % self.EPOCH + 1
        ins_fn().then_inc(sem, 1)
        self.cnt[e] = n + 1
        ts = (sem, val, e)
        self.mark(ts, reads, writes)
        return ts

    def dma(self, q, out_ap, in_ap, reads=(), writes=(), **kw):
        if q not in self.dsem:
            self.dsem[q] = [[self._newsem(f"d_{q}_{i}"), 0] for i in range(self.NDSEM)]
            self.dptr[q] = 0
        slot = self.dsem[q][self.dptr[q]]
        self.dptr[q] = (self.dptr[q] + 1) % self.NDSEM
        sem, issued = slot
        if issued:
            self._wait(q, (sem, 16 * issued, None))
        self.deps(q, reads, writes)
        self.eng[q].dma_start(out=out_ap, in_=in_ap, **kw).then_inc(sem, 16)
        slot[1] = issued + 1
        ts = (sem, 16 * (issued + 1), None)
        self.mark(ts, reads, writes)
        return ts

    def update_fence(self):
        f = {}
        for e in self.eng:
            n = self.cnt[e]
            if n:
                sem = self.esem[e][(n - 1) // self.EPOCH]
                f[id(sem)] = (sem, (n - 1) % self.EPOCH + 1, e)
        for q in self.dsem:
            for sem, issued in self.dsem[q]:
                if issued:
                    f[id(sem)] = (sem, 16 * issued, None)
        self.fence = f

    def drain(self):
        for q in self.dsem:
            for sem, issued in self.dsem[q]:
                if issued:
                    self._wait(q, (sem, 16 * issued, None))


def _t5_bucket_np(dist):
    n = np.maximum(dist, 0)
    nf = np.maximum(n, 1).astype(np.float64)
    large = 16 + (np.log(nf / 16) / math.log(128 / 16) * 16).astype(np.int64)
    large = np.minimum(large, 31)
    return np.where(n < 16, n, large)


def host_consts():
    c = {}
    c["c_ident"] = np.eye(128, dtype=np.float32)
    c["c_J"] = np.ascontiguousarray(np.eye(128, dtype=np.float32)[::-1])
    hb = np.arange(128) // 64
    bd = (hb[:, None] == hb[None, :]).astype(np.float32)
    c["c_bdones"] = bd
    c["c_ones"] = np.ones((128, 128), np.float32)
    s = np.arange(128) % 64
    strict = bd * (s[:, None] < s[None, :])
    incl = bd * (s[:, None] <= s[None, :])
    c["c_mask2"] = np.concatenate([strict, incl], axis=1).astype(np.float32)
    bd5 = np.zeros((128, 5, 2, 64), np.float32)
    for h in range(2):
        bd5[64 * h:64 * h + 64, :, h, :] = 1.0
    c["c_bdmask5"] = bd5.reshape(128, 640)
    seg = np.ones((128, 1024), np.float32)
    seg[:, ::64] = 0.0
    c["c_segmask"] = seg
    dist = np.arange(LVEC) - OFFC
    bk = _t5_bucket_np(dist)
    oh = np.zeros((33, LVEC), np.float32)
    oh[bk, np.arange(LVEC)] = 1.0
    ec = oh.copy()
    ec[32] = np.where(dist >= 0, 0.0, -BIG)
    ec[:32, dist < 0] = 0.0
    ew = oh.copy()
    ok = (dist >= 0) & (dist < 512)
    ew[32] = np.where(ok, 0.0, -BIG)
    ew[:32, ~ok] = 0.0
    c["c_e33c"] = ec
    c["c_e33w"] = ew
    t = np.arange(T)
    cur = t // 64
    blk = np.arange(64)
    forced = (blk[:, None] == 0) | (blk[:, None] == cur[None, :]) | (blk[:, None] == cur[None, :] - 1)
    c["c_forced"] = np.where(forced, 1e4, 0.0).astype(np.float32)
    ex = np.zeros((64, 32, 128), np.float32)
    for kt in range(32):
        for p in range(128):
            ex[2 * kt + p // 64, kt, p] = 1.0
    c["c_expand"] = ex.reshape(64, 32 * 128)
    ncmp = 255
    ci = np.arange(256)[:, None] * 16
    sj = np.arange(64)[None, :] * 64
    c2s = ((ci <= sj + 63) & (ci + 31 >= sj)).astype(np.float32)
    c2s[ncmp:] = 0.0
    c["c_c2s"] = c2s
    return c


CONST_SHAPES = {k: v.shape for k, v in host_consts().items()}

W_SPECS = [
    ("w_in", 1024, IN_COLS), ("rwkv_w2", 64, 1024), ("rwkv_a2", 64, 1024), ("rwkv_g2", 160, 1024),
    ("cmp_w1_k", 2048, 256), ("cmp_w2_k", 256, 64), ("cmp_w1_v", 2048, 256), ("cmp_w2_v", 256, 64),
    ("w_branch_rwkv", 1024, 1024), ("w_branch_nsa", 1024, 1024), ("w_mix_out", 1024, 1024),
    ("ca_wq", 1024, 1024), ("ca_wkv", 1024, 2048), ("ca_wo", 1024, 1024),
    ("ffn_up", 1024, 2 * DFF), ("ffn_down", DFF, 1024),
]
V_SPECS = [
    ("rel_bias", (32, 16)), ("norm_mix", (1, 1024)), ("rwkv_mu", (1, 3360)), ("rwkv_w0", (1, 1024)),
    ("rwkv_a0", (1, 1024)), ("rwkv_kk", (1, 1024)), ("rwkv_ka", (1, 1024)), ("rwkv_rk", (1, 1024)),
    ("rwkv_lnx_w", (1, 1024)), ("rwkv_lnx_b", (1, 1024)), ("nsa_q_gain", (1, 64)), ("nsa_k_gain", (3, 64)),
    ("cmp_pe_k", (1, 2048)), ("cmp_pe_v", (1, 2048)), ("norm_cross", (1, 1024)), ("norm_mem", (1, 1024)),
    ("ca_q_gain", (1, 256)), ("ca_k_gain", (1, 256)), ("norm_ffn", (1, 1024)),
    ("ffn_conv", (3, DFF)), ("ffn_conv_b", (1, DFF)),
]


class Prog:
    def __init__(self, upto="all", dbg=()):
        self.upto = upto
        self.dbg = set(dbg)
        nc = self.nc = bass.Bass("TRN2", target_bir_lowering=False)
        self.es = contextlib.ExitStack()
        self.tk = TK(nc, self.es)
        self.din = {}
        self.dbuf = {}

    def dram_in(self, name, shape):
        self.din[name] = self.nc.dram_tensor(name, list(shape), F32, kind="ExternalInput")
        self.dbuf[name] = Buf(name, dj=True)
        return self.din[name]

    def scratch(self, name, shape, dt):
        if name in self.din:
            return self.din[name]
        kind = "ExternalOutput" if name in self.dbg else "Internal"
        self.din[name] = self.nc.dram_tensor(name, list(shape), dt, kind=kind)
        self.dbuf[name] = Buf(name, dj=True)
        return self.din[name]

    def sb(self, es, name, shape, dt, dj=False):
        self.uid = getattr(self, "uid", 0) + 1
        name = f"{name}_{self.uid}"
        t = Tile(es.enter_context(self.nc.sbuf_tensor(name, list(shape), dt)), name, dj)
        t.b.r = dict(self.tk.fence)
        return t

    @contextlib.contextmanager
    def scope(self):
        with contextlib.ExitStack() as es:
            yield es
        self.tk.update_fence()

    def ring(self, es, name, n, shape, dt):
        return Ring([self.sb(es, f"{name}{i}", shape, dt) for i in range(n)])

    def psum(self):
        return self.psr.next()

    def dap(self, name, offset, ap):
        return bass.AP(tensor=self.din[name], offset=offset, ap=[list(x) for x in ap])

    def load_const(self, es, name, dt=F32, tmp_es=None):
        nc, tk = self.nc, self.tk
        shp = CONST_SHAPES[name]
        t32 = self.sb(es if dt == F32 else tmp_es, name + "_f", shp, F32)
        tk.dma("sp", t32[:], self.din[name].ap()[:, :], reads=[self.dbuf[name]], writes=[t32.b])
        if dt == F32:
            return t32
        t16 = self.sb(es, name + "_h", shp, BF16)
        tk.op("dve", lambda: nc.vector.tensor_copy(t16[:], t32[:]), reads=[t32.b], writes=[t16.b])
        return t16

    def bcast_vec(self, es, name, row, c0, n, tname):
        t = self.sb(es, tname, [128, n], F32)
        src = self.din[name].ap()[row:row + 1, c0:c0 + n].partition_broadcast(128)
        self.tk.dma("sp", t[:], src, reads=[self.dbuf[name]], writes=[t.b])
        return t

    def col_vec(self, es, name, row, c0, nchunk, tname, p=128):
        nc, tk = self.nc, self.tk
        t = self.sb(es, tname, [p, nchunk], F32)
        ncols = self.din[name].shape[1]
        with self.scope() as es2:
            raw = self.sb(es2, tname + "_raw", [nchunk, p], F32)
            tk.dma("sp", raw[:], self.dap(name, row * ncols + c0, [[p, nchunk], [1, p]]), reads=[self.dbuf[name]], writes=[raw.b])
            ps_ = self.psum()
            tk.op("pe", lambda: nc.tensor.transpose(ps_[:p, 0:nchunk], raw[:], self.ident[:nchunk, :nchunk]),
                  reads=[raw.b, self.ident.b], writes=[ps_.b])
            tk.op("dve", lambda: nc.vector.tensor_copy(t[:], ps_[:p, 0:nchunk]), reads=[ps_.b], writes=[t.b])
        return t

    def phase_w(self):
        nc, tk = self.nc, self.tk
        with self.scope() as es:
            st = self.ring(es, "wst", 3, [128, 2048], F32)
            sh = self.ring(es, "wsh", 3, [128, 2048], BF16)
            k = 0
            for name, R, C in W_SPECS:
                dst = self.scratch(name + "_bf", [R, C], BF16)
                src = self.din[name].ap()
                for r0 in range(0, R, 128):
                    rr = min(128, R - r0)
                    for c0 in range(0, C, 2048):
                        cc = min(2048, C - c0)
                        a = st.next()
                        h = sh.next()
                        tk.dma("sp", a[:rr, :cc], src[r0:r0 + rr, c0:c0 + cc], reads=[self.dbuf[name]], writes=[a.b])
                        e = ("dve", "pool", "act")[k % 3]
                        k += 1
                        if e == "act":
                            tk.op(e, lambda: nc.scalar.copy(h[:rr, :cc], a[:rr, :cc]), reads=[a.b], writes=[h.b])
                        elif e == "dve":
                            tk.op(e, lambda: nc.vector.tensor_copy(h[:rr, :cc], a[:rr, :cc]), reads=[a.b], writes=[h.b])
                        else:
                            tk.op(e, lambda: nc.gpsimd.tensor_copy(h[:rr, :cc], a[:rr, :cc]), reads=[a.b], writes=[h.b])
                        tk.dma("pool", dst.ap()[r0:r0 + rr, c0:c0 + cc], h[:rr, :cc], reads=[h.b],
                               writes=[self.dbuf[name + "_bf"]])

    def norm_T(self, src_name, src_row0, ntok, gname, dstT):
        nc, tk = self.nc, self.tk
        with self.scope() as es:
            gbc = self.bcast_vec(es, gname, 0, 0, D, "nt_g")
            xr = self.ring(es, "nt_x", 2, [128, D], F32)
            xs = self.ring(es, "nt_xs", 2, [128, D], F32)
            junk = self.sb(es, "nt_junk", [128, D], BF16)
            st = self.ring(es, "nt_st", 2, [128, 4], F32)
            src = self.din[src_name].ap()
            for i in range(ntok // 128):
                x = xr.next()
                s = st.next()
                y = xs.next()
                tk.dma("sp", x[:], src[src_row0 + i * 128: src_row0 + (i + 1) * 128, :],
                       reads=[self.dbuf[src_name]], writes=[x.b])
                tk.op("act", lambda: nc.scalar.activation(out=junk[:], in_=x[:], func=AF.Square, accum_out=s[:, 0:1]),
                      reads=[x.b], writes=[junk.b, s.b])
                tk.op("dve", lambda: nc.vector.tensor_scalar(s[:, 1:2], s[:, 0:1], 1.0 / D, 1e-6, ALU.mult, ALU.add),
                      reads=[s.b], writes=[s.b])
                tk.op("act", lambda: nc.scalar.sqrt(s[:, 2:3], s[:, 1:2]), reads=[s.b], writes=[s.b])
                tk.op("dve", lambda: nc.vector.reciprocal(s[:, 3:4], s[:, 2:3]), reads=[s.b], writes=[s.b])
                tk.op("dve", lambda: nc.vector.scalar_tensor_tensor(out=y[:], in0=x[:], scalar=s[:, 3:4], in1=gbc[:],
                                                                    op0=ALU.mult, op1=ALU.mult),
                      reads=[x.b, s.b, gbc.b], writes=[y.b])
                for half in range(2):
                    p = self.psum()
                    for j in range(4):
                        kc = half * 4 + j
                        tk.op("pe", lambda: nc.tensor.transpose(p[:, j * 128:(j + 1) * 128], y[:, kc * 128:(kc + 1) * 128],
                                                                self.ident[:]),
                              reads=[y.b, self.ident.b], writes=[p.b])
                    o = dstT[:, half * 4:half * 4 + 4, i * 128:(i + 1) * 128]
                    pin = p[:, :].rearrange("p (a b) -> p a b", a=4)
                    if half == 0:
                        tk.op("act", lambda: nc.scalar.copy(o, pin), reads=[p.b], writes=[dstT.b])
                    else:
                        tk.op("dve", lambda: nc.vector.tensor_copy(o, pin), reads=[p.b], writes=[dstT.b])

    def load_w(self, tile, wname, c0, ncols, kchunks=8, r0=0):
        C = self.din[wname].shape[1]
        src = self.dap(wname, r0 * C + c0, [[C, 128], [128 * C, kchunks], [1, ncols]])
        self.tk.dma("sp", tile[:, 0:kchunks, 0:ncols], src, reads=[self.dbuf[wname]], writes=[tile.b])

    def proj_fm(self, wname, c0, ncols_total, actT, ntok, epi, kchunks=8, wring=None):
        nc, tk = self.nc, self.tk
        nct = (ncols_total + 127) // 128
        for ci in range(nct):
            cc = min(128, ncols_total - ci * 128)
            w = wring.next()
            self.load_w(w, wname, c0 + ci * 128, cc, kchunks)
            for tt in range(ntok // 512):
                p = self.psum()
                for kc in range(kchunks):
                    tk.op("pe", lambda: nc.tensor.matmul(p[:cc, :], lhsT=w[:, kc, 0:cc],
                                                          rhs=actT[:, kc, tt * 512:(tt + 1) * 512],
                                                          start=(kc == 0), stop=(kc == kchunks - 1)),
                          reads=[w.b, actT.b], writes=[p.b])
                epi(p, ci, tt, cc)

    def phase_b(self, xT):
        nc, tk = self.nc, self.tk
        self.scratch("zr_fm", [3360, T], F32)
        self.scratch("q_fm", [1024, T], BF16)
        self.scratch("kcvc_fm", [512, T], BF16)
        self.scratch("ks_fm", [256, T], BF16)
        self.scratch("kw_fm", [256, T], BF16)
        self.scratch("vsw_tm", [T, 512], BF16)
        self.scratch("gates_fm", [48, T], F32)
        self.scratch("gm_fm", [2048, T], F32)
        with self.scope() as es:
            wring = self.ring(es, "pb_w", 2, [128, 8, 128], BF16)
            o32 = self.ring(es, "pb_o32", 3, [128, 512], F32)
            o16 = self.ring(es, "pb_o16", 3, [128, 512], BF16)
            sq = self.ring(es, "pb_sq", 2, [128, 512], F32)
            qg = self.sb(es, "pb_qg", [128, 4], F32)
            eps = self.sb(es, "pb_eps", [128, 1], F32)
            tk.op("pool", lambda: nc.gpsimd.memset(eps[:], 1e-6), writes=[eps.b])
            for h in range(2):
                tk.dma("sp", qg[64 * h:64 * h + 64, 0:1], self.dap("nsa_q_gain", 0, [[1, 64], [1, 1]]),
                       reads=[self.dbuf["nsa_q_gain"]], writes=[qg.b])
                for j in (1, 2):
                    tk.dma("sp", qg[64 * h:64 * h + 64, j + 1:j + 2], self.dap("nsa_k_gain", 64 * j, [[1, 64], [1, 1]]),
                           reads=[self.dbuf["nsa_k_gain"]], writes=[qg.b])
            cnt = [0]

            def store(dname, row0, t0, tile, rows):
                tk.dma("pool", self.din[dname].ap()[row0:row0 + rows, t0:t0 + 512], tile[:rows, :], reads=[tile.b],
                       writes=[self.dbuf[dname]])

            def epi_copy(dname, row_base, dt):
                def f(p, ci, tt, cc):
                    o = (o32 if dt == F32 else o16).next()
                    cnt[0] += 1
                    if cnt[0] % 2:
                        tk.op("act", lambda: nc.scalar.copy(o[:cc, :], p[:cc, :]), reads=[p.b], writes=[o.b])
                    else:
                        tk.op("dve", lambda: nc.vector.tensor_copy(o[:cc, :], p[:cc, :]), reads=[p.b], writes=[o.b])
                    store(dname, row_base + ci * 128, tt * 512, o, cc)
                return f

            def epi_sig(dname, row_base):
                def f(p, ci, tt, cc):
                    o = o32.next()
                    tk.op("act", lambda: nc.scalar.activation(out=o[:cc, :], in_=p[:cc, :], func=AF.Sigmoid),
                          reads=[p.b], writes=[o.b])
                    store(dname, row_base + ci * 128, tt * 512, o, cc)
                return f

            def epi_norm(dname, row_base, gcol, scale):
                def f(p, ci, tt, cc):
                    s = sq.next()
                    tk.op("act", lambda: nc.scalar.activation(out=s[:], in_=p[:], func=AF.Square), reads=[p.b], writes=[s.b])
                    p2 = self.psum()
                    tk.op("pe", lambda: nc.tensor.matmul(p2[:], lhsT=self.bdones[:], rhs=s[:], start=True, stop=True),
                          reads=[self.bdones.b, s.b], writes=[p2.b])
                    r = o32.next()
                    tk.op("dve", lambda: nc.vector.tensor_scalar(r[:], p2[:], 1.0 / 64, 1e-6, ALU.mult, ALU.add),
                          reads=[p2.b], writes=[r.b])
                    tk.op("act", lambda: nc.scalar.sqrt(r[:], r[:]), reads=[r.b], writes=[r.b])
                    tk.op("dve", lambda: nc.vector.reciprocal(r[:], r[:]), reads=[r.b], writes=[r.b])
                    tk.op("dve", lambda: nc.vector.tensor_tensor(out=r[:], in0=p[:], in1=r[:], op=ALU.mult),
                          reads=[p.b, r.b], writes=[r.b])
                    o = o16.next()
                    tk.op("dve", lambda: nc.vector.tensor_scalar(o[:], r[:], qg[:, gcol:gcol + 1], scale, ALU.mult, ALU.mult),
                          reads=[r.b, qg.b], writes=[o.b])
                    store(dname, row_base + ci * 128, tt * 512, o, cc)
                return f

            segs = [
                (0, 3360, epi_copy("zr_fm", 0, F32)),
                (3360, 1024, epi_norm("q_fm", 0, 0, SCALE_NSA)),
                (4384, 512, epi_copy("kcvc_fm", 0, BF16)),
                (4896, 256, epi_norm("ks_fm", 0, 2, 1.0)),
                (5408, 256, epi_norm("kw_fm", 0, 3, 1.0)),
                (5920, 48, epi_sig("gates_fm", 0)),
                (5968, 2048, epi_sig("gm_fm", 0)),
            ]
            for c0, n, epi in segs:
                self.proj_fm("w_in_bf", c0, n, xT, T, epi, wring=wring)
            wv = self.sb(es, "pb_wv", [128, 8, 512], BF16)
            self.load_w(wv, "w_in_bf", 5152, 256)
            C = IN_COLS
            tk.dma("sp", wv[:, :, 256:512], self.dap("w_in_bf", 5664, [[C, 128], [128 * C, 8], [1, 256]]),
                   reads=[self.dbuf["w_in_bf"]], writes=[wv.b])
            for i in range(T // 128):
                p = self.psum()
                for kc in range(8):
                    tk.op("pe", lambda: nc.tensor.matmul(p[:], lhsT=xT[:, kc, i * 128:(i + 1) * 128], rhs=wv[:, kc, :],
                                                          start=(kc == 0), stop=(kc == 7)), reads=[xT.b, wv.b], writes=[p.b])
                o = o16.next()
                tk.op("act", lambda: nc.scalar.copy(o[:], p[:]), reads=[p.b], writes=[o.b])
                tk.dma("pool", self.din["vsw_tm"].ap()[i * 128:(i + 1) * 128, :], o[:], reads=[o.b], writes=[self.dbuf["vsw_tm"]])

    def build(self):
        nc, tk = self.nc, self.tk
        self.dram_in("x", [NB * T, D])
        self.dram_in("mem", [NB * NMEM, D])
        for name, R, C in W_SPECS:
            self.dram_in(name, [R, C])
        for name, shp in V_SPECS:
            self.dram_in(name, shp)
        for name, shp in CONST_SHAPES.items():
            self.dram_in(name, shp)
        self.out = self.nc.dram_tensor("out", [NB * T, D], F32, kind="ExternalOutput")
        self.din["out"] = self.out
        self.dbuf["out"] = Buf("out", dj=True)
        es = self.es
        self.psr = Ring([Tile(es.enter_context(nc.psum_tensor(f"ps{i}", [128, 512], F32)), f"ps{i}") for i in range(8)])
        for t_ in self.psr.tiles:
            t_.b.xr = True
        self.ident = self.load_const(es, "c_ident")
        self.bdones = self.load_const(es, "c_bdones")
        self.phase_w()
        if self.upto == "w":
            return self.finish()
        self.nsa_bias()
        for bi in range(NB):
            self.seq(bi)
            if self.upto != "all":
                break
        return self.finish()

    def seq(self, bi):
        tk = self.tk
        with self.scope() as es1:
            xT = self.sb(es1, "xT", [128, 8, T], BF16, dj=True)
            self.norm_T("x", bi * T, T, "norm_mix", xT)
            if bi == 0 and "xT_dbg" in self.dbg:
                d = self.scratch("xT_dbg", [128, 8 * T], BF16)
                tk.dma("sp", d.ap()[:, :], xT[:, :, :].rearrange("p a b -> p (a b)"), reads=[xT.b], writes=[self.dbuf["xT_dbg"]])
            if self.upto == "a":
                return
            self.phase_b(xT)
        if self.upto == "b":
            return
        if not getattr(self, "skip_rwkv", False):
            self.phase_rwkv()
        if self.upto == "rwkv":
            return
        self.phase_nsa()
        if self.upto == "nsa":
            return
        self.phase_merge(bi)
        if self.upto == "merge":
            return
        self.phase_cross(bi)
        if self.upto == "cross":
            return
        self.phase_ffn(bi)

    def finish(self):
        self.tk.drain()
        self.es.close()
        return self.nc


def make_in_maps(inputs, cores=range(NCORES)):
    consts = host_consts()
    shared = {}
    for name, R, C in W_SPECS:
        shared[name] = np.ascontiguousarray(np.asarray(inputs[name], np.float32).reshape(R, C))
    for name, shp in V_SPECS:
        shared[name] = np.ascontiguousarray(np.asarray(inputs[name], np.float32).reshape(shp))
    shared.update(consts)
    x = np.asarray(inputs["x"], np.float32)
    mem = np.asarray(inputs["mem"], np.float32)
    maps = []
    for c in cores:
        m = dict(shared)
        m["x"] = np.ascontiguousarray(x[NB * c:NB * c + NB].reshape(NB * T, D))
        m["mem"] = np.ascontiguousarray(mem[NB * c:NB * c + NB].reshape(NB * NMEM, D))
        maps.append(m)
    return maps


def kernel(**inputs):
    prog = Prog()
    nc = prog.build()
    maps = make_in_maps(inputs)
    res = run_bass_kernel_spmd(nc, maps, core_ids=list(range(NCORES)))
    outs = [np.asarray(r["out"]).reshape(NB, T, D) for r in res.results]
    return np.concatenate(outs, axis=0).astype(np.float32)


def _rwkv(self):
    nc, tk = self.nc, self.tk
    TB = 512
    self.scratch("yr_fm", [1024, T], BF16)
    zr = self.din["zr_fm"].ap()
    zb = self.dbuf["zr_fm"]

    def shift_load(dst_ap, dst_buf, r0, nrows, t0, nt, mucol, X, dtile):
        if t0 == 0:
            tk.op("pool", lambda: nc.gpsimd.memset(X[:nrows, 0:1], 0.0), writes=[X.b])
            tk.dma("sp", X[:nrows, 1:nt + 1], zr[r0:r0 + nrows, 0:nt], reads=[zb], writes=[X.b])
        else:
            tk.dma("sp", X[:nrows, 0:nt + 1], zr[r0:r0 + nrows, t0 - 1:t0 + nt], reads=[zb], writes=[X.b])
        tk.op("pool", lambda: nc.gpsimd.tensor_tensor(out=dtile[:nrows, :nt], in0=X[:nrows, 0:nt], in1=X[:nrows, 1:nt + 1],
                                                      op=ALU.subtract), reads=[X.b], writes=[dtile.b])
        tk.op("dve", lambda: nc.vector.scalar_tensor_tensor(out=dst_ap, in0=dtile[:nrows, :nt], scalar=mucol,
                                                             in1=X[:nrows, 1:nt + 1], op0=ALU.mult, op1=ALU.add),
              reads=[dtile.b, X.b], writes=[dst_buf])

    with self.scope() as es:
        mask4 = self.sb(es, "rk_mask4", [128, 512], F32)
        for j in range(2):
            tk.dma("sp", mask4[:, j * 256:(j + 1) * 256], self.din["c_mask2"].ap()[:, :], reads=[self.dbuf["c_mask2"]], writes=[mask4.b])
        bdm5 = self.load_const(es, "c_bdmask5")
        segm = self.sb(es, "rk_seg", [128, TB], F32)
        tk.dma("sp", segm[:], self.din["c_segmask"].ap()[:, 0:TB], reads=[self.dbuf["c_segmask"]], writes=[segm.b])
        lw = self.sb(es, "rk_lw", [64, T], BF16)
        la = self.sb(es, "rk_la", [64, T], BF16)
        lg = self.sb(es, "rk_lg", [128, 2, T], BF16)
        w2 = self.sb(es, "rk_w2", [64, 1024], BF16)
        a2 = self.sb(es, "rk_a2", [64, 1024], BF16)
        g2 = self.sb(es, "rk_g2", [128, 2, 1024], BF16)
        tk.dma("sp", w2[:], self.din["rwkv_w2_bf"].ap()[:, :], reads=[self.dbuf["rwkv_w2_bf"]], writes=[w2.b])
        tk.dma("sp", a2[:], self.din["rwkv_a2_bf"].ap()[:, :], reads=[self.dbuf["rwkv_a2_bf"]], writes=[a2.b])
        tk.dma("sp", g2[:, 0, :], self.din["rwkv_g2_bf"].ap()[0:128, :], reads=[self.dbuf["rwkv_g2_bf"]], writes=[g2.b])
        tk.dma("sp", g2[0:32, 1, :], self.din["rwkv_g2_bf"].ap()[128:160, :], reads=[self.dbuf["rwkv_g2_bf"]], writes=[g2.b])
        pc = {}
        for nm in ("rwkv_w0", "rwkv_a0", "rwkv_kk", "rwkv_ka", "rwkv_rk", "rwkv_lnx_w", "rwkv_lnx_b"):
            pc[nm] = self.col_vec(es, nm, 0, 0, 8, "rk_" + nm)
        mu = self.col_vec(es, "rwkv_mu", 0, 0, 24, "rk_mu")
        omk = self.sb(es, "rk_omk", [128, 8], F32)
        tk.op("dve", lambda: nc.vector.tensor_scalar(omk[:], pc["rwkv_ka"][:], -1.0, 1.0, ALU.mult, ALU.add),
              reads=[pc["rwkv_ka"].b], writes=[omk.b])
        with self.scope() as es2:
            X = self.sb(es2, "rk_LX", [128, T + 1], F32)
            dt_ = self.sb(es2, "rk_Ld", [128, T], F32)
            zt = self.sb(es2, "rk_Lz", [128, T], F32)
            for (r0, nrows, kind) in ((3072, 64, "w"), (3136, 64, "a"), (3200, 128, "g0"), (3328, 32, "g1")):
                mucol = self.sb(es2, "rk_Lmu" + kind, [128, 1], F32)
                tk.dma("sp", mucol[:nrows, :], self.dap("rwkv_mu", r0, [[1, nrows], [1, 1]]), reads=[self.dbuf["rwkv_mu"]], writes=[mucol.b])
                shift_load(zt[:nrows, :], zt.b, r0, nrows, 0, T, mucol[:nrows, 0:1], X, dt_)
                if kind == "w":
                    tk.op("act", lambda: nc.scalar.activation(out=lw[:, :], in_=zt[:64, :], func=AF.Tanh), reads=[zt.b], writes=[lw.b])
                elif kind == "a":
                    tk.op("act", lambda: nc.scalar.copy(la[:, :], zt[:64, :]), reads=[zt.b], writes=[la.b])
                elif kind == "g0":
                    tk.op("act", lambda: nc.scalar.activation(out=lg[:, 0, :], in_=zt[:, :], func=AF.Sigmoid), reads=[zt.b], writes=[lg.b])
                else:
                    tk.op("act", lambda: nc.scalar.activation(out=lg[:32, 1, :], in_=zt[:32, :], func=AF.Sigmoid), reads=[zt.b], writes=[lg.b])
        LIM = getattr(self, "rk_lim", 99)
        if LIM <= 1:
            return
        f = lambda n: self.sb(es, n, [128, TB], F32)
        Xr = self.ring(es, "rk_X", 2, [128, TB + 1], F32)
        dtl = f("rk_d")
        rr, kp, logw, aa, gg, kkr, sq, kmod, kb, cum, cex, epv, eng, bonus, tmp = [f("rk_t%d" % i) for i in range(15)]
        einr = self.ring(es, "rk_ein", 2, [128, TB], F32)
        Q5r = self.ring(es, "rk_Q5", 2, [128, 5, TB], F32)
        yfm = f("rk_yfm")
        dd = f("rk_dd")
        ob = self.ring(es, "rk_ob", 2, [128, TB], BF16)
        BD5l = [self.sb(es, f"rk_BD5{i}", [128, 5, 2, 64], F32) for i in range(4)]
        GBKl = [self.sb(es, f"rk_GBK{i}", [128, 512], F32) for i in range(4)]
        NTl = [self.sb(es, f"rk_NT{i}", [128, 128], F32) for i in range(4)]
        MXl = [[self.sb(es, f"rk_MX{i}{k}", [128, 256], F32) for k in range(2)] for i in range(4)]
        MTl = [[self.sb(es, f"rk_MT{i}{k}", [128, 128], F32) for k in range(2)] for i in range(4)]
        TTl = [self.sb(es, f"rk_TT{i}", [128, 128], F32) for i in range(4)]
        TM3l = [self.sb(es, f"rk_TM3{i}", [128, 384], F32) for i in range(4)]
        RHr = self.ring(es, "rk_RH", 2, [128, 128], F32)
        Ur = self.ring(es, "rk_U", 2, [128, 128], F32)
        Sr = self.ring(es, "rk_S", 2, [128, 128], F32)
        SPr = self.ring(es, "rk_SP", 2, [128, 128], F32)
        ident, bdones = self.ident, self.bdones

        def mm(p_ap, pbuf, lhsT, lb, rhs, rb, start=True, stop=True):
            tk.op("pe", lambda: nc.tensor.matmul(p_ap, lhsT=lhsT, rhs=rhs, start=start, stop=stop), reads=[lb, rb], writes=[pbuf])

        for hp in range(8):
            c0 = 128 * hp
            S = Sr.next()
            tk.op("pool", lambda: nc.gpsimd.memset(S[:], 0.0), writes=[S.b])
            for tb in range(T // TB):
                t0 = tb * TB
                Q5 = Q5r.next()
                ein = einr.next()
                shift_load(rr[:, :], rr.b, c0, 128, t0, TB, mu[:, hp:hp + 1], Xr.next(), dtl)
                shift_load(kp[:, :], kp.b, 1024 + c0, 128, t0, TB, mu[:, 8 + hp:9 + hp], Xr.next(), dtl)
                shift_load(Q5[:, 4, :], Q5.b, 2048 + c0, 128, t0, TB, mu[:, 16 + hp:17 + hp], Xr.next(), dtl)
                p = self.psum()
                mm(p[:], p.b, w2[:, c0:c0 + 128], w2.b, lw[:, t0:t0 + TB], lw.b)
                tk.op("act", lambda: nc.scalar.activation(out=logw[:], in_=p[:], func=AF.Sigmoid, bias=pc["rwkv_w0"][:, hp:hp + 1]),
                      reads=[p.b, pc["rwkv_w0"].b], writes=[logw.b])
                tk.op("pool", lambda: nc.gpsimd.tensor_scalar_mul(logw[:], logw[:], -math.exp(-0.5)), reads=[logw.b], writes=[logw.b])
                p = self.psum()
                mm(p[:], p.b, a2[:, c0:c0 + 128], a2.b, la[:, t0:t0 + TB], la.b)
                tk.op("act", lambda: nc.scalar.activation(out=aa[:], in_=p[:], func=AF.Sigmoid, bias=pc["rwkv_a0"][:, hp:hp + 1]),
                      reads=[p.b, pc["rwkv_a0"].b], writes=[aa.b])
                p = self.psum()
                mm(p[:], p.b, g2[:, 0, c0:c0 + 128], g2.b, lg[:, 0, t0:t0 + TB], lg.b, True, False)
                mm(p[:], p.b, g2[:32, 1, c0:c0 + 128], g2.b, lg[:32, 1, t0:t0 + TB], lg.b, False, True)
                tk.op("act", lambda: nc.scalar.copy(gg[:], p[:]), reads=[p.b], writes=[gg.b])
                tk.op("dve", lambda: nc.vector.tensor_scalar_mul(kkr[:], kp[:], pc["rwkv_kk"][:, hp:hp + 1]),
                      reads=[kp.b, pc["rwkv_kk"].b], writes=[kkr.b])
                tk.op("act", lambda: nc.scalar.activation(out=sq[:], in_=kkr[:], func=AF.Square), reads=[kkr.b], writes=[sq.b])
                p = self.psum()
                mm(p[:], p.b, bdones[:], bdones.b, sq[:], sq.b)
                tk.op("act", lambda: nc.scalar.sqrt(tmp[:], p[:]), reads=[p.b], writes=[tmp.b])
                tk.op("dve", lambda: nc.vector.tensor_scalar_max(tmp[:], tmp[:], 1e-12), reads=[tmp.b], writes=[tmp.b])
                tk.op("dve", lambda: nc.vector.reciprocal(tmp[:], tmp[:]), reads=[tmp.b], writes=[tmp.b])
                tk.op("dve", lambda: nc.vector.tensor_tensor(out=kkr[:], in0=kkr[:], in1=tmp[:], op=ALU.mult), reads=[kkr.b, tmp.b], writes=[kkr.b])
                tk.op("dve", lambda: nc.vector.tensor_scalar(kmod[:], aa[:], pc["rwkv_ka"][:, hp:hp + 1], omk[:, hp:hp + 1], ALU.mult, ALU.add),
                      reads=[aa.b, pc["rwkv_ka"].b, omk.b], writes=[kmod.b])
                tk.op("pool", lambda: nc.gpsimd.tensor_tensor(out=kmod[:], in0=kmod[:], in1=kp[:], op=ALU.mult), reads=[kmod.b, kp.b], writes=[kmod.b])
                tk.op("pool", lambda: nc.gpsimd.tensor_tensor(out=kb[:], in0=kkr[:], in1=aa[:], op=ALU.mult), reads=[kkr.b, aa.b], writes=[kb.b])
                tk.op("dve", lambda: nc.vector.scalar_tensor_tensor(out=tmp[:], in0=rr[:], scalar=pc["rwkv_rk"][:, hp:hp + 1], in1=kmod[:],
                                                                    op0=ALU.mult, op1=ALU.mult), reads=[rr.b, kmod.b, pc["rwkv_rk"].b], writes=[tmp.b])
                p = self.psum()
                mm(p[:], p.b, bdones[:], bdones.b, tmp[:], tmp.b)
                tk.op("dve", lambda: nc.vector.tensor_tensor(out=bonus[:], in0=p[:], in1=Q5[:, 4, :], op=ALU.mult), reads=[p.b, Q5.b], writes=[bonus.b])
                tk.op("dve", lambda: nc.vector.tensor_tensor_scan(out=cum[:], data0=segm[:], data1=logw[:], initial=0.0, op0=ALU.mult, op1=ALU.add),
                      reads=[segm.b, logw.b], writes=[cum.b])
                tk.op("pool", lambda: nc.gpsimd.tensor_tensor(out=cex[:], in0=cum[:], in1=logw[:], op=ALU.subtract), reads=[cum.b, logw.b], writes=[cex.b])
                tk.op("act", lambda: nc.scalar.activation(out=epv[:], in_=cex[:], func=AF.Exp), reads=[cex.b], writes=[epv.b])
                tk.op("act", lambda: nc.scalar.activation(out=ein[:], in_=cum[:], func=AF.Exp), reads=[cum.b], writes=[ein.b])
                tk.op("act", lambda: nc.scalar.activation(out=eng[:], in_=cum[:], func=AF.Exp, scale=-1.0), reads=[cum.b], writes=[eng.b])
                tk.op("dve", lambda: nc.vector.scalar_tensor_tensor(out=Q5[:, 0, :], in0=kkr[:], scalar=-1.0, in1=epv[:], op0=ALU.mult, op1=ALU.mult),
                      reads=[kkr.b, epv.b], writes=[Q5.b])
                tk.op("pool", lambda: nc.gpsimd.tensor_tensor(out=Q5[:, 1, :], in0=rr[:], in1=ein[:], op=ALU.mult), reads=[rr.b, ein.b], writes=[Q5.b])
                tk.op("dve", lambda: nc.vector.tensor_tensor(out=Q5[:, 2, :], in0=kb[:], in1=eng[:], op=ALU.mult), reads=[kb.b, eng.b], writes=[Q5.b])
                tk.op("pool", lambda: nc.gpsimd.tensor_tensor(out=Q5[:, 3, :], in0=kmod[:], in1=eng[:], op=ALU.mult), reads=[kmod.b, eng.b], writes=[Q5.b])
                if LIM <= 2:
                    return
                NBC = 4
                for cb in range(TB // 64 // NBC):
                    chunks = list(range(cb * NBC, (cb + 1) * NBC))
                    st = {}
                    for c in chunks:
                        cs = slice(c * 64, (c + 1) * 64)
                        BD5 = BD5l[c % NBC]
                        src = Q5[:, :, cs].unsqueeze(2).to_broadcast([128, 5, 2, 64])
                        tk.op("dve", lambda: nc.vector.tensor_tensor(out=BD5[:], in0=src, in1=bdm5[:, :].rearrange("p (a h b) -> p a h b", a=5, h=2),
                                                                     op=ALU.mult), reads=[Q5.b, bdm5.b], writes=[BD5.b])
                        bd = lambda j, BD5=BD5: BD5[:, j, :, :].rearrange("p h b -> p (h b)")
                        p = self.psum()
                        ar = BD5[:, 0:2, :, :].rearrange("p a h b -> p (a h b)")
                        mm(p[:, 0:256], p.b, bd(2), BD5.b, ar, BD5.b)
                        mm(p[:, 256:512], p.b, bd(3), BD5.b, ar, BD5.b)
                        GBK = GBKl[c % NBC]
                        tk.op("dve", lambda: nc.vector.tensor_tensor(out=GBK[:], in0=p[:], in1=mask4[:], op=ALU.mult), reads=[p.b, mask4.b], writes=[GBK.b])
                        p3 = self.psum()
                        for j in range(3):
                            tk.op("pe", lambda: nc.tensor.transpose(p3[:, j * 128:(j + 1) * 128], bd(2 + j), ident[:]), reads=[BD5.b, ident.b], writes=[p3.b])
                        TM3 = TM3l[c % NBC]
                        tk.op("act", lambda: nc.scalar.copy(TM3[:], p3[:, 0:384]), reads=[p3.b], writes=[TM3.b])
                        st[c] = dict(BD5=BD5, bd=bd, GBK=GBK, TM3=TM3, cs=cs)
                    for c in chunks:
                        d = st[c]
                        GBK = d["GBK"]
                        NT = NTl[c % NBC]
                        p = self.psum()
                        tk.op("pe", lambda: nc.tensor.transpose(p[:, 0:128], GBK[:, 0:128], ident[:]), reads=[GBK.b, ident.b], writes=[p.b])
                        tk.op("act", lambda: nc.scalar.copy(NT[:], p[:, 0:128]), reads=[p.b], writes=[NT.b])
                        MX = MXl[c % NBC][0]
                        tk.op("pool", lambda: nc.gpsimd.tensor_tensor(out=MX[:, 128:256], in0=ident[:], in1=GBK[:, 0:128], op=ALU.add),
                              reads=[ident.b, GBK.b], writes=[MX.b])
                        d["NT"] = NT
                    for c in chunks:
                        d = st[c]
                        GBK, NT = d["GBK"], d["NT"]
                        MX, MT = MXl[c % NBC][0], MTl[c % NBC][0]
                        p = self.psum()
                        mm(p[:, 0:128], p.b, NT[:], NT.b, GBK[:, 0:128], GBK.b)
                        pt = self.psum()
                        mm(pt[:, 0:128], pt.b, GBK[:, 0:128], GBK.b, NT[:], NT.b)
                        tk.op("act", lambda: nc.scalar.copy(MX[:, 0:128], p[:, 0:128]), reads=[p.b], writes=[MX.b])
                        tk.op("dve", lambda: nc.vector.tensor_copy(MT[:], pt[:, 0:128]), reads=[pt.b], writes=[MT.b])
                        d["MX"], d["MT"], d["par"] = MX, MT, 0
                    for j in range(2, 6):
                        for c in chunks:
                            d = st[c]
                            MX, MT = d["MX"], d["MT"]
                            par = 1 - d["par"]
                            MX2, MT2 = MXl[c % NBC][par], MTl[c % NBC][par]
                            pm = self.psum()
                            mm(pm[:, 0:128], pm.b, MT[:], MT.b, MX[:, 0:128], MX.b)
                            px = self.psum()
                            mm(px[:, 0:128], px.b, MT[:], MT.b, MX[:, 128:256], MX.b)
                            pt = self.psum()
                            mm(pt[:, 0:128], pt.b, MX[:, 0:128], MX.b, MT[:], MT.b)
                            tk.op("act", lambda: nc.scalar.copy(MX2[:, 0:128], pm[:, 0:128]), reads=[pm.b], writes=[MX2.b])
                            tk.op("dve", lambda: nc.vector.tensor_tensor(out=MX2[:, 128:256], in0=px[:, 0:128], in1=MX[:, 128:256], op=ALU.add),
                                  reads=[px.b, MX.b], writes=[MX2.b])
                            tk.op("act", lambda: nc.scalar.copy(MT2[:], pt[:, 0:128]), reads=[pt.b], writes=[MT2.b])
                            d["MX"], d["MT"], d["par"] = MX2, MT2, par
                    for c in chunks:
                        d = st[c]
                        MX, MT = d["MX"], d["MT"]
                        p = self.psum()
                        mm(p[:, 0:128], p.b, MT[:], MT.b, MX[:, 128:256], MX.b)
                        TT = TTl[c % NBC]
                        tk.op("dve", lambda: nc.vector.tensor_tensor(out=TT[:], in0=p[:, 0:128], in1=MX[:, 128:256], op=ALU.add), reads=[p.b, MX.b], writes=[TT.b])
                        d["TT"] = TT
                    for c in chunks:
                        d = st[c]
                        bd, GBK, TM3, TT, cs = d["bd"], d["GBK"], d["TM3"], d["TT"], d["cs"]
                        BD5 = d["BD5"]
                        PCc = ein[:, c * 64 + 63:c * 64 + 64]
                        SP = SPr.next()
                        tk.op("act", lambda: nc.scalar.activation(out=SP[:], in_=S[:], func=AF.Identity, scale=PCc), reads=[S.b, ein.b], writes=[SP.b])
                        p = self.psum()
                        mm(p[:, 0:128], p.b, bd(0), BD5.b, S[:], S.b, True, False)
                        mm(p[:, 0:128], p.b, GBK[:, 256:384], GBK.b, TM3[:, 256:384], TM3.b, False, True)
                        RH = RHr.next()
                        tk.op("act", lambda: nc.scalar.copy(RH[:], p[:, 0:128]), reads=[p.b], writes=[RH.b])
                        p = self.psum()
                        mm(p[:, 0:128], p.b, TT[:], TT.b, RH[:], RH.b)
                        U = Ur.next()
                        tk.op("dve", lambda: nc.vector.tensor_copy(U[:], p[:, 0:128]), reads=[p.b], writes=[U.b])
                        pS = self.psum()
                        mm(pS[:, 0:128], pS.b, TM3[:, 0:128], TM3.b, U[:], U.b, True, False)
                        mm(pS[:, 0:128], pS.b, TM3[:, 128:256], TM3.b, TM3[:, 256:384], TM3.b, False, True)
                        S2 = Sr.next()
                        tk.op("dve", lambda: nc.vector.scalar_tensor_tensor(out=S2[:], in0=pS[:, 0:128], scalar=PCc, in1=SP[:], op0=ALU.mult, op1=ALU.add),
                              reads=[pS.b, ein.b, SP.b], writes=[S2.b])
                        p = self.psum()
                        mm(p[:, 0:128], p.b, S[:], S.b, bd(1), BD5.b, True, False)
                        mm(p[:, 0:128], p.b, U[:], U.b, GBK[:, 128:256], GBK.b, False, False)
                        mm(p[:, 0:128], p.b, TM3[:, 256:384], TM3.b, GBK[:, 384:512], GBK.b, False, True)
                        tk.op("act", lambda: nc.scalar.copy(yfm[0:64, cs], p[0:64, 0:64]), reads=[p.b], writes=[yfm.b])
                        tk.op("act", lambda: nc.scalar.copy(yfm[64:128, cs], p[64:128, 64:128]), reads=[p.b], writes=[yfm.b])
                        S = S2
                if LIM <= 5:
                    return
                p = self.psum()
                mm(p[:], p.b, bdones[:], bdones.b, yfm[:], yfm.b)
                tk.op("dve", lambda: nc.vector.scalar_tensor_tensor(out=dd[:], in0=p[:], scalar=-1.0 / 64, in1=yfm[:], op0=ALU.mult, op1=ALU.add),
                      reads=[p.b, yfm.b], writes=[dd.b])
                tk.op("act", lambda: nc.scalar.activation(out=sq[:], in_=dd[:], func=AF.Square), reads=[dd.b], writes=[sq.b])
                p = self.psum()
                mm(p[:], p.b, bdones[:], bdones.b, sq[:], sq.b)
                tk.op("dve", lambda: nc.vector.tensor_scalar(tmp[:], p[:], 1.0 / 64, 64e-5, ALU.mult, ALU.add), reads=[p.b], writes=[tmp.b])
                tk.op("act", lambda: nc.scalar.sqrt(tmp[:], tmp[:]), reads=[tmp.b], writes=[tmp.b])
                tk.op("dve", lambda: nc.vector.reciprocal(tmp[:], tmp[:]), reads=[tmp.b], writes=[tmp.b])
                tk.op("dve", lambda: nc.vector.tensor_tensor(out=dd[:], in0=dd[:], in1=tmp[:], op=ALU.mult), reads=[dd.b, tmp.b], writes=[dd.b])
                tk.op("act", lambda: nc.scalar.activation(out=dd[:], in_=dd[:], func=AF.Identity, bias=pc["rwkv_lnx_b"][:, hp:hp + 1],
                                                          scale=pc["rwkv_lnx_w"][:, hp:hp + 1]),
                      reads=[dd.b, pc["rwkv_lnx_b"].b, pc["rwkv_lnx_w"].b], writes=[dd.b])
                tk.op("pool", lambda: nc.gpsimd.tensor_tensor(out=dd[:], in0=dd[:], in1=bonus[:], op=ALU.add), reads=[dd.b, bonus.b], writes=[dd.b])
                o = ob.next()
                tk.op("dve", lambda: nc.vector.tensor_tensor(out=o[:], in0=dd[:], in1=gg[:], op=ALU.mult), reads=[dd.b, gg.b], writes=[o.b])
                tk.dma("pool", self.din["yr_fm"].ap()[c0:c0 + 128, t0:t0 + TB], o[:], reads=[o.b], writes=[self.dbuf["yr_fm"]])
                if LIM <= 6:
                    return


Prog.phase_rwkv = _rwkv


def _nsa_bias(self):
    nc, tk = self.nc, self.tk
    self.scratch("bvec_c", [16, LVEC], BF16)
    self.scratch("bvec_w", [16, LVEC], BF16)
    with self.scope() as es:
        tab = self.sb(es, "nb_tab", [33, 16], F32)
        tk.op("pool", lambda: nc.gpsimd.memset(tab[:], 1.0), writes=[tab.b])
        tk.dma("sp", tab[0:32, :], self.din["rel_bias"].ap()[:, :], reads=[self.dbuf["rel_bias"]], writes=[tab.b])
        e33 = self.sb(es, "nb_e33", [33, LVEC], F32)
        ob = self.ring(es, "nb_o", 2, [16, 512], BF16)
        for cname, dname in (("c_e33c", "bvec_c"), ("c_e33w", "bvec_w")):
            tk.dma("sp", e33[:], self.din[cname].ap()[:, :], reads=[self.dbuf[cname]], writes=[e33.b])
            for j in range(LVEC // 512):
                p = self.psum()
                tk.op("pe", lambda: nc.tensor.matmul(p[:16, :], lhsT=tab[:], rhs=e33[:, j * 512:(j + 1) * 512], start=True, stop=True),
                      reads=[tab.b, e33.b], writes=[p.b])
                o = ob.next()
                tk.op("act", lambda: nc.scalar.copy(o[:], p[:16, :]), reads=[p.b], writes=[o.b])
                tk.dma("pool", self.din[dname].ap()[:, j * 512:(j + 1) * 512], o[:], reads=[o.b], writes=[self.dbuf[dname]])


def _nsa(self):
    nc, tk = self.nc, self.tk
    self.scratch("yn_fm", [1024, T], BF16)
    gen = Ring(self.psr.tiles[0:4])
    accp = Ring(self.psr.tiles[4:8])

    def mm(p_ap, pbuf, lhsT, lb, rhs, rb, start=True, stop=True):
        tk.op("pe", lambda: nc.tensor.matmul(p_ap, lhsT=lhsT, rhs=rhs, start=start, stop=stop), reads=[lb, rb], writes=[pbuf])

    with self.scope() as es:
        Jb = self.load_const(es, "c_J", BF16, tmp_es=es)
        c2s_f = self.sb(es, "ns_c2sf", [128, 2, 64], F32)
        tk.dma("sp", c2s_f[:, :, :], self.dap("c_c2s", 0, [[64, 128], [128 * 64, 2], [1, 64]]), reads=[self.dbuf["c_c2s"]], writes=[c2s_f.b])
        c2s = self.sb(es, "ns_c2s", [128, 2, 64], BF16)
        tk.op("dve", lambda: nc.vector.tensor_copy(c2s[:], c2s_f[:]), reads=[c2s_f.b], writes=[c2s.b])
        exf = self.sb(es, "ns_exf", [64, 4096], F32)
        tk.dma("sp", exf[:], self.din["c_expand"].ap()[:, :], reads=[self.dbuf["c_expand"]], writes=[exf.b])
        exb = self.sb(es, "ns_exb", [64, 4096], BF16)
        tk.op("dve", lambda: nc.vector.tensor_scalar_mul(exb[:], exf[:], BIG), reads=[exf.b], writes=[exb.b])
        ones = self.sb(es, "ns_ones", [128, 64], BF16)
        tk.op("pool", lambda: nc.gpsimd.memset(ones[:], 1.0), writes=[ones.b])
        kgain = self.col_vec(es, "nsa_k_gain", 0, 0, 1, "ns_kg", p=64)
        ident, bdones = self.ident, self.bdones
        kcmpT = [self.sb(es, f"ns_kcT{g}", [64, 256], BF16) for g in range(4)]
        vcmp = [self.sb(es, f"ns_vc{g}", [128, 2, 64], BF16) for g in range(4)]
        with self.scope() as es2:
            kc2 = self.sb(es2, "ns_kc2", [128, T], BF16)
            w1t = self.sb(es2, "ns_w1", [128, 16, 256], BF16)
            w2t = self.sb(es2, "ns_w2", [128, 2, 64], BF16)
            hg_ = self.sb(es2, "ns_hg", [128, 2, 256], BF16)
            xx = self.sb(es2, "ns_x", [128, 256], F32)
            x2 = self.sb(es2, "ns_x2", [128, 256], F32)
            pvb = self.sb(es2, "ns_pvb", [128, 2], F32)
            t64 = self.sb(es2, "ns_t64", [64, 256], F32)
            t64b = self.sb(es2, "ns_t64b", [64, 256], F32)
            for kind in range(2):
                sfx = "_k" if kind == 0 else "_v"
                self.load_w(w1t, "cmp_w1" + sfx + "_bf", 0, 256, kchunks=16)
                self.load_w(w2t, "cmp_w2" + sfx + "_bf", 0, 64, kchunks=2)
                pe_f = self.col_vec(es2, "cmp_pe" + sfx, 0, 0, 16, "ns_pe" + sfx)
                pe_b = self.sb(es2, "ns_peb" + sfx, [128, 16], BF16)
                tk.op("dve", lambda: nc.vector.tensor_copy(pe_b[:], pe_f[:]), reads=[pe_f.b], writes=[pe_b.b])
                for ct in range(2):
                    p = gen.next()
                    for l2 in range(16):
                        mm(p[:, 0:1], p.b, w1t[:, l2, ct * 128:(ct + 1) * 128], w1t.b, pe_b[:, l2:l2 + 1], pe_b.b, l2 == 0, l2 == 15)
                    tk.op("dve", lambda: nc.vector.tensor_copy(pvb[:, ct:ct + 1], p[:, 0:1]), reads=[p.b], writes=[pvb.b])
                for g in range(4):
                    r0 = 256 * kind + 64 * g
                    tk.op("pool", lambda: nc.gpsimd.memset(kc2[64:128, T - 1:T], 0.0), writes=[kc2.b])
                    tk.dma("sp", kc2[0:64, :], self.din["kcvc_fm"].ap()[r0:r0 + 64, :], reads=[self.dbuf["kcvc_fm"]], writes=[kc2.b])
                    tk.dma("sp", kc2[64:128, 0:T - 1], self.din["kcvc_fm"].ap()[r0:r0 + 64, 1:T], reads=[self.dbuf["kcvc_fm"]], writes=[kc2.b])
                    tk.op("pool", lambda: nc.gpsimd.memset(hg_[:], 0.0), writes=[hg_.b])
                    for ct in range(2):
                        p = gen.next()
                        for l2 in range(16):
                            rhs = kc2[:, 2 * l2: 2 * l2 + 16 * 254 + 1: 16]
                            mm(p[:, 0:255], p.b, w1t[:, l2, ct * 128:(ct + 1) * 128], w1t.b, rhs, kc2.b, l2 == 0, l2 == 15)
                        tk.op("act", lambda: nc.scalar.activation(out=xx[:, 0:255], in_=p[:, 0:255], func=AF.Identity, bias=pvb[:, ct:ct + 1]),
                              reads=[p.b, pvb.b], writes=[xx.b])
                        tk.op("act", lambda: nc.scalar.activation(out=x2[:, 0:255], in_=xx[:, 0:255], func=AF.Square), reads=[xx.b], writes=[x2.b])
                        tk.op("dve", lambda: nc.vector.tensor_scalar(x2[:, 0:255], x2[:, 0:255], 0.044715, 1.0, ALU.mult, ALU.add), reads=[x2.b], writes=[x2.b])
                        tk.op("dve", lambda: nc.vector.tensor_tensor(out=x2[:, 0:255], in0=x2[:, 0:255], in1=xx[:, 0:255], op=ALU.mult), reads=[x2.b, xx.b], writes=[x2.b])
                        tk.op("act", lambda: nc.scalar.activation(out=x2[:, 0:255], in_=x2[:, 0:255], func=AF.Sigmoid, scale=1.5957691216057308),
                              reads=[x2.b], writes=[x2.b])
                        tk.op("dve", lambda: nc.vector.tensor_tensor(out=hg_[:, ct, 0:255], in0=x2[:, 0:255], in1=xx[:, 0:255], op=ALU.mult),
                              reads=[x2.b, xx.b], writes=[hg_.b])
                    if kind == 0:
                        p = gen.next()
                        for ct in range(2):
                            mm(p[0:64, 0:256], p.b, w2t[:, ct, :], w2t.b, hg_[:, ct, :], hg_.b, ct == 0, ct == 1)
                        tk.op("act", lambda: nc.scalar.activation(out=t64[:], in_=p[0:64, 0:256], func=AF.Square), reads=[p.b], writes=[t64.b])
                        p2 = gen.next()
                        mm(p2[0:64, 0:256], p2.b, bdones[0:64, 0:64], bdones.b, t64[:], t64.b)
                        tk.op("dve", lambda: nc.vector.tensor_scalar(t64[:], p2[0:64, 0:256], 1.0 / 64, 1e-6, ALU.mult, ALU.add), reads=[p2.b], writes=[t64.b])
                        tk.op("act", lambda: nc.scalar.sqrt(t64[:], t64[:]), reads=[t64.b], writes=[t64.b])
                        tk.op("dve", lambda: nc.vector.reciprocal(t64[:], t64[:]), reads=[t64.b], writes=[t64.b])
                        tk.op("dve", lambda: nc.vector.tensor_tensor(out=t64b[:], in0=p[0:64, 0:256], in1=t64[:], op=ALU.mult), reads=[p.b, t64.b], writes=[t64b.b])
                        tk.op("dve", lambda: nc.vector.tensor_scalar_mul(kcmpT[g][:], t64b[:], kgain[:, 0:1]), reads=[t64b.b, kgain.b], writes=[kcmpT[g].b])
                    else:
                        for nt in range(2):
                            p = gen.next()
                            for ct in range(2):
                                mm(p[:, 0:64], p.b, hg_[:, ct, nt * 128:(nt + 1) * 128], hg_.b, w2t[:, ct, :], w2t.b, ct == 0, ct == 1)
                            tk.op("act", lambda: nc.scalar.copy(vcmp[g][:, nt, :], p[:, 0:64]), reads=[p.b], writes=[vcmp[g].b])
        if getattr(self, "ns_lim", 99) <= 1:
            return
        ksT = self.sb(es, "ns_ksT", [64, T], BF16)
        kwT = self.sb(es, "ns_kwT", [64, T], BF16)
        Vs = self.sb(es, "ns_Vs", [128, 32, 128], BF16)
        Vw = self.sb(es, "ns_Vw", [128, 32, 128], BF16)
        tk.op("pool", lambda: nc.gpsimd.memset(Vs[:], 1.0), writes=[Vs.b])
        tk.op("pool", lambda: nc.gpsimd.memset(Vw[:], 1.0), writes=[Vw.b])
        vco = [self.sb(es, f"ns_vco{g}", [128, 2, 128], BF16) for g in range(4)]
        for g in range(4):
            tk.op("pool", lambda: nc.gpsimd.memset(vco[g][:], 1.0), writes=[vco[g].b])
            tk.op("dve", lambda: nc.vector.tensor_copy(vco[g][:, :, 0:64], vcmp[g][:]), reads=[vcmp[g].b], writes=[vco[g].b])
        bfar = self.sb(es, "ns_bfar", [128, 16], F32)
        tk.dma("sp", bfar[:], self.din["rel_bias"].ap()[31:32, :].partition_broadcast(128), reads=[self.dbuf["rel_bias"]], writes=[bfar.b])
        qTr = self.ring(es, "ns_qT", 2, [64, 4, 512], BF16)
        gbr = self.ring(es, "ns_gb", 2, [64, 12, 512], F32)
        Hr = self.ring(es, "ns_H", 6, [128, 512], BF16)
        Er = self.ring(es, "ns_E", 4, [128, 512], BF16)
        Ec = self.sb(es, "ns_Ec", [128, 8, 512], BF16, dj=True)
        acc = self.sb(es, "ns_acc", [64, 4, 512], F32)
        impa = self.sb(es, "ns_impa", [64, 512], F32)
        frc = self.ring(es, "ns_frc", 2, [64, 512], F32)
        rdr = self.ring(es, "ns_rd", 3, [64, 512], F32)
        t1r = self.ring(es, "ns_t1", 3, [64, 512], F32)
        impq = self.sb(es, "ns_impq", [128, 4, 64], F32)
        selq = self.sb(es, "ns_selq", [128, 4, 64], F32)
        wk = self.sb(es, "ns_wk", [128, 64], F32)
        m8 = self.sb(es, "ns_m8", [128, 16], F32)
        selT = self.sb(es, "ns_selT", [64, 512], BF16)
        obr = self.ring(es, "ns_ob", 2, [64, 512], BF16)
        pend = []

        def flush():
            while pend:
                pend.pop(0)()

        def hankel(vname, h, c, pstep):
            H = Hr.next()
            src = self.dap(vname, h * LVEC + c, [[pstep, 128], [1, 512]])
            tk.dma("sp", H[:], src, reads=[self.dbuf[vname]], writes=[H.b])
            return H

        def finish_branch(pn, hg, gcol, gb, first):
            rd = rdr.next()
            tk.op("dve", lambda: nc.vector.tensor_scalar_max(rd[:], pn[64:128, :], 1e-30), reads=[pn.b], writes=[rd.b])
            tk.op("dve", lambda: nc.vector.reciprocal(rd[:], rd[:]), reads=[rd.b], writes=[rd.b])
            t1 = t1r.next()
            tk.op("dve", lambda: nc.vector.tensor_tensor(out=t1[:], in0=pn[0:64, :], in1=rd[:], op=ALU.mult), reads=[pn.b, rd.b], writes=[t1.b])
            if first:
                tk.op("pool", lambda: nc.gpsimd.tensor_tensor(out=acc[:, hg, :], in0=t1[:], in1=gb[:, gcol, :], op=ALU.mult),
                      reads=[t1.b, gb.b], writes=[acc.b])
            else:
                tk.op("pool", lambda: nc.gpsimd.tensor_tensor(out=t1[:], in0=t1[:], in1=gb[:, gcol, :], op=ALU.mult), reads=[t1.b, gb.b], writes=[t1.b])
                tk.op("pool", lambda: nc.gpsimd.tensor_tensor(out=acc[:, hg, :], in0=acc[:, hg, :], in1=t1[:], op=ALU.add), reads=[t1.b, acc.b], writes=[acc.b])
            return rd

        def key_tile(s_mms, e_ap, e_buf, act_bias, pv):
            p = gen.next()
            for i, (lhsT, lb, rhs, rb) in enumerate(s_mms):
                mm(p[:], p.b, lhsT, lb, rhs, rb, i == 0, i == len(s_mms) - 1)
            if act_bias is None:
                tk.op("act", lambda: nc.scalar.activation(out=e_ap, in_=p[:], func=AF.Exp), reads=[p.b], writes=[e_buf])
            else:
                tk.op("act", lambda: nc.scalar.activation(out=e_ap, in_=p[:], func=AF.Exp, bias=act_bias), reads=[p.b, bfar.b], writes=[e_buf])
            flush()
            pend.append(pv)

        for g in range(4):
            flush()
            tk.dma("sp", ksT[:], self.din["ks_fm"].ap()[64 * g:64 * g + 64, :], reads=[self.dbuf["ks_fm"]], writes=[ksT.b])
            tk.dma("sp", kwT[:], self.din["kw_fm"].ap()[64 * g:64 * g + 64, :], reads=[self.dbuf["kw_fm"]], writes=[kwT.b])
            for k8 in range(4):
                tk.dma("sp", Vs[:, 8 * k8:8 * k8 + 8, 0:64], self.dap("vsw_tm", 64 * g + 8 * k8 * 128 * 512, [[512, 128], [128 * 512, 8], [1, 64]]),
                       reads=[self.dbuf["vsw_tm"]], writes=[Vs.b])
                tk.dma("sp", Vw[:, 8 * k8:8 * k8 + 8, 0:64], self.dap("vsw_tm", 256 + 64 * g + 8 * k8 * 128 * 512, [[512, 128], [128 * 512, 8], [1, 64]]),
                       reads=[self.dbuf["vsw_tm"]], writes=[Vw.b])
            for qt in range(T // 512):
                t0 = qt * 512
                qT = qTr.next()
                tk.dma("sp", qT[:], self.dap("q_fm", 256 * g * T + t0, [[T, 64], [64 * T, 4], [1, 512]]), reads=[self.dbuf["q_fm"]], writes=[qT.b])
                gb = gbr.next()
                for j in range(12):
                    row = 12 * g + j
                    tk.dma("sp", gb[:, j, :], self.din["gates_fm"].ap()[row:row + 1, t0:t0 + 512].partition_broadcast(64),
                           reads=[self.dbuf["gates_fm"]], writes=[gb.b])
                fr = frc.next()
                tk.dma("sp", fr[:], self.din["c_forced"].ap()[:, t0:t0 + 512], reads=[self.dbuf["c_forced"]], writes=[fr.b])
                nnt = 2 if t0 >= 2048 else 1
                for hg in range(4):
                    h = 4 * g + hg
                    pn, pi = accp.next(), accp.next()
                    for nt in range(nnt):
                        H = hankel("bvec_c", h, OFFC + t0 - 16 * 128 * nt - 2063, 16)
                        e_ap = Ec[:, hg * 2 + nt, :]

                        def pv(pn=pn, pi=pi, nt=nt, e_ap=e_ap):
                            mm(pn[:], pn.b, vco[g][:, nt, :], vco[g].b, e_ap, Ec.b, nt == 0, nt == nnt - 1)
                            mm(pi[0:64, :], pi.b, c2s[:, nt, :], c2s.b, e_ap, Ec.b, nt == 0, nt == nnt - 1)
                        key_tile([(kcmpT[g][:, nt * 128:(nt + 1) * 128], kcmpT[g].b, qT[:, hg, :], qT.b), (Jb[:], Jb.b, H[:], H.b)], e_ap, Ec.b, None, pv)

                    def fin(pn=pn, pi=pi, hg=hg, gb=gb):
                        rd = finish_branch(pn, hg, 3 * hg + 0, gb, True)
                        if hg == 0:
                            tk.op("dve", lambda: nc.vector.tensor_tensor(out=impa[:], in0=pi[0:64, :], in1=rd[:], op=ALU.mult), reads=[pi.b, rd.b], writes=[impa.b])
                        else:
                            t1 = t1r.next()
                            tk.op("dve", lambda: nc.vector.tensor_tensor(out=t1[:], in0=pi[0:64, :], in1=rd[:], op=ALU.mult), reads=[pi.b, rd.b], writes=[t1.b])
                            tk.op("pool", lambda: nc.gpsimd.tensor_tensor(out=impa[:], in0=impa[:], in1=t1[:], op=ALU.add), reads=[impa.b, t1.b], writes=[impa.b])
                    pend.append(fin)
                flush()
                tk.op("dve", lambda: nc.vector.tensor_tensor(out=impa[:], in0=impa[:], in1=fr[:], op=ALU.max), reads=[impa.b, fr.b], writes=[impa.b])
                p = gen.next()
                for s4 in range(4):
                    tk.op("pe", lambda: nc.tensor.transpose(p[:, s4 * 64:(s4 + 1) * 64], impa[:, s4 * 128:(s4 + 1) * 128], ident[0:64, 0:64]),
                          reads=[impa.b, ident.b], writes=[p.b])
                tk.op("act", lambda: nc.scalar.copy(impq[:], p[:, 0:256].rearrange("p (a b) -> p a b", a=4)), reads=[p.b], writes=[impq.b])
                for s4 in range(4):
                    tk.op("dve", lambda: nc.vector.max(out=m8[:, 0:8], in_=impq[:, s4, :]), reads=[impq.b], writes=[m8.b])
                    tk.op("dve", lambda: nc.vector.match_replace(out=wk[:], in_to_replace=m8[:, 0:8], in_values=impq[:, s4, :], imm_value=-1e30),
                          reads=[impq.b, m8.b], writes=[wk.b])
                    tk.op("dve", lambda: nc.vector.max(out=m8[:, 8:16], in_=wk[:]), reads=[wk.b], writes=[m8.b])
                    tk.op("dve", lambda: nc.vector.tensor_scalar(selq[:, s4, :], impq[:, s4, :], m8[:, 15:16], 1.0, ALU.is_ge, ALU.subtract),
                          reads=[impq.b, m8.b], writes=[selq.b])
                p = gen.next()
                for s4 in range(4):
                    tk.op("pe", lambda: nc.tensor.transpose(p[0:64, s4 * 128:(s4 + 1) * 128], selq[:, s4, :], ident[:]),
                          reads=[selq.b, ident.b], writes=[p.b])
                tk.op("act", lambda: nc.scalar.copy(selT[:], p[0:64, :]), reads=[p.b], writes=[selT.b])
                for hg in range(4):
                    h = 4 * g + hg
                    kts = list(range(max(0, (t0 - 512) // 128), (t0 + 511) // 128 + 1))
                    pn = accp.next()
                    for i, kt in enumerate(kts):
                        H = hankel("bvec_w", h, OFFC + t0 - 128 * kt - 127, 1)
                        E = Er.next()

                        def pv(pn=pn, kt=kt, E=E, first=(i == 0), last=(i == len(kts) - 1)):
                            mm(pn[:], pn.b, Vw[:, kt, :], Vw.b, E[:], E.b, first, last)
                        key_tile([(kwT[:, kt * 128:(kt + 1) * 128], kwT.b, qT[:, hg, :], qT.b), (Jb[:], Jb.b, H[:], H.b)], E[:], E.b, None, pv)
                    pend.append(lambda pn=pn, hg=hg, gb=gb: finish_branch(pn, hg, 3 * hg + 2, gb, False))
                    kts = list(range(0, (t0 + 511) // 128 + 1))
                    pn = accp.next()
                    for i, kt in enumerate(kts):
                        far = (128 * kt <= t0 - 256)
                        E = Er.next()
                        s_mms = [(ksT[:, kt * 128:(kt + 1) * 128], ksT.b, qT[:, hg, :], qT.b)]
                        if not far:
                            H = hankel("bvec_c", h, OFFC + t0 - 128 * kt - 127, 1)
                            s_mms.append((Jb[:], Jb.b, H[:], H.b))
                        s_mms.append((exb[:, kt * 128:(kt + 1) * 128], exb.b, selT[:], selT.b))

                        def pv(pn=pn, kt=kt, E=E, first=(i == 0), last=(i == len(kts) - 1)):
                            mm(pn[:], pn.b, Vs[:, kt, :], Vs.b, E[:], E.b, first, last)
                        key_tile(s_mms, E[:], E.b, bfar[:, h:h + 1] if far else None, pv)

                    def fin2(pn=pn, hg=hg, gb=gb, h=h, t0=t0):
                        finish_branch(pn, hg, 3 * hg + 1, gb, False)
                        o = obr.next()
                        tk.op("act", lambda: nc.scalar.copy(o[:], acc[:, hg, :]), reads=[acc.b], writes=[o.b])
                        tk.dma("pool", self.din["yn_fm"].ap()[64 * h:64 * h + 64, t0:t0 + 512], o[:], reads=[o.b], writes=[self.dbuf["yn_fm"]])
                    pend.append(fin2)
                if getattr(self, "ns_lim", 99) <= 2:
                    flush()
                    return
        flush()


Prog.nsa_bias = _nsa_bias
Prog.phase_nsa = _nsa


def _proj_tm_res(self, actT, tok0, ntok, kchunks, w, res_name, res_row0, dst_name, dst_row0, es):
    nc, tk = self.nc, self.tk
    xr = self.ring(es, "pt_x", 2, [128, D], F32)
    orr = self.ring(es, "pt_o", 2, [128, D], F32)
    for i in range(ntok // 128):
        x = xr.next()
        o = orr.next()
        tk.dma("sp", x[:], self.din[res_name].ap()[res_row0 + i * 128:res_row0 + (i + 1) * 128, :], reads=[self.dbuf[res_name]], writes=[x.b])
        for half in range(2):
            p = self.psum()
            for kc in range(kchunks):
                tk.op("pe", lambda: nc.tensor.matmul(p[:], lhsT=actT[:, kc, tok0 + i * 128:tok0 + (i + 1) * 128], rhs=w[:, kc, half * 512:(half + 1) * 512],
                                                      start=(kc == 0), stop=(kc == kchunks - 1)), reads=[actT.b, w.b], writes=[p.b])
            tk.op("dve", lambda: nc.vector.tensor_tensor(out=o[:, half * 512:(half + 1) * 512], in0=p[:], in1=x[:, half * 512:(half + 1) * 512], op=ALU.add),
                  reads=[p.b, x.b], writes=[o.b])
        tk.dma("pool", self.din[dst_name].ap()[dst_row0 + i * 128:dst_row0 + (i + 1) * 128, :], o[:], reads=[o.b], writes=[self.dbuf[dst_name]])


def _merge(self, bi):
    nc, tk = self.nc, self.tk
    self.scratch("h1", [T, D], F32)
    with self.scope() as es:
        mT = self.sb(es, "mg_mT", [128, 8, T], BF16, dj=True)
        with self.scope() as es2:
            wr = self.sb(es2, "mg_wr", [128, 8, 1024], BF16)
            wn = self.sb(es2, "mg_wn", [128, 8, 1024], BF16)
            self.load_w(wr, "w_branch_rwkv_bf", 0, 1024)
            self.load_w(wn, "w_branch_nsa_bf", 0, 1024)
            yr = self.ring(es2, "mg_yr", 2, [128, 8, 512], BF16)
            yn = self.ring(es2, "mg_yn", 2, [128, 8, 512], BF16)
            gr = self.ring(es2, "mg_g", 4, [128, 512], F32)
            tr = self.ring(es2, "mg_t", 4, [128, 512], F32)
            for tt in range(T // 512):
                a, b = yr.next(), yn.next()
                tk.dma("sp", a[:], self.dap("yr_fm", tt * 512, [[T, 128], [128 * T, 8], [1, 512]]), reads=[self.dbuf["yr_fm"]], writes=[a.b])
                tk.dma("sp", b[:], self.dap("yn_fm", tt * 512, [[T, 128], [128 * T, 8], [1, 512]]), reads=[self.dbuf["yn_fm"]], writes=[b.b])
                for ci in range(8):
                    g0, g1 = gr.next(), gr.next()
                    tk.dma("sp", g0[:], self.din["gm_fm"].ap()[ci * 128:(ci + 1) * 128, tt * 512:(tt + 1) * 512], reads=[self.dbuf["gm_fm"]], writes=[g0.b])
                    tk.dma("sp", g1[:], self.din["gm_fm"].ap()[1024 + ci * 128:1024 + (ci + 1) * 128, tt * 512:(tt + 1) * 512], reads=[self.dbuf["gm_fm"]], writes=[g1.b])
                    pr, pn = self.psum(), self.psum()
                    for kc in range(8):
                        tk.op("pe", lambda: nc.tensor.matmul(pr[:], lhsT=wr[:, kc, ci * 128:(ci + 1) * 128], rhs=a[:, kc, :], start=(kc == 0), stop=(kc == 7)),
                              reads=[wr.b, a.b], writes=[pr.b])
                    for kc in range(8):
                        tk.op("pe", lambda: nc.tensor.matmul(pn[:], lhsT=wn[:, kc, ci * 128:(ci + 1) * 128], rhs=b[:, kc, :], start=(kc == 0), stop=(kc == 7)),
                              reads=[wn.b, b.b], writes=[pn.b])
                    t0_, t1_ = tr.next(), tr.next()
                    tk.op("dve", lambda: nc.vector.tensor_tensor(out=t0_[:], in0=pr[:], in1=g0[:], op=ALU.mult), reads=[pr.b, g0.b], writes=[t0_.b])
                    tk.op("dve", lambda: nc.vector.tensor_tensor(out=t1_[:], in0=pn[:], in1=g1[:], op=ALU.mult), reads=[pn.b, g1.b], writes=[t1_.b])
                    tk.op("pool", lambda: nc.gpsimd.tensor_tensor(out=mT[:, ci, tt * 512:(tt + 1) * 512], in0=t0_[:], in1=t1_[:], op=ALU.add),
                          reads=[t0_.b, t1_.b], writes=[mT.b])
        with self.scope() as es3:
            wm = self.sb(es3, "mg_wm", [128, 8, 1024], BF16)
            self.load_w(wm, "w_mix_out_bf", 0, 1024)
            self.proj_tm_res(mT, 0, T, 8, wm, "x", bi * T, "h1", 0, es3)


def _cross(self, bi):
    nc, tk = self.nc, self.tk
    self.scratch("h2", [T, D], F32)
    HT = 2048
    with self.scope() as es:
        wq = self.sb(es, "ca_wq", [128, 8, 1024], BF16)
        wo = self.sb(es, "ca_wo", [128, 8, 1024], BF16)
        self.load_w(wq, "ca_wq_bf", 0, 1024)
        self.load_w(wo, "ca_wo_bf", 0, 1024)
        kT = self.sb(es, "ca_kT", [128, 8, NMEM], BF16, dj=True)
        Vc = self.sb(es, "ca_V", [128, 2, 1024], BF16, dj=True)
        qgain = self.col_vec(es, "ca_q_gain", 0, 0, 2, "ca_qg")
        kgain = self.col_vec(es, "ca_k_gain", 0, 0, 2, "ca_kg")
        ones_f = self.load_const(es, "c_ones")
        ones_b = self.sb(es, "ca_1b", [128, 128], BF16)
        tk.op("dve", lambda: nc.vector.tensor_copy(ones_b[:], ones_f[:]), reads=[ones_f.b], writes=[ones_b.b])
        sqr = self.ring(es, "ca_sq", 2, [128, 2, 512], F32)
        rr = self.ring(es, "ca_r", 2, [128, 512], F32)
        tmpr = self.ring(es, "ca_tmp", 2, [128, 512], F32)
        qh = self.ring(es, "ca_qh", 2, [128, 2, 512], BF16)
        Er = self.ring(es, "ca_E", 2, [128, 2, 512], BF16)

        def qk_norm(p0, p1, n, gain, scale, out_aps, out_buf):
            s = sqr.next()
            tk.op("act", lambda: nc.scalar.activation(out=s[:, 0, 0:n], in_=p0[:, 0:n], func=AF.Square), reads=[p0.b], writes=[s.b])
            tk.op("act", lambda: nc.scalar.activation(out=s[:, 1, 0:n], in_=p1[:, 0:n], func=AF.Square), reads=[p1.b], writes=[s.b])
            p2 = self.psum()
            for j in range(2):
                tk.op("pe", lambda: nc.tensor.matmul(p2[:, 0:n], lhsT=ones_f[:], rhs=s[:, j, 0:n], start=(j == 0), stop=(j == 1)),
                      reads=[ones_f.b, s.b], writes=[p2.b])
            r = rr.next()
            tk.op("dve", lambda: nc.vector.tensor_scalar(r[:, 0:n], p2[:, 0:n], 1.0 / 256, 1e-6, ALU.mult, ALU.add), reads=[p2.b], writes=[r.b])
            tk.op("act", lambda: nc.scalar.sqrt(r[:, 0:n], r[:, 0:n]), reads=[r.b], writes=[r.b])
            tk.op("dve", lambda: nc.vector.reciprocal(r[:, 0:n], r[:, 0:n]), reads=[r.b], writes=[r.b])
            for j, pj in enumerate((p0, p1)):
                t = tmpr.next()
                tk.op("dve", lambda: nc.vector.tensor_tensor(out=t[:, 0:n], in0=pj[:, 0:n], in1=r[:, 0:n], op=ALU.mult), reads=[pj.b, r.b], writes=[t.b])
                tk.op("dve", lambda: nc.vector.tensor_scalar(out_aps[j], t[:, 0:n], gain[:, j:j + 1], scale, ALU.mult, ALU.mult),
                      reads=[t.b, gain.b], writes=[out_buf])

        with self.scope() as es2:
            mnT = self.sb(es2, "ca_mnT", [128, 8, NMEM], BF16, dj=True)
            self.norm_T("mem", bi * NMEM, NMEM, "norm_mem", mnT)
            wk = self.sb(es2, "ca_wk", [128, 8, 1024], BF16)
            wv = self.sb(es2, "ca_wv", [128, 8, 1024], BF16)
            self.load_w(wk, "ca_wkv_bf", 0, 1024)
            self.load_w(wv, "ca_wkv_bf", 1024, 1024)
            for h in range(4):
                ps_ = []
                for j in range(2):
                    p = self.psum()
                    ci = 2 * h + j
                    for kc in range(8):
                        tk.op("pe", lambda: nc.tensor.matmul(p[:, 0:NMEM], lhsT=wk[:, kc, ci * 128:(ci + 1) * 128], rhs=mnT[:, kc, :], start=(kc == 0), stop=(kc == 7)),
                              reads=[wk.b, mnT.b], writes=[p.b])
                    ps_.append(p)
                qk_norm(ps_[0], ps_[1], NMEM, kgain, 1.0, [kT[:, 2 * h, :], kT[:, 2 * h + 1, :]], kT.b)
            for mt in range(2):
                for half in range(2):
                    p = self.psum()
                    for kc in range(8):
                        tk.op("pe", lambda: nc.tensor.matmul(p[:], lhsT=mnT[:, kc, mt * 128:(mt + 1) * 128], rhs=wv[:, kc, half * 512:(half + 1) * 512],
                                                              start=(kc == 0), stop=(kc == 7)), reads=[mnT.b, wv.b], writes=[p.b])
                    tk.op("act", lambda: nc.scalar.copy(Vc[:, mt, half * 512:(half + 1) * 512], p[:]), reads=[p.b], writes=[Vc.b])
        for hf in range(T // HT):
            with self.scope() as es2:
                hnT = self.sb(es2, "ca_hnT", [128, 8, HT], BF16, dj=True)
                oT = self.sb(es2, "ca_oT", [128, 8, HT], BF16, dj=True)
                self.norm_T("h1", hf * HT, HT, "norm_cross", hnT)
                for h in range(4):
                    for tt in range(HT // 512):
                        ps_ = []
                        for j in range(2):
                            p = self.psum()
                            ci = 2 * h + j
                            for kc in range(8):
                                tk.op("pe", lambda: nc.tensor.matmul(p[:], lhsT=wq[:, kc, ci * 128:(ci + 1) * 128], rhs=hnT[:, kc, tt * 512:(tt + 1) * 512],
                                                                      start=(kc == 0), stop=(kc == 7)), reads=[wq.b, hnT.b], writes=[p.b])
                            ps_.append(p)
                        q = qh.next()
                        qk_norm(ps_[0], ps_[1], 512, qgain, 1.0 / 16, [q[:, 0, :], q[:, 1, :]], q.b)
                        E = Er.next()
                        for mt in range(2):
                            p = self.psum()
                            for j in range(2):
                                tk.op("pe", lambda: nc.tensor.matmul(p[:], lhsT=kT[:, 2 * h + j, mt * 128:(mt + 1) * 128], rhs=q[:, j, :], start=(j == 0), stop=(j == 1)),
                                      reads=[kT.b, q.b], writes=[p.b])
                            tk.op("act", lambda: nc.scalar.activation(out=E[:, mt, :], in_=p[:], func=AF.Exp), reads=[p.b], writes=[E.b])
                        pd = self.psum()
                        for mt in range(2):
                            tk.op("pe", lambda: nc.tensor.matmul(pd[:], lhsT=ones_b[:], rhs=E[:, mt, :], start=(mt == 0), stop=(mt == 1)),
                                  reads=[ones_b.b, E.b], writes=[pd.b])
                        r = rr.next()
                        tk.op("dve", lambda: nc.vector.reciprocal(r[:], pd[:]), reads=[pd.b], writes=[r.b])
                        for j in range(2):
                            pn = self.psum()
                            for mt in range(2):
                                tk.op("pe", lambda: nc.tensor.matmul(pn[:], lhsT=Vc[:, mt, h * 256 + j * 128:h * 256 + (j + 1) * 128], rhs=E[:, mt, :],
                                                                      start=(mt == 0), stop=(mt == 1)), reads=[Vc.b, E.b], writes=[pn.b])
                            tk.op("dve", lambda: nc.vector.tensor_tensor(out=oT[:, 2 * h + j, tt * 512:(tt + 1) * 512], in0=pn[:], in1=r[:], op=ALU.mult),
                                  reads=[pn.b, r.b], writes=[oT.b])
                self.proj_tm_res(oT, 0, HT, 8, wo, "h1", hf * HT, "h2", hf * HT, es2)


def _ffn(self, bi):
    nc, tk = self.nc, self.tk
    self.scratch("ff_fm", [DFF, T], BF16)
    NCT = DFF // 128
    with self.scope() as es:
        hnT = self.sb(es, "ff_hnT", [128, 8, T], BF16, dj=True)
        self.norm_T("h2", 0, T, "norm_ffn", hnT)
        cw = [self.col_vec(es, "ffn_conv", j, 0, NCT, f"ff_cw{j}") for j in range(3)]
        cb = self.col_vec(es, "ffn_conv_b", 0, 0, NCT, "ff_cb")
        wring = self.ring(es, "ff_w", 4, [128, 8, 128], BF16)
        at = self.sb(es, "ff_a", [128, T + 2], F32)
        bt = self.sb(es, "ff_b", [128, T], F32)
        acc = self.sb(es, "ff_acc", [128, T], F32)
        ob = self.ring(es, "ff_ob", 2, [128, T], BF16)
        tk.op("pool", lambda: nc.gpsimd.memset(at[:, 0:2], 0.0), writes=[at.b])
        for ci in range(NCT):
            wa, wb = wring.next(), wring.next()
            self.load_w(wa, "ffn_up_bf", ci * 128, 128)
            self.load_w(wb, "ffn_up_bf", DFF + ci * 128, 128)
            for tt in range(T // 512):
                pa, pb = self.psum(), self.psum()
                for kc in range(8):
                    tk.op("pe", lambda: nc.tensor.matmul(pa[:], lhsT=wa[:, kc, :], rhs=hnT[:, kc, tt * 512:(tt + 1) * 512], start=(kc == 0), stop=(kc == 7)),
                          reads=[wa.b, hnT.b], writes=[pa.b])
                for kc in range(8):
                    tk.op("pe", lambda: nc.tensor.matmul(pb[:], lhsT=wb[:, kc, :], rhs=hnT[:, kc, tt * 512:(tt + 1) * 512], start=(kc == 0), stop=(kc == 7)),
                          reads=[wb.b, hnT.b], writes=[pb.b])
                tk.op("act", lambda: nc.scalar.copy(at[:, 2 + tt * 512:2 + (tt + 1) * 512], pa[:]), reads=[pa.b], writes=[at.b])
                tk.op("dve", lambda: nc.vector.tensor_copy(bt[:, tt * 512:(tt + 1) * 512], pb[:]), reads=[pb.b], writes=[bt.b])
            tk.op("dve", lambda: nc.vector.tensor_scalar(acc[:], at[:, 2:T + 2], cw[2][:, ci:ci + 1], cb[:, ci:ci + 1], ALU.mult, ALU.add),
                  reads=[at.b, cw[2].b, cb.b], writes=[acc.b])
            tk.op("dve", lambda: nc.vector.scalar_tensor_tensor(out=acc[:], in0=at[:, 1:T + 1], scalar=cw[1][:, ci:ci + 1], in1=acc[:], op0=ALU.mult, op1=ALU.add),
                  reads=[at.b, cw[1].b, acc.b], writes=[acc.b])
            tk.op("dve", lambda: nc.vector.scalar_tensor_tensor(out=acc[:], in0=at[:, 0:T], scalar=cw[0][:, ci:ci + 1], in1=acc[:], op0=ALU.mult, op1=ALU.add),
                  reads=[at.b, cw[0].b, acc.b], writes=[acc.b])
            tk.op("act", lambda: nc.scalar.activation(out=acc[:], in_=acc[:], func=AF.Silu), reads=[acc.b], writes=[acc.b])
            o = ob.next()
            tk.op("pool", lambda: nc.gpsimd.tensor_tensor(out=o[:], in0=acc[:], in1=bt[:], op=ALU.mult), reads=[acc.b, bt.b], writes=[o.b])
            tk.dma("pool", self.din["ff_fm"].ap()[ci * 128:(ci + 1) * 128, :], o[:], reads=[o.b], writes=[self.dbuf["ff_fm"]])
    with self.scope() as es:
        wd = self.sb(es, "ff_wd", [128, NCT, 1024], BF16)
        self.load_w(wd, "ffn_down_bf", 0, 1024, kchunks=NCT)
        TBK = 1024
        for blk in range(T // TBK):
            with self.scope() as es2:
                fT = self.sb(es2, "ff_fT", [128, NCT, TBK], BF16)
                tk.dma("sp", fT[:], self.dap("ff_fm", blk * TBK, [[T, 128], [128 * T, NCT], [1, TBK]]), reads=[self.dbuf["ff_fm"]], writes=[fT.b])
                self.proj_tm_res(fT, 0, TBK, NCT, wd, "h2", blk * TBK, "out", bi * T + blk * TBK, es2)


Prog.proj_tm_res = _proj_tm_res
Prog.phase_merge = _merge
Prog.phase_cross = _cross
Prog.phase_ffn = _ffn
```

```python
import contextlib
import math
import numpy as np
import concourse.bass as bass
import concourse.mybir as mybir
from concourse.bass_utils import run_bass_kernel_spmd

F32 = mybir.dt.float32
BF16 = mybir.dt.bfloat16
AF = mybir.ActivationFunctionType
ALU = mybir.AluOpType
AX = mybir.AxisListType

NCORES = 8
NB = 2
T = 4096
D = 1024
NMEM = 256
DFF = 2816
IN_COLS = 8016
BIG = 30000.0
OFFC = 2176
LVEC = 7680
SCALE_NSA = 0.125


class Buf:
    __slots__ = ("w", "r", "name", "dj", "xr")

    def __init__(self, name="", dj=False):
        self.w = {}
        self.r = {}
        self.name = name
        self.dj = dj
        self.xr = False


class Tile:
    def __init__(self, t, name, dj=False):
        self.t = t
        self.b = Buf(name, dj)

    def __getitem__(self, k):
        return self.t[k]


class Ring:
    def __init__(self, tiles):
        self.tiles = tiles
        self.i = 0

    def next(self):
        t = self.tiles[self.i]
        self.i = (self.i + 1) % len(self.tiles)
        return t


class TK:
    EPOCH = 20000
    NDSEM = 10

    def __init__(self, nc, es):
        self.nc = nc
        self.es = es
        self.eng = {"pe": nc.tensor, "act": nc.scalar, "dve": nc.vector,
                    "pool": nc.gpsimd, "sp": nc.sync}
        self.cnt = {e: 0 for e in self.eng}
        self.esem = {e: [] for e in self.eng}
        self.seen = {e: {} for e in self.eng}
        self.dsem = {}
        self.dptr = {}
        self.nwait = 0
        self.fence = {}

    def _newsem(self, name):
        return self.es.enter_context(self.nc.semaphore(name))

    def _engsem(self, e, epoch):
        while len(self.esem[e]) <= epoch:
            self.esem[e].append(self._newsem(f"s_{e}_{len(self.esem[e])}"))
        return self.esem[e][epoch]

    def _wait(self, e, ts):
        sem, val, src = ts
        if src == "pe" and e == "pe":
            return
        k = id(sem)
        if self.seen[e].get(k, 0) >= val:
            return
        self.seen[e][k] = val
        self.eng[e].wait_ge(sem, val)
        self.nwait += 1

    def deps(self, e, reads, writes):
        for b in reads:
            for ts in b.w.values():
                self._wait(e, ts)
            if b.xr:
                for ts in b.r.values():
                    if ts[2] != e:
                        self._wait(e, ts)
        for b in writes:
            if not (b.dj and not b.r):
                for ts in b.w.values():
                    self._wait(e, ts)
            for ts in b.r.values():
                self._wait(e, ts)

    def mark(self, ts, reads, writes):
        k = id(ts[0])
        for b in reads:
            b.r[k] = ts
        for b in writes:
            if b.dj and not b.r:
                b.w[k] = ts
            else:
                b.w = {k: ts}
                b.r = {}

    def op(self, e, ins_fn, reads=(), writes=()):
        self.deps(e, reads, writes)
        n = self.cnt[e]
        sem = self._engsem(e, n // self.EPOCH)
        val = n % self.EPOCH + 1
        ins_fn().then_inc(sem, 1)
        self.cnt[e] = n + 1
        ts = (sem, val, e)
        self.mark(ts, reads, writes)
        return ts

    def dma(self, q, out_ap, in_ap, reads=(), writes=(), **kw):
        if q not in self.dsem:
            self.dsem[q] = [[self._newsem(f"d_{q}_{i}"), 0] for i in range(self.NDSEM)]
            self.dptr[q] = 0
        slot = self.dsem[q][self.dptr[q]]
        self.dptr[q] = (self.dptr[q] + 1) % self.NDSEM
        sem, issued = slot
        if issued:
            self._wait(q, (sem, 16 * issued, None))
        self.deps(q, reads, writes)
        self.eng[q].dma_start(out=out_ap, in_=in_ap, **kw).then_inc(sem, 16)
        slot[1] = issued + 1
        ts = (sem, 16 * (issued + 1), None)
        self.mark(ts, reads, writes)
        return ts

    def update_fence(self):
        f = {}
        for e in self.eng:
            n = self.cnt[e]
            if n:
                sem = self.esem[e][(n - 1) // self.EPOCH]
                f[id(sem)] = (sem, (n - 1) % self.EPOCH + 1, e)
        for q in self.dsem:
            for sem, issued in self.dsem[q]:
                if issued:
                    f[id(sem)] = (sem, 16 * issued, None)
        self.fence = f

    def drain(self):
        for q in self.dsem:
            for sem, issued in self.dsem[q]:
                if issued:
                    self._wait(q, (sem, 16 * issued, None))


def _t5_bucket_np(dist):
    n = np.maximum(dist, 0)
    nf = np.maximum(n, 1).astype(np.float64)
    large = 16 + (np.log(nf / 16) / math.log(128 / 16) * 16).astype(np.int64)
    large = np.minimum(large, 31)
    return np.where(n < 16, n, large)


def host_consts():
    c = {}
    c["c_ident"] = np.eye(128, dtype=np.float32)
    c["c_J"] = np.ascontiguousarray(np.eye(128, dtype=np.float32)[::-1])
    hb = np.arange(128) // 64
    bd = (hb[:, None] == hb[None, :]).astype(np.float32)
    c["c_bdones"] = bd
    c["c_ones"] = np.ones((128, 128), np.float32)
    s = np.arange(128) % 64
    strict = bd * (s[:, None] < s[None, :])
    incl = bd * (s[:, None] <= s[None, :])
    c["c_mask2"] = np.concatenate([strict, incl], axis=1).astype(np.float32)
    bd5 = np.zeros((128, 5, 2, 64), np.float32)
    for h in range(2):
        bd5[64 * h:64 * h + 64, :, h, :] = 1.0
    c["c_bdmask5"] = bd5.reshape(128, 640)
    seg = np.ones((128, 1024), np.float32)
    seg[:, ::64] = 0.0
    c["c_segmask"] = seg
    dist = np.arange(LVEC) - OFFC
    bk = _t5_bucket_np(dist)
    oh = np.zeros((33, LVEC), np.float32)
    oh[bk, np.arange(LVEC)] = 1.0
    ec = oh.copy()
    ec[32] = np.where(dist >= 0, 0.0, -BIG)
    ec[:32, dist < 0] = 0.0
    ew = oh.copy()
    ok = (dist >= 0) & (dist < 512)
    ew[32] = np.where(ok, 0.0, -BIG)
    ew[:32, ~ok] = 0.0
    c["c_e33c"] = ec
    c["c_e33w"] = ew
    t = np.arange(T)
    cur = t // 64
    blk = np.arange(64)
    forced = (blk[:, None] == 0) | (blk[:, None] == cur[None, :]) | (blk[:, None] == cur[None, :] - 1)
    c["c_forced"] = np.where(forced, 1e4, 0.0).astype(np.float32)
    ex = np.zeros((64, 32, 128), np.float32)
    for kt in range(32):
        for p in range(128):
            ex[2 * kt + p // 64, kt, p] = 1.0
    c["c_expand"] = ex.reshape(64, 32 * 128)
    ncmp = 255
    ci = np.arange(256)[:, None] * 16
    sj = np.arange(64)[None, :] * 64
    c2s = ((ci <= sj + 63) & (ci + 31 >= sj)).astype(np.float32)
    c2s[ncmp:] = 0.0
    c["c_c2s"] = c2s
    return c


CONST_SHAPES = {k: v.shape for k, v in host_consts().items()}

W_SPECS = [
    ("w_in", 1024, IN_COLS), ("rwkv_w2", 64, 1024), ("rwkv_a2", 64, 1024), ("rwkv_g2", 160, 1024),
    ("cmp_w1_k", 2048, 256), ("cmp_w2_k", 256, 64), ("cmp_w1_v", 2048, 256), ("cmp_w2_v", 256, 64),
    ("w_branch_rwkv", 1024, 1024), ("w_branch_nsa", 1024, 1024), ("w_mix_out", 1024, 1024),
    ("ca_wq", 1024, 1024), ("ca_wkv", 1024, 2048), ("ca_wo", 1024, 1024),
    ("ffn_up", 1024, 2 * DFF), ("ffn_down", DFF, 1024),
]
V_SPECS = [
    ("rel_bias", (32, 16)), ("norm_mix", (1, 1024)), ("rwkv_mu", (1, 3360)), ("rwkv_w0", (1, 1024)),
    ("rwkv_a0", (1, 1024)), ("rwkv_kk", (1, 1024)), ("rwkv_ka", (1, 1024)), ("rwkv_rk", (1, 1024)),
    ("rwkv_lnx_w", (1, 1024)), ("rwkv_lnx_b", (1, 1024)), ("nsa_q_gain", (1, 64)), ("nsa_k_gain", (3, 64)),
    ("cmp_pe_k", (1, 2048)), ("cmp_pe_v", (1, 2048)), ("norm_cross", (1, 1024)), ("norm_mem", (1, 1024)),
    ("ca_q_gain", (1, 256)), ("ca_k_gain", (1, 256)), ("norm_ffn", (1, 1024)),
    ("ffn_conv", (3, DFF)), ("ffn_conv_b", (1, DFF)),
]


class Prog:
    def __init__(self, upto="all", dbg=()):
        self.upto = upto
        self.dbg = set(dbg)
        nc = self.nc = bass.Bass("TRN2", target_bir_lowering=False)
        self.es = contextlib.ExitStack()
        self.tk = TK(nc, self.es)
        self.din = {}
        self.dbuf = {}

    def dram_in(self, name, shape):
        self.din[name] = self.nc.dram_tensor(name, list(shape), F32, kind="ExternalInput")
        self.dbuf[name] = Buf(name, dj=True)
        return self.din[name]

    def scratch(self, name, shape, dt):
        if name in self.din:
            return self.din[name]
        kind = "ExternalOutput" if name in self.dbg else "Internal"
        self.din[name] = self.nc.dram_tensor(name, list(shape), dt, kind=kind)
        self.dbuf[name] = Buf(name, dj=True)
        return self.din[name]

    def sb(self, es, name, shape, dt, dj=False):
        self.uid = getattr(self, "uid", 0) + 1
        name = f"{name}_{self.uid}"
        t = Tile(es.enter_context(self.nc.sbuf_tensor(name, list(shape), dt)), name, dj)
        t.b.r = dict(self.tk.fence)
        return t

    @contextlib.contextmanager
    def scope(self):
        with contextlib.ExitStack() as es:
            yield es
        self.tk.update_fence()

    def ring(self, es, name, n, shape, dt):
        return Ring([self.sb(es, f"{name}{i}", shape, dt) for i in range(n)])

    def psum(self):
        return self.psr.next()

    def rpow(self, ap, buf, power):
        nc, tk = self.nc, self.tk
        tk.op("act", lambda: nc.scalar.activation(out=ap, in_=ap, func=AF.Ln), reads=[buf], writes=[buf])
        tk.op("act", lambda: nc.scalar.activation(out=ap, in_=ap, func=AF.Exp, scale=float(power)), reads=[buf], writes=[buf])

    def dap(self, name, offset, ap):
        return bass.AP(tensor=self.din[name], offset=offset, ap=[list(x) for x in ap])

    def load_const(self, es, name, dt=F32, tmp_es=None):
        nc, tk = self.nc, self.tk
        shp = CONST_SHAPES[name]
        t32 = self.sb(es if dt == F32 else tmp_es, name + "_f", shp, F32)
        tk.dma("sp", t32[:], self.din[name].ap()[:, :], reads=[self.dbuf[name]], writes=[t32.b])
        if dt == F32:
            return t32
        t16 = self.sb(es, name + "_h", shp, BF16)
        tk.op("dve", lambda: nc.vector.tensor_copy(t16[:], t32[:]), reads=[t32.b], writes=[t16.b])
        return t16

    def bcast_vec(self, es, name, row, c0, n, tname):
        t = self.sb(es, tname, [128, n], F32)
        src = self.din[name].ap()[row:row + 1, c0:c0 + n].partition_broadcast(128)
        self.tk.dma("sp", t[:], src, reads=[self.dbuf[name]], writes=[t.b])
        return t

    def col_vec(self, es, name, row, c0, nchunk, tname, p=128):
        nc, tk = self.nc, self.tk
        t = self.sb(es, tname, [p, nchunk], F32)
        ncols = self.din[name].shape[1]
        with self.scope() as es2:
            raw = self.sb(es2, tname + "_raw", [nchunk, p], F32)
            tk.dma("sp", raw[:], self.dap(name, row * ncols + c0, [[p, nchunk], [1, p]]), reads=[self.dbuf[name]], writes=[raw.b])
            ps_ = self.psum()
            tk.op("pe", lambda: nc.tensor.transpose(ps_[:p, 0:nchunk], raw[:], self.ident[:nchunk, :nchunk]),
                  reads=[raw.b, self.ident.b], writes=[ps_.b])
            tk.op("dve", lambda: nc.vector.tensor_copy(t[:], ps_[:p, 0:nchunk]), reads=[ps_.b], writes=[t.b])
        return t

    def phase_w(self):
        nc, tk = self.nc, self.tk
        with self.scope() as es:
            st = self.ring(es, "wst", 3, [128, 2048], F32)
            sh = self.ring(es, "wsh", 3, [128, 2048], BF16)
            k = 0
            for name, R, C in W_SPECS:
                dst = self.scratch(name + "_bf", [R, C], BF16)
                src = self.din[name].ap()
                for r0 in range(0, R, 128):
                    rr = min(128, R - r0)
                    for c0 in range(0, C, 2048):
                        cc = min(2048, C - c0)
                        a = st.next()
                        h = sh.next()
                        tk.dma("sp", a[:rr, :cc], src[r0:r0 + rr, c0:c0 + cc], reads=[self.dbuf[name]], writes=[a.b])
                        e = ("dve", "pool", "act")[k % 3]
                        k += 1
                        if e == "act":
                            tk.op(e, lambda: nc.scalar.copy(h[:rr, :cc], a[:rr, :cc]), reads=[a.b], writes=[h.b])
                        elif e == "dve":
                            tk.op(e, lambda: nc.vector.tensor_copy(h[:rr, :cc], a[:rr, :cc]), reads=[a.b], writes=[h.b])
                        else:
                            tk.op(e, lambda: nc.gpsimd.tensor_copy(h[:rr, :cc], a[:rr, :cc]), reads=[a.b], writes=[h.b])
                        tk.dma("pool", dst.ap()[r0:r0 + rr, c0:c0 + cc], h[:rr, :cc], reads=[h.b],
                               writes=[self.dbuf[name + "_bf"]])

    def norm_T(self, src_name, src_row0, ntok, gname, dstT):
        nc, tk = self.nc, self.tk
        with self.scope() as es:
            gbc = self.bcast_vec(es, gname, 0, 0, D, "nt_g")
            xr = self.ring(es, "nt_x", 2, [128, D], F32)
            xs = self.ring(es, "nt_xs", 2, [128, D], F32)
            junk = self.sb(es, "nt_junk", [128, D], BF16)
            st = self.ring(es, "nt_st", 2, [128, 4], F32)
            src = self.din[src_name].ap()
            for i in range(ntok // 128):
                x = xr.next()
                s = st.next()
                y = xs.next()
                tk.dma("sp", x[:], src[src_row0 + i * 128: src_row0 + (i + 1) * 128, :],
                       reads=[self.dbuf[src_name]], writes=[x.b])
                tk.op("act", lambda: nc.scalar.activation(out=junk[:], in_=x[:], func=AF.Square, accum_out=s[:, 0:1]),
                      reads=[x.b], writes=[junk.b, s.b])
                tk.op("dve", lambda: nc.vector.tensor_scalar(s[:, 1:2], s[:, 0:1], 1.0 / D, 1e-6, ALU.mult, ALU.add),
                      reads=[s.b], writes=[s.b])
                tk.op("act", lambda: nc.scalar.sqrt(s[:, 2:3], s[:, 1:2]), reads=[s.b], writes=[s.b])
                tk.op("dve", lambda: nc.vector.reciprocal(s[:, 3:4], s[:, 2:3]), reads=[s.b], writes=[s.b])
                tk.op("dve", lambda: nc.vector.scalar_tensor_tensor(out=y[:], in0=x[:], scalar=s[:, 3:4], in1=gbc[:],
                                                                    op0=ALU.mult, op1=ALU.mult),
                      reads=[x.b, s.b, gbc.b], writes=[y.b])
                for half in range(2):
                    p = self.psum()
                    for j in range(4):
                        kc = half * 4 + j
                        tk.op("pe", lambda: nc.tensor.transpose(p[:, j * 128:(j + 1) * 128], y[:, kc * 128:(kc + 1) * 128],
                                                                self.ident[:]),
                              reads=[y.b, self.ident.b], writes=[p.b])
                    o = dstT[:, half * 4:half * 4 + 4, i * 128:(i + 1) * 128]
                    pin = p[:, :].rearrange("p (a b) -> p a b", a=4)
                    if half == 0:
                        tk.op("act", lambda: nc.scalar.copy(o, pin), reads=[p.b], writes=[dstT.b])
                    else:
                        tk.op("dve", lambda: nc.vector.tensor_copy(o, pin), reads=[p.b], writes=[dstT.b])

    def load_w(self, tile, wname, c0, ncols, kchunks=8, r0=0):
        C = self.din[wname].shape[1]
        src = self.dap(wname, r0 * C + c0, [[C, 128], [128 * C, kchunks], [1, ncols]])
        self.tk.dma("sp", tile[:, 0:kchunks, 0:ncols], src, reads=[self.dbuf[wname]], writes=[tile.b])

    def proj_fm(self, wname, c0, ncols_total, actT, ntok, epi, kchunks=8, wring=None):
        nc, tk = self.nc, self.tk
        nct = (ncols_total + 127) // 128
        for ci in range(nct):
            cc = min(128, ncols_total - ci * 128)
            w = wring.next()
            self.load_w(w, wname, c0 + ci * 128, cc, kchunks)
            for tt in range(ntok // 512):
                p = self.psum()
                for kc in range(kchunks):
                    tk.op("pe", lambda: nc.tensor.matmul(p[:cc, :], lhsT=w[:, kc, 0:cc],
                                                          rhs=actT[:, kc, tt * 512:(tt + 1) * 512],
                                                          start=(kc == 0), stop=(kc == kchunks - 1)),
                          reads=[w.b, actT.b], writes=[p.b])
                epi(p, ci, tt, cc)

    def phase_b(self, xT):
        nc, tk = self.nc, self.tk
        self.scratch("zr_fm", [3360, T], F32)
        self.scratch("q_fm", [1024, T], BF16)
        self.scratch("kcvc_fm", [512, T], BF16)
        self.scratch("ks_fm", [256, T], BF16)
        self.scratch("kw_fm", [256, T], BF16)
        self.scratch("vsw_tm", [T, 512], BF16)
        self.scratch("gates_fm", [48, T], F32)
        self.scratch("gm_fm", [2048, T], F32)
        with self.scope() as es:
            wring = self.ring(es, "pb_w", 2, [128, 8, 128], BF16)
            o32 = self.ring(es, "pb_o32", 3, [128, 512], F32)
            o16 = self.ring(es, "pb_o16", 3, [128, 512], BF16)
            sq = self.ring(es, "pb_sq", 2, [128, 512], F32)
            qg = self.sb(es, "pb_qg", [128, 4], F32)
            eps = self.sb(es, "pb_eps", [128, 1], F32)
            tk.op("pool", lambda: nc.gpsimd.memset(eps[:], 1e-6), writes=[eps.b])
            for h in range(2):
                tk.dma("sp", qg[64 * h:64 * h + 64, 0:1], self.dap("nsa_q_gain", 0, [[1, 64], [1, 1]]),
                       reads=[self.dbuf["nsa_q_gain"]], writes=[qg.b])
                for j in (1, 2):
                    tk.dma("sp", qg[64 * h:64 * h + 64, j + 1:j + 2], self.dap("nsa_k_gain", 64 * j, [[1, 64], [1, 1]]),
                           reads=[self.dbuf["nsa_k_gain"]], writes=[qg.b])
            cnt = [0]

            def store(dname, row0, t0, tile, rows):
                tk.dma("pool", self.din[dname].ap()[row0:row0 + rows, t0:t0 + 512], tile[:rows, :], reads=[tile.b],
                       writes=[self.dbuf[dname]])

            def epi_copy(dname, row_base, dt):
                def f(p, ci, tt, cc):
                    o = (o32 if dt == F32 else o16).next()
                    cnt[0] += 1
                    if cnt[0] % 2:
                        tk.op("act", lambda: nc.scalar.copy(o[:cc, :], p[:cc, :]), reads=[p.b], writes=[o.b])
                    else:
                        tk.op("dve", lambda: nc.vector.tensor_copy(o[:cc, :], p[:cc, :]), reads=[p.b], writes=[o.b])
                    store(dname, row_base + ci * 128, tt * 512, o, cc)
                return f

            def epi_sig(dname, row_base):
                def f(p, ci, tt, cc):
                    o = o32.next()
                    tk.op("act", lambda: nc.scalar.activation(out=o[:cc, :], in_=p[:cc, :], func=AF.Sigmoid),
                          reads=[p.b], writes=[o.b])
                    store(dname, row_base + ci * 128, tt * 512, o, cc)
                return f

            def epi_norm(dname, row_base, gcol, scale):
                def f(p, ci, tt, cc):
                    s = sq.next()
                    tk.op("act", lambda: nc.scalar.activation(out=s[:], in_=p[:], func=AF.Square), reads=[p.b], writes=[s.b])
                    p2 = self.psum()
                    tk.op("pe", lambda: nc.tensor.matmul(p2[:], lhsT=self.bdones[:], rhs=s[:], start=True, stop=True),
                          reads=[self.bdones.b, s.b], writes=[p2.b])
                    r = o32.next()
                    tk.op("dve", lambda: nc.vector.tensor_scalar(r[:], p2[:], 1.0 / 64, 1e-6, ALU.mult, ALU.add),
                          reads=[p2.b], writes=[r.b])
                    self.rpow(r[:], r.b, -0.5)
                    tk.op("dve", lambda: nc.vector.tensor_tensor(out=r[:], in0=p[:], in1=r[:], op=ALU.mult),
                          reads=[p.b, r.b], writes=[r.b])
                    o = o16.next()
                    tk.op("dve", lambda: nc.vector.tensor_scalar(o[:], r[:], qg[:, gcol:gcol + 1], scale, ALU.mult, ALU.mult),
                          reads=[r.b, qg.b], writes=[o.b])
                    store(dname, row_base + ci * 128, tt * 512, o, cc)
                return f

            segs = [
                (0, 3360, epi_copy("zr_fm", 0, F32)),
                (3360, 1024, epi_norm("q_fm", 0, 0, SCALE_NSA)),
                (4384, 512, epi_copy("kcvc_fm", 0, BF16)),
                (4896, 256, epi_norm("ks_fm", 0, 2, 1.0)),
                (5408, 256, epi_norm("kw_fm", 0, 3, 1.0)),
                (5920, 48, epi_sig("gates_fm", 0)),
                (5968, 2048, epi_sig("gm_fm", 0)),
            ]
            for c0, n, epi in segs:
                self.proj_fm("w_in_bf", c0, n, xT, T, epi, wring=wring)
            wv = self.sb(es, "pb_wv", [128, 8, 512], BF16)
            self.load_w(wv, "w_in_bf", 5152, 256)
            C = IN_COLS
            tk.dma("sp", wv[:, :, 256:512], self.dap("w_in_bf", 5664, [[C, 128], [128 * C, 8], [1, 256]]),
                   reads=[self.dbuf["w_in_bf"]], writes=[wv.b])
            for i in range(T // 128):
                p = self.psum()
                for kc in range(8):
                    tk.op("pe", lambda: nc.tensor.matmul(p[:], lhsT=xT[:, kc, i * 128:(i + 1) * 128], rhs=wv[:, kc, :],
                                                          start=(kc == 0), stop=(kc == 7)), reads=[xT.b, wv.b], writes=[p.b])
                o = o16.next()
                tk.op("act", lambda: nc.scalar.copy(o[:], p[:]), reads=[p.b], writes=[o.b])
                tk.dma("pool", self.din["vsw_tm"].ap()[i * 128:(i + 1) * 128, :], o[:], reads=[o.b], writes=[self.dbuf["vsw_tm"]])

    def build(self):
        nc, tk = self.nc, self.tk
        self.dram_in("x", [NB * T, D])
        self.dram_in("mem", [NB * NMEM, D])
        for name, R, C in W_SPECS:
            self.dram_in(name, [R, C])
        for name, shp in V_SPECS:
            self.dram_in(name, shp)
        for name, shp in CONST_SHAPES.items():
            self.dram_in(name, shp)
        self.out = self.nc.dram_tensor("out", [NB * T, D], F32, kind="ExternalOutput")
        self.din["out"] = self.out
        self.dbuf["out"] = Buf("out", dj=True)
        es = self.es
        self.psr = Ring([Tile(es.enter_context(nc.psum_tensor(f"ps{i}", [128, 512], F32)), f"ps{i}") for i in range(8)])
        for t_ in self.psr.tiles:
            t_.b.xr = True
        self.ident = self.load_const(es, "c_ident")
        self.bdones = self.load_const(es, "c_bdones")
        self.phase_w()
        if self.upto == "w":
            return self.finish()
        self.nsa_bias()
        for bi in range(NB):
            self.seq(bi)
            if self.upto != "all":
                break
        return self.finish()

    def seq(self, bi):
        tk = self.tk
        with self.scope() as es1:
            xT = self.sb(es1, "xT", [128, 8, T], BF16, dj=True)
            self.norm_T("x", bi * T, T, "norm_mix", xT)
            if bi == 0 and "xT_dbg" in self.dbg:
                d = self.scratch("xT_dbg", [128, 8 * T], BF16)
                tk.dma("sp", d.ap()[:, :], xT[:, :, :].rearrange("p a b -> p (a b)"), reads=[xT.b], writes=[self.dbuf["xT_dbg"]])
            if self.upto == "a":
                return
            self.phase_b(xT)
        if self.upto == "b":
            return
        if not getattr(self, "skip_rwkv", False):
            self.phase_rwkv()
        if self.upto == "rwkv":
            return
        self.phase_nsa()
        if self.upto == "nsa":
            return
        self.phase_merge(bi)
        if self.upto == "merge":
            return
        self.phase_cross(bi)
        if self.upto == "cross":
            return
        self.phase_ffn(bi)

    def finish(self):
        self.tk.drain()
        self.es.close()
        return self.nc


def make_in_maps(inputs, cores=range(NCORES)):
    consts = host_consts()
    shared = {}
    for name, R, C in W_SPECS:
        shared[name] = np.ascontiguousarray(np.asarray(inputs[name], np.float32).reshape(R, C))
    for name, shp in V_SPECS:
        shared[name] = np.ascontiguousarray(np.asarray(inputs[name], np.float32).reshape(shp))
    shared.update(consts)
    x = np.asarray(inputs["x"], np.float32)
    mem = np.asarray(inputs["mem"], np.float32)
    maps = []
    for c in cores:
        m = dict(shared)
        m["x"] = np.ascontiguousarray(x[NB * c:NB * c + NB].reshape(NB * T, D))
        m["mem"] = np.ascontiguousarray(mem[NB * c:NB * c + NB].reshape(NB * NMEM, D))
        maps.append(m)
    return maps


def kernel(**inputs):
    prog = Prog()
    nc = prog.build()
    maps = make_in_maps(inputs)
    res = run_bass_kernel_spmd(nc, maps, core_ids=list(range(NCORES)))
    outs = [np.asarray(r["out"]).reshape(NB, T, D) for r in res.results]
    return np.concatenate(outs, axis=0).astype(np.float32)


def _rwkv(self):
    nc, tk = self.nc, self.tk
    TB = 512
    self.scratch("yr_fm", [1024, T], BF16)
    zr = self.din["zr_fm"].ap()
    zb = self.dbuf["zr_fm"]

    def shift_load(dst_ap, dst_buf, r0, nrows, t0, nt, mucol, X, dtile):
        if t0 == 0:
            tk.op("pool", lambda: nc.gpsimd.memset(X[:nrows, 0:1], 0.0), writes=[X.b])
            tk.dma("sp", X[:nrows, 1:nt + 1], zr[r0:r0 + nrows, 0:nt], reads=[zb], writes=[X.b])
        else:
            tk.dma("sp", X[:nrows, 0:nt + 1], zr[r0:r0 + nrows, t0 - 1:t0 + nt], reads=[zb], writes=[X.b])
        tk.op("pool", lambda: nc.gpsimd.tensor_tensor(out=dtile[:nrows, :nt], in0=X[:nrows, 0:nt], in1=X[:nrows, 1:nt + 1],
                                                      op=ALU.subtract), reads=[X.b], writes=[dtile.b])
        tk.op("dve", lambda: nc.vector.scalar_tensor_tensor(out=dst_ap, in0=dtile[:nrows, :nt], scalar=mucol,
                                                             in1=X[:nrows, 1:nt + 1], op0=ALU.mult, op1=ALU.add),
              reads=[dtile.b, X.b], writes=[dst_buf])

    with self.scope() as es:
        mask4 = self.sb(es, "rk_mask4", [128, 512], F32)
        for j in range(2):
            tk.dma("sp", mask4[:, j * 256:(j + 1) * 256], self.din["c_mask2"].ap()[:, :], reads=[self.dbuf["c_mask2"]], writes=[mask4.b])
        bdm5 = self.load_const(es, "c_bdmask5")
        segm = self.sb(es, "rk_seg", [128, TB], F32)
        tk.dma("sp", segm[:], self.din["c_segmask"].ap()[:, 0:TB], reads=[self.dbuf["c_segmask"]], writes=[segm.b])
        lw = self.sb(es, "rk_lw", [64, T], BF16)
        la = self.sb(es, "rk_la", [64, T], BF16)
        lg = self.sb(es, "rk_lg", [128, 2, T], BF16)
        w2 = self.sb(es, "rk_w2", [64, 1024], BF16)
        a2 = self.sb(es, "rk_a2", [64, 1024], BF16)
        g2 = self.sb(es, "rk_g2", [128, 2, 1024], BF16)
        tk.dma("sp", w2[:], self.din["rwkv_w2_bf"].ap()[:, :], reads=[self.dbuf["rwkv_w2_bf"]], writes=[w2.b])
        tk.dma("sp", a2[:], self.din["rwkv_a2_bf"].ap()[:, :], reads=[self.dbuf["rwkv_a2_bf"]], writes=[a2.b])
        tk.dma("sp", g2[:, 0, :], self.din["rwkv_g2_bf"].ap()[0:128, :], reads=[self.dbuf["rwkv_g2_bf"]], writes=[g2.b])
        tk.dma("sp", g2[0:32, 1, :], self.din["rwkv_g2_bf"].ap()[128:160, :], reads=[self.dbuf["rwkv_g2_bf"]], writes=[g2.b])
        pc = {}
        for nm in ("rwkv_w0", "rwkv_a0", "rwkv_kk", "rwkv_ka", "rwkv_rk", "rwkv_lnx_w", "rwkv_lnx_b"):
            pc[nm] = self.col_vec(es, nm, 0, 0, 8, "rk_" + nm)
        mu = self.col_vec(es, "rwkv_mu", 0, 0, 24, "rk_mu")
        omk = self.sb(es, "rk_omk", [128, 8], F32)
        tk.op("dve", lambda: nc.vector.tensor_scalar(omk[:], pc["rwkv_ka"][:], -1.0, 1.0, ALU.mult, ALU.add),
              reads=[pc["rwkv_ka"].b], writes=[omk.b])
        with self.scope() as es2:
            X = self.sb(es2, "rk_LX", [128, T + 1], F32)
            dt_ = self.sb(es2, "rk_Ld", [128, T], F32)
            zt = self.sb(es2, "rk_Lz", [128, T], F32)
            for (r0, nrows, kind) in ((3072, 64, "w"), (3136, 64, "a"), (3200, 128, "g0"), (3328, 32, "g1")):
                mucol = self.sb(es2, "rk_Lmu" + kind, [128, 1], F32)
                tk.dma("sp", mucol[:nrows, :], self.dap("rwkv_mu", r0, [[1, nrows], [1, 1]]), reads=[self.dbuf["rwkv_mu"]], writes=[mucol.b])
                shift_load(zt[:nrows, :], zt.b, r0, nrows, 0, T, mucol[:nrows, 0:1], X, dt_)
                if kind == "w":
                    tk.op("act", lambda: nc.scalar.activation(out=lw[:, :], in_=zt[:64, :], func=AF.Tanh), reads=[zt.b], writes=[lw.b])
                elif kind == "a":
                    tk.op("act", lambda: nc.scalar.copy(la[:, :], zt[:64, :]), reads=[zt.b], writes=[la.b])
                elif kind == "g0":
                    tk.op("act", lambda: nc.scalar.activation(out=lg[:, 0, :], in_=zt[:, :], func=AF.Sigmoid), reads=[zt.b], writes=[lg.b])
                else:
                    tk.op("act", lambda: nc.scalar.activation(out=lg[:32, 1, :], in_=zt[:32, :], func=AF.Sigmoid), reads=[zt.b], writes=[lg.b])
        LIM = getattr(self, "rk_lim", 99)
        if LIM <= 1:
            return
        f = lambda n: self.sb(es, n, [128, TB], F32)
        Xr = self.ring(es, "rk_X", 2, [128, TB + 1], F32)
        dtl = f("rk_d")
        rr, kp, logw, aa, gg, kkr, sq, kmod, kb, cum, cex, epv, eng, bonus, tmp = [f("rk_t%d" % i) for i in range(15)]
        einr = self.ring(es, "rk_ein", 2, [128, TB], F32)
        Q5r = self.ring(es, "rk_Q5", 2, [128, 5, TB], F32)
        yfm = f("rk_yfm")
        dd = f("rk_dd")
        ob = self.ring(es, "rk_ob", 2, [128, TB], BF16)
        BD5l = [self.sb(es, f"rk_BD5{i}", [128, 5, 2, 64], F32) for i in range(4)]
        GBKl = [self.sb(es, f"rk_GBK{i}", [128, 512], F32) for i in range(4)]
        NTl = [self.sb(es, f"rk_NT{i}", [128, 128], F32) for i in range(4)]
        MXl = [[self.sb(es, f"rk_MX{i}{k}", [128, 256], F32) for k in range(2)] for i in range(4)]
        MTl = [[self.sb(es, f"rk_MT{i}{k}", [128, 128], F32) for k in range(2)] for i in range(4)]
        TTl = [self.sb(es, f"rk_TT{i}", [128, 128], F32) for i in range(4)]
        TM3l = [self.sb(es, f"rk_TM3{i}", [128, 384], F32) for i in range(4)]
        RHr = self.ring(es, "rk_RH", 2, [128, 128], F32)
        Ur = self.ring(es, "rk_U", 2, [128, 128], F32)
        Sr = self.ring(es, "rk_S", 2, [128, 128], F32)
        SPr = self.ring(es, "rk_SP", 2, [128, 128], F32)
        ident, bdones = self.ident, self.bdones

        def mm(p_ap, pbuf, lhsT, lb, rhs, rb, start=True, stop=True):
            tk.op("pe", lambda: nc.tensor.matmul(p_ap, lhsT=lhsT, rhs=rhs, start=start, stop=stop), reads=[lb, rb], writes=[pbuf])

        for hp in range(8):
            c0 = 128 * hp
            S = Sr.next()
            tk.op("pool", lambda: nc.gpsimd.memset(S[:], 0.0), writes=[S.b])
            for tb in range(T // TB):
                t0 = tb * TB
                Q5 = Q5r.next()
                ein = einr.next()
                shift_load(rr[:, :], rr.b, c0, 128, t0, TB, mu[:, hp:hp + 1], Xr.next(), dtl)
                shift_load(kp[:, :], kp.b, 1024 + c0, 128, t0, TB, mu[:, 8 + hp:9 + hp], Xr.next(), dtl)
                shift_load(Q5[:, 4, :], Q5.b, 2048 + c0, 128, t0, TB, mu[:, 16 + hp:17 + hp], Xr.next(), dtl)
                p = self.psum()
                mm(p[:], p.b, w2[:, c0:c0 + 128], w2.b, lw[:, t0:t0 + TB], lw.b)
                tk.op("act", lambda: nc.scalar.activation(out=logw[:], in_=p[:], func=AF.Sigmoid, bias=pc["rwkv_w0"][:, hp:hp + 1]),
                      reads=[p.b, pc["rwkv_w0"].b], writes=[logw.b])
                tk.op("pool", lambda: nc.gpsimd.tensor_scalar_mul(logw[:], logw[:], -math.exp(-0.5)), reads=[logw.b], writes=[logw.b])
                p = self.psum()
                mm(p[:], p.b, a2[:, c0:c0 + 128], a2.b, la[:, t0:t0 + TB], la.b)
                tk.op("act", lambda: nc.scalar.activation(out=aa[:], in_=p[:], func=AF.Sigmoid, bias=pc["rwkv_a0"][:, hp:hp + 1]),
                      reads=[p.b, pc["rwkv_a0"].b], writes=[aa.b])
                p = self.psum()
                mm(p[:], p.b, g2[:, 0, c0:c0 + 128], g2.b, lg[:, 0, t0:t0 + TB], lg.b, True, False)
                mm(p[:], p.b, g2[:32, 1, c0:c0 + 128], g2.b, lg[:32, 1, t0:t0 + TB], lg.b, False, True)
                tk.op("act", lambda: nc.scalar.copy(gg[:], p[:]), reads=[p.b], writes=[gg.b])
                tk.op("dve", lambda: nc.vector.tensor_scalar_mul(kkr[:], kp[:], pc["rwkv_kk"][:, hp:hp + 1]),
                      reads=[kp.b, pc["rwkv_kk"].b], writes=[kkr.b])
                tk.op("act", lambda: nc.scalar.activation(out=sq[:], in_=kkr[:], func=AF.Square), reads=[kkr.b], writes=[sq.b])
                p = self.psum()
                mm(p[:], p.b, bdones[:], bdones.b, sq[:], sq.b)
                tk.op("dve", lambda: nc.vector.tensor_scalar_max(tmp[:], p[:], 1e-24), reads=[p.b], writes=[tmp.b])
                self.rpow(tmp[:], tmp.b, -0.5)
                tk.op("dve", lambda: nc.vector.tensor_tensor(out=kkr[:], in0=kkr[:], in1=tmp[:], op=ALU.mult), reads=[kkr.b, tmp.b], writes=[kkr.b])
                tk.op("dve", lambda: nc.vector.tensor_scalar(kmod[:], aa[:], pc["rwkv_ka"][:, hp:hp + 1], omk[:, hp:hp + 1], ALU.mult, ALU.add),
                      reads=[aa.b, pc["rwkv_ka"].b, omk.b], writes=[kmod.b])
                tk.op("pool", lambda: nc.gpsimd.tensor_tensor(out=kmod[:], in0=kmod[:], in1=kp[:], op=ALU.mult), reads=[kmod.b, kp.b], writes=[kmod.b])
                tk.op("pool", lambda: nc.gpsimd.tensor_tensor(out=kb[:], in0=kkr[:], in1=aa[:], op=ALU.mult), reads=[kkr.b, aa.b], writes=[kb.b])
                tk.op("dve", lambda: nc.vector.scalar_tensor_tensor(out=tmp[:], in0=rr[:], scalar=pc["rwkv_rk"][:, hp:hp + 1], in1=kmod[:],
                                                                    op0=ALU.mult, op1=ALU.mult), reads=[rr.b, kmod.b, pc["rwkv_rk"].b], writes=[tmp.b])
                p = self.psum()
                mm(p[:], p.b, bdones[:], bdones.b, tmp[:], tmp.b)
                tk.op("dve", lambda: nc.vector.tensor_tensor(out=bonus[:], in0=p[:], in1=Q5[:, 4, :], op=ALU.mult), reads=[p.b, Q5.b], writes=[bonus.b])
                tk.op("dve", lambda: nc.vector.tensor_tensor_scan(out=cum[:], data0=segm[:], data1=logw[:], initial=0.0, op0=ALU.mult, op1=ALU.add),
                      reads=[segm.b, logw.b], writes=[cum.b])
                tk.op("pool", lambda: nc.gpsimd.tensor_tensor(out=cex[:], in0=cum[:], in1=logw[:], op=ALU.subtract), reads=[cum.b, logw.b], writes=[cex.b])
                tk.op("act", lambda: nc.scalar.activation(out=epv[:], in_=cex[:], func=AF.Exp), reads=[cex.b], writes=[epv.b])
                tk.op("act", lambda: nc.scalar.activation(out=ein[:], in_=cum[:], func=AF.Exp), reads=[cum.b], writes=[ein.b])
                tk.op("act", lambda: nc.scalar.activation(out=eng[:], in_=cum[:], func=AF.Exp, scale=-1.0), reads=[cum.b], writes=[eng.b])
                tk.op("dve", lambda: nc.vector.scalar_tensor_tensor(out=Q5[:, 0, :], in0=kkr[:], scalar=-1.0, in1=epv[:], op0=ALU.mult, op1=ALU.mult),
                      reads=[kkr.b, epv.b], writes=[Q5.b])
                tk.op("pool", lambda: nc.gpsimd.tensor_tensor(out=Q5[:, 1, :], in0=rr[:], in1=ein[:], op=ALU.mult), reads=[rr.b, ein.b], writes=[Q5.b])
                tk.op("dve", lambda: nc.vector.tensor_tensor(out=Q5[:, 2, :], in0=kb[:], in1=eng[:], op=ALU.mult), reads=[kb.b, eng.b], writes=[Q5.b])
                tk.op("pool", lambda: nc.gpsimd.tensor_tensor(out=Q5[:, 3, :], in0=kmod[:], in1=eng[:], op=ALU.mult), reads=[kmod.b, eng.b], writes=[Q5.b])
                if LIM <= 2:
                    return
                NBC = 4
                for cb in range(TB // 64 // NBC):
                    chunks = list(range(cb * NBC, (cb + 1) * NBC))
                    st = {}
                    for c in chunks:
                        cs = slice(c * 64, (c + 1) * 64)
                        BD5 = BD5l[c % NBC]
                        src = Q5[:, :, cs].unsqueeze(2).to_broadcast([128, 5, 2, 64])
                        tk.op("dve", lambda: nc.vector.tensor_tensor(out=BD5[:], in0=src, in1=bdm5[:, :].rearrange("p (a h b) -> p a h b", a=5, h=2),
                                                                     op=ALU.mult), reads=[Q5.b, bdm5.b], writes=[BD5.b])
                        bd = lambda j, BD5=BD5: BD5[:, j, :, :].rearrange("p h b -> p (h b)")
                        p = self.psum()
                        ar = BD5[:, 0:2, :, :].rearrange("p a h b -> p (a h b)")
                        mm(p[:, 0:256], p.b, bd(2), BD5.b, ar, BD5.b)
                        mm(p[:, 256:512], p.b, bd(3), BD5.b, ar, BD5.b)
                        GBK = GBKl[c % NBC]
                        tk.op("dve", lambda: nc.vector.tensor_tensor(out=GBK[:], in0=p[:], in1=mask4[:], op=ALU.mult), reads=[p.b, mask4.b], writes=[GBK.b])
                        p3 = self.psum()
                        for j in range(3):
                            tk.op("pe", lambda: nc.tensor.transpose(p3[:, j * 128:(j + 1) * 128], bd(2 + j), ident[:]), reads=[BD5.b, ident.b], writes=[p3.b])
                        TM3 = TM3l[c % NBC]
                        tk.op("act", lambda: nc.scalar.copy(TM3[:], p3[:, 0:384]), reads=[p3.b], writes=[TM3.b])
                        st[c] = dict(BD5=BD5, bd=bd, GBK=GBK, TM3=TM3, cs=cs)
                    for c in chunks:
                        d = st[c]
                        GBK = d["GBK"]
                        NT = NTl[c % NBC]
                        p = self.psum()
                        tk.op("pe", lambda: nc.tensor.transpose(p[:, 0:128], GBK[:, 0:128], ident[:]), reads=[GBK.b, ident.b], writes=[p.b])
                        tk.op("act", lambda: nc.scalar.copy(NT[:], p[:, 0:128]), reads=[p.b], writes=[NT.b])
                        MX = MXl[c % NBC][0]
                        tk.op("pool", lambda: nc.gpsimd.tensor_tensor(out=MX[:, 128:256], in0=ident[:], in1=GBK[:, 0:128], op=ALU.add),
                              reads=[ident.b, GBK.b], writes=[MX.b])
                        d["NT"] = NT
                    for c in chunks:
                        d = st[c]
                        GBK, NT = d["GBK"], d["NT"]
                        MX, MT = MXl[c % NBC][0], MTl[c % NBC][0]
                        p = self.psum()
                        mm(p[:, 0:128], p.b, NT[:], NT.b, GBK[:, 0:128], GBK.b)
                        pt = self.psum()
                        mm(pt[:, 0:128], pt.b, GBK[:, 0:128], GBK.b, NT[:], NT.b)
                        tk.op("act", lambda: nc.scalar.copy(MX[:, 0:128], p[:, 0:128]), reads=[p.b], writes=[MX.b])
                        tk.op("dve", lambda: nc.vector.tensor_copy(MT[:], pt[:, 0:128]), reads=[pt.b], writes=[MT.b])
                        d["MX"], d["MT"], d["par"] = MX, MT, 0
                    for j in range(2, 6):
                        for c in chunks:
                            d = st[c]
                            MX, MT = d["MX"], d["MT"]
                            par = 1 - d["par"]
                            MX2, MT2 = MXl[c % NBC][par], MTl[c % NBC][par]
                            pm = self.psum()
                            mm(pm[:, 0:128], pm.b, MT[:], MT.b, MX[:, 0:128], MX.b)
                            px = self.psum()
                            mm(px[:, 0:128], px.b, MT[:], MT.b, MX[:, 128:256], MX.b)
                            pt = self.psum()
                            mm(pt[:, 0:128], pt.b, MX[:, 0:128], MX.b, MT[:], MT.b)
                            tk.op("act", lambda: nc.scalar.copy(MX2[:, 0:128], pm[:, 0:128]), reads=[pm.b], writes=[MX2.b])
                            tk.op("dve", lambda: nc.vector.tensor_tensor(out=MX2[:, 128:256], in0=px[:, 0:128], in1=MX[:, 128:256], op=ALU.add),
                                  reads=[px.b, MX.b], writes=[MX2.b])
                            tk.op("act", lambda: nc.scalar.copy(MT2[:], pt[:, 0:128]), reads=[pt.b], writes=[MT2.b])
                            d["MX"], d["MT"], d["par"] = MX2, MT2, par
                    for c in chunks:
                        d = st[c]
                        MX, MT = d["MX"], d["MT"]
                        p = self.psum()
                        mm(p[:, 0:128], p.b, MT[:], MT.b, MX[:, 128:256], MX.b)
                        TT = TTl[c % NBC]
                        tk.op("dve", lambda: nc.vector.tensor_tensor(out=TT[:], in0=p[:, 0:128], in1=MX[:, 128:256], op=ALU.add), reads=[p.b, MX.b], writes=[TT.b])
                        d["TT"] = TT
                    for c in chunks:
                        d = st[c]
                        bd, GBK, TM3, TT, cs = d["bd"], d["GBK"], d["TM3"], d["TT"], d["cs"]
                        BD5 = d["BD5"]
                        PCc = ein[:, c * 64 + 63:c * 64 + 64]
                        SP = SPr.next()
                        tk.op("act", lambda: nc.scalar.activation(out=SP[:], in_=S[:], func=AF.Identity, scale=PCc), reads=[S.b, ein.b], writes=[SP.b])
                        p = self.psum()
                        mm(p[:, 0:128], p.b, bd(0), BD5.b, S[:], S.b, True, False)
                        mm(p[:, 0:128], p.b, GBK[:, 256:384], GBK.b, TM3[:, 256:384], TM3.b, False, True)
                        RH = RHr.next()
                        tk.op("act", lambda: nc.scalar.copy(RH[:], p[:, 0:128]), reads=[p.b], writes=[RH.b])
                        p = self.psum()
                        mm(p[:, 0:128], p.b, TT[:], TT.b, RH[:], RH.b)
                        U = Ur.next()
                        tk.op("dve", lambda: nc.vector.tensor_copy(U[:], p[:, 0:128]), reads=[p.b], writes=[U.b])
                        pS = self.psum()
                        mm(pS[:, 0:128], pS.b, TM3[:, 0:128], TM3.b, U[:], U.b, True, False)
                        mm(pS[:, 0:128], pS.b, TM3[:, 128:256], TM3.b, TM3[:, 256:384], TM3.b, False, True)
                        S2 = Sr.next()
                        tk.op("dve", lambda: nc.vector.scalar_tensor_tensor(out=S2[:], in0=pS[:, 0:128], scalar=PCc, in1=SP[:], op0=ALU.mult, op1=ALU.add),
                              reads=[pS.b, ein.b, SP.b], writes=[S2.b])
                        p = self.psum()
                        mm(p[:, 0:128], p.b, S[:], S.b, bd(1), BD5.b, True, False)
                        mm(p[:, 0:128], p.b, U[:], U.b, GBK[:, 128:256], GBK.b, False, False)
                        mm(p[:, 0:128], p.b, TM3[:, 256:384], TM3.b, GBK[:, 384:512], GBK.b, False, True)
                        tk.op("act", lambda: nc.scalar.copy(yfm[0:64, cs], p[0:64, 0:64]), reads=[p.b], writes=[yfm.b])
                        tk.op("act", lambda: nc.scalar.copy(yfm[64:128, cs], p[64:128, 64:128]), reads=[p.b], writes=[yfm.b])
                        S = S2
                if LIM <= 5:
                    return
                p = self.psum()
                mm(p[:], p.b, bdones[:], bdones.b, yfm[:], yfm.b)
                tk.op("dve", lambda: nc.vector.scalar_tensor_tensor(out=dd[:], in0=p[:], scalar=-1.0 / 64, in1=yfm[:], op0=ALU.mult, op1=ALU.add),
                      reads=[p.b, yfm.b], writes=[dd.b])
                tk.op("act", lambda: nc.scalar.activation(out=sq[:], in_=dd[:], func=AF.Square), reads=[dd.b], writes=[sq.b])
                p = self.psum()
                mm(p[:], p.b, bdones[:], bdones.b, sq[:], sq.b)
                tk.op("dve", lambda: nc.vector.tensor_scalar(tmp[:], p[:], 1.0 / 64, 64e-5, ALU.mult, ALU.add), reads=[p.b], writes=[tmp.b])
                self.rpow(tmp[:], tmp.b, -0.5)
                tk.op("dve", lambda: nc.vector.tensor_tensor(out=dd[:], in0=dd[:], in1=tmp[:], op=ALU.mult), reads=[dd.b, tmp.b], writes=[dd.b])
                tk.op("act", lambda: nc.scalar.activation(out=dd[:], in_=dd[:], func=AF.Identity, bias=pc["rwkv_lnx_b"][:, hp:hp + 1],
                                                          scale=pc["rwkv_lnx_w"][:, hp:hp + 1]),
                      reads=[dd.b, pc["rwkv_lnx_b"].b, pc["rwkv_lnx_w"].b], writes=[dd.b])
                tk.op("pool", lambda: nc.gpsimd.tensor_tensor(out=dd[:], in0=dd[:], in1=bonus[:], op=ALU.add), reads=[dd.b, bonus.b], writes=[dd.b])
                o = ob.next()
                tk.op("dve", lambda: nc.vector.tensor_tensor(out=o[:], in0=dd[:], in1=gg[:], op=ALU.mult), reads=[dd.b, gg.b], writes=[o.b])
                tk.dma("pool", self.din["yr_fm"].ap()[c0:c0 + 128, t0:t0 + TB], o[:], reads=[o.b], writes=[self.dbuf["yr_fm"]])
                if LIM <= 6:
                    return


Prog.phase_rwkv = _rwkv


def _nsa_bias(self):
    nc, tk = self.nc, self.tk
    self.scratch("bvec_c", [16, LVEC], BF16)
    self.scratch("bvec_w", [16, LVEC], BF16)
    with self.scope() as es:
        tab = self.sb(es, "nb_tab", [33, 16], F32)
        tk.op("pool", lambda: nc.gpsimd.memset(tab[:], 1.0), writes=[tab.b])
        tk.dma("sp", tab[0:32, :], self.din["rel_bias"].ap()[:, :], reads=[self.dbuf["rel_bias"]], writes=[tab.b])
        e33 = self.sb(es, "nb_e33", [33, LVEC], F32)
        ob = self.ring(es, "nb_o", 2, [16, 512], BF16)
        for cname, dname in (("c_e33c", "bvec_c"), ("c_e33w", "bvec_w")):
            tk.dma("sp", e33[:], self.din[cname].ap()[:, :], reads=[self.dbuf[cname]], writes=[e33.b])
            for j in range(LVEC // 512):
                p = self.psum()
                tk.op("pe", lambda: nc.tensor.matmul(p[:16, :], lhsT=tab[:], rhs=e33[:, j * 512:(j + 1) * 512], start=True, stop=True),
                      reads=[tab.b, e33.b], writes=[p.b])
                o = ob.next()
                tk.op("act", lambda: nc.scalar.copy(o[:], p[:16, :]), reads=[p.b], writes=[o.b])
                tk.dma("pool", self.din[dname].ap()[:, j * 512:(j + 1) * 512], o[:], reads=[o.b], writes=[self.dbuf[dname]])


def _nsa(self):
    nc, tk = self.nc, self.tk
    self.scratch("yn_fm", [1024, T], BF16)
    gen = Ring(self.psr.tiles[0:4])
    accp = Ring(self.psr.tiles[4:8])

    def mm(p_ap, pbuf, lhsT, lb, rhs, rb, start=True, stop=True):
        tk.op("pe", lambda: nc.tensor.matmul(p_ap, lhsT=lhsT, rhs=rhs, start=start, stop=stop), reads=[lb, rb], writes=[pbuf])

    with self.scope() as es:
        Jb = self.load_const(es, "c_J", BF16, tmp_es=es)
        c2s_f = self.sb(es, "ns_c2sf", [128, 2, 64], F32)
        tk.dma("sp", c2s_f[:, :, :], self.dap("c_c2s", 0, [[64, 128], [128 * 64, 2], [1, 64]]), reads=[self.dbuf["c_c2s"]], writes=[c2s_f.b])
        c2s = self.sb(es, "ns_c2s", [128, 2, 64], BF16)
        tk.op("dve", lambda: nc.vector.tensor_copy(c2s[:], c2s_f[:]), reads=[c2s_f.b], writes=[c2s.b])
        exb = self.sb(es, "ns_exb", [64, 4096], BF16)
        with self.scope() as es0:
            exf = self.sb(es0, "ns_exf", [64, 4096], F32)
            tk.dma("sp", exf[:], self.din["c_expand"].ap()[:, :], reads=[self.dbuf["c_expand"]], writes=[exf.b])
            tk.op("dve", lambda: nc.vector.tensor_scalar_mul(exb[:], exf[:], BIG), reads=[exf.b], writes=[exb.b])
        ones = self.sb(es, "ns_ones", [128, 64], BF16)
        tk.op("pool", lambda: nc.gpsimd.memset(ones[:], 1.0), writes=[ones.b])
        kgain = self.col_vec(es, "nsa_k_gain", 0, 0, 1, "ns_kg", p=64)
        ident, bdones = self.ident, self.bdones
        kcmpT = [self.sb(es, f"ns_kcT{g}", [64, 256], BF16) for g in range(4)]
        vcmp = [self.sb(es, f"ns_vc{g}", [128, 2, 64], BF16) for g in range(4)]
        with self.scope() as es2:
            kc2 = self.sb(es2, "ns_kc2", [128, T], BF16)
            w1t = self.sb(es2, "ns_w1", [128, 16, 256], BF16)
            w2t = self.sb(es2, "ns_w2", [128, 2, 64], BF16)
            hg_ = self.sb(es2, "ns_hg", [128, 2, 256], BF16)
            xx = self.sb(es2, "ns_x", [128, 256], F32)
            x2 = self.sb(es2, "ns_x2", [128, 256], F32)
            pvb = self.sb(es2, "ns_pvb", [128, 2], F32)
            t64 = self.sb(es2, "ns_t64", [64, 256], F32)
            t64b = self.sb(es2, "ns_t64b", [64, 256], F32)
            for kind in range(2):
                sfx = "_k" if kind == 0 else "_v"
                self.load_w(w1t, "cmp_w1" + sfx + "_bf", 0, 256, kchunks=16)
                self.load_w(w2t, "cmp_w2" + sfx + "_bf", 0, 64, kchunks=2)
                pe_f = self.col_vec(es2, "cmp_pe" + sfx, 0, 0, 16, "ns_pe" + sfx)
                pe_b = self.sb(es2, "ns_peb" + sfx, [128, 16], BF16)
                tk.op("dve", lambda: nc.vector.tensor_copy(pe_b[:], pe_f[:]), reads=[pe_f.b], writes=[pe_b.b])
                for ct in range(2):
                    p = gen.next()
                    for l2 in range(16):
                        mm(p[:, 0:1], p.b, w1t[:, l2, ct * 128:(ct + 1) * 128], w1t.b, pe_b[:, l2:l2 + 1], pe_b.b, l2 == 0, l2 == 15)
                    tk.op("dve", lambda: nc.vector.tensor_copy(pvb[:, ct:ct + 1], p[:, 0:1]), reads=[p.b], writes=[pvb.b])
                for g in range(4):
                    r0 = 256 * kind + 64 * g
                    tk.op("pool", lambda: nc.gpsimd.memset(kc2[64:128, T - 1:T], 0.0), writes=[kc2.b])
                    tk.dma("sp", kc2[0:64, :], self.din["kcvc_fm"].ap()[r0:r0 + 64, :], reads=[self.dbuf["kcvc_fm"]], writes=[kc2.b])
                    tk.dma("sp", kc2[64:128, 0:T - 1], self.din["kcvc_fm"].ap()[r0:r0 + 64, 1:T], reads=[self.dbuf["kcvc_fm"]], writes=[kc2.b])
                    tk.op("pool", lambda: nc.gpsimd.memset(hg_[:], 0.0), writes=[hg_.b])
                    for ct in range(2):
                        p = gen.next()
                        for l2 in range(16):
                            rhs = kc2[:, 2 * l2: 2 * l2 + 16 * 254 + 1: 16]
                            mm(p[:, 0:255], p.b, w1t[:, l2, ct * 128:(ct + 1) * 128], w1t.b, rhs, kc2.b, l2 == 0, l2 == 15)
                        tk.op("act", lambda: nc.scalar.activation(out=xx[:, 0:255], in_=p[:, 0:255], func=AF.Identity, bias=pvb[:, ct:ct + 1]),
                              reads=[p.b, pvb.b], writes=[xx.b])
                        tk.op("act", lambda: nc.scalar.activation(out=x2[:, 0:255], in_=xx[:, 0:255], func=AF.Square), reads=[xx.b], writes=[x2.b])
                        tk.op("dve", lambda: nc.vector.tensor_scalar(x2[:, 0:255], x2[:, 0:255], 0.044715, 1.0, ALU.mult, ALU.add), reads=[x2.b], writes=[x2.b])
                        tk.op("dve", lambda: nc.vector.tensor_tensor(out=x2[:, 0:255], in0=x2[:, 0:255], in1=xx[:, 0:255], op=ALU.mult), reads=[x2.b, xx.b], writes=[x2.b])
                        tk.op("act", lambda: nc.scalar.activation(out=x2[:, 0:255], in_=x2[:, 0:255], func=AF.Sigmoid, scale=1.5957691216057308),
                              reads=[x2.b], writes=[x2.b])
                        tk.op("dve", lambda: nc.vector.tensor_tensor(out=hg_[:, ct, 0:255], in0=x2[:, 0:255], in1=xx[:, 0:255], op=ALU.mult),
                              reads=[x2.b, xx.b], writes=[hg_.b])
                    if kind == 0:
                        p = gen.next()
                        for ct in range(2):
                            mm(p[0:64, 0:256], p.b, w2t[:, ct, :], w2t.b, hg_[:, ct, :], hg_.b, ct == 0, ct == 1)
                        tk.op("act", lambda: nc.scalar.activation(out=t64[:], in_=p[0:64, 0:256], func=AF.Square), reads=[p.b], writes=[t64.b])
                        p2 = gen.next()
                        mm(p2[0:64, 0:256], p2.b, bdones[0:64, 0:64], bdones.b, t64[:], t64.b)
                        tk.op("dve", lambda: nc.vector.tensor_scalar(t64[:], p2[0:64, 0:256], 1.0 / 64, 1e-6, ALU.mult, ALU.add), reads=[p2.b], writes=[t64.b])
                        self.rpow(t64[:], t64.b, -0.5)
                        tk.op("dve", lambda: nc.vector.tensor_tensor(out=t64b[:], in0=p[0:64, 0:256], in1=t64[:], op=ALU.mult), reads=[p.b, t64.b], writes=[t64b.b])
                        tk.op("dve", lambda: nc.vector.tensor_scalar_mul(kcmpT[g][:], t64b[:], kgain[:, 0:1]), reads=[t64b.b, kgain.b], writes=[kcmpT[g].b])
                    else:
                        for nt in range(2):
                            p = gen.next()
                            for ct in range(2):
                                mm(p[:, 0:64], p.b, hg_[:, ct, nt * 128:(nt + 1) * 128], hg_.b, w2t[:, ct, :], w2t.b, ct == 0, ct == 1)
                            tk.op("act", lambda: nc.scalar.copy(vcmp[g][:, nt, :], p[:, 0:64]), reads=[p.b], writes=[vcmp[g].b])
        if getattr(self, "ns_lim", 99) <= 1:
            return
        ksT = self.sb(es, "ns_ksT", [64, T], BF16)
        kwT = self.sb(es, "ns_kwT", [64, T], BF16)
        Vs = self.sb(es, "ns_Vs", [128, 32, 128], BF16)
        Vw = self.sb(es, "ns_Vw", [128, 32, 128], BF16)
        tk.op("pool", lambda: nc.gpsimd.memset(Vs[:], 1.0), writes=[Vs.b])
        tk.op("pool", lambda: nc.gpsimd.memset(Vw[:], 1.0), writes=[Vw.b])
        vco = [self.sb(es, f"ns_vco{g}", [128, 2, 128], BF16) for g in range(4)]
        for g in range(4):
            tk.op("pool", lambda: nc.gpsimd.memset(vco[g][:], 1.0), writes=[vco[g].b])
            tk.op("dve", lambda: nc.vector.tensor_copy(vco[g][:, :, 0:64], vcmp[g][:]), reads=[vcmp[g].b], writes=[vco[g].b])
        bfar = self.sb(es, "ns_bfar", [128, 16], F32)
        tk.dma("sp", bfar[:], self.din["rel_bias"].ap()[31:32, :].partition_broadcast(128), reads=[self.dbuf["rel_bias"]], writes=[bfar.b])
        qTr = self.ring(es, "ns_qT", 2, [64, 4, 512], BF16)
        gbr = self.ring(es, "ns_gb", 3, [64, 4, 512], F32)
        Hr = self.ring(es, "ns_H", 4, [128, 512], BF16)
        Er = self.ring(es, "ns_E", 4, [128, 512], BF16)
        E2r = self.ring(es, "ns_E2", 4, [128, 512], BF16)
        Ec = self.sb(es, "ns_Ec", [128, 8, 512], BF16, dj=True)
        accC = self.sb(es, "ns_accC", [64, 4, 8, 512], BF16, dj=True)
        selTa = self.sb(es, "ns_selTa", [64, 8, 512], BF16, dj=True)
        acc = self.ring(es, "ns_acc", 2, [64, 512], F32)
        impa = self.sb(es, "ns_impa", [64, 512], F32)
        frc = self.ring(es, "ns_frc", 2, [64, 512], F32)
        rdr = self.ring(es, "ns_rd", 3, [64, 512], F32)
        t1r = self.ring(es, "ns_t1", 3, [64, 512], F32)
        impq = self.sb(es, "ns_impq", [128, 4, 64], F32)
        selq = self.sb(es, "ns_selq", [128, 4, 64], F32)
        wk = self.sb(es, "ns_wk", [128, 64], F32)
        m8 = self.sb(es, "ns_m8", [128, 16], F32)
        obr = self.ring(es, "ns_ob", 2, [64, 512], BF16)
        qh = self.ring(es, "ns_qh", 2, [64, T], BF16)
        XBr = [[self.sb(es, f"ns_xb{k}_{i}", [128, 512], BF16) for i in range(13)] for k in range(1)]
        pend = []
        eng_alt = [0]

        def flush():
            while pend:
                pend.pop(0)()

        def hankel(vname, h, c, pstep):
            H = Hr.next()
            src = self.dap(vname, h * LVEC + c, [[pstep, 128], [1, 512]])
            tk.dma("sp", H[:], src, reads=[self.dbuf[vname]], writes=[H.b])
            return H

        def ratio(pn):
            rd = rdr.next()
            tk.op("dve", lambda: nc.vector.tensor_scalar_max(rd[:], pn[64:128, :], 1e-30), reads=[pn.b], writes=[rd.b])
            self.rpow(rd[:], rd.b, -1.0)
            t1 = t1r.next()
            tk.op("dve", lambda: nc.vector.tensor_tensor(out=t1[:], in0=pn[0:64, :], in1=rd[:], op=ALU.mult), reads=[pn.b, rd.b], writes=[t1.b])
            return rd, t1

        def key_tile(s_mms, e_ap, e_buf, act_bias, pv, mult=None):
            p = gen.next()
            for i, (lhsT, lb, rhs, rb) in enumerate(s_mms):
                mm(p[:], p.b, lhsT, lb, rhs, rb, i == 0, i == len(s_mms) - 1)
            if mult is None:
                if act_bias is None:
                    tk.op("act", lambda: nc.scalar.activation(out=e_ap, in_=p[:], func=AF.Exp), reads=[p.b], writes=[e_buf])
                else:
                    tk.op("act", lambda: nc.scalar.activation(out=e_ap, in_=p[:], func=AF.Exp, bias=act_bias), reads=[p.b, bfar.b], writes=[e_buf])
            else:
                E0 = E2r.next()
                tk.op("act", lambda: nc.scalar.activation(out=E0[:], in_=p[:], func=AF.Exp), reads=[p.b], writes=[E0.b])
                eng_alt[0] += 1
                if False:
                    tk.op("pool", lambda: nc.gpsimd.tensor_tensor(out=e_ap, in0=E0[:], in1=mult[:], op=ALU.mult), reads=[E0.b, mult.b], writes=[e_buf])
                else:
                    tk.op("dve", lambda: nc.vector.tensor_tensor(out=e_ap, in0=E0[:], in1=mult[:], op=ALU.mult), reads=[E0.b, mult.b], writes=[e_buf])
            while len(pend) >= SKEW:
                pend.pop(0)()
            pend.append(pv)

        SKEW = getattr(self, "ns_skew", 2)
        for g in range(4):
            flush()
            tk.dma("sp", ksT[:], self.din["ks_fm"].ap()[64 * g:64 * g + 64, :], reads=[self.dbuf["ks_fm"]], writes=[ksT.b])
            tk.dma("sp", kwT[:], self.din["kw_fm"].ap()[64 * g:64 * g + 64, :], reads=[self.dbuf["kw_fm"]], writes=[kwT.b])
            for k8 in range(4):
                tk.dma("sp", Vs[:, 8 * k8:8 * k8 + 8, 0:64], self.dap("vsw_tm", 64 * g + 8 * k8 * 128 * 512, [[512, 128], [128 * 512, 8], [1, 64]]),
                       reads=[self.dbuf["vsw_tm"]], writes=[Vs.b])
                tk.dma("sp", Vw[:, 8 * k8:8 * k8 + 8, 0:64], self.dap("vsw_tm", 256 + 64 * g + 8 * k8 * 128 * 512, [[512, 128], [128 * 512, 8], [1, 64]]),
                       reads=[self.dbuf["vsw_tm"]], writes=[Vw.b])
            for qt in range(T // 512):
                t0 = qt * 512
                qT = qTr.next()
                tk.dma("sp", qT[:], self.dap("q_fm", 256 * g * T + t0, [[T, 64], [64 * T, 4], [1, 512]]), reads=[self.dbuf["q_fm"]], writes=[qT.b])
                gb = gbr.next()
                for j in range(4):
                    row = 12 * g + 3 * j
                    tk.dma("sp", gb[:, j, :], self.din["gates_fm"].ap()[row:row + 1, t0:t0 + 512].partition_broadcast(64),
                           reads=[self.dbuf["gates_fm"]], writes=[gb.b])
                fr = frc.next()
                tk.dma("sp", fr[:], self.din["c_forced"].ap()[:, t0:t0 + 512], reads=[self.dbuf["c_forced"]], writes=[fr.b])
                nnt = 2 if t0 >= 2048 else 1
                for hg in range(4):
                    h = 4 * g + hg
                    pn, pi = accp.next(), accp.next()
                    for nt in range(nnt):
                        H = hankel("bvec_c", h, OFFC + t0 - 16 * 128 * nt - 2063, 16)
                        e_ap = Ec[:, hg * 2 + nt, :]

                        def pv(pn=pn, pi=pi, nt=nt, e_ap=e_ap, nnt=nnt):
                            mm(pn[:], pn.b, vco[g][:, nt, :], vco[g].b, e_ap, Ec.b, nt == 0, nt == nnt - 1)
                            mm(pi[0:64, :], pi.b, c2s[:, nt, :], c2s.b, e_ap, Ec.b, nt == 0, nt == nnt - 1)
                        key_tile([(kcmpT[g][:, nt * 128:(nt + 1) * 128], kcmpT[g].b, qT[:, hg, :], qT.b), (Jb[:], Jb.b, H[:], H.b)], e_ap, Ec.b, None, pv)

                    def fin(pn=pn, pi=pi, hg=hg, gb=gb, qt=qt):
                        rd, t1 = ratio(pn)
                        tk.op("pool", lambda: nc.gpsimd.tensor_tensor(out=accC[:, hg, qt, :], in0=t1[:], in1=gb[:, hg, :], op=ALU.mult),
                              reads=[t1.b, gb.b], writes=[accC.b])
                        if hg == 0:
                            tk.op("dve", lambda: nc.vector.tensor_tensor(out=impa[:], in0=pi[0:64, :], in1=rd[:], op=ALU.mult), reads=[pi.b, rd.b], writes=[impa.b])
                        else:
                            t2 = t1r.next()
                            tk.op("dve", lambda: nc.vector.tensor_tensor(out=t2[:], in0=pi[0:64, :], in1=rd[:], op=ALU.mult), reads=[pi.b, rd.b], writes=[t2.b])
                            tk.op("pool", lambda: nc.gpsimd.tensor_tensor(out=impa[:], in0=impa[:], in1=t2[:], op=ALU.add), reads=[impa.b, t2.b], writes=[impa.b])
                    pend.append(fin)
                flush()
                tk.op("dve", lambda: nc.vector.tensor_tensor(out=impa[:], in0=impa[:], in1=fr[:], op=ALU.max), reads=[impa.b, fr.b], writes=[impa.b])
                p = gen.next()
                for s4 in range(4):
                    tk.op("pe", lambda: nc.tensor.transpose(p[:, s4 * 64:(s4 + 1) * 64], impa[:, s4 * 128:(s4 + 1) * 128], ident[0:64, 0:64]),
                          reads=[impa.b, ident.b], writes=[p.b])
                tk.op("act", lambda: nc.scalar.copy(impq[:], p[:, 0:256].rearrange("p (a b) -> p a b", a=4)), reads=[p.b], writes=[impq.b])
                for s4 in range(4):
                    tk.op("dve", lambda: nc.vector.max(out=m8[:, 0:8], in_=impq[:, s4, :]), reads=[impq.b], writes=[m8.b])
                    tk.op("dve", lambda: nc.vector.match_replace(out=wk[:], in_to_replace=m8[:, 0:8], in_values=impq[:, s4, :], imm_value=-1e30),
                          reads=[impq.b, m8.b], writes=[wk.b])
                    tk.op("dve", lambda: nc.vector.max(out=m8[:, 8:16], in_=wk[:]), reads=[wk.b], writes=[m8.b])
                    tk.op("dve", lambda: nc.vector.tensor_scalar(selq[:, s4, :], impq[:, s4, :], m8[:, 15:16], 1.0, ALU.is_ge, ALU.subtract),
                          reads=[impq.b, m8.b], writes=[selq.b])
                p = gen.next()
                for s4 in range(4):
                    tk.op("pe", lambda: nc.tensor.transpose(p[0:64, s4 * 128:(s4 + 1) * 128], selq[:, s4, :], ident[:]),
                          reads=[selq.b, ident.b], writes=[p.b])
                tk.op("act", lambda: nc.scalar.copy(selTa[:, qt, :], p[0:64, :]), reads=[p.b], writes=[selTa.b])
            for hg in range(4):
                h = 4 * g + hg
                flush()
                XB = XBr[0]
                xw, xs = {}, {}
                for i, (vname, d) in enumerate([("bvec_w", dd_) for dd_ in range(512, -385, -128)] + [("bvec_c", dd_) for dd_ in range(128, -385, -128)]):
                    H = hankel(vname, h, OFFC + d - 127, 1)
                    p = gen.next()
                    mm(p[:], p.b, Jb[:], Jb.b, H[:], H.b)
                    tk.op("act", lambda: nc.scalar.activation(out=XB[i][:], in_=p[:], func=AF.Exp), reads=[p.b], writes=[XB[i].b])
                    (xw if vname == "bvec_w" else xs)[d] = XB[i]
                q_ = qh.next()
                tk.dma("sp", q_[:], self.din["q_fm"].ap()[64 * h:64 * h + 64, :], reads=[self.dbuf["q_fm"]], writes=[q_.b])
                for qt in range(T // 512):
                    t0 = qt * 512
                    qs = q_[:, t0:t0 + 512]
                    gb = gbr.next()
                    for j in range(2):
                        row = 3 * h + 1 + j
                        tk.dma("sp", gb[:, j, :], self.din["gates_fm"].ap()[row:row + 1, t0:t0 + 512].partition_broadcast(64),
                               reads=[self.dbuf["gates_fm"]], writes=[gb.b])
                    ac = acc.next()
                    kts = list(range(max(0, (t0 - 512) // 128), (t0 + 511) // 128 + 1))
                    pn = accp.next()
                    for i, kt in enumerate(kts):
                        E = Er.next()

                        def pv(pn=pn, kt=kt, E=E, first=(i == 0), last=(i == len(kts) - 1)):
                            mm(pn[:], pn.b, Vw[:, kt, :], Vw.b, E[:], E.b, first, last)
                        key_tile([(kwT[:, kt * 128:(kt + 1) * 128], kwT.b, qs, q_.b)], E[:], E.b, None, pv, mult=xw[t0 - 128 * kt])

                    def finw(pn=pn, gb=gb, ac=ac):
                        rd, t1 = ratio(pn)
                        tk.op("dve", lambda: nc.vector.tensor_tensor(out=ac[:], in0=t1[:], in1=gb[:, 1, :], op=ALU.mult), reads=[t1.b, gb.b], writes=[ac.b])
                    pend.append(finw)
                    kts = list(range(0, (t0 + 511) // 128 + 1))
                    pn = accp.next()
                    for i, kt in enumerate(kts):
                        far = (128 * kt <= t0 - 256)
                        E = Er.next()
                        s_mms = [(ksT[:, kt * 128:(kt + 1) * 128], ksT.b, qs, q_.b), (exb[:, kt * 128:(kt + 1) * 128], exb.b, selTa[:, qt, :], selTa.b)]

                        def pv(pn=pn, kt=kt, E=E, first=(i == 0), last=(i == len(kts) - 1)):
                            mm(pn[:], pn.b, Vs[:, kt, :], Vs.b, E[:], E.b, first, last)
                        key_tile(s_mms, E[:], E.b, bfar[:, h:h + 1] if far else None, pv, mult=None if far else xs[t0 - 128 * kt])

                    def fins(pn=pn, gb=gb, ac=ac, hg=hg, h=h, t0=t0, qt=qt):
                        rd, t1 = ratio(pn)
                        tk.op("dve", lambda: nc.vector.tensor_tensor(out=t1[:], in0=t1[:], in1=gb[:, 0, :], op=ALU.mult), reads=[t1.b, gb.b], writes=[t1.b])
                        tk.op("dve", lambda: nc.vector.tensor_tensor(out=ac[:], in0=ac[:], in1=t1[:], op=ALU.add), reads=[t1.b, ac.b], writes=[ac.b])
                        o = obr.next()
                        tk.op("dve", lambda: nc.vector.tensor_tensor(out=o[:], in0=ac[:], in1=accC[:, hg, qt, :], op=ALU.add), reads=[ac.b, accC.b], writes=[o.b])
                        tk.dma("pool", self.din["yn_fm"].ap()[64 * h:64 * h + 64, t0:t0 + 512], o[:], reads=[o.b], writes=[self.dbuf["yn_fm"]])
                    pend.append(fins)
        flush()


Prog.nsa_bias = _nsa_bias
Prog.phase_nsa = _nsa


def _proj_tm_res(self, actT, tok0, ntok, kchunks, w, res_name, res_row0, dst_name, dst_row0, es):
    nc, tk = self.nc, self.tk
    xr = self.ring(es, "pt_x", 2, [128, D], F32)
    orr = self.ring(es, "pt_o", 2, [128, D], F32)
    for i in range(ntok // 128):
        x = xr.next()
        o = orr.next()
        tk.dma("sp", x[:], self.din[res_name].ap()[res_row0 + i * 128:res_row0 + (i + 1) * 128, :], reads=[self.dbuf[res_name]], writes=[x.b])
        for half in range(2):
            p = self.psum()
            for kc in range(kchunks):
                tk.op("pe", lambda: nc.tensor.matmul(p[:], lhsT=actT[:, kc, tok0 + i * 128:tok0 + (i + 1) * 128], rhs=w[:, kc, half * 512:(half + 1) * 512],
                                                      start=(kc == 0), stop=(kc == kchunks - 1)), reads=[actT.b, w.b], writes=[p.b])
            tk.op("dve", lambda: nc.vector.tensor_tensor(out=o[:, half * 512:(half + 1) * 512], in0=p[:], in1=x[:, half * 512:(half + 1) * 512], op=ALU.add),
                  reads=[p.b, x.b], writes=[o.b])
        tk.dma("pool", self.din[dst_name].ap()[dst_row0 + i * 128:dst_row0 + (i + 1) * 128, :], o[:], reads=[o.b], writes=[self.dbuf[dst_name]])


def _merge(self, bi):
    nc, tk = self.nc, self.tk
    self.scratch("h1", [T, D], F32)
    with self.scope() as es:
        mT = self.sb(es, "mg_mT", [128, 8, T], BF16, dj=True)
        with self.scope() as es2:
            wr = self.sb(es2, "mg_wr", [128, 8, 1024], BF16)
            wn = self.sb(es2, "mg_wn", [128, 8, 1024], BF16)
            self.load_w(wr, "w_branch_rwkv_bf", 0, 1024)
            self.load_w(wn, "w_branch_nsa_bf", 0, 1024)
            yr = self.ring(es2, "mg_yr", 2, [128, 8, 512], BF16)
            yn = self.ring(es2, "mg_yn", 2, [128, 8, 512], BF16)
            gr = self.ring(es2, "mg_g", 4, [128, 512], F32)
            tr = self.ring(es2, "mg_t", 4, [128, 512], F32)
            for tt in range(T // 512):
                a, b = yr.next(), yn.next()
                tk.dma("sp", a[:], self.dap("yr_fm", tt * 512, [[T, 128], [128 * T, 8], [1, 512]]), reads=[self.dbuf["yr_fm"]], writes=[a.b])
                tk.dma("sp", b[:], self.dap("yn_fm", tt * 512, [[T, 128], [128 * T, 8], [1, 512]]), reads=[self.dbuf["yn_fm"]], writes=[b.b])
                for ci in range(8):
                    g0, g1 = gr.next(), gr.next()
                    tk.dma("sp", g0[:], self.din["gm_fm"].ap()[ci * 128:(ci + 1) * 128, tt * 512:(tt + 1) * 512], reads=[self.dbuf["gm_fm"]], writes=[g0.b])
                    tk.dma("sp", g1[:], self.din["gm_fm"].ap()[1024 + ci * 128:1024 + (ci + 1) * 128, tt * 512:(tt + 1) * 512], reads=[self.dbuf["gm_fm"]], writes=[g1.b])
                    pr, pn = self.psum(), self.psum()
                    for kc in range(8):
                        tk.op("pe", lambda: nc.tensor.matmul(pr[:], lhsT=wr[:, kc, ci * 128:(ci + 1) * 128], rhs=a[:, kc, :], start=(kc == 0), stop=(kc == 7)),
                              reads=[wr.b, a.b], writes=[pr.b])
                    for kc in range(8):
                        tk.op("pe", lambda: nc.tensor.matmul(pn[:], lhsT=wn[:, kc, ci * 128:(ci + 1) * 128], rhs=b[:, kc, :], start=(kc == 0), stop=(kc == 7)),
                              reads=[wn.b, b.b], writes=[pn.b])
                    t0_, t1_ = tr.next(), tr.next()
                    tk.op("dve", lambda: nc.vector.tensor_tensor(out=t0_[:], in0=pr[:], in1=g0[:], op=ALU.mult), reads=[pr.b, g0.b], writes=[t0_.b])
                    tk.op("dve", lambda: nc.vector.tensor_tensor(out=t1_[:], in0=pn[:], in1=g1[:], op=ALU.mult), reads=[pn.b, g1.b], writes=[t1_.b])
                    tk.op("pool", lambda: nc.gpsimd.tensor_tensor(out=mT[:, ci, tt * 512:(tt + 1) * 512], in0=t0_[:], in1=t1_[:], op=ALU.add),
                          reads=[t0_.b, t1_.b], writes=[mT.b])
        with self.scope() as es3:
            wm = self.sb(es3, "mg_wm", [128, 8, 1024], BF16)
            self.load_w(wm, "w_mix_out_bf", 0, 1024)
            self.proj_tm_res(mT, 0, T, 8, wm, "x", bi * T, "h1", 0, es3)


def _cross(self, bi):
    nc, tk = self.nc, self.tk
    self.scratch("h2", [T, D], F32)
    HT = 2048
    with self.scope() as es:
        wq = self.sb(es, "ca_wq", [128, 8, 1024], BF16)
        wo = self.sb(es, "ca_wo", [128, 8, 1024], BF16)
        self.load_w(wq, "ca_wq_bf", 0, 1024)
        self.load_w(wo, "ca_wo_bf", 0, 1024)
        kT = self.sb(es, "ca_kT", [128, 8, NMEM], BF16, dj=True)
        Vc = self.sb(es, "ca_V", [128, 2, 1024], BF16, dj=True)
        qgain = self.col_vec(es, "ca_q_gain", 0, 0, 2, "ca_qg")
        kgain = self.col_vec(es, "ca_k_gain", 0, 0, 2, "ca_kg")
        ones_f = self.load_const(es, "c_ones")
        ones_b = self.sb(es, "ca_1b", [128, 128], BF16)
        tk.op("dve", lambda: nc.vector.tensor_copy(ones_b[:], ones_f[:]), reads=[ones_f.b], writes=[ones_b.b])
        sqr = self.ring(es, "ca_sq", 2, [128, 2, 512], F32)
        rr = self.ring(es, "ca_r", 2, [128, 512], F32)
        tmpr = self.ring(es, "ca_tmp", 2, [128, 512], F32)
        qh = self.ring(es, "ca_qh", 2, [128, 2, 512], BF16)
        Er = self.ring(es, "ca_E", 2, [128, 2, 512], BF16)

        def qk_norm(p0, p1, n, gain, scale, out_aps, out_buf):
            s = sqr.next()
            tk.op("act", lambda: nc.scalar.activation(out=s[:, 0, 0:n], in_=p0[:, 0:n], func=AF.Square), reads=[p0.b], writes=[s.b])
            tk.op("act", lambda: nc.scalar.activation(out=s[:, 1, 0:n], in_=p1[:, 0:n], func=AF.Square), reads=[p1.b], writes=[s.b])
            p2 = self.psum()
            for j in range(2):
                tk.op("pe", lambda: nc.tensor.matmul(p2[:, 0:n], lhsT=ones_f[:], rhs=s[:, j, 0:n], start=(j == 0), stop=(j == 1)),
                      reads=[ones_f.b, s.b], writes=[p2.b])
            r = rr.next()
            tk.op("dve", lambda: nc.vector.tensor_scalar(r[:, 0:n], p2[:, 0:n], 1.0 / 256, 1e-6, ALU.mult, ALU.add), reads=[p2.b], writes=[r.b])
            self.rpow(r[:, 0:n], r.b, -0.5)
            for j, pj in enumerate((p0, p1)):
                t = tmpr.next()
                tk.op("dve", lambda: nc.vector.tensor_tensor(out=t[:, 0:n], in0=pj[:, 0:n], in1=r[:, 0:n], op=ALU.mult), reads=[pj.b, r.b], writes=[t.b])
                tk.op("dve", lambda: nc.vector.tensor_scalar(out_aps[j], t[:, 0:n], gain[:, j:j + 1], scale, ALU.mult, ALU.mult),
                      reads=[t.b, gain.b], writes=[out_buf])

        with self.scope() as es2:
            mnT = self.sb(es2, "ca_mnT", [128, 8, NMEM], BF16, dj=True)
            self.norm_T("mem", bi * NMEM, NMEM, "norm_mem", mnT)
            wk = self.sb(es2, "ca_wk", [128, 8, 1024], BF16)
            wv = self.sb(es2, "ca_wv", [128, 8, 1024], BF16)
            self.load_w(wk, "ca_wkv_bf", 0, 1024)
            self.load_w(wv, "ca_wkv_bf", 1024, 1024)
            for h in range(4):
                ps_ = []
                for j in range(2):
                    p = self.psum()
                    ci = 2 * h + j
                    for kc in range(8):
                        tk.op("pe", lambda: nc.tensor.matmul(p[:, 0:NMEM], lhsT=wk[:, kc, ci * 128:(ci + 1) * 128], rhs=mnT[:, kc, :], start=(kc == 0), stop=(kc == 7)),
                              reads=[wk.b, mnT.b], writes=[p.b])
                    ps_.append(p)
                qk_norm(ps_[0], ps_[1], NMEM, kgain, 1.0, [kT[:, 2 * h, :], kT[:, 2 * h + 1, :]], kT.b)
            for mt in range(2):
                for half in range(2):
                    p = self.psum()
                    for kc in range(8):
                        tk.op("pe", lambda: nc.tensor.matmul(p[:], lhsT=mnT[:, kc, mt * 128:(mt + 1) * 128], rhs=wv[:, kc, half * 512:(half + 1) * 512],
                                                              start=(kc == 0), stop=(kc == 7)), reads=[mnT.b, wv.b], writes=[p.b])
                    tk.op("act", lambda: nc.scalar.copy(Vc[:, mt, half * 512:(half + 1) * 512], p[:]), reads=[p.b], writes=[Vc.b])
        for hf in range(T // HT):
            with self.scope() as es2:
                hnT = self.sb(es2, "ca_hnT", [128, 8, HT], BF16, dj=True)
                oT = self.sb(es2, "ca_oT", [128, 8, HT], BF16, dj=True)
                self.norm_T("h1", hf * HT, HT, "norm_cross", hnT)
                for h in range(4):
                    for tt in range(HT // 512):
                        ps_ = []
                        for j in range(2):
                            p = self.psum()
                            ci = 2 * h + j
                            for kc in range(8):
                                tk.op("pe", lambda: nc.tensor.matmul(p[:], lhsT=wq[:, kc, ci * 128:(ci + 1) * 128], rhs=hnT[:, kc, tt * 512:(tt + 1) * 512],
                                                                      start=(kc == 0), stop=(kc == 7)), reads=[wq.b, hnT.b], writes=[p.b])
                            ps_.append(p)
                        q = qh.next()
                        qk_norm(ps_[0], ps_[1], 512, qgain, 1.0 / 16, [q[:, 0, :], q[:, 1, :]], q.b)
                        E = Er.next()
                        for mt in range(2):
                            p = self.psum()
                            for j in range(2):
                                tk.op("pe", lambda: nc.tensor.matmul(p[:], lhsT=kT[:, 2 * h + j, mt * 128:(mt + 1) * 128], rhs=q[:, j, :], start=(j == 0), stop=(j == 1)),
                                      reads=[kT.b, q.b], writes=[p.b])
                            tk.op("act", lambda: nc.scalar.activation(out=E[:, mt, :], in_=p[:], func=AF.Exp), reads=[p.b], writes=[E.b])
                        pd = self.psum()
                        for mt in range(2):
                            tk.op("pe", lambda: nc.tensor.matmul(pd[:], lhsT=ones_b[:], rhs=E[:, mt, :], start=(mt == 0), stop=(mt == 1)),
                                  reads=[ones_b.b, E.b], writes=[pd.b])
                        r = rr.next()
                        tk.op("act", lambda: nc.scalar.activation(out=r[:], in_=pd[:], func=AF.Ln), reads=[pd.b], writes=[r.b])
                        tk.op("act", lambda: nc.scalar.activation(out=r[:], in_=r[:], func=AF.Exp, scale=-1.0), reads=[r.b], writes=[r.b])
                        for j in range(2):
                            pn = self.psum()
                            for mt in range(2):
                                tk.op("pe", lambda: nc.tensor.matmul(pn[:], lhsT=Vc[:, mt, h * 256 + j * 128:h * 256 + (j + 1) * 128], rhs=E[:, mt, :],
                                                                      start=(mt == 0), stop=(mt == 1)), reads=[Vc.b, E.b], writes=[pn.b])
                            tk.op("dve", lambda: nc.vector.tensor_tensor(out=oT[:, 2 * h + j, tt * 512:(tt + 1) * 512], in0=pn[:], in1=r[:], op=ALU.mult),
                                  reads=[pn.b, r.b], writes=[oT.b])
                self.proj_tm_res(oT, 0, HT, 8, wo, "h1", hf * HT, "h2", hf * HT, es2)


def _ffn(self, bi):
    nc, tk = self.nc, self.tk
    self.scratch("ff_fm", [DFF, T], BF16)
    NCT = DFF // 128
    with self.scope() as es:
        hnT = self.sb(es, "ff_hnT", [128, 8, T], BF16, dj=True)
        self.norm_T("h2", 0, T, "norm_ffn", hnT)
        cw = [self.col_vec(es, "ffn_conv", j, 0, NCT, f"ff_cw{j}") for j in range(3)]
        cb = self.col_vec(es, "ffn_conv_b", 0, 0, NCT, "ff_cb")
        wring = self.ring(es, "ff_w", 4, [128, 8, 128], BF16)
        at = self.sb(es, "ff_a", [128, T + 2], F32)
        bt = self.sb(es, "ff_b", [128, T], F32)
        acc = self.sb(es, "ff_acc", [128, T], F32)
        ob = self.ring(es, "ff_ob", 2, [128, T], BF16)
        tk.op("pool", lambda: nc.gpsimd.memset(at[:, 0:2], 0.0), writes=[at.b])
        for ci in range(NCT):
            wa, wb = wring.next(), wring.next()
            self.load_w(wa, "ffn_up_bf", ci * 128, 128)
            self.load_w(wb, "ffn_up_bf", DFF + ci * 128, 128)
            for tt in range(T // 512):
                pa, pb = self.psum(), self.psum()
                for kc in range(8):
                    tk.op("pe", lambda: nc.tensor.matmul(pa[:], lhsT=wa[:, kc, :], rhs=hnT[:, kc, tt * 512:(tt + 1) * 512], start=(kc == 0), stop=(kc == 7)),
                          reads=[wa.b, hnT.b], writes=[pa.b])
                for kc in range(8):
                    tk.op("pe", lambda: nc.tensor.matmul(pb[:], lhsT=wb[:, kc, :], rhs=hnT[:, kc, tt * 512:(tt + 1) * 512], start=(kc == 0), stop=(kc == 7)),
                          reads=[wb.b, hnT.b], writes=[pb.b])
                tk.op("act", lambda: nc.scalar.copy(at[:, 2 + tt * 512:2 + (tt + 1) * 512], pa[:]), reads=[pa.b], writes=[at.b])
                tk.op("dve", lambda: nc.vector.tensor_copy(bt[:, tt * 512:(tt + 1) * 512], pb[:]), reads=[pb.b], writes=[bt.b])
            tk.op("dve", lambda: nc.vector.tensor_scalar(acc[:], at[:, 2:T + 2], cw[2][:, ci:ci + 1], cb[:, ci:ci + 1], ALU.mult, ALU.add),
                  reads=[at.b, cw[2].b, cb.b], writes=[acc.b])
            tk.op("dve", lambda: nc.vector.scalar_tensor_tensor(out=acc[:], in0=at[:, 1:T + 1], scalar=cw[1][:, ci:ci + 1], in1=acc[:], op0=ALU.mult, op1=ALU.add),
                  reads=[at.b, cw[1].b, acc.b], writes=[acc.b])
            tk.op("dve", lambda: nc.vector.scalar_tensor_tensor(out=acc[:], in0=at[:, 0:T], scalar=cw[0][:, ci:ci + 1], in1=acc[:], op0=ALU.mult, op1=ALU.add),
                  reads=[at.b, cw[0].b, acc.b], writes=[acc.b])
            tk.op("act", lambda: nc.scalar.activation(out=acc[:], in_=acc[:], func=AF.Silu), reads=[acc.b], writes=[acc.b])
            o = ob.next()
            tk.op("pool", lambda: nc.gpsimd.tensor_tensor(out=o[:], in0=acc[:], in1=bt[:], op=ALU.mult), reads=[acc.b, bt.b], writes=[o.b])
            tk.dma("pool", self.din["ff_fm"].ap()[ci * 128:(ci + 1) * 128, :], o[:], reads=[o.b], writes=[self.dbuf["ff_fm"]])
    with self.scope() as es:
        wd = self.sb(es, "ff_wd", [128, NCT, 1024], BF16)
        self.load_w(wd, "ffn_down_bf", 0, 1024, kchunks=NCT)
        TBK = 1024
        for blk in range(T // TBK):
            with self.scope() as es2:
                fT = self.sb(es2, "ff_fT", [128, NCT, TBK], BF16)
                tk.dma("sp", fT[:], self.dap("ff_fm", blk * TBK, [[T, 128], [128 * T, NCT], [1, TBK]]), reads=[self.dbuf["ff_fm"]], writes=[fT.b])
                self.proj_tm_res(fT, 0, TBK, NCT, wd, "h2", blk * TBK, "out", bi * T + blk * TBK, es2)


Prog.proj_tm_res = _proj_tm_res
Prog.phase_merge = _merge
Prog.phase_cross = _cross
Prog.phase_ffn = _ffn
```

```python
import contextlib
import math
import numpy as np
import concourse.bass as bass
import concourse.mybir as mybir
from concourse.bass_utils import run_bass_kernel_spmd

F32 = mybir.dt.float32
BF16 = mybir.dt.bfloat16
AF = mybir.ActivationFunctionType
ALU = mybir.AluOpType
AX = mybir.AxisListType

NCORES = 8
NB = 2
T = 4096
D = 1024
NMEM = 256
DFF = 2816
IN_COLS = 8016
BIG = 30000.0
OFFC = 2176
LVEC = 7680
SCALE_NSA = 0.125


class Buf:
    __slots__ = ("w", "r", "name", "dj", "xr")

    def __init__(self, name="", dj=False):
        self.w = {}
        self.r = {}
        self.name = name
        self.dj = dj
        self.xr = False


class Tile:
    def __init__(self, t, name, dj=False):
        self.t = t
        self.b = Buf(name, dj)

    def __getitem__(self, k):
        return self.t[k]


class Ring:
    def __init__(self, tiles):
        self.tiles = tiles
        self.i = 0

    def next(self):
        t = self.tiles[self.i]
        self.i = (self.i + 1) % len(self.tiles)
        return t


class TK:
    EPOCH = 20000
    NDSEM = 10

    def __init__(self, nc, es):
        self.nc = nc
        self.es = es
        self.eng = {"pe": nc.tensor, "act": nc.scalar, "dve": nc.vector,
                    "pool": nc.gpsimd, "sp": nc.sync}
        self.cnt = {e: 0 for e in self.eng}
        self.esem = {e: [] for e in self.eng}
        self.seen = {e: {} for e in self.eng}
        self.dsem = {}
        self.dptr = {}
        self.nwait = 0
        self.fence = {}

    def _newsem(self, name):
        return self.es.enter_context(self.nc.semaphore(name))

    def _engsem(self, e, epoch):
        while len(self.esem[e]) <= epoch:
            self.esem[e].append(self._newsem(f"s_{e}_{len(self.esem[e])}"))
        return self.esem[e][epoch]

    def _wait(self, e, ts):
        sem, val, src = ts
        if src == "pe" and e == "pe":
            return
        k = id(sem)
        if self.seen[e].get(k, 0) >= val:
            return
        self.seen[e][k] = val
        self.eng[e].wait_ge(sem, val)
        self.nwait += 1

    def deps(self, e, reads, writes):
        for b in reads:
            for ts in b.w.values():
                self._wait(e, ts)
            if b.xr:
                for ts in b.r.values():
                    if ts[2] != e:
                        self._wait(e, ts)
        for b in writes:
            if not (b.dj and not b.r):
                for ts in b.w.values():
                    self._wait(e, ts)
            for ts in b.r.values():
                self._wait(e, ts)

    def mark(self, ts, reads, writes):
        k = id(ts[0])
        for b in reads:
            b.r[k] = ts
        for b in writes:
            if b.dj and not b.r:
                b.w[k] = ts
            else:
                b.w = {k: ts}
                b.r = {}

    def op(self, e, ins_fn, reads=(), writes=()):
        self.deps(e, reads, writes)
        n = self.cnt[e]
        sem = self._engsem(e, n // self.EPOCH)
        val = n % self.EPOCH + 1
        ins_fn().then_inc(sem, 1)
        self.cnt[e] = n + 1
        ts = (sem, val, e)
        self.mark(ts, reads, writes)
        return ts

    def dma(self, q, out_ap, in_ap, reads=(), writes=(), **kw):
        if q not in self.dsem:
            self.dsem[q] = [[self._newsem(f"d_{q}_{i}"), 0] for i in range(self.NDSEM)]
            self.dptr[q] = 0
        slot = self.dsem[q][self.dptr[q]]
        self.dptr[q] = (self.dptr[q] + 1) % self.NDSEM
        sem, issued = slot
        if issued:
            self._wait(q, (sem, 16 * issued, None))
        self.deps(q, reads, writes)
        self.eng[q].dma_start(out=out_ap, in_=in_ap, **kw).then_inc(sem, 16)
        slot[1] = issued + 1
        ts = (sem, 16 * (issued + 1), None)
        self.mark(ts, reads, writes)
        return ts

    def update_fence(self):
        f = {}
        for e in self.eng:
            n = self.cnt[e]
            if n:
                sem = self.esem[e][(n - 1) // self.EPOCH]
                f[id(sem)] = (sem, (n - 1) % self.EPOCH + 1, e)
        for q in self.dsem:
            for sem, issued in self.dsem[q]:
                if issued:
                    f[id(sem)] = (sem, 16 * issued, None)
        self.fence = f

    def drain(self):
        for q in self.dsem:
            for sem, issued in self.dsem[q]:
                if issued:
                    self._wait(q, (sem, 16 * issued, None))


def _t5_bucket_np(dist):
    n = np.maximum(dist, 0)
    nf = np.maximum(n, 1).astype(np.float64)
    large = 16 + (np.log(nf / 16) / math.log(128 / 16) * 16).astype(np.int64)
    large = np.minimum(large, 31)
    return np.where(n < 16, n, large)


def host_consts():
    c = {}
    c["c_ident"] = np.eye(128, dtype=np.float32)
    c["c_J"] = np.ascontiguousarray(np.eye(128, dtype=np.float32)[::-1])
    hb = np.arange(128) // 64
    bd = (hb[:, None] == hb[None, :]).astype(np.float32)
    c["c_bdones"] = bd
    c["c_ones"] = np.ones((128, 128), np.float32)
    s = np.arange(128) % 64
    strict = bd * (s[:, None] < s[None, :])
    incl = bd * (s[:, None] <= s[None, :])
    c["c_mask2"] = np.concatenate([strict, incl], axis=1).astype(np.float32)
    bd5 = np.zeros((128, 5, 2, 64), np.float32)
    for h in range(2):
        bd5[64 * h:64 * h + 64, :, h, :] = 1.0
    c["c_bdmask5"] = bd5.reshape(128, 640)
    seg = np.ones((128, 1024), np.float32)
    seg[:, ::64] = 0.0
    c["c_segmask"] = seg
    dist = np.arange(LVEC) - OFFC
    bk = _t5_bucket_np(dist)
    oh = np.zeros((33, LVEC), np.float32)
    oh[bk, np.arange(LVEC)] = 1.0
    ec = oh.copy()
    ec[32] = np.where(dist >= 0, 0.0, -BIG)
    ec[:32, dist < 0] = 0.0
    ew = oh.copy()
    ok = (dist >= 0) & (dist < 512)
    ew[32] = np.where(ok, 0.0, -BIG)
    ew[:32, ~ok] = 0.0
    c["c_e33c"] = ec
    c["c_e33w"] = ew
    t = np.arange(T)
    cur = t // 64
    blk = np.arange(64)
    forced = (blk[:, None] == 0) | (blk[:, None] == cur[None, :]) | (blk[:, None] == cur[None, :] - 1)
    c["c_forced"] = np.where(forced, 1e4, 0.0).astype(np.float32)
    ex = np.zeros((64, 32, 128), np.float32)
    for kt in range(32):
        for p in range(128):
            ex[2 * kt + p // 64, kt, p] = 1.0
    c["c_expand"] = ex.reshape(64, 32 * 128)
    ncmp = 255
    ci = np.arange(256)[:, None] * 16
    sj = np.arange(64)[None, :] * 64
    c2s = ((ci <= sj + 63) & (ci + 31 >= sj)).astype(np.float32)
    c2s[ncmp:] = 0.0
    c["c_c2s"] = c2s
    return c


CONST_SHAPES = {k: v.shape for k, v in host_consts().items()}

W_SPECS = [
    ("w_in", 1024, IN_COLS), ("rwkv_w2", 64, 1024), ("rwkv_a2", 64, 1024), ("rwkv_g2", 160, 1024),
    ("cmp_w1_k", 2048, 256), ("cmp_w2_k", 256, 64), ("cmp_w1_v", 2048, 256), ("cmp_w2_v", 256, 64),
    ("w_branch_rwkv", 1024, 1024), ("w_branch_nsa", 1024, 1024), ("w_mix_out", 1024, 1024),
    ("ca_wq", 1024, 1024), ("ca_wkv", 1024, 2048), ("ca_wo", 1024, 1024),
    ("ffn_up", 1024, 2 * DFF), ("ffn_down", DFF, 1024),
]
V_SPECS = [
    ("rel_bias", (32, 16)), ("norm_mix", (1, 1024)), ("rwkv_mu", (1, 3360)), ("rwkv_w0", (1, 1024)),
    ("rwkv_a0", (1, 1024)), ("rwkv_kk", (1, 1024)), ("rwkv_ka", (1, 1024)), ("rwkv_rk", (1, 1024)),
    ("rwkv_lnx_w", (1, 1024)), ("rwkv_lnx_b", (1, 1024)), ("nsa_q_gain", (1, 64)), ("nsa_k_gain", (3, 64)),
    ("cmp_pe_k", (1, 2048)), ("cmp_pe_v", (1, 2048)), ("norm_cross", (1, 1024)), ("norm_mem", (1, 1024)),
    ("ca_q_gain", (1, 256)), ("ca_k_gain", (1, 256)), ("norm_ffn", (1, 1024)),
    ("ffn_conv", (3, DFF)), ("ffn_conv_b", (1, DFF)),
]


class Prog:
    def __init__(self, upto="all", dbg=()):
        self.upto = upto
        self.dbg = set(dbg)
        nc = self.nc = bass.Bass("TRN2", target_bir_lowering=False)
        self.es = contextlib.ExitStack()
        self.tk = TK(nc, self.es)
        self.din = {}
        self.dbuf = {}

    def dram_in(self, name, shape):
        self.din[name] = self.nc.dram_tensor(name, list(shape), F32, kind="ExternalInput")
        self.dbuf[name] = Buf(name, dj=True)
        return self.din[name]

    def scratch(self, name, shape, dt):
        if name in self.din:
            return self.din[name]
        kind = "ExternalOutput" if name in self.dbg else "Internal"
        self.din[name] = self.nc.dram_tensor(name, list(shape), dt, kind=kind)
        self.dbuf[name] = Buf(name, dj=True)
        return self.din[name]

    def sb(self, es, name, shape, dt, dj=False):
        self.uid = getattr(self, "uid", 0) + 1
        name = f"{name}_{self.uid}"
        t = Tile(es.enter_context(self.nc.sbuf_tensor(name, list(shape), dt)), name, dj)
        t.b.r = dict(self.tk.fence)
        return t

    @contextlib.contextmanager
    def scope(self):
        with contextlib.ExitStack() as es:
            yield es
        self.tk.update_fence()

    def ring(self, es, name, n, shape, dt):
        return Ring([self.sb(es, f"{name}{i}", shape, dt) for i in range(n)])

    def psum(self):
        return self.psr.next()

    def rpow(self, ap, buf, power):
        nc, tk = self.nc, self.tk
        tk.op("act", lambda: nc.scalar.activation(out=ap, in_=ap, func=AF.Ln), reads=[buf], writes=[buf])
        tk.op("act", lambda: nc.scalar.activation(out=ap, in_=ap, func=AF.Exp, scale=float(power)), reads=[buf], writes=[buf])

    def dap(self, name, offset, ap):
        return bass.AP(tensor=self.din[name], offset=offset, ap=[list(x) for x in ap])

    def load_const(self, es, name, dt=F32, tmp_es=None):
        nc, tk = self.nc, self.tk
        shp = CONST_SHAPES[name]
        t32 = self.sb(es if dt == F32 else tmp_es, name + "_f", shp, F32)
        tk.dma("sp", t32[:], self.din[name].ap()[:, :], reads=[self.dbuf[name]], writes=[t32.b])
        if dt == F32:
            return t32
        t16 = self.sb(es, name + "_h", shp, BF16)
        tk.op("dve", lambda: nc.vector.tensor_copy(t16[:], t32[:]), reads=[t32.b], writes=[t16.b])
        return t16

    def bcast_vec(self, es, name, row, c0, n, tname):
        t = self.sb(es, tname, [128, n], F32)
        src = self.din[name].ap()[row:row + 1, c0:c0 + n].partition_broadcast(128)
        self.tk.dma("sp", t[:], src, reads=[self.dbuf[name]], writes=[t.b])
        return t

    def col_vec(self, es, name, row, c0, nchunk, tname, p=128):
        nc, tk = self.nc, self.tk
        t = self.sb(es, tname, [p, nchunk], F32)
        ncols = self.din[name].shape[1]
        with self.scope() as es2:
            raw = self.sb(es2, tname + "_raw", [nchunk, p], F32)
            tk.dma("sp", raw[:], self.dap(name, row * ncols + c0, [[p, nchunk], [1, p]]), reads=[self.dbuf[name]], writes=[raw.b])
            ps_ = self.psum()
            tk.op("pe", lambda: nc.tensor.transpose(ps_[:p, 0:nchunk], raw[:], self.ident[:nchunk, :nchunk]),
                  reads=[raw.b, self.ident.b], writes=[ps_.b])
            tk.op("dve", lambda: nc.vector.tensor_copy(t[:], ps_[:p, 0:nchunk]), reads=[ps_.b], writes=[t.b])
        return t

    def phase_w(self):
        nc, tk = self.nc, self.tk
        with self.scope() as es:
            st = self.ring(es, "wst", 3, [128, 2048], F32)
            sh = self.ring(es, "wsh", 3, [128, 2048], BF16)
            k = 0
            for name, R, C in W_SPECS:
                dst = self.scratch(name + "_bf", [R, C], BF16)
                src = self.din[name].ap()
                for r0 in range(0, R, 128):
                    rr = min(128, R - r0)
                    for c0 in range(0, C, 2048):
                        cc = min(2048, C - c0)
                        a = st.next()
                        h = sh.next()
                        tk.dma("sp", a[:rr, :cc], src[r0:r0 + rr, c0:c0 + cc], reads=[self.dbuf[name]], writes=[a.b])
                        e = ("dve", "pool", "act")[k % 3]
                        k += 1
                        if e == "act":
                            tk.op(e, lambda: nc.scalar.copy(h[:rr, :cc], a[:rr, :cc]), reads=[a.b], writes=[h.b])
                        elif e == "dve":
                            tk.op(e, lambda: nc.vector.tensor_copy(h[:rr, :cc], a[:rr, :cc]), reads=[a.b], writes=[h.b])
                        else:
                            tk.op(e, lambda: nc.gpsimd.tensor_copy(h[:rr, :cc], a[:rr, :cc]), reads=[a.b], writes=[h.b])
                        tk.dma("pool", dst.ap()[r0:r0 + rr, c0:c0 + cc], h[:rr, :cc], reads=[h.b],
                               writes=[self.dbuf[name + "_bf"]])

    def norm_T(self, src_name, src_row0, ntok, gname, dstT):
        nc, tk = self.nc, self.tk
        with self.scope() as es:
            gbc = self.bcast_vec(es, gname, 0, 0, D, "nt_g")
            xr = self.ring(es, "nt_x", 2, [128, D], F32)
            xs = self.ring(es, "nt_xs", 2, [128, D], F32)
            junk = self.sb(es, "nt_junk", [128, D], BF16)
            st = self.ring(es, "nt_st", 2, [128, 4], F32)
            src = self.din[src_name].ap()
            for i in range(ntok // 128):
                x = xr.next()
                s = st.next()
                y = xs.next()
                tk.dma("sp", x[:], src[src_row0 + i * 128: src_row0 + (i + 1) * 128, :],
                       reads=[self.dbuf[src_name]], writes=[x.b])
                tk.op("act", lambda: nc.scalar.activation(out=junk[:], in_=x[:], func=AF.Square, accum_out=s[:, 0:1]),
                      reads=[x.b], writes=[junk.b, s.b])
                tk.op("dve", lambda: nc.vector.tensor_scalar(s[:, 1:2], s[:, 0:1], 1.0 / D, 1e-6, ALU.mult, ALU.add),
                      reads=[s.b], writes=[s.b])
                tk.op("act", lambda: nc.scalar.sqrt(s[:, 2:3], s[:, 1:2]), reads=[s.b], writes=[s.b])
                tk.op("dve", lambda: nc.vector.reciprocal(s[:, 3:4], s[:, 2:3]), reads=[s.b], writes=[s.b])
                tk.op("dve", lambda: nc.vector.scalar_tensor_tensor(out=y[:], in0=x[:], scalar=s[:, 3:4], in1=gbc[:],
                                                                    op0=ALU.mult, op1=ALU.mult),
                      reads=[x.b, s.b, gbc.b], writes=[y.b])
                for half in range(2):
                    p = self.psum()
                    for j in range(4):
                        kc = half * 4 + j
                        tk.op("pe", lambda: nc.tensor.transpose(p[:, j * 128:(j + 1) * 128], y[:, kc * 128:(kc + 1) * 128],
                                                                self.ident[:]),
                              reads=[y.b, self.ident.b], writes=[p.b])
                    o = dstT[:, half * 4:half * 4 + 4, i * 128:(i + 1) * 128]
                    pin = p[:, :].rearrange("p (a b) -> p a b", a=4)
                    if half == 0:
                        tk.op("act", lambda: nc.scalar.copy(o, pin), reads=[p.b], writes=[dstT.b])
                    else:
                        tk.op("dve", lambda: nc.vector.tensor_copy(o, pin), reads=[p.b], writes=[dstT.b])

    def load_w(self, tile, wname, c0, ncols, kchunks=8, r0=0):
        C = self.din[wname].shape[1]
        src = self.dap(wname, r0 * C + c0, [[C, 128], [128 * C, kchunks], [1, ncols]])
        self.tk.dma("sp", tile[:, 0:kchunks, 0:ncols], src, reads=[self.dbuf[wname]], writes=[tile.b])

    def proj_fm(self, wname, c0, ncols_total, actT, ntok, epi, kchunks=8, wring=None):
        nc, tk = self.nc, self.tk
        nct = (ncols_total + 127) // 128
        for ci in range(nct):
            cc = min(128, ncols_total - ci * 128)
            w = wring.next()
            self.load_w(w, wname, c0 + ci * 128, cc, kchunks)
            for tt in range(ntok // 512):
                p = self.psum()
                for kc in range(kchunks):
                    tk.op("pe", lambda: nc.tensor.matmul(p[:cc, :], lhsT=w[:, kc, 0:cc],
                                                          rhs=actT[:, kc, tt * 512:(tt + 1) * 512],
                                                          start=(kc == 0), stop=(kc == kchunks - 1)),
                          reads=[w.b, actT.b], writes=[p.b])
                epi(p, ci, tt, cc)

    def phase_b(self, xT):
        nc, tk = self.nc, self.tk
        self.scratch("zr_fm", [3360, T], F32)
        self.scratch("q_fm", [1024, T], BF16)
        self.scratch("kcvc_fm", [512, T], BF16)
        self.scratch("ks_fm", [256, T], BF16)
        self.scratch("kw_fm", [256, T], BF16)
        self.scratch("vsw_tm", [T, 512], BF16)
        self.scratch("gates_fm", [48, T], F32)
        self.scratch("gm_fm", [2048, T], F32)
        with self.scope() as es:
            wring = self.ring(es, "pb_w", 2, [128, 8, 128], BF16)
            o32 = self.ring(es, "pb_o32", 3, [128, 512], F32)
            o16 = self.ring(es, "pb_o16", 3, [128, 512], BF16)
            sq = self.ring(es, "pb_sq", 2, [128, 512], F32)
            qg = self.sb(es, "pb_qg", [128, 4], F32)
            eps = self.sb(es, "pb_eps", [128, 1], F32)
            tk.op("pool", lambda: nc.gpsimd.memset(eps[:], 1e-6), writes=[eps.b])
            for h in range(2):
                tk.dma("sp", qg[64 * h:64 * h + 64, 0:1], self.dap("nsa_q_gain", 0, [[1, 64], [1, 1]]),
                       reads=[self.dbuf["nsa_q_gain"]], writes=[qg.b])
                for j in (1, 2):
                    tk.dma("sp", qg[64 * h:64 * h + 64, j + 1:j + 2], self.dap("nsa_k_gain", 64 * j, [[1, 64], [1, 1]]),
                           reads=[self.dbuf["nsa_k_gain"]], writes=[qg.b])
            cnt = [0]

            def store(dname, row0, t0, tile, rows):
                tk.dma("pool", self.din[dname].ap()[row0:row0 + rows, t0:t0 + 512], tile[:rows, :], reads=[tile.b],
                       writes=[self.dbuf[dname]])

            def epi_copy(dname, row_base, dt):
                def f(p, ci, tt, cc):
                    o = (o32 if dt == F32 else o16).next()
                    cnt[0] += 1
                    if cnt[0] % 2:
                        tk.op("act", lambda: nc.scalar.copy(o[:cc, :], p[:cc, :]), reads=[p.b], writes=[o.b])
                    else:
                        tk.op("dve", lambda: nc.vector.tensor_copy(o[:cc, :], p[:cc, :]), reads=[p.b], writes=[o.b])
                    store(dname, row_base + ci * 128, tt * 512, o, cc)
                return f

            def epi_sig(dname, row_base):
                def f(p, ci, tt, cc):
                    o = o32.next()
                    tk.op("act", lambda: nc.scalar.activation(out=o[:cc, :], in_=p[:cc, :], func=AF.Sigmoid),
                          reads=[p.b], writes=[o.b])
                    store(dname, row_base + ci * 128, tt * 512, o, cc)
                return f

            def epi_norm(dname, row_base, gcol, scale):
                def f(p, ci, tt, cc):
                    s = sq.next()
                    tk.op("act", lambda: nc.scalar.activation(out=s[:], in_=p[:], func=AF.Square), reads=[p.b], writes=[s.b])
                    p2 = self.psum()
                    tk.op("pe", lambda: nc.tensor.matmul(p2[:], lhsT=self.bdones[:], rhs=s[:], start=True, stop=True),
                          reads=[self.bdones.b, s.b], writes=[p2.b])
                    r = o32.next()
                    tk.op("dve", lambda: nc.vector.tensor_scalar(r[:], p2[:], 1.0 / 64, 1e-6, ALU.mult, ALU.add),
                          reads=[p2.b], writes=[r.b])
                    self.rpow(r[:], r.b, -0.5)
                    tk.op("dve", lambda: nc.vector.tensor_tensor(out=r[:], in0=p[:], in1=r[:], op=ALU.mult),
                          reads=[p.b, r.b], writes=[r.b])
                    o = o16.next()
                    tk.op("dve", lambda: nc.vector.tensor_scalar(o[:], r[:], qg[:, gcol:gcol + 1], scale, ALU.mult, ALU.mult),
                          reads=[r.b, qg.b], writes=[o.b])
                    store(dname, row_base + ci * 128, tt * 512, o, cc)
                return f

            segs = [
                (0, 3360, epi_copy("zr_fm", 0, F32)),
                (3360, 1024, epi_norm("q_fm", 0, 0, SCALE_NSA)),
                (4384, 512, epi_copy("kcvc_fm", 0, BF16)),
                (4896, 256, epi_norm("ks_fm", 0, 2, 1.0)),
                (5408, 256, epi_norm("kw_fm", 0, 3, 1.0)),
                (5920, 48, epi_sig("gates_fm", 0)),
                (5968, 2048, epi_sig("gm_fm", 0)),
            ]
            for c0, n, epi in segs:
                self.proj_fm("w_in_bf", c0, n, xT, T, epi, wring=wring)
            wv = self.sb(es, "pb_wv", [128, 8, 512], BF16)
            self.load_w(wv, "w_in_bf", 5152, 256)
            C = IN_COLS
            tk.dma("sp", wv[:, :, 256:512], self.dap("w_in_bf", 5664, [[C, 128], [128 * C, 8], [1, 256]]),
                   reads=[self.dbuf["w_in_bf"]], writes=[wv.b])
            for i in range(T // 128):
                p = self.psum()
                for kc in range(8):
                    tk.op("pe", lambda: nc.tensor.matmul(p[:], lhsT=xT[:, kc, i * 128:(i + 1) * 128], rhs=wv[:, kc, :],
                                                          start=(kc == 0), stop=(kc == 7)), reads=[xT.b, wv.b], writes=[p.b])
                o = o16.next()
                tk.op("act", lambda: nc.scalar.copy(o[:], p[:]), reads=[p.b], writes=[o.b])
                tk.dma("pool", self.din["vsw_tm"].ap()[i * 128:(i + 1) * 128, :], o[:], reads=[o.b], writes=[self.dbuf["vsw_tm"]])

    def build(self):
        nc, tk = self.nc, self.tk
        self.dram_in("x", [NB * T, D])
        self.dram_in("mem", [NB * NMEM, D])
        for name, R, C in W_SPECS:
            self.dram_in(name, [R, C])
        for name, shp in V_SPECS:
            self.dram_in(name, shp)
        for name, shp in CONST_SHAPES.items():
            self.dram_in(name, shp)
        self.out = self.nc.dram_tensor("out", [NB * T, D], F32, kind="ExternalOutput")
        self.din["out"] = self.out
        self.dbuf["out"] = Buf("out", dj=True)
        es = self.es
        self.psr = Ring([Tile(es.enter_context(nc.psum_tensor(f"ps{i}", [128, 512], F32)), f"ps{i}") for i in range(8)])
        for t_ in self.psr.tiles:
            t_.b.xr = True
        self.ident = self.load_const(es, "c_ident")
        self.bdones = self.load_const(es, "c_bdones")
        self.phase_w()
        if self.upto == "w":
            return self.finish()
        self.nsa_bias()
        for bi in range(NB):
            self.seq(bi)
            if self.upto != "all":
                break
        return self.finish()

    def seq(self, bi):
        tk = self.tk
        with self.scope() as es1:
            xT = self.sb(es1, "xT", [128, 8, T], BF16, dj=True)
            self.norm_T("x", bi * T, T, "norm_mix", xT)
            if bi == 0 and "xT_dbg" in self.dbg:
                d = self.scratch("xT_dbg", [128, 8 * T], BF16)
                tk.dma("sp", d.ap()[:, :], xT[:, :, :].rearrange("p a b -> p (a b)"), reads=[xT.b], writes=[self.dbuf["xT_dbg"]])
            if self.upto == "a":
                return
            self.phase_b(xT)
        if self.upto == "b":
            return
        if not getattr(self, "skip_rwkv", False):
            self.phase_rwkv()
        if self.upto == "rwkv":
            return
        self.phase_nsa()
        if self.upto == "nsa":
            return
        self.phase_merge(bi)
        if self.upto == "merge":
            return
        self.phase_cross(bi)
        if self.upto == "cross":
            return
        self.phase_ffn(bi)

    def finish(self):
        self.tk.drain()
        self.es.close()
        return self.nc


def make_in_maps(inputs, cores=range(NCORES)):
    consts = host_consts()
    shared = {}
    for name, R, C in W_SPECS:
        shared[name] = np.ascontiguousarray(np.asarray(inputs[name], np.float32).reshape(R, C))
    for name, shp in V_SPECS:
        shared[name] = np.ascontiguousarray(np.asarray(inputs[name], np.float32).reshape(shp))
    shared.update(consts)
    x = np.asarray(inputs["x"], np.float32)
    mem = np.asarray(inputs["mem"], np.float32)
    maps = []
    for c in cores:
        m = dict(shared)
        m["x"] = np.ascontiguousarray(x[NB * c:NB * c + NB].reshape(NB * T, D))
        m["mem"] = np.ascontiguousarray(mem[NB * c:NB * c + NB].reshape(NB * NMEM, D))
        maps.append(m)
    return maps


def kernel(**inputs):
    prog = Prog()
    nc = prog.build()
    maps = make_in_maps(inputs)
    res = run_bass_kernel_spmd(nc, maps, core_ids=list(range(NCORES)))
    outs = [np.asarray(r["out"]).reshape(NB, T, D) for r in res.results]
    return np.concatenate(outs, axis=0).astype(np.float32)


def _rwkv(self):
    nc, tk = self.nc, self.tk
    TB = 512
    self.scratch("yr_fm", [1024, T], BF16)
    zr = self.din["zr_fm"].ap()
    zb = self.dbuf["zr_fm"]

    def shift_load(dst_ap, dst_buf, r0, nrows, t0, nt, mucol, X, dtile):
        if t0 == 0:
            tk.op("pool", lambda: nc.gpsimd.memset(X[:nrows, 0:1], 0.0), writes=[X.b])
            tk.dma("sp", X[:nrows, 1:nt + 1], zr[r0:r0 + nrows, 0:nt], reads=[zb], writes=[X.b])
        else:
            tk.dma("sp", X[:nrows, 0:nt + 1], zr[r0:r0 + nrows, t0 - 1:t0 + nt], reads=[zb], writes=[X.b])
        tk.op("pool", lambda: nc.gpsimd.tensor_tensor(out=dtile[:nrows, :nt], in0=X[:nrows, 0:nt], in1=X[:nrows, 1:nt + 1],
                                                      op=ALU.subtract), reads=[X.b], writes=[dtile.b])
        tk.op("dve", lambda: nc.vector.scalar_tensor_tensor(out=dst_ap, in0=dtile[:nrows, :nt], scalar=mucol,
                                                             in1=X[:nrows, 1:nt + 1], op0=ALU.mult, op1=ALU.add),
              reads=[dtile.b, X.b], writes=[dst_buf])

    with self.scope() as es:
        mask4 = self.sb(es, "rk_mask4", [128, 512], F32)
        for j in range(2):
            tk.dma("sp", mask4[:, j * 256:(j + 1) * 256], self.din["c_mask2"].ap()[:, :], reads=[self.dbuf["c_mask2"]], writes=[mask4.b])
        bdm5 = self.load_const(es, "c_bdmask5")
        segm = self.sb(es, "rk_seg", [128, TB], F32)
        tk.dma("sp", segm[:], self.din["c_segmask"].ap()[:, 0:TB], reads=[self.dbuf["c_segmask"]], writes=[segm.b])
        lw = self.sb(es, "rk_lw", [64, T], BF16)
        la = self.sb(es, "rk_la", [64, T], BF16)
        lg = self.sb(es, "rk_lg", [128, 2, T], BF16)
        w2 = self.sb(es, "rk_w2", [64, 1024], BF16)
        a2 = self.sb(es, "rk_a2", [64, 1024], BF16)
        g2 = self.sb(es, "rk_g2", [128, 2, 1024], BF16)
        tk.dma("sp", w2[:], self.din["rwkv_w2_bf"].ap()[:, :], reads=[self.dbuf["rwkv_w2_bf"]], writes=[w2.b])
        tk.dma("sp", a2[:], self.din["rwkv_a2_bf"].ap()[:, :], reads=[self.dbuf["rwkv_a2_bf"]], writes=[a2.b])
        tk.dma("sp", g2[:, 0, :], self.din["rwkv_g2_bf"].ap()[0:128, :], reads=[self.dbuf["rwkv_g2_bf"]], writes=[g2.b])
        tk.dma("sp", g2[0:32, 1, :], self.din["rwkv_g2_bf"].ap()[128:160, :], reads=[self.dbuf["rwkv_g2_bf"]], writes=[g2.b])
        pc = {}
        for nm in ("rwkv_w0", "rwkv_a0", "rwkv_kk", "rwkv_ka", "rwkv_rk", "rwkv_lnx_w", "rwkv_lnx_b"):
            pc[nm] = self.col_vec(es, nm, 0, 0, 8, "rk_" + nm)
        mu = self.col_vec(es, "rwkv_mu", 0, 0, 24, "rk_mu")
        omk = self.sb(es, "rk_omk", [128, 8], F32)
        tk.op("dve", lambda: nc.vector.tensor_scalar(omk[:], pc["rwkv_ka"][:], -1.0, 1.0, ALU.mult, ALU.add),
              reads=[pc["rwkv_ka"].b], writes=[omk.b])
        with self.scope() as es2:
            X = self.sb(es2, "rk_LX", [128, T + 1], F32)
            dt_ = self.sb(es2, "rk_Ld", [128, T], F32)
            zt = self.sb(es2, "rk_Lz", [128, T], F32)
            for (r0, nrows, kind) in ((3072, 64, "w"), (3136, 64, "a"), (3200, 128, "g0"), (3328, 32, "g1")):
                mucol = self.sb(es2, "rk_Lmu" + kind, [128, 1], F32)
                tk.dma("sp", mucol[:nrows, :], self.dap("rwkv_mu", r0, [[1, nrows], [1, 1]]), reads=[self.dbuf["rwkv_mu"]], writes=[mucol.b])
                shift_load(zt[:nrows, :], zt.b, r0, nrows, 0, T, mucol[:nrows, 0:1], X, dt_)
                if kind == "w":
                    tk.op("act", lambda: nc.scalar.activation(out=lw[:, :], in_=zt[:64, :], func=AF.Tanh), reads=[zt.b], writes=[lw.b])
                elif kind == "a":
                    tk.op("act", lambda: nc.scalar.copy(la[:, :], zt[:64, :]), reads=[zt.b], writes=[la.b])
                elif kind == "g0":
                    tk.op("act", lambda: nc.scalar.activation(out=lg[:, 0, :], in_=zt[:, :], func=AF.Sigmoid), reads=[zt.b], writes=[lg.b])
                else:
                    tk.op("act", lambda: nc.scalar.activation(out=lg[:32, 1, :], in_=zt[:32, :], func=AF.Sigmoid), reads=[zt.b], writes=[lg.b])
        LIM = getattr(self, "rk_lim", 99)
        if LIM <= 1:
            return
        f = lambda n: self.sb(es, n, [128, TB], F32)
        Xr = self.ring(es, "rk_X", 2, [128, TB + 1], F32)
        dtl = f("rk_d")
        rr, kp, logw, aa, gg, kkr, sq, kmod, kb, cum, cex, epv, eng, bonus, tmp = [f("rk_t%d" % i) for i in range(15)]
        einr = self.ring(es, "rk_ein", 2, [128, TB], F32)
        Q5r = self.ring(es, "rk_Q5", 2, [128, 5, TB], F32)
        yfm = f("rk_yfm")
        dd = f("rk_dd")
        ob = self.ring(es, "rk_ob", 2, [128, TB], BF16)
        BD5l = [self.sb(es, f"rk_BD5{i}", [128, 5, 2, 64], F32) for i in range(4)]
        GBKl = [self.sb(es, f"rk_GBK{i}", [128, 512], F32) for i in range(4)]
        NTl = [self.sb(es, f"rk_NT{i}", [128, 128], F32) for i in range(4)]
        MXl = [[self.sb(es, f"rk_MX{i}{k}", [128, 256], F32) for k in range(2)] for i in range(4)]
        MTl = [[self.sb(es, f"rk_MT{i}{k}", [128, 128], F32) for k in range(2)] for i in range(4)]
        TTl = [self.sb(es, f"rk_TT{i}", [128, 128], F32) for i in range(4)]
        TM3l = [self.sb(es, f"rk_TM3{i}", [128, 384], F32) for i in range(4)]
        RHr = self.ring(es, "rk_RH", 2, [128, 128], F32)
        Ur = self.ring(es, "rk_U", 2, [128, 128], F32)
        Sr = self.ring(es, "rk_S", 2, [128, 128], F32)
        SPr = self.ring(es, "rk_SP", 2, [128, 128], F32)
        ident, bdones = self.ident, self.bdones

        def mm(p_ap, pbuf, lhsT, lb, rhs, rb, start=True, stop=True):
            tk.op("pe", lambda: nc.tensor.matmul(p_ap, lhsT=lhsT, rhs=rhs, start=start, stop=stop), reads=[lb, rb], writes=[pbuf])

        for hp in range(8):
            c0 = 128 * hp
            S = Sr.next()
            tk.op("pool", lambda: nc.gpsimd.memset(S[:], 0.0), writes=[S.b])
            for tb in range(T // TB):
                t0 = tb * TB
                Q5 = Q5r.next()
                ein = einr.next()
                shift_load(rr[:, :], rr.b, c0, 128, t0, TB, mu[:, hp:hp + 1], Xr.next(), dtl)
                shift_load(kp[:, :], kp.b, 1024 + c0, 128, t0, TB, mu[:, 8 + hp:9 + hp], Xr.next(), dtl)
                shift_load(Q5[:, 4, :], Q5.b, 2048 + c0, 128, t0, TB, mu[:, 16 + hp:17 + hp], Xr.next(), dtl)
                p = self.psum()
                mm(p[:], p.b, w2[:, c0:c0 + 128], w2.b, lw[:, t0:t0 + TB], lw.b)
                tk.op("act", lambda: nc.scalar.activation(out=logw[:], in_=p[:], func=AF.Sigmoid, bias=pc["rwkv_w0"][:, hp:hp + 1]),
                      reads=[p.b, pc["rwkv_w0"].b], writes=[logw.b])
                tk.op("pool", lambda: nc.gpsimd.tensor_scalar_mul(logw[:], logw[:], -math.exp(-0.5)), reads=[logw.b], writes=[logw.b])
                p = self.psum()
                mm(p[:], p.b, a2[:, c0:c0 + 128], a2.b, la[:, t0:t0 + TB], la.b)
                tk.op("act", lambda: nc.scalar.activation(out=aa[:], in_=p[:], func=AF.Sigmoid, bias=pc["rwkv_a0"][:, hp:hp + 1]),
                      reads=[p.b, pc["rwkv_a0"].b], writes=[aa.b])
                p = self.psum()
                mm(p[:], p.b, g2[:, 0, c0:c0 + 128], g2.b, lg[:, 0, t0:t0 + TB], lg.b, True, False)
                mm(p[:], p.b, g2[:32, 1, c0:c0 + 128], g2.b, lg[:32, 1, t0:t0 + TB], lg.b, False, True)
                tk.op("act", lambda: nc.scalar.copy(gg[:], p[:]), reads=[p.b], writes=[gg.b])
                tk.op("dve", lambda: nc.vector.tensor_scalar_mul(kkr[:], kp[:], pc["rwkv_kk"][:, hp:hp + 1]),
                      reads=[kp.b, pc["rwkv_kk"].b], writes=[kkr.b])
                tk.op("act", lambda: nc.scalar.activation(out=sq[:], in_=kkr[:], func=AF.Square), reads=[kkr.b], writes=[sq.b])
                p = self.psum()
                mm(p[:], p.b, bdones[:], bdones.b, sq[:], sq.b)
                tk.op("dve", lambda: nc.vector.tensor_scalar_max(tmp[:], p[:], 1e-24), reads=[p.b], writes=[tmp.b])
                self.rpow(tmp[:], tmp.b, -0.5)
                tk.op("dve", lambda: nc.vector.tensor_tensor(out=kkr[:], in0=kkr[:], in1=tmp[:], op=ALU.mult), reads=[kkr.b, tmp.b], writes=[kkr.b])
                tk.op("dve", lambda: nc.vector.tensor_scalar(kmod[:], aa[:], pc["rwkv_ka"][:, hp:hp + 1], omk[:, hp:hp + 1], ALU.mult, ALU.add),
                      reads=[aa.b, pc["rwkv_ka"].b, omk.b], writes=[kmod.b])
                tk.op("pool", lambda: nc.gpsimd.tensor_tensor(out=kmod[:], in0=kmod[:], in1=kp[:], op=ALU.mult), reads=[kmod.b, kp.b], writes=[kmod.b])
                tk.op("pool", lambda: nc.gpsimd.tensor_tensor(out=kb[:], in0=kkr[:], in1=aa[:], op=ALU.mult), reads=[kkr.b, aa.b], writes=[kb.b])
                tk.op("dve", lambda: nc.vector.scalar_tensor_tensor(out=tmp[:], in0=rr[:], scalar=pc["rwkv_rk"][:, hp:hp + 1], in1=kmod[:],
                                                                    op0=ALU.mult, op1=ALU.mult), reads=[rr.b, kmod.b, pc["rwkv_rk"].b], writes=[tmp.b])
                p = self.psum()
                mm(p[:], p.b, bdones[:], bdones.b, tmp[:], tmp.b)
                tk.op("dve", lambda: nc.vector.tensor_tensor(out=bonus[:], in0=p[:], in1=Q5[:, 4, :], op=ALU.mult), reads=[p.b, Q5.b], writes=[bonus.b])
                tk.op("dve", lambda: nc.vector.tensor_tensor_scan(out=cum[:], data0=segm[:], data1=logw[:], initial=0.0, op0=ALU.mult, op1=ALU.add),
                      reads=[segm.b, logw.b], writes=[cum.b])
                tk.op("pool", lambda: nc.gpsimd.tensor_tensor(out=cex[:], in0=cum[:], in1=logw[:], op=ALU.subtract), reads=[cum.b, logw.b], writes=[cex.b])
                tk.op("act", lambda: nc.scalar.activation(out=epv[:], in_=cex[:], func=AF.Exp), reads=[cex.b], writes=[epv.b])
                tk.op("act", lambda: nc.scalar.activation(out=ein[:], in_=cum[:], func=AF.Exp), reads=[cum.b], writes=[ein.b])
                tk.op("act", lambda: nc.scalar.activation(out=eng[:], in_=cum[:], func=AF.Exp, scale=-1.0), reads=[cum.b], writes=[eng.b])
                tk.op("dve", lambda: nc.vector.scalar_tensor_tensor(out=Q5[:, 0, :], in0=kkr[:], scalar=-1.0, in1=epv[:], op0=ALU.mult, op1=ALU.mult),
                      reads=[kkr.b, epv.b], writes=[Q5.b])
                tk.op("pool", lambda: nc.gpsimd.tensor_tensor(out=Q5[:, 1, :], in0=rr[:], in1=ein[:], op=ALU.mult), reads=[rr.b, ein.b], writes=[Q5.b])
                tk.op("dve", lambda: nc.vector.tensor_tensor(out=Q5[:, 2, :], in0=kb[:], in1=eng[:], op=ALU.mult), reads=[kb.b, eng.b], writes=[Q5.b])
                tk.op("pool", lambda: nc.gpsimd.tensor_tensor(out=Q5[:, 3, :], in0=kmod[:], in1=eng[:], op=ALU.mult), reads=[kmod.b, eng.b], writes=[Q5.b])
                if LIM <= 2:
                    return
                NBC = 2
                st = {}

                def batch_gen(chunks):
                    for c in chunks:
                        cs = slice(c * 64, (c + 1) * 64)
                        BD5 = BD5l[c % 4]
                        src = Q5[:, :, cs].unsqueeze(2).to_broadcast([128, 5, 2, 64])
                        tk.op("dve", lambda: nc.vector.tensor_tensor(out=BD5[:], in0=src, in1=bdm5[:, :].rearrange("p (a h b) -> p a h b", a=5, h=2),
                                                                     op=ALU.mult), reads=[Q5.b, bdm5.b], writes=[BD5.b])
                        bd = lambda j, BD5=BD5: BD5[:, j, :, :].rearrange("p h b -> p (h b)")
                        p = self.psum()
                        ar = BD5[:, 0:2, :, :].rearrange("p a h b -> p (a h b)")
                        mm(p[:, 0:256], p.b, bd(2), BD5.b, ar, BD5.b)
                        mm(p[:, 256:512], p.b, bd(3), BD5.b, ar, BD5.b)
                        GBK = GBKl[c % 4]
                        tk.op("dve", lambda: nc.vector.tensor_tensor(out=GBK[:], in0=p[:], in1=mask4[:], op=ALU.mult), reads=[p.b, mask4.b], writes=[GBK.b])
                        yield
                        p3 = self.psum()
                        for j in range(3):
                            tk.op("pe", lambda: nc.tensor.transpose(p3[:, j * 128:(j + 1) * 128], bd(2 + j), ident[:]), reads=[BD5.b, ident.b], writes=[p3.b])
                        TM3 = TM3l[c % 4]
                        tk.op("act", lambda: nc.scalar.copy(TM3[:], p3[:, 0:384]), reads=[p3.b], writes=[TM3.b])
                        st[c] = dict(BD5=BD5, bd=bd, GBK=GBK, TM3=TM3, cs=cs)
                        yield
                    for c in chunks:
                        d = st[c]
                        GBK = d["GBK"]
                        NT = NTl[c % 4]
                        p = self.psum()
                        tk.op("pe", lambda: nc.tensor.transpose(p[:, 0:128], GBK[:, 0:128], ident[:]), reads=[GBK.b, ident.b], writes=[p.b])
                        tk.op("act", lambda: nc.scalar.copy(NT[:], p[:, 0:128]), reads=[p.b], writes=[NT.b])
                        MX = MXl[c % 4][0]
                        tk.op("pool", lambda: nc.gpsimd.tensor_tensor(out=MX[:, 128:256], in0=ident[:], in1=GBK[:, 0:128], op=ALU.add),
                              reads=[ident.b, GBK.b], writes=[MX.b])
                        d["NT"] = NT
                        yield
                    for c in chunks:
                        d = st[c]
                        GBK, NT = d["GBK"], d["NT"]
                        MX, MT = MXl[c % 4][0], MTl[c % 4][0]
                        p = self.psum()
                        mm(p[:, 0:128], p.b, NT[:], NT.b, GBK[:, 0:128], GBK.b)
                        pt = self.psum()
                        mm(pt[:, 0:128], pt.b, GBK[:, 0:128], GBK.b, NT[:], NT.b)
                        tk.op("act", lambda: nc.scalar.copy(MX[:, 0:128], p[:, 0:128]), reads=[p.b], writes=[MX.b])
                        tk.op("dve", lambda: nc.vector.tensor_copy(MT[:], pt[:, 0:128]), reads=[pt.b], writes=[MT.b])
                        d["MX"], d["MT"], d["par"] = MX, MT, 0
                        yield
                    for j in range(2, 6):
                        for c in chunks:
                            d = st[c]
                            MX, MT = d["MX"], d["MT"]
                            par = 1 - d["par"]
                            MX2, MT2 = MXl[c % 4][par], MTl[c % 4][par]
                            pm = self.psum()
                            mm(pm[:, 0:128], pm.b, MT[:], MT.b, MX[:, 0:128], MX.b)
                            px = self.psum()
                            mm(px[:, 0:128], px.b, MT[:], MT.b, MX[:, 128:256], MX.b)
                            pt = self.psum()
                            mm(pt[:, 0:128], pt.b, MX[:, 0:128], MX.b, MT[:], MT.b)
                            tk.op("act", lambda: nc.scalar.copy(MX2[:, 0:128], pm[:, 0:128]), reads=[pm.b], writes=[MX2.b])
                            tk.op("dve", lambda: nc.vector.tensor_tensor(out=MX2[:, 128:256], in0=px[:, 0:128], in1=MX[:, 128:256], op=ALU.add),
                                  reads=[px.b, MX.b], writes=[MX2.b])
                            tk.op("act", lambda: nc.scalar.copy(MT2[:], pt[:, 0:128]), reads=[pt.b], writes=[MT2.b])
                            d["MX"], d["MT"], d["par"] = MX2, MT2, par
                            yield
                    for c in chunks:
                        d = st[c]
                        MX, MT = d["MX"], d["MT"]
                        p = self.psum()
                        mm(p[:, 0:128], p.b, MT[:], MT.b, MX[:, 128:256], MX.b)
                        TT = TTl[c % 4]
                        tk.op("dve", lambda: nc.vector.tensor_tensor(out=TT[:], in0=p[:, 0:128], in1=MX[:, 128:256], op=ALU.add), reads=[p.b, MX.b], writes=[TT.b])
                        d["TT"] = TT
                        yield

                Sh = [S]

                def chain_gen(chunks):
                    for c in chunks:
                        d = st[c]
                        bd, GBK, TM3, TT, cs = d["bd"], d["GBK"], d["TM3"], d["TT"], d["cs"]
                        BD5 = d["BD5"]
                        S = Sh[0]
                        PCc = ein[:, c * 64 + 63:c * 64 + 64]
                        SP = SPr.next()
                        tk.op("act", lambda: nc.scalar.activation(out=SP[:], in_=S[:], func=AF.Identity, scale=PCc), reads=[S.b, ein.b], writes=[SP.b])
                        p = self.psum()
                        mm(p[:, 0:128], p.b, bd(0), BD5.b, S[:], S.b, True, False)
                        mm(p[:, 0:128], p.b, GBK[:, 256:384], GBK.b, TM3[:, 256:384], TM3.b, False, True)
                        RH = RHr.next()
                        tk.op("act", lambda: nc.scalar.copy(RH[:], p[:, 0:128]), reads=[p.b], writes=[RH.b])
                        yield
                        p = self.psum()
                        mm(p[:, 0:128], p.b, TT[:], TT.b, RH[:], RH.b)
                        U = Ur.next()
                        tk.op("dve", lambda: nc.vector.tensor_copy(U[:], p[:, 0:128]), reads=[p.b], writes=[U.b])
                        yield
                        pS = self.psum()
                        mm(pS[:, 0:128], pS.b, TM3[:, 0:128], TM3.b, U[:], U.b, True, False)
                        mm(pS[:, 0:128], pS.b, TM3[:, 128:256], TM3.b, TM3[:, 256:384], TM3.b, False, True)
                        S2 = Sr.next()
                        tk.op("dve", lambda: nc.vector.scalar_tensor_tensor(out=S2[:], in0=pS[:, 0:128], scalar=PCc, in1=SP[:], op0=ALU.mult, op1=ALU.add),
                              reads=[pS.b, ein.b, SP.b], writes=[S2.b])
                        p = self.psum()
                        mm(p[:, 0:128], p.b, S[:], S.b, bd(1), BD5.b, True, False)
                        mm(p[:, 0:128], p.b, U[:], U.b, GBK[:, 128:256], GBK.b, False, False)
                        mm(p[:, 0:128], p.b, TM3[:, 256:384], TM3.b, GBK[:, 384:512], GBK.b, False, True)
                        tk.op("act", lambda: nc.scalar.copy(yfm[0:64, cs], p[0:64, 0:64]), reads=[p.b], writes=[yfm.b])
                        tk.op("act", lambda: nc.scalar.copy(yfm[64:128, cs], p[64:128, 64:128]), reads=[p.b], writes=[yfm.b])
                        Sh[0] = S2
                        yield

                nbt = TB // 64 // NBC
                batches = [list(range(k * NBC, (k + 1) * NBC)) for k in range(nbt)]
                for _ in batch_gen(batches[0]):
                    pass
                for k in range(nbt):
                    cg = chain_gen(batches[k])
                    bg = batch_gen(batches[k + 1]) if k + 1 < nbt else iter(())
                    done_b = done_c = False
                    while not (done_b and done_c):
                        for _ in range(3):
                            if not done_b:
                                try:
                                    next(bg)
                                except StopIteration:
                                    done_b = True
                        if not done_c:
                            try:
                                next(cg)
                            except StopIteration:
                                done_c = True
                S = Sh[0]
                if LIM <= 5:
                    return
                p = self.psum()
                mm(p[:], p.b, bdones[:], bdones.b, yfm[:], yfm.b)
                tk.op("dve", lambda: nc.vector.scalar_tensor_tensor(out=dd[:], in0=p[:], scalar=-1.0 / 64, in1=yfm[:], op0=ALU.mult, op1=ALU.add),
                      reads=[p.b, yfm.b], writes=[dd.b])
                tk.op("act", lambda: nc.scalar.activation(out=sq[:], in_=dd[:], func=AF.Square), reads=[dd.b], writes=[sq.b])
                p = self.psum()
                mm(p[:], p.b, bdones[:], bdones.b, sq[:], sq.b)
                tk.op("dve", lambda: nc.vector.tensor_scalar(tmp[:], p[:], 1.0 / 64, 64e-5, ALU.mult, ALU.add), reads=[p.b], writes=[tmp.b])
                self.rpow(tmp[:], tmp.b, -0.5)
                tk.op("dve", lambda: nc.vector.tensor_tensor(out=dd[:], in0=dd[:], in1=tmp[:], op=ALU.mult), reads=[dd.b, tmp.b], writes=[dd.b])
                tk.op("act", lambda: nc.scalar.activation(out=dd[:], in_=dd[:], func=AF.Identity, bias=pc["rwkv_lnx_b"][:, hp:hp + 1],
                                                          scale=pc["rwkv_lnx_w"][:, hp:hp + 1]),
                      reads=[dd.b, pc["rwkv_lnx_b"].b, pc["rwkv_lnx_w"].b], writes=[dd.b])
                tk.op("pool", lambda: nc.gpsimd.tensor_tensor(out=dd[:], in0=dd[:], in1=bonus[:], op=ALU.add), reads=[dd.b, bonus.b], writes=[dd.b])
                o = ob.next()
                tk.op("dve", lambda: nc.vector.tensor_tensor(out=o[:], in0=dd[:], in1=gg[:], op=ALU.mult), reads=[dd.b, gg.b], writes=[o.b])
                tk.dma("pool", self.din["yr_fm"].ap()[c0:c0 + 128, t0:t0 + TB], o[:], reads=[o.b], writes=[self.dbuf["yr_fm"]])
                if LIM <= 6:
                    return


Prog.phase_rwkv = _rwkv


def _nsa_bias(self):
    nc, tk = self.nc, self.tk
    self.scratch("bvec_c", [16, LVEC], BF16)
    self.scratch("bvec_w", [16, LVEC], BF16)
    with self.scope() as es:
        tab = self.sb(es, "nb_tab", [33, 16], F32)
        tk.op("pool", lambda: nc.gpsimd.memset(tab[:], 1.0), writes=[tab.b])
        tk.dma("sp", tab[0:32, :], self.din["rel_bias"].ap()[:, :], reads=[self.dbuf["rel_bias"]], writes=[tab.b])
        e33 = self.sb(es, "nb_e33", [33, LVEC], F32)
        ob = self.ring(es, "nb_o", 2, [16, 512], BF16)
        for cname, dname in (("c_e33c", "bvec_c"), ("c_e33w", "bvec_w")):
            tk.dma("sp", e33[:], self.din[cname].ap()[:, :], reads=[self.dbuf[cname]], writes=[e33.b])
            for j in range(LVEC // 512):
                p = self.psum()
                tk.op("pe", lambda: nc.tensor.matmul(p[:16, :], lhsT=tab[:], rhs=e33[:, j * 512:(j + 1) * 512], start=True, stop=True),
                      reads=[tab.b, e33.b], writes=[p.b])
                o = ob.next()
                tk.op("act", lambda: nc.scalar.copy(o[:], p[:16, :]), reads=[p.b], writes=[o.b])
                tk.dma("pool", self.din[dname].ap()[:, j * 512:(j + 1) * 512], o[:], reads=[o.b], writes=[self.dbuf[dname]])


def _nsa(self):
    nc, tk = self.nc, self.tk
    self.scratch("yn_fm", [1024, T], BF16)
    ngen = getattr(self, "ns_ngen", 4)
    gen = Ring(self.psr.tiles[0:ngen])
    accp = Ring(self.psr.tiles[ngen:8])

    def mm(p_ap, pbuf, lhsT, lb, rhs, rb, start=True, stop=True):
        tk.op("pe", lambda: nc.tensor.matmul(p_ap, lhsT=lhsT, rhs=rhs, start=start, stop=stop), reads=[lb, rb], writes=[pbuf])

    with self.scope() as es:
        Jb = self.load_const(es, "c_J", BF16, tmp_es=es)
        c2s_f = self.sb(es, "ns_c2sf", [128, 2, 64], F32)
        tk.dma("sp", c2s_f[:, :, :], self.dap("c_c2s", 0, [[64, 128], [128 * 64, 2], [1, 64]]), reads=[self.dbuf["c_c2s"]], writes=[c2s_f.b])
        c2s = self.sb(es, "ns_c2s", [128, 2, 64], BF16)
        tk.op("dve", lambda: nc.vector.tensor_copy(c2s[:], c2s_f[:]), reads=[c2s_f.b], writes=[c2s.b])
        exb = self.sb(es, "ns_exb", [64, 4096], BF16)
        with self.scope() as es0:
            exf = self.sb(es0, "ns_exf", [64, 4096], F32)
            tk.dma("sp", exf[:], self.din["c_expand"].ap()[:, :], reads=[self.dbuf["c_expand"]], writes=[exf.b])
            tk.op("dve", lambda: nc.vector.tensor_scalar_mul(exb[:], exf[:], BIG), reads=[exf.b], writes=[exb.b])
        ones = self.sb(es, "ns_ones", [128, 64], BF16)
        tk.op("pool", lambda: nc.gpsimd.memset(ones[:], 1.0), writes=[ones.b])
        kgain = self.col_vec(es, "nsa_k_gain", 0, 0, 1, "ns_kg", p=64)
        ident, bdones = self.ident, self.bdones
        kcmpT = [self.sb(es, f"ns_kcT{g}", [64, 256], BF16) for g in range(4)]
        vcmp = [self.sb(es, f"ns_vc{g}", [128, 2, 64], BF16) for g in range(4)]
        with self.scope() as es2:
            kc2 = self.sb(es2, "ns_kc2", [128, T], BF16)
            w1t = self.sb(es2, "ns_w1", [128, 16, 256], BF16)
            w2t = self.sb(es2, "ns_w2", [128, 2, 64], BF16)
            hg_ = self.sb(es2, "ns_hg", [128, 2, 256], BF16)
            xx = self.sb(es2, "ns_x", [128, 256], F32)
            x2 = self.sb(es2, "ns_x2", [128, 256], F32)
            pvb = self.sb(es2, "ns_pvb", [128, 2], F32)
            t64 = self.sb(es2, "ns_t64", [64, 256], F32)
            t64b = self.sb(es2, "ns_t64b", [64, 256], F32)
            for kind in range(2):
                sfx = "_k" if kind == 0 else "_v"
                self.load_w(w1t, "cmp_w1" + sfx + "_bf", 0, 256, kchunks=16)
                self.load_w(w2t, "cmp_w2" + sfx + "_bf", 0, 64, kchunks=2)
                pe_f = self.col_vec(es2, "cmp_pe" + sfx, 0, 0, 16, "ns_pe" + sfx)
                pe_b = self.sb(es2, "ns_peb" + sfx, [128, 16], BF16)
                tk.op("dve", lambda: nc.vector.tensor_copy(pe_b[:], pe_f[:]), reads=[pe_f.b], writes=[pe_b.b])
                for ct in range(2):
                    p = gen.next()
                    for l2 in range(16):
                        mm(p[:, 0:1], p.b, w1t[:, l2, ct * 128:(ct + 1) * 128], w1t.b, pe_b[:, l2:l2 + 1], pe_b.b, l2 == 0, l2 == 15)
                    tk.op("dve", lambda: nc.vector.tensor_copy(pvb[:, ct:ct + 1], p[:, 0:1]), reads=[p.b], writes=[pvb.b])
                for g in range(4):
                    r0 = 256 * kind + 64 * g
                    tk.op("pool", lambda: nc.gpsimd.memset(kc2[64:128, T - 1:T], 0.0), writes=[kc2.b])
                    tk.dma("sp", kc2[0:64, :], self.din["kcvc_fm"].ap()[r0:r0 + 64, :], reads=[self.dbuf["kcvc_fm"]], writes=[kc2.b])
                    tk.dma("sp", kc2[64:128, 0:T - 1], self.din["kcvc_fm"].ap()[r0:r0 + 64, 1:T], reads=[self.dbuf["kcvc_fm"]], writes=[kc2.b])
                    tk.op("pool", lambda: nc.gpsimd.memset(hg_[:], 0.0), writes=[hg_.b])
                    for ct in range(2):
                        p = gen.next()
                        for l2 in range(16):
                            rhs = kc2[:, 2 * l2: 2 * l2 + 16 * 254 + 1: 16]
                            mm(p[:, 0:255], p.b, w1t[:, l2, ct * 128:(ct + 1) * 128], w1t.b, rhs, kc2.b, l2 == 0, l2 == 15)
                        tk.op("act", lambda: nc.scalar.activation(out=xx[:, 0:255], in_=p[:, 0:255], func=AF.Identity, bias=pvb[:, ct:ct + 1]),
                              reads=[p.b, pvb.b], writes=[xx.b])
                        tk.op("act", lambda: nc.scalar.activation(out=x2[:, 0:255], in_=xx[:, 0:255], func=AF.Square), reads=[xx.b], writes=[x2.b])
                        tk.op("dve", lambda: nc.vector.tensor_scalar(x2[:, 0:255], x2[:, 0:255], 0.044715, 1.0, ALU.mult, ALU.add), reads=[x2.b], writes=[x2.b])
                        tk.op("dve", lambda: nc.vector.tensor_tensor(out=x2[:, 0:255], in0=x2[:, 0:255], in1=xx[:, 0:255], op=ALU.mult), reads=[x2.b, xx.b], writes=[x2.b])
                        tk.op("act", lambda: nc.scalar.activation(out=x2[:, 0:255], in_=x2[:, 0:255], func=AF.Sigmoid, scale=1.5957691216057308),
                              reads=[x2.b], writes=[x2.b])
                        tk.op("dve", lambda: nc.vector.tensor_tensor(out=hg_[:, ct, 0:255], in0=x2[:, 0:255], in1=xx[:, 0:255], op=ALU.mult),
                              reads=[x2.b, xx.b], writes=[hg_.b])
                    if kind == 0:
                        p = gen.next()
                        for ct in range(2):
                            mm(p[0:64, 0:256], p.b, w2t[:, ct, :], w2t.b, hg_[:, ct, :], hg_.b, ct == 0, ct == 1)
                        tk.op("act", lambda: nc.scalar.activation(out=t64[:], in_=p[0:64, 0:256], func=AF.Square), reads=[p.b], writes=[t64.b])
                        p2 = gen.next()
                        mm(p2[0:64, 0:256], p2.b, bdones[0:64, 0:64], bdones.b, t64[:], t64.b)
                        tk.op("dve", lambda: nc.vector.tensor_scalar(t64[:], p2[0:64, 0:256], 1.0 / 64, 1e-6, ALU.mult, ALU.add), reads=[p2.b], writes=[t64.b])
                        self.rpow(t64[:], t64.b, -0.5)
                        tk.op("dve", lambda: nc.vector.tensor_tensor(out=t64b[:], in0=p[0:64, 0:256], in1=t64[:], op=ALU.mult), reads=[p.b, t64.b], writes=[t64b.b])
                        tk.op("dve", lambda: nc.vector.tensor_scalar_mul(kcmpT[g][:], t64b[:], kgain[:, 0:1]), reads=[t64b.b, kgain.b], writes=[kcmpT[g].b])
                    else:
                        for nt in range(2):
                            p = gen.next()
                            for ct in range(2):
                                mm(p[:, 0:64], p.b, hg_[:, ct, nt * 128:(nt + 1) * 128], hg_.b, w2t[:, ct, :], w2t.b, ct == 0, ct == 1)
                            tk.op("act", lambda: nc.scalar.copy(vcmp[g][:, nt, :], p[:, 0:64]), reads=[p.b], writes=[vcmp[g].b])
        if getattr(self, "ns_lim", 99) <= 1:
            return
        ksT = self.sb(es, "ns_ksT", [64, T], BF16)
        kwT = self.sb(es, "ns_kwT", [64, T], BF16)
        Vs = self.sb(es, "ns_Vs", [128, 32, 128], BF16)
        Vw = self.sb(es, "ns_Vw", [128, 32, 128], BF16)
        tk.op("pool", lambda: nc.gpsimd.memset(Vs[:], 1.0), writes=[Vs.b])
        tk.op("pool", lambda: nc.gpsimd.memset(Vw[:], 1.0), writes=[Vw.b])
        vco = [self.sb(es, f"ns_vco{g}", [128, 2, 128], BF16) for g in range(4)]
        for g in range(4):
            tk.op("pool", lambda: nc.gpsimd.memset(vco[g][:], 1.0), writes=[vco[g].b])
            tk.op("dve", lambda: nc.vector.tensor_copy(vco[g][:, :, 0:64], vcmp[g][:]), reads=[vcmp[g].b], writes=[vco[g].b])
        bfar = self.sb(es, "ns_bfar", [128, 16], F32)
        tk.dma("sp", bfar[:], self.din["rel_bias"].ap()[31:32, :].partition_broadcast(128), reads=[self.dbuf["rel_bias"]], writes=[bfar.b])
        qTr = self.ring(es, "ns_qT", 2, [64, 4, 512], BF16)
        gbr = self.ring(es, "ns_gb", 3, [64, 4, 512], F32)
        Hr = self.ring(es, "ns_H", 4, [128, 512], BF16)
        Er = self.ring(es, "ns_E", 4, [128, 512], BF16)
        E2r = self.ring(es, "ns_E2", 4, [128, 512], BF16)
        Ec = self.sb(es, "ns_Ec", [128, 8, 512], BF16, dj=True)
        accC = self.sb(es, "ns_accC", [64, 4, 8, 512], BF16, dj=True)
        selTa = self.sb(es, "ns_selTa", [64, 8, 512], BF16, dj=True)
        acc = self.ring(es, "ns_acc", 2, [64, 512], F32)
        impa = self.sb(es, "ns_impa", [64, 512], F32)
        frc = self.ring(es, "ns_frc", 2, [64, 512], F32)
        rdr = self.ring(es, "ns_rd", 3, [64, 512], F32)
        t1r = self.ring(es, "ns_t1", 3, [64, 512], F32)
        impq = self.sb(es, "ns_impq", [128, 4, 64], F32)
        selq = self.sb(es, "ns_selq", [128, 4, 64], F32)
        wk = self.sb(es, "ns_wk", [128, 64], F32)
        m8 = self.sb(es, "ns_m8", [128, 16], F32)
        obr = self.ring(es, "ns_ob", 2, [64, 512], BF16)
        qh = self.ring(es, "ns_qh", 2, [64, T], BF16)
        XBr = [[self.sb(es, f"ns_xb{k}_{i}", [128, 512], BF16) for i in range(13)] for k in range(1)]
        pend = []
        eng_alt = [0]

        def flush():
            while pend:
                pend.pop(0)()

        def hankel(vname, h, c, pstep):
            H = Hr.next()
            src = self.dap(vname, h * LVEC + c, [[pstep, 128], [1, 512]])
            tk.dma("sp", H[:], src, reads=[self.dbuf[vname]], writes=[H.b])
            return H

        def ratio(pn):
            rd = rdr.next()
            tk.op("dve", lambda: nc.vector.tensor_scalar_max(rd[:], pn[64:128, :], 1e-30), reads=[pn.b], writes=[rd.b])
            self.rpow(rd[:], rd.b, -1.0)
            t1 = t1r.next()
            tk.op("dve", lambda: nc.vector.tensor_tensor(out=t1[:], in0=pn[0:64, :], in1=rd[:], op=ALU.mult), reads=[pn.b, rd.b], writes=[t1.b])
            return rd, t1

        def key_tile(s_mms, e_ap, e_buf, act_bias, pv, mult=None):
            p = gen.next()
            for i, (lhsT, lb, rhs, rb) in enumerate(s_mms):
                mm(p[:], p.b, lhsT, lb, rhs, rb, i == 0, i == len(s_mms) - 1)
            if mult is None:
                if act_bias is None:
                    tk.op("act", lambda: nc.scalar.activation(out=e_ap, in_=p[:], func=AF.Exp), reads=[p.b], writes=[e_buf])
                else:
                    tk.op("act", lambda: nc.scalar.activation(out=e_ap, in_=p[:], func=AF.Exp, bias=act_bias), reads=[p.b, bfar.b], writes=[e_buf])
            else:
                E0 = E2r.next()
                tk.op("act", lambda: nc.scalar.activation(out=E0[:], in_=p[:], func=AF.Exp), reads=[p.b], writes=[E0.b])
                eng_alt[0] += 1
                if False:
                    tk.op("pool", lambda: nc.gpsimd.tensor_tensor(out=e_ap, in0=E0[:], in1=mult[:], op=ALU.mult), reads=[E0.b, mult.b], writes=[e_buf])
                else:
                    tk.op("dve", lambda: nc.vector.tensor_tensor(out=e_ap, in0=E0[:], in1=mult[:], op=ALU.mult), reads=[E0.b, mult.b], writes=[e_buf])
            while len(pend) >= SKEW:
                pend.pop(0)()
            pend.append(pv)

        SKEW = getattr(self, "ns_skew", 2)
        for g in range(4):
            flush()
            tk.dma("sp", ksT[:], self.din["ks_fm"].ap()[64 * g:64 * g + 64, :], reads=[self.dbuf["ks_fm"]], writes=[ksT.b])
            tk.dma("sp", kwT[:], self.din["kw_fm"].ap()[64 * g:64 * g + 64, :], reads=[self.dbuf["kw_fm"]], writes=[kwT.b])
            for k8 in range(4):
                tk.dma("sp", Vs[:, 8 * k8:8 * k8 + 8, 0:64], self.dap("vsw_tm", 64 * g + 8 * k8 * 128 * 512, [[512, 128], [128 * 512, 8], [1, 64]]),
                       reads=[self.dbuf["vsw_tm"]], writes=[Vs.b])
                tk.dma("sp", Vw[:, 8 * k8:8 * k8 + 8, 0:64], self.dap("vsw_tm", 256 + 64 * g + 8 * k8 * 128 * 512, [[512, 128], [128 * 512, 8], [1, 64]]),
                       reads=[self.dbuf["vsw_tm"]], writes=[Vw.b])
            for qt in range(T // 512):
                t0 = qt * 512
                qT = qTr.next()
                tk.dma("sp", qT[:], self.dap("q_fm", 256 * g * T + t0, [[T, 64], [64 * T, 4], [1, 512]]), reads=[self.dbuf["q_fm"]], writes=[qT.b])
                gb = gbr.next()
                for j in range(4):
                    row = 12 * g + 3 * j
                    tk.dma("sp", gb[:, j, :], self.din["gates_fm"].ap()[row:row + 1, t0:t0 + 512].partition_broadcast(64),
                           reads=[self.dbuf["gates_fm"]], writes=[gb.b])
                fr = frc.next()
                tk.dma("sp", fr[:], self.din["c_forced"].ap()[:, t0:t0 + 512], reads=[self.dbuf["c_forced"]], writes=[fr.b])
                nnt = 2 if t0 >= 2048 else 1
                for hg in range(4):
                    h = 4 * g + hg
                    pn, pi = accp.next(), accp.next()
                    for nt in range(nnt):
                        H = hankel("bvec_c", h, OFFC + t0 - 16 * 128 * nt - 2063, 16)
                        e_ap = Ec[:, hg * 2 + nt, :]

                        def pv(pn=pn, pi=pi, nt=nt, e_ap=e_ap, nnt=nnt):
                            mm(pn[:], pn.b, vco[g][:, nt, :], vco[g].b, e_ap, Ec.b, nt == 0, nt == nnt - 1)
                            mm(pi[0:64, :], pi.b, c2s[:, nt, :], c2s.b, e_ap, Ec.b, nt == 0, nt == nnt - 1)
                        key_tile([(kcmpT[g][:, nt * 128:(nt + 1) * 128], kcmpT[g].b, qT[:, hg, :], qT.b), (Jb[:], Jb.b, H[:], H.b)], e_ap, Ec.b, None, pv)

                    def fin(pn=pn, pi=pi, hg=hg, gb=gb, qt=qt):
                        rd, t1 = ratio(pn)
                        tk.op("pool", lambda: nc.gpsimd.tensor_tensor(out=accC[:, hg, qt, :], in0=t1[:], in1=gb[:, hg, :], op=ALU.mult),
                              reads=[t1.b, gb.b], writes=[accC.b])
                        if hg == 0:
                            tk.op("dve", lambda: nc.vector.tensor_tensor(out=impa[:], in0=pi[0:64, :], in1=rd[:], op=ALU.mult), reads=[pi.b, rd.b], writes=[impa.b])
                        else:
                            t2 = t1r.next()
                            tk.op("dve", lambda: nc.vector.tensor_tensor(out=t2[:], in0=pi[0:64, :], in1=rd[:], op=ALU.mult), reads=[pi.b, rd.b], writes=[t2.b])
                            tk.op("pool", lambda: nc.gpsimd.tensor_tensor(out=impa[:], in0=impa[:], in1=t2[:], op=ALU.add), reads=[impa.b, t2.b], writes=[impa.b])
                    pend.append(fin)
                flush()
                tk.op("dve", lambda: nc.vector.tensor_tensor(out=impa[:], in0=impa[:], in1=fr[:], op=ALU.max), reads=[impa.b, fr.b], writes=[impa.b])
                p = gen.next()
                for s4 in range(4):
                    tk.op("pe", lambda: nc.tensor.transpose(p[:, s4 * 64:(s4 + 1) * 64], impa[:, s4 * 128:(s4 + 1) * 128], ident[0:64, 0:64]),
                          reads=[impa.b, ident.b], writes=[p.b])
                tk.op("act", lambda: nc.scalar.copy(impq[:], p[:, 0:256].rearrange("p (a b) -> p a b", a=4)), reads=[p.b], writes=[impq.b])
                for s4 in range(4):
                    tk.op("dve", lambda: nc.vector.max(out=m8[:, 0:8], in_=impq[:, s4, :]), reads=[impq.b], writes=[m8.b])
                    tk.op("dve", lambda: nc.vector.match_replace(out=wk[:], in_to_replace=m8[:, 0:8], in_values=impq[:, s4, :], imm_value=-1e30),
                          reads=[impq.b, m8.b], writes=[wk.b])
                    tk.op("dve", lambda: nc.vector.max(out=m8[:, 8:16], in_=wk[:]), reads=[wk.b], writes=[m8.b])
                    tk.op("dve", lambda: nc.vector.tensor_scalar(selq[:, s4, :], impq[:, s4, :], m8[:, 15:16], 1.0, ALU.is_ge, ALU.subtract),
                          reads=[impq.b, m8.b], writes=[selq.b])
                p = gen.next()
                for s4 in range(4):
                    tk.op("pe", lambda: nc.tensor.transpose(p[0:64, s4 * 128:(s4 + 1) * 128], selq[:, s4, :], ident[:]),
                          reads=[selq.b, ident.b], writes=[p.b])
                tk.op("act", lambda: nc.scalar.copy(selTa[:, qt, :], p[0:64, :]), reads=[p.b], writes=[selTa.b])
            for hg in range(4):
                h = 4 * g + hg
                flush()
                XB = XBr[0]
                xw, xs = {}, {}
                for i, (vname, d) in enumerate([("bvec_w", dd_) for dd_ in range(512, -385, -128)] + [("bvec_c", dd_) for dd_ in range(128, -385, -128)]):
                    H = hankel(vname, h, OFFC + d - 127, 1)
                    p = gen.next()
                    mm(p[:], p.b, Jb[:], Jb.b, H[:], H.b)
                    tk.op("act", lambda: nc.scalar.activation(out=XB[i][:], in_=p[:], func=AF.Exp), reads=[p.b], writes=[XB[i].b])
                    (xw if vname == "bvec_w" else xs)[d] = XB[i]
                q_ = qh.next()
                tk.dma("sp", q_[:], self.din["q_fm"].ap()[64 * h:64 * h + 64, :], reads=[self.dbuf["q_fm"]], writes=[q_.b])
                for qt in range(T // 512):
                    t0 = qt * 512
                    qs = q_[:, t0:t0 + 512]
                    gb = gbr.next()
                    for j in range(2):
                        row = 3 * h + 1 + j
                        tk.dma("sp", gb[:, j, :], self.din["gates_fm"].ap()[row:row + 1, t0:t0 + 512].partition_broadcast(64),
                               reads=[self.dbuf["gates_fm"]], writes=[gb.b])
                    ac = acc.next()
                    kts = list(range(max(0, (t0 - 512) // 128), (t0 + 511) // 128 + 1))
                    pn = accp.next()
                    for i, kt in enumerate(kts):
                        E = Er.next()

                        def pv(pn=pn, kt=kt, E=E, first=(i == 0), last=(i == len(kts) - 1)):
                            mm(pn[:], pn.b, Vw[:, kt, :], Vw.b, E[:], E.b, first, last)
                        key_tile([(kwT[:, kt * 128:(kt + 1) * 128], kwT.b, qs, q_.b)], E[:], E.b, None, pv, mult=xw[t0 - 128 * kt])

                    def finw(pn=pn, gb=gb, ac=ac):
                        rd, t1 = ratio(pn)
                        tk.op("dve", lambda: nc.vector.tensor_tensor(out=ac[:], in0=t1[:], in1=gb[:, 1, :], op=ALU.mult), reads=[t1.b, gb.b], writes=[ac.b])
                    pend.append(finw)
                    kts = list(range(0, (t0 + 511) // 128 + 1))
                    pn = accp.next()
                    for i, kt in enumerate(kts):
                        far = (128 * kt <= t0 - 256)
                        E = Er.next()
                        s_mms = [(ksT[:, kt * 128:(kt + 1) * 128], ksT.b, qs, q_.b), (exb[:, kt * 128:(kt + 1) * 128], exb.b, selTa[:, qt, :], selTa.b)]

                        def pv(pn=pn, kt=kt, E=E, first=(i == 0), last=(i == len(kts) - 1)):
                            mm(pn[:], pn.b, Vs[:, kt, :], Vs.b, E[:], E.b, first, last)
                        key_tile(s_mms, E[:], E.b, bfar[:, h:h + 1] if far else None, pv, mult=None if far else xs[t0 - 128 * kt])

                    def fins(pn=pn, gb=gb, ac=ac, hg=hg, h=h, t0=t0, qt=qt):
                        rd, t1 = ratio(pn)
                        tk.op("dve", lambda: nc.vector.tensor_tensor(out=t1[:], in0=t1[:], in1=gb[:, 0, :], op=ALU.mult), reads=[t1.b, gb.b], writes=[t1.b])
                        tk.op("dve", lambda: nc.vector.tensor_tensor(out=ac[:], in0=ac[:], in1=t1[:], op=ALU.add), reads=[t1.b, ac.b], writes=[ac.b])
                        o = obr.next()
                        tk.op("dve", lambda: nc.vector.tensor_tensor(out=o[:], in0=ac[:], in1=accC[:, hg, qt, :], op=ALU.add), reads=[ac.b, accC.b], writes=[o.b])
                        tk.dma("pool", self.din["yn_fm"].ap()[64 * h:64 * h + 64, t0:t0 + 512], o[:], reads=[o.b], writes=[self.dbuf["yn_fm"]])
                    pend.append(fins)
        flush()


Prog.nsa_bias = _nsa_bias
Prog.phase_nsa = _nsa


def _proj_tm_res(self, actT, tok0, ntok, kchunks, w, res_name, res_row0, dst_name, dst_row0, es):
    nc, tk = self.nc, self.tk
    xr = self.ring(es, "pt_x", 2, [128, D], F32)
    orr = self.ring(es, "pt_o", 2, [128, D], F32)
    for i in range(ntok // 128):
        x = xr.next()
        o = orr.next()
        tk.dma("sp", x[:], self.din[res_name].ap()[res_row0 + i * 128:res_row0 + (i + 1) * 128, :], reads=[self.dbuf[res_name]], writes=[x.b])
        for half in range(2):
            p = self.psum()
            for kc in range(kchunks):
                tk.op("pe", lambda: nc.tensor.matmul(p[:], lhsT=actT[:, kc, tok0 + i * 128:tok0 + (i + 1) * 128], rhs=w[:, kc, half * 512:(half + 1) * 512],
                                                      start=(kc == 0), stop=(kc == kchunks - 1)), reads=[actT.b, w.b], writes=[p.b])
            tk.op("dve", lambda: nc.vector.tensor_tensor(out=o[:, half * 512:(half + 1) * 512], in0=p[:], in1=x[:, half * 512:(half + 1) * 512], op=ALU.add),
                  reads=[p.b, x.b], writes=[o.b])
        tk.dma("pool", self.din[dst_name].ap()[dst_row0 + i * 128:dst_row0 + (i + 1) * 128, :], o[:], reads=[o.b], writes=[self.dbuf[dst_name]])


def _merge(self, bi):
    nc, tk = self.nc, self.tk
    self.scratch("h1", [T, D], F32)
    with self.scope() as es:
        mT = self.sb(es, "mg_mT", [128, 8, T], BF16, dj=True)
        with self.scope() as es2:
            wr = self.sb(es2, "mg_wr", [128, 8, 1024], BF16)
            wn = self.sb(es2, "mg_wn", [128, 8, 1024], BF16)
            self.load_w(wr, "w_branch_rwkv_bf", 0, 1024)
            self.load_w(wn, "w_branch_nsa_bf", 0, 1024)
            yr = self.ring(es2, "mg_yr", 2, [128, 8, 512], BF16)
            yn = self.ring(es2, "mg_yn", 2, [128, 8, 512], BF16)
            gr = self.ring(es2, "mg_g", 4, [128, 512], F32)
            tr = self.ring(es2, "mg_t", 4, [128, 512], F32)
            for tt in range(T // 512):
                a, b = yr.next(), yn.next()
                tk.dma("sp", a[:], self.dap("yr_fm", tt * 512, [[T, 128], [128 * T, 8], [1, 512]]), reads=[self.dbuf["yr_fm"]], writes=[a.b])
                tk.dma("sp", b[:], self.dap("yn_fm", tt * 512, [[T, 128], [128 * T, 8], [1, 512]]), reads=[self.dbuf["yn_fm"]], writes=[b.b])
                for ci in range(8):
                    g0, g1 = gr.next(), gr.next()
                    tk.dma("sp", g0[:], self.din["gm_fm"].ap()[ci * 128:(ci + 1) * 128, tt * 512:(tt + 1) * 512], reads=[self.dbuf["gm_fm"]], writes=[g0.b])
                    tk.dma("sp", g1[:], self.din["gm_fm"].ap()[1024 + ci * 128:1024 + (ci + 1) * 128, tt * 512:(tt + 1) * 512], reads=[self.dbuf["gm_fm"]], writes=[g1.b])
                    pr, pn = self.psum(), self.psum()
                    for kc in range(8):
                        tk.op("pe", lambda: nc.tensor.matmul(pr[:], lhsT=wr[:, kc, ci * 128:(ci + 1) * 128], rhs=a[:, kc, :], start=(kc == 0), stop=(kc == 7)),
                              reads=[wr.b, a.b], writes=[pr.b])
                    for kc in range(8):
                        tk.op("pe", lambda: nc.tensor.matmul(pn[:], lhsT=wn[:, kc, ci * 128:(ci + 1) * 128], rhs=b[:, kc, :], start=(kc == 0), stop=(kc == 7)),
                              reads=[wn.b, b.b], writes=[pn.b])
                    t0_, t1_ = tr.next(), tr.next()
                    tk.op("dve", lambda: nc.vector.tensor_tensor(out=t0_[:], in0=pr[:], in1=g0[:], op=ALU.mult), reads=[pr.b, g0.b], writes=[t0_.b])
                    tk.op("dve", lambda: nc.vector.tensor_tensor(out=t1_[:], in0=pn[:], in1=g1[:], op=ALU.mult), reads=[pn.b, g1.b], writes=[t1_.b])
                    tk.op("pool", lambda: nc.gpsimd.tensor_tensor(out=mT[:, ci, tt * 512:(tt + 1) * 512], in0=t0_[:], in1=t1_[:], op=ALU.add),
                          reads=[t0_.b, t1_.b], writes=[mT.b])
        with self.scope() as es3:
            wm = self.sb(es3, "mg_wm", [128, 8, 1024], BF16)
            self.load_w(wm, "w_mix_out_bf", 0, 1024)
            self.proj_tm_res(mT, 0, T, 8, wm, "x", bi * T, "h1", 0, es3)


def _cross(self, bi):
    nc, tk = self.nc, self.tk
    self.scratch("h2", [T, D], F32)
    HT = 2048
    with self.scope() as es:
        wq = self.sb(es, "ca_wq", [128, 8, 1024], BF16)
        wo = self.sb(es, "ca_wo", [128, 8, 1024], BF16)
        self.load_w(wq, "ca_wq_bf", 0, 1024)
        self.load_w(wo, "ca_wo_bf", 0, 1024)
        kT = self.sb(es, "ca_kT", [128, 8, NMEM], BF16, dj=True)
        Vc = self.sb(es, "ca_V", [128, 2, 1024], BF16, dj=True)
        qgain = self.col_vec(es, "ca_q_gain", 0, 0, 2, "ca_qg")
        kgain = self.col_vec(es, "ca_k_gain", 0, 0, 2, "ca_kg")
        ones_f = self.load_const(es, "c_ones")
        ones_b = self.sb(es, "ca_1b", [128, 128], BF16)
        tk.op("dve", lambda: nc.vector.tensor_copy(ones_b[:], ones_f[:]), reads=[ones_f.b], writes=[ones_b.b])
        sqr = self.ring(es, "ca_sq", 2, [128, 2, 512], F32)
        rr = self.ring(es, "ca_r", 4, [128, 512], F32)
        tmpr = self.ring(es, "ca_tmp", 2, [128, 512], F32)
        qh = self.ring(es, "ca_qh", 3, [128, 2, 512], BF16)
        Er = self.ring(es, "ca_E", 2, [128, 2, 512], BF16)

        def qk_norm(p0, p1, n, gain, scale, out_aps, out_buf):
            s = sqr.next()
            tk.op("act", lambda: nc.scalar.activation(out=s[:, 0, 0:n], in_=p0[:, 0:n], func=AF.Square), reads=[p0.b], writes=[s.b])
            tk.op("act", lambda: nc.scalar.activation(out=s[:, 1, 0:n], in_=p1[:, 0:n], func=AF.Square), reads=[p1.b], writes=[s.b])
            p2 = self.psum()
            for j in range(2):
                tk.op("pe", lambda: nc.tensor.matmul(p2[:, 0:n], lhsT=ones_f[:], rhs=s[:, j, 0:n], start=(j == 0), stop=(j == 1)),
                      reads=[ones_f.b, s.b], writes=[p2.b])
            r = rr.next()
            tk.op("dve", lambda: nc.vector.tensor_scalar(r[:, 0:n], p2[:, 0:n], 1.0 / 256, 1e-6, ALU.mult, ALU.add), reads=[p2.b], writes=[r.b])
            self.rpow(r[:, 0:n], r.b, -0.5)
            for j, pj in enumerate((p0, p1)):
                t = tmpr.next()
                tk.op("dve", lambda: nc.vector.tensor_tensor(out=t[:, 0:n], in0=pj[:, 0:n], in1=r[:, 0:n], op=ALU.mult), reads=[pj.b, r.b], writes=[t.b])
                tk.op("dve", lambda: nc.vector.tensor_scalar(out_aps[j], t[:, 0:n], gain[:, j:j + 1], scale, ALU.mult, ALU.mult),
                      reads=[t.b, gain.b], writes=[out_buf])

        with self.scope() as es2:
            mnT = self.sb(es2, "ca_mnT", [128, 8, NMEM], BF16, dj=True)
            self.norm_T("mem", bi * NMEM, NMEM, "norm_mem", mnT)
            wk = self.sb(es2, "ca_wk", [128, 8, 1024], BF16)
            wv = self.sb(es2, "ca_wv", [128, 8, 1024], BF16)
            self.load_w(wk, "ca_wkv_bf", 0, 1024)
            self.load_w(wv, "ca_wkv_bf", 1024, 1024)
            for h in range(4):
                ps_ = []
                for j in range(2):
                    p = self.psum()
                    ci = 2 * h + j
                    for kc in range(8):
                        tk.op("pe", lambda: nc.tensor.matmul(p[:, 0:NMEM], lhsT=wk[:, kc, ci * 128:(ci + 1) * 128], rhs=mnT[:, kc, :], start=(kc == 0), stop=(kc == 7)),
                              reads=[wk.b, mnT.b], writes=[p.b])
                    ps_.append(p)
                qk_norm(ps_[0], ps_[1], NMEM, kgain, 1.0, [kT[:, 2 * h, :], kT[:, 2 * h + 1, :]], kT.b)
            for mt in range(2):
                for half in range(2):
                    p = self.psum()
                    for kc in range(8):
                        tk.op("pe", lambda: nc.tensor.matmul(p[:], lhsT=mnT[:, kc, mt * 128:(mt + 1) * 128], rhs=wv[:, kc, half * 512:(half + 1) * 512],
                                                              start=(kc == 0), stop=(kc == 7)), reads=[mnT.b, wv.b], writes=[p.b])
                    tk.op("act", lambda: nc.scalar.copy(Vc[:, mt, half * 512:(half + 1) * 512], p[:]), reads=[p.b], writes=[Vc.b])
        for hf in range(T // HT):
            with self.scope() as es2:
                hnT = self.sb(es2, "ca_hnT", [128, 8, HT], BF16, dj=True)
                oT = self.sb(es2, "ca_oT", [128, 8, HT], BF16, dj=True)
                self.norm_T("h1", hf * HT, HT, "norm_cross", hnT)
                def stage_q(h, tt):
                    ps_ = []
                    for j in range(2):
                        p = self.psum()
                        ci = 2 * h + j
                        for kc in range(8):
                            tk.op("pe", lambda: nc.tensor.matmul(p[:], lhsT=wq[:, kc, ci * 128:(ci + 1) * 128], rhs=hnT[:, kc, tt * 512:(tt + 1) * 512],
                                                                  start=(kc == 0), stop=(kc == 7)), reads=[wq.b, hnT.b], writes=[p.b])
                        ps_.append(p)
                    q = qh.next()
                    qk_norm(ps_[0], ps_[1], 512, qgain, 1.0 / 16, [q[:, 0, :], q[:, 1, :]], q.b)
                    return q

                def stage_att(h, tt, q):
                    E = Er.next()
                    for mt in range(2):
                        p = self.psum()
                        for j in range(2):
                            tk.op("pe", lambda: nc.tensor.matmul(p[:], lhsT=kT[:, 2 * h + j, mt * 128:(mt + 1) * 128], rhs=q[:, j, :], start=(j == 0), stop=(j == 1)),
                                  reads=[kT.b, q.b], writes=[p.b])
                        tk.op("act", lambda: nc.scalar.activation(out=E[:, mt, :], in_=p[:], func=AF.Exp), reads=[p.b], writes=[E.b])
                    pd = self.psum()
                    for mt in range(2):
                        tk.op("pe", lambda: nc.tensor.matmul(pd[:], lhsT=ones_b[:], rhs=E[:, mt, :], start=(mt == 0), stop=(mt == 1)),
                              reads=[ones_b.b, E.b], writes=[pd.b])
                    r = rr.next()
                    tk.op("act", lambda: nc.scalar.activation(out=r[:], in_=pd[:], func=AF.Ln), reads=[pd.b], writes=[r.b])
                    tk.op("act", lambda: nc.scalar.activation(out=r[:], in_=r[:], func=AF.Exp, scale=-1.0), reads=[r.b], writes=[r.b])
                    for j in range(2):
                        pn = self.psum()
                        for mt in range(2):
                            tk.op("pe", lambda: nc.tensor.matmul(pn[:], lhsT=Vc[:, mt, h * 256 + j * 128:h * 256 + (j + 1) * 128], rhs=E[:, mt, :],
                                                                  start=(mt == 0), stop=(mt == 1)), reads=[Vc.b, E.b], writes=[pn.b])
                        tk.op("dve", lambda: nc.vector.tensor_tensor(out=oT[:, 2 * h + j, tt * 512:(tt + 1) * 512], in0=pn[:], in1=r[:], op=ALU.mult),
                              reads=[pn.b, r.b], writes=[oT.b])

                its = [(h, tt) for h in range(4) for tt in range(HT // 512)]
                prev = None
                for (h, tt) in its:
                    q = stage_q(h, tt)
                    if prev is not None:
                        stage_att(*prev)
                    prev = (h, tt, q)
                stage_att(*prev)
                self.proj_tm_res(oT, 0, HT, 8, wo, "h1", hf * HT, "h2", hf * HT, es2)


def _ffn(self, bi):
    nc, tk = self.nc, self.tk
    self.scratch("ff_fm", [DFF, T], BF16)
    NCT = DFF // 128
    with self.scope() as es:
        hnT = self.sb(es, "ff_hnT", [128, 8, T], BF16, dj=True)
        self.norm_T("h2", 0, T, "norm_ffn", hnT)
        cw = [self.col_vec(es, "ffn_conv", j, 0, NCT, f"ff_cw{j}") for j in range(3)]
        cb = self.col_vec(es, "ffn_conv_b", 0, 0, NCT, "ff_cb")
        wring = self.ring(es, "ff_w", 4, [128, 8, 128], BF16)
        atr = self.ring(es, "ff_a", 2, [128, T + 2], F32)
        btr = self.ring(es, "ff_b", 2, [128, T], F32)
        acc = self.sb(es, "ff_acc", [128, T], F32)
        ob = self.ring(es, "ff_ob", 2, [128, T], BF16)
        for at in atr.tiles:
            tk.op("pool", lambda: nc.gpsimd.memset(at[:, 0:2], 0.0), writes=[at.b])
        for ci in range(NCT):
            at, bt = atr.next(), btr.next()
            wa, wb = wring.next(), wring.next()
            self.load_w(wa, "ffn_up_bf", ci * 128, 128)
            self.load_w(wb, "ffn_up_bf", DFF + ci * 128, 128)
            for tt in range(T // 512):
                pa, pb = self.psum(), self.psum()
                for kc in range(8):
                    tk.op("pe", lambda: nc.tensor.matmul(pa[:], lhsT=wa[:, kc, :], rhs=hnT[:, kc, tt * 512:(tt + 1) * 512], start=(kc == 0), stop=(kc == 7)),
                          reads=[wa.b, hnT.b], writes=[pa.b])
                for kc in range(8):
                    tk.op("pe", lambda: nc.tensor.matmul(pb[:], lhsT=wb[:, kc, :], rhs=hnT[:, kc, tt * 512:(tt + 1) * 512], start=(kc == 0), stop=(kc == 7)),
                          reads=[wb.b, hnT.b], writes=[pb.b])
                tk.op("act", lambda: nc.scalar.copy(at[:, 2 + tt * 512:2 + (tt + 1) * 512], pa[:]), reads=[pa.b], writes=[at.b])
                tk.op("dve", lambda: nc.vector.tensor_copy(bt[:, tt * 512:(tt + 1) * 512], pb[:]), reads=[pb.b], writes=[bt.b])
            tk.op("dve", lambda: nc.vector.tensor_scalar(acc[:], at[:, 2:T + 2], cw[2][:, ci:ci + 1], cb[:, ci:ci + 1], ALU.mult, ALU.add),
                  reads=[at.b, cw[2].b, cb.b], writes=[acc.b])
            tk.op("dve", lambda: nc.vector.scalar_tensor_tensor(out=acc[:], in0=at[:, 1:T + 1], scalar=cw[1][:, ci:ci + 1], in1=acc[:], op0=ALU.mult, op1=ALU.add),
                  reads=[at.b, cw[1].b, acc.b], writes=[acc.b])
            tk.op("dve", lambda: nc.vector.scalar_tensor_tensor(out=acc[:], in0=at[:, 0:T], scalar=cw[0][:, ci:ci + 1], in1=acc[:], op0=ALU.mult, op1=ALU.add),
                  reads=[at.b, cw[0].b, acc.b], writes=[acc.b])
            tk.op("act", lambda: nc.scalar.activation(out=acc[:], in_=acc[:], func=AF.Silu), reads=[acc.b], writes=[acc.b])
            o = ob.next()
            tk.op("dve", lambda: nc.vector.tensor_tensor(out=o[:], in0=acc[:], in1=bt[:], op=ALU.mult), reads=[acc.b, bt.b], writes=[o.b])
            tk.dma("pool", self.din["ff_fm"].ap()[ci * 128:(ci + 1) * 128, :], o[:], reads=[o.b], writes=[self.dbuf["ff_fm"]])
    with self.scope() as es:
        wd = self.sb(es, "ff_wd", [128, NCT, 1024], BF16)
        self.load_w(wd, "ffn_down_bf", 0, 1024, kchunks=NCT)
        TBK = 1024
        for blk in range(T // TBK):
            with self.scope() as es2:
                fT = self.sb(es2, "ff_fT", [128, NCT, TBK], BF16)
                tk.dma("sp", fT[:], self.dap("ff_fm", blk * TBK, [[T, 128], [128 * T, NCT], [1, TBK]]), reads=[self.dbuf["ff_fm"]], writes=[fT.b])
                self.proj_tm_res(fT, 0, TBK, NCT, wd, "h2", blk * TBK, "out", bi * T + blk * TBK, es2)


Prog.proj_tm_res = _proj_tm_res
Prog.phase_merge = _merge
Prog.phase_cross = _cross
Prog.phase_ffn = _ffn
```

```python
import contextlib
import math
import numpy as np
import concourse.bass as bass
import concourse.mybir as mybir
from concourse.bass_utils import run_bass_kernel_spmd

F32 = mybir.dt.float32
BF16 = mybir.dt.bfloat16
AF = mybir.ActivationFunctionType
ALU = mybir.AluOpType
AX = mybir.AxisListType

NCORES = 8
NB = 2
T = 4096
D = 1024
NMEM = 256
DFF = 2816
IN_COLS = 8016
BIG = 30000.0
OFFC = 2176
LVEC = 7680
SCALE_NSA = 0.125


class Buf:
    __slots__ = ("w", "r", "name", "dj", "xr")

    def __init__(self, name="", dj=False):
        self.w = {}
        self.r = {}
        self.name = name
        self.dj = dj
        self.xr = False


class Tile:
    def __init__(self, t, name, dj=False):
        self.t = t
        self.b = Buf(name, dj)

    def __getitem__(self, k):
        return self.t[k]


class Ring:
    def __init__(self, tiles):
        self.tiles = tiles
        self.i = 0

    def next(self):
        t = self.tiles[self.i]
        self.i = (self.i + 1) % len(self.tiles)
        return t


class TK:
    EPOCH = 20000
    NDSEM = 10

    def __init__(self, nc, es):
        self.nc = nc
        self.es = es
        self.eng = {"pe": nc.tensor, "act": nc.scalar, "dve": nc.vector,
                    "pool": nc.gpsimd, "sp": nc.sync}
        self.cnt = {e: 0 for e in self.eng}
        self.esem = {e: [] for e in self.eng}
        self.seen = {e: {} for e in self.eng}
        self.dsem = {}
        self.dptr = {}
        self.nwait = 0
        self.fence = {}

    def _newsem(self, name):
        return self.es.enter_context(self.nc.semaphore(name))

    def _engsem(self, e, epoch):
        while len(self.esem[e]) <= epoch:
            self.esem[e].append(self._newsem(f"s_{e}_{len(self.esem[e])}"))
        return self.esem[e][epoch]

    def _wait(self, e, ts):
        sem, val, src = ts
        if src == "pe" and e == "pe":
            return
        k = id(sem)
        if self.seen[e].get(k, 0) >= val:
            return
        self.seen[e][k] = val
        self.eng[e].wait_ge(sem, val)
        self.nwait += 1

    def deps(self, e, reads, writes):
        for b in reads:
            for ts in b.w.values():
                self._wait(e, ts)
            if b.xr:
                for ts in b.r.values():
                    if ts[2] != e:
                        self._wait(e, ts)
        for b in writes:
            if not (b.dj and not b.r):
                for ts in b.w.values():
                    self._wait(e, ts)
            for ts in b.r.values():
                self._wait(e, ts)

    def mark(self, ts, reads, writes):
        k = id(ts[0])
        for b in reads:
            b.r[k] = ts
        for b in writes:
            if b.dj and not b.r:
                b.w[k] = ts
            else:
                b.w = {k: ts}
                b.r = {}

    def op(self, e, ins_fn, reads=(), writes=()):
        self.deps(e, reads, writes)
        n = self.cnt[e]
        sem = self._engsem(e, n // self.EPOCH)
        val = n % self.EPOCH + 1
        ins_fn().then_inc(sem, 1)
        self.cnt[e] = n + 1
        ts = (sem, val, e)
        self.mark(ts, reads, writes)
        return ts

    def dma(self, q, out_ap, in_ap, reads=(), writes=(), **kw):
        if q not in self.dsem:
            self.dsem[q] = [[self._newsem(f"d_{q}_{i}"), 0] for i in range(self.NDSEM)]
            self.dptr[q] = 0
        slot = self.dsem[q][self.dptr[q]]
        self.dptr[q] = (self.dptr[q] + 1) % self.NDSEM
        sem, issued = slot
        if issued:
            self._wait(q, (sem, 16 * issued, None))
        self.deps(q, reads, writes)
        self.eng[q].dma_start(out=out_ap, in_=in_ap, **kw).then_inc(sem, 16)
        slot[1] = issued + 1
        ts = (sem, 16 * (issued + 1), None)
        self.mark(ts, reads, writes)
        return ts

    def update_fence(self):
        f = {}
        for e in self.eng:
            n = self.cnt[e]
            if n:
                sem = self.esem[e][(n - 1) // self.EPOCH]
                f[id(sem)] = (sem, (n - 1) % self.EPOCH + 1, e)
        for q in self.dsem:
            for sem, issued in self.dsem[q]:
                if issued:
                    f[id(sem)] = (sem, 16 * issued, None)
        self.fence = f

    def drain(self):
        for q in self.dsem:
            for sem, issued in self.dsem[q]:
                if issued:
                    self._wait(q, (sem, 16 * issued, None))


def _t5_bucket_np(dist):
    n = np.maximum(dist, 0)
    nf = np.maximum(n, 1).astype(np.float64)
    large = 16 + (np.log(nf / 16) / math.log(128 / 16) * 16).astype(np.int64)
    large = np.minimum(large, 31)
    return np.where(n < 16, n, large)


def host_consts():
    c = {}
    c["c_ident"] = np.eye(128, dtype=np.float32)
    c["c_J"] = np.ascontiguousarray(np.eye(128, dtype=np.float32)[::-1])
    hb = np.arange(128) // 64
    bd = (hb[:, None] == hb[None, :]).astype(np.float32)
    c["c_bdones"] = bd
    c["c_ones"] = np.ones((128, 128), np.float32)
    s = np.arange(128) % 64
    strict = bd * (s[:, None] < s[None, :])
    incl = bd * (s[:, None] <= s[None, :])
    c["c_mask2"] = np.concatenate([strict, incl], axis=1).astype(np.float32)
    bd5 = np.zeros((128, 5, 2, 64), np.float32)
    for h in range(2):
        bd5[64 * h:64 * h + 64, :, h, :] = 1.0
    c["c_bdmask5"] = bd5.reshape(128, 640)
    seg = np.ones((128, 1024), np.float32)
    seg[:, ::64] = 0.0
    c["c_segmask"] = seg
    dist = np.arange(LVEC) - OFFC
    bk = _t5_bucket_np(dist)
    oh = np.zeros((33, LVEC), np.float32)
    oh[bk, np.arange(LVEC)] = 1.0
    ec = oh.copy()
    ec[32] = np.where(dist >= 0, 0.0, -BIG)
    ec[:32, dist < 0] = 0.0
    ew = oh.copy()
    ok = (dist >= 0) & (dist < 512)
    ew[32] = np.where(ok, 0.0, -BIG)
    ew[:32, ~ok] = 0.0
    c["c_e33c"] = ec
    c["c_e33w"] = ew
    t = np.arange(T)
    cur = t // 64
    blk = np.arange(64)
    forced = (blk[:, None] == 0) | (blk[:, None] == cur[None, :]) | (blk[:, None] == cur[None, :] - 1)
    c["c_forced"] = np.where(forced, 1e4, 0.0).astype(np.float32)
    ex = np.zeros((64, 32, 128), np.float32)
    for kt in range(32):
        for p in range(128):
            ex[2 * kt + p // 64, kt, p] = 1.0
    c["c_expand"] = ex.reshape(64, 32 * 128)
    ncmp = 255
    ci = np.arange(256)[:, None] * 16
    sj = np.arange(64)[None, :] * 64
    c2s = ((ci <= sj + 63) & (ci + 31 >= sj)).astype(np.float32)
    c2s[ncmp:] = 0.0
    c["c_c2s"] = c2s
    return c


CONST_SHAPES = {k: v.shape for k, v in host_consts().items()}

W_SPECS = [
    ("w_in", 1024, IN_COLS), ("rwkv_w2", 64, 1024), ("rwkv_a2", 64, 1024), ("rwkv_g2", 160, 1024),
    ("cmp_w1_k", 2048, 256), ("cmp_w2_k", 256, 64), ("cmp_w1_v", 2048, 256), ("cmp_w2_v", 256, 64),
    ("w_branch_rwkv", 1024, 1024), ("w_branch_nsa", 1024, 1024), ("w_mix_out", 1024, 1024),
    ("ca_wq", 1024, 1024), ("ca_wkv", 1024, 2048), ("ca_wo", 1024, 1024),
    ("ffn_up", 1024, 2 * DFF), ("ffn_down", DFF, 1024),
]
V_SPECS = [
    ("rel_bias", (32, 16)), ("norm_mix", (1, 1024)), ("rwkv_mu", (1, 3360)), ("rwkv_w0", (1, 1024)),
    ("rwkv_a0", (1, 1024)), ("rwkv_kk", (1, 1024)), ("rwkv_ka", (1, 1024)), ("rwkv_rk", (1, 1024)),
    ("rwkv_lnx_w", (1, 1024)), ("rwkv_lnx_b", (1, 1024)), ("nsa_q_gain", (1, 64)), ("nsa_k_gain", (3, 64)),
    ("cmp_pe_k", (1, 2048)), ("cmp_pe_v", (1, 2048)), ("norm_cross", (1, 1024)), ("norm_mem", (1, 1024)),
    ("ca_q_gain", (1, 256)), ("ca_k_gain", (1, 256)), ("norm_ffn", (1, 1024)),
    ("ffn_conv", (3, DFF)), ("ffn_conv_b", (1, DFF)),
]


class Prog:
    def __init__(self, upto="all", dbg=()):
        self.upto = upto
        self.dbg = set(dbg)
        nc = self.nc = bass.Bass("TRN2", target_bir_lowering=False)
        self.es = contextlib.ExitStack()
        self.tk = TK(nc, self.es)
        self.din = {}
        self.dbuf = {}

    def dram_in(self, name, shape):
        self.din[name] = self.nc.dram_tensor(name, list(shape), F32, kind="ExternalInput")
        self.dbuf[name] = Buf(name, dj=True)
        return self.din[name]

    def scratch(self, name, shape, dt):
        if name in self.din:
            return self.din[name]
        kind = "ExternalOutput" if name in self.dbg else "Internal"
        self.din[name] = self.nc.dram_tensor(name, list(shape), dt, kind=kind)
        self.dbuf[name] = Buf(name, dj=True)
        return self.din[name]

    def sb(self, es, name, shape, dt, dj=False):
        self.uid = getattr(self, "uid", 0) + 1
        name = f"{name}_{self.uid}"
        t = Tile(es.enter_context(self.nc.sbuf_tensor(name, list(shape), dt)), name, dj)
        t.b.r = dict(self.tk.fence)
        return t

    @contextlib.contextmanager
    def scope(self):
        with contextlib.ExitStack() as es:
            yield es
        self.tk.update_fence()

    def ring(self, es, name, n, shape, dt):
        return Ring([self.sb(es, f"{name}{i}", shape, dt) for i in range(n)])

    def psum(self):
        return self.psr.next()

    def rpow(self, ap, buf, power):
        nc, tk = self.nc, self.tk
        tk.op("act", lambda: nc.scalar.activation(out=ap, in_=ap, func=AF.Ln), reads=[buf], writes=[buf])
        tk.op("act", lambda: nc.scalar.activation(out=ap, in_=ap, func=AF.Exp, scale=float(power)), reads=[buf], writes=[buf])

    def dap(self, name, offset, ap):
        return bass.AP(tensor=self.din[name], offset=offset, ap=[list(x) for x in ap])

    def load_const(self, es, name, dt=F32, tmp_es=None):
        nc, tk = self.nc, self.tk
        shp = CONST_SHAPES[name]
        t32 = self.sb(es if dt == F32 else tmp_es, name + "_f", shp, F32)
        tk.dma("sp", t32[:], self.din[name].ap()[:, :], reads=[self.dbuf[name]], writes=[t32.b])
        if dt == F32:
            return t32
        t16 = self.sb(es, name + "_h", shp, BF16)
        tk.op("dve", lambda: nc.vector.tensor_copy(t16[:], t32[:]), reads=[t32.b], writes=[t16.b])
        return t16

    def bcast_vec(self, es, name, row, c0, n, tname):
        t = self.sb(es, tname, [128, n], F32)
        src = self.din[name].ap()[row:row + 1, c0:c0 + n].partition_broadcast(128)
        self.tk.dma("sp", t[:], src, reads=[self.dbuf[name]], writes=[t.b])
        return t

    def col_vec(self, es, name, row, c0, nchunk, tname, p=128):
        nc, tk = self.nc, self.tk
        t = self.sb(es, tname, [p, nchunk], F32)
        ncols = self.din[name].shape[1]
        with self.scope() as es2:
            raw = self.sb(es2, tname + "_raw", [nchunk, p], F32)
            tk.dma("sp", raw[:], self.dap(name, row * ncols + c0, [[p, nchunk], [1, p]]), reads=[self.dbuf[name]], writes=[raw.b])
            ps_ = self.psum()
            tk.op("pe", lambda: nc.tensor.transpose(ps_[:p, 0:nchunk], raw[:], self.ident[:nchunk, :nchunk]),
                  reads=[raw.b, self.ident.b], writes=[ps_.b])
            tk.op("dve", lambda: nc.vector.tensor_copy(t[:], ps_[:p, 0:nchunk]), reads=[ps_.b], writes=[t.b])
        return t

    def phase_w(self):
        nc, tk = self.nc, self.tk
        with self.scope() as es:
            st = self.ring(es, "wst", 3, [128, 2048], F32)
            sh = self.ring(es, "wsh", 3, [128, 2048], BF16)
            k = 0
            for name, R, C in W_SPECS:
                dst = self.scratch(name + "_bf", [R, C], BF16)
                src = self.din[name].ap()
                for r0 in range(0, R, 128):
                    rr = min(128, R - r0)
                    for c0 in range(0, C, 2048):
                        cc = min(2048, C - c0)
                        a = st.next()
                        h = sh.next()
                        tk.dma("sp", a[:rr, :cc], src[r0:r0 + rr, c0:c0 + cc], reads=[self.dbuf[name]], writes=[a.b])
                        e = ("dve", "pool", "act")[k % 3]
                        k += 1
                        if e == "act":
                            tk.op(e, lambda: nc.scalar.copy(h[:rr, :cc], a[:rr, :cc]), reads=[a.b], writes=[h.b])
                        elif e == "dve":
                            tk.op(e, lambda: nc.vector.tensor_copy(h[:rr, :cc], a[:rr, :cc]), reads=[a.b], writes=[h.b])
                        else:
                            tk.op(e, lambda: nc.gpsimd.tensor_copy(h[:rr, :cc], a[:rr, :cc]), reads=[a.b], writes=[h.b])
                        tk.dma("pool", dst.ap()[r0:r0 + rr, c0:c0 + cc], h[:rr, :cc], reads=[h.b],
                               writes=[self.dbuf[name + "_bf"]])

    def norm_T(self, src_name, src_row0, ntok, gname, dstT):
        nc, tk = self.nc, self.tk
        with self.scope() as es:
            gbc = self.bcast_vec(es, gname, 0, 0, D, "nt_g")
            xr = self.ring(es, "nt_x", 2, [128, D], F32)
            xs = self.ring(es, "nt_xs", 2, [128, D], F32)
            junk = self.sb(es, "nt_junk", [128, D], BF16)
            st = self.ring(es, "nt_st", 2, [128, 4], F32)
            src = self.din[src_name].ap()
            for i in range(ntok // 128):
                x = xr.next()
                s = st.next()
                y = xs.next()
                tk.dma("sp", x[:], src[src_row0 + i * 128: src_row0 + (i + 1) * 128, :],
                       reads=[self.dbuf[src_name]], writes=[x.b])
                tk.op("act", lambda: nc.scalar.activation(out=junk[:], in_=x[:], func=AF.Square, accum_out=s[:, 0:1]),
                      reads=[x.b], writes=[junk.b, s.b])
                tk.op("dve", lambda: nc.vector.tensor_scalar(s[:, 1:2], s[:, 0:1], 1.0 / D, 1e-6, ALU.mult, ALU.add),
                      reads=[s.b], writes=[s.b])
                tk.op("act", lambda: nc.scalar.sqrt(s[:, 2:3], s[:, 1:2]), reads=[s.b], writes=[s.b])
                tk.op("dve", lambda: nc.vector.reciprocal(s[:, 3:4], s[:, 2:3]), reads=[s.b], writes=[s.b])
                tk.op("dve", lambda: nc.vector.scalar_tensor_tensor(out=y[:], in0=x[:], scalar=s[:, 3:4], in1=gbc[:],
                                                                    op0=ALU.mult, op1=ALU.mult),
                      reads=[x.b, s.b, gbc.b], writes=[y.b])
                for half in range(2):
                    p = self.psum()
                    for j in range(4):
                        kc = half * 4 + j
                        tk.op("pe", lambda: nc.tensor.transpose(p[:, j * 128:(j + 1) * 128], y[:, kc * 128:(kc + 1) * 128],
                                                                self.ident[:]),
                              reads=[y.b, self.ident.b], writes=[p.b])
                    o = dstT[:, half * 4:half * 4 + 4, i * 128:(i + 1) * 128]
                    pin = p[:, :].rearrange("p (a b) -> p a b", a=4)
                    if half == 0:
                        tk.op("act", lambda: nc.scalar.copy(o, pin), reads=[p.b], writes=[dstT.b])
                    else:
                        tk.op("dve", lambda: nc.vector.tensor_copy(o, pin), reads=[p.b], writes=[dstT.b])

    def load_w(self, tile, wname, c0, ncols, kchunks=8, r0=0):
        C = self.din[wname].shape[1]
        src = self.dap(wname, r0 * C + c0, [[C, 128], [128 * C, kchunks], [1, ncols]])
        self.tk.dma("sp", tile[:, 0:kchunks, 0:ncols], src, reads=[self.dbuf[wname]], writes=[tile.b])

    def proj_fm(self, wname, c0, ncols_total, actT, ntok, epi, kchunks=8, wring=None):
        nc, tk = self.nc, self.tk
        nct = (ncols_total + 127) // 128
        for ci in range(nct):
            cc = min(128, ncols_total - ci * 128)
            w = wring.next()
            self.load_w(w, wname, c0 + ci * 128, cc, kchunks)
            for tt in range(ntok // 512):
                p = self.psum()
                for kc in range(kchunks):
                    tk.op("pe", lambda: nc.tensor.matmul(p[:cc, :], lhsT=w[:, kc, 0:cc],
                                                          rhs=actT[:, kc, tt * 512:(tt + 1) * 512],
                                                          start=(kc == 0), stop=(kc == kchunks - 1)),
                          reads=[w.b, actT.b], writes=[p.b])
                epi(p, ci, tt, cc)

    def phase_b(self, xT):
        nc, tk = self.nc, self.tk
        self.scratch("zr_fm", [3360, T], F32)
        self.scratch("q_fm", [1024, T], BF16)
        self.scratch("kcvc_fm", [512, T], BF16)
        self.scratch("ks_fm", [256, T], BF16)
        self.scratch("kw_fm", [256, T], BF16)
        self.scratch("vsw_tm", [T, 512], BF16)
        self.scratch("gates_fm", [48, T], F32)
        self.scratch("gm_fm", [2048, T], F32)
        with self.scope() as es:
            wring = self.ring(es, "pb_w", 2, [128, 8, 128], BF16)
            o32 = self.ring(es, "pb_o32", 3, [128, 512], F32)
            o16 = self.ring(es, "pb_o16", 3, [128, 512], BF16)
            sq = self.ring(es, "pb_sq", 2, [128, 512], F32)
            qg = self.sb(es, "pb_qg", [128, 4], F32)
            eps = self.sb(es, "pb_eps", [128, 1], F32)
            tk.op("pool", lambda: nc.gpsimd.memset(eps[:], 1e-6), writes=[eps.b])
            for h in range(2):
                tk.dma("sp", qg[64 * h:64 * h + 64, 0:1], self.dap("nsa_q_gain", 0, [[1, 64], [1, 1]]),
                       reads=[self.dbuf["nsa_q_gain"]], writes=[qg.b])
                for j in (1, 2):
                    tk.dma("sp", qg[64 * h:64 * h + 64, j + 1:j + 2], self.dap("nsa_k_gain", 64 * j, [[1, 64], [1, 1]]),
                           reads=[self.dbuf["nsa_k_gain"]], writes=[qg.b])
            cnt = [0]

            def store(dname, row0, t0, tile, rows):
                tk.dma("pool", self.din[dname].ap()[row0:row0 + rows, t0:t0 + 512], tile[:rows, :], reads=[tile.b],
                       writes=[self.dbuf[dname]])

            def epi_copy(dname, row_base, dt):
                def f(p, ci, tt, cc):
                    o = (o32 if dt == F32 else o16).next()
                    cnt[0] += 1
                    if cnt[0] % 2:
                        tk.op("act", lambda: nc.scalar.copy(o[:cc, :], p[:cc, :]), reads=[p.b], writes=[o.b])
                    else:
                        tk.op("dve", lambda: nc.vector.tensor_copy(o[:cc, :], p[:cc, :]), reads=[p.b], writes=[o.b])
                    store(dname, row_base + ci * 128, tt * 512, o, cc)
                return f

            def epi_sig(dname, row_base):
                def f(p, ci, tt, cc):
                    o = o32.next()
                    tk.op("act", lambda: nc.scalar.activation(out=o[:cc, :], in_=p[:cc, :], func=AF.Sigmoid),
                          reads=[p.b], writes=[o.b])
                    store(dname, row_base + ci * 128, tt * 512, o, cc)
                return f

            def epi_norm(dname, row_base, gcol, scale):
                def f(p, ci, tt, cc):
                    s = sq.next()
                    tk.op("act", lambda: nc.scalar.activation(out=s[:], in_=p[:], func=AF.Square), reads=[p.b], writes=[s.b])
                    p2 = self.psum()
                    tk.op("pe", lambda: nc.tensor.matmul(p2[:], lhsT=self.bdones[:], rhs=s[:], start=True, stop=True),
                          reads=[self.bdones.b, s.b], writes=[p2.b])
                    r = o32.next()
                    tk.op("dve", lambda: nc.vector.tensor_scalar(r[:], p2[:], 1.0 / 64, 1e-6, ALU.mult, ALU.add),
                          reads=[p2.b], writes=[r.b])
                    self.rpow(r[:], r.b, -0.5)
                    tk.op("dve", lambda: nc.vector.tensor_tensor(out=r[:], in0=p[:], in1=r[:], op=ALU.mult),
                          reads=[p.b, r.b], writes=[r.b])
                    o = o16.next()
                    tk.op("dve", lambda: nc.vector.tensor_scalar(o[:], r[:], qg[:, gcol:gcol + 1], scale, ALU.mult, ALU.mult),
                          reads=[r.b, qg.b], writes=[o.b])
                    store(dname, row_base + ci * 128, tt * 512, o, cc)
                return f

            segs = [
                (0, 3360, epi_copy("zr_fm", 0, F32)),
                (3360, 1024, epi_norm("q_fm", 0, 0, SCALE_NSA)),
                (4384, 512, epi_copy("kcvc_fm", 0, BF16)),
                (4896, 256, epi_norm("ks_fm", 0, 2, 1.0)),
                (5408, 256, epi_norm("kw_fm", 0, 3, 1.0)),
                (5920, 48, epi_sig("gates_fm", 0)),
                (5968, 2048, epi_sig("gm_fm", 0)),
            ]
            for c0, n, epi in segs:
                self.proj_fm("w_in_bf", c0, n, xT, T, epi, wring=wring)
            wv = self.sb(es, "pb_wv", [128, 8, 512], BF16)
            self.load_w(wv, "w_in_bf", 5152, 256)
            C = IN_COLS
            tk.dma("sp", wv[:, :, 256:512], self.dap("w_in_bf", 5664, [[C, 128], [128 * C, 8], [1, 256]]),
                   reads=[self.dbuf["w_in_bf"]], writes=[wv.b])
            for i in range(T // 128):
                p = self.psum()
                for kc in range(8):
                    tk.op("pe", lambda: nc.tensor.matmul(p[:], lhsT=xT[:, kc, i * 128:(i + 1) * 128], rhs=wv[:, kc, :],
                                                          start=(kc == 0), stop=(kc == 7)), reads=[xT.b, wv.b], writes=[p.b])
                o = o16.next()
                tk.op("act", lambda: nc.scalar.copy(o[:], p[:]), reads=[p.b], writes=[o.b])
                tk.dma("pool", self.din["vsw_tm"].ap()[i * 128:(i + 1) * 128, :], o[:], reads=[o.b], writes=[self.dbuf["vsw_tm"]])

    def build(self):
        nc, tk = self.nc, self.tk
        self.dram_in("x", [NB * T, D])
        self.dram_in("mem", [NB * NMEM, D])
        for name, R, C in W_SPECS:
            self.dram_in(name, [R, C])
        for name, shp in V_SPECS:
            self.dram_in(name, shp)
        for name, shp in CONST_SHAPES.items():
            self.dram_in(name, shp)
        self.out = self.nc.dram_tensor("out", [NB * T, D], F32, kind="ExternalOutput")
        self.din["out"] = self.out
        self.dbuf["out"] = Buf("out", dj=True)
        es = self.es
        self.psr = Ring([Tile(es.enter_context(nc.psum_tensor(f"ps{i}", [128, 512], F32)), f"ps{i}") for i in range(8)])
        for t_ in self.psr.tiles:
            t_.b.xr = True
        self.ident = self.load_const(es, "c_ident")
        self.bdones = self.load_const(es, "c_bdones")
        self.phase_w()
        if self.upto == "w":
            return self.finish()
        self.nsa_bias()
        for bi in range(NB):
            self.seq(bi)
            if self.upto != "all":
                break
        return self.finish()

    def seq(self, bi):
        tk = self.tk
        with self.scope() as es1:
            xT = self.sb(es1, "xT", [128, 8, T], BF16, dj=True)
            self.norm_T("x", bi * T, T, "norm_mix", xT)
            if bi == 0 and "xT_dbg" in self.dbg:
                d = self.scratch("xT_dbg", [128, 8 * T], BF16)
                tk.dma("sp", d.ap()[:, :], xT[:, :, :].rearrange("p a b -> p (a b)"), reads=[xT.b], writes=[self.dbuf["xT_dbg"]])
            if self.upto == "a":
                return
            self.phase_b(xT)
        if self.upto == "b":
            return
        if not getattr(self, "skip_rwkv", False):
            self.phase_rwkv()
        if self.upto == "rwkv":
            return
        self.phase_nsa()
        if self.upto == "nsa":
            return
        self.phase_merge(bi)
        if self.upto == "merge":
            return
        self.phase_cross(bi)
        if self.upto == "cross":
            return
        self.phase_ffn(bi)

    def finish(self):
        self.tk.drain()
        self.es.close()
        return self.nc


def make_in_maps(inputs, cores=range(NCORES)):
    consts = host_consts()
    shared = {}
    for name, R, C in W_SPECS:
        shared[name] = np.ascontiguousarray(np.asarray(inputs[name], np.float32).reshape(R, C))
    for name, shp in V_SPECS:
        shared[name] = np.ascontiguousarray(np.asarray(inputs[name], np.float32).reshape(shp))
    shared.update(consts)
    x = np.asarray(inputs["x"], np.float32)
    mem = np.asarray(inputs["mem"], np.float32)
    maps = []
    for c in cores:
        m = dict(shared)
        m["x"] = np.ascontiguousarray(x[NB * c:NB * c + NB].reshape(NB * T, D))
        m["mem"] = np.ascontiguousarray(mem[NB * c:NB * c + NB].reshape(NB * NMEM, D))
        maps.append(m)
    return maps


def kernel(**inputs):
    prog = Prog()
    nc = prog.build()
    maps = make_in_maps(inputs)
    res = run_bass_kernel_spmd(nc, maps, core_ids=list(range(NCORES)))
    outs = [np.asarray(r["out"]).reshape(NB, T, D) for r in res.results]
    return np.concatenate(outs, axis=0).astype(np.float32)


def _rwkv(self):
    nc, tk = self.nc, self.tk
    TB = 512
    self.scratch("yr_fm", [1024, T], BF16)
    zr = self.din["zr_fm"].ap()
    zb = self.dbuf["zr_fm"]

    def shift_load(dst_ap, dst_buf, r0, nrows, t0, nt, mucol, X, dtile):
        if t0 == 0:
            tk.op("pool", lambda: nc.gpsimd.memset(X[:nrows, 0:1], 0.0), writes=[X.b])
            tk.dma("sp", X[:nrows, 1:nt + 1], zr[r0:r0 + nrows, 0:nt], reads=[zb], writes=[X.b])
        else:
            tk.dma("sp", X[:nrows, 0:nt + 1], zr[r0:r0 + nrows, t0 - 1:t0 + nt], reads=[zb], writes=[X.b])
        tk.op("pool", lambda: nc.gpsimd.tensor_tensor(out=dtile[:nrows, :nt], in0=X[:nrows, 0:nt], in1=X[:nrows, 1:nt + 1],
                                                      op=ALU.subtract), reads=[X.b], writes=[dtile.b])
        tk.op("dve", lambda: nc.vector.scalar_tensor_tensor(out=dst_ap, in0=dtile[:nrows, :nt], scalar=mucol,
                                                             in1=X[:nrows, 1:nt + 1], op0=ALU.mult, op1=ALU.add),
              reads=[dtile.b, X.b], writes=[dst_buf])

    with self.scope() as es:
        mask4 = self.sb(es, "rk_mask4", [128, 512], F32)
        for j in range(2):
            tk.dma("sp", mask4[:, j * 256:(j + 1) * 256], self.din["c_mask2"].ap()[:, :], reads=[self.dbuf["c_mask2"]], writes=[mask4.b])
        bdm5 = self.load_const(es, "c_bdmask5")
        segm = self.sb(es, "rk_seg", [128, TB], F32)
        tk.dma("sp", segm[:], self.din["c_segmask"].ap()[:, 0:TB], reads=[self.dbuf["c_segmask"]], writes=[segm.b])
        lw = self.sb(es, "rk_lw", [64, T], BF16)
        la = self.sb(es, "rk_la", [64, T], BF16)
        lg = self.sb(es, "rk_lg", [128, 2, T], BF16)
        w2 = self.sb(es, "rk_w2", [64, 1024], BF16)
        a2 = self.sb(es, "rk_a2", [64, 1024], BF16)
        g2 = self.sb(es, "rk_g2", [128, 2, 1024], BF16)
        tk.dma("sp", w2[:], self.din["rwkv_w2_bf"].ap()[:, :], reads=[self.dbuf["rwkv_w2_bf"]], writes=[w2.b])
        tk.dma("sp", a2[:], self.din["rwkv_a2_bf"].ap()[:, :], reads=[self.dbuf["rwkv_a2_bf"]], writes=[a2.b])
        tk.dma("sp", g2[:, 0, :], self.din["rwkv_g2_bf"].ap()[0:128, :], reads=[self.dbuf["rwkv_g2_bf"]], writes=[g2.b])
        tk.dma("sp", g2[0:32, 1, :], self.din["rwkv_g2_bf"].ap()[128:160, :], reads=[self.dbuf["rwkv_g2_bf"]], writes=[g2.b])
        pc = {}
        for nm in ("rwkv_w0", "rwkv_a0", "rwkv_kk", "rwkv_ka", "rwkv_rk", "rwkv_lnx_w", "rwkv_lnx_b"):
            pc[nm] = self.col_vec(es, nm, 0, 0, 8, "rk_" + nm)
        mu = self.col_vec(es, "rwkv_mu", 0, 0, 24, "rk_mu")
        omk = self.sb(es, "rk_omk", [128, 8], F32)
        tk.op("dve", lambda: nc.vector.tensor_scalar(omk[:], pc["rwkv_ka"][:], -1.0, 1.0, ALU.mult, ALU.add),
              reads=[pc["rwkv_ka"].b], writes=[omk.b])
        with self.scope() as es2:
            X = self.sb(es2, "rk_LX", [128, T + 1], F32)
            dt_ = self.sb(es2, "rk_Ld", [128, T], F32)
            zt = self.sb(es2, "rk_Lz", [128, T], F32)
            for (r0, nrows, kind) in ((3072, 64, "w"), (3136, 64, "a"), (3200, 128, "g0"), (3328, 32, "g1")):
                mucol = self.sb(es2, "rk_Lmu" + kind, [128, 1], F32)
                tk.dma("sp", mucol[:nrows, :], self.dap("rwkv_mu", r0, [[1, nrows], [1, 1]]), reads=[self.dbuf["rwkv_mu"]], writes=[mucol.b])
                shift_load(zt[:nrows, :], zt.b, r0, nrows, 0, T, mucol[:nrows, 0:1], X, dt_)
                if kind == "w":
                    tk.op("act", lambda: nc.scalar.activation(out=lw[:, :], in_=zt[:64, :], func=AF.Tanh), reads=[zt.b], writes=[lw.b])
                elif kind == "a":
                    tk.op("act", lambda: nc.scalar.copy(la[:, :], zt[:64, :]), reads=[zt.b], writes=[la.b])
                elif kind == "g0":
                    tk.op("act", lambda: nc.scalar.activation(out=lg[:, 0, :], in_=zt[:, :], func=AF.Sigmoid), reads=[zt.b], writes=[lg.b])
                else:
                    tk.op("act", lambda: nc.scalar.activation(out=lg[:32, 1, :], in_=zt[:32, :], func=AF.Sigmoid), reads=[zt.b], writes=[lg.b])
        LIM = getattr(self, "rk_lim", 99)
        if LIM <= 1:
            return
        f = lambda n: self.sb(es, n, [128, TB], F32)
        Xr = self.ring(es, "rk_X", 2, [128, TB + 1], F32)
        dtl = f("rk_d")
        rr, kp, logw, aa, gg, kkr, sq, kmod, kb, cum, cex, epv, eng, bonus, tmp = [f("rk_t%d" % i) for i in range(15)]
        einr = self.ring(es, "rk_ein", 2, [128, TB], F32)
        Q5r = self.ring(es, "rk_Q5", 2, [128, 5, TB], F32)
        yfm = f("rk_yfm")
        dd = f("rk_dd")
        ob = self.ring(es, "rk_ob", 2, [128, TB], BF16)
        BD5l = [self.sb(es, f"rk_BD5{i}", [128, 5, 2, 64], F32) for i in range(4)]
        GBKl = [self.sb(es, f"rk_GBK{i}", [128, 512], F32) for i in range(4)]
        NTl = [self.sb(es, f"rk_NT{i}", [128, 128], F32) for i in range(4)]
        MXl = [[self.sb(es, f"rk_MX{i}{k}", [128, 256], F32) for k in range(2)] for i in range(4)]
        MTl = [[self.sb(es, f"rk_MT{i}{k}", [128, 128], F32) for k in range(2)] for i in range(4)]
        TTl = [self.sb(es, f"rk_TT{i}", [128, 128], F32) for i in range(4)]
        TM3l = [self.sb(es, f"rk_TM3{i}", [128, 384], F32) for i in range(4)]
        RHr = self.ring(es, "rk_RH", 2, [128, 128], F32)
        Ur = self.ring(es, "rk_U", 2, [128, 128], F32)
        Sr = self.ring(es, "rk_S", 2, [128, 128], F32)
        SPr = self.ring(es, "rk_SP", 2, [128, 128], F32)
        ident, bdones = self.ident, self.bdones

        def mm(p_ap, pbuf, lhsT, lb, rhs, rb, start=True, stop=True):
            tk.op("pe", lambda: nc.tensor.matmul(p_ap, lhsT=lhsT, rhs=rhs, start=start, stop=stop), reads=[lb, rb], writes=[pbuf])

        for hp in range(8):
            c0 = 128 * hp
            S = Sr.next()
            tk.op("pool", lambda: nc.gpsimd.memset(S[:], 0.0), writes=[S.b])
            for tb in range(T // TB):
                t0 = tb * TB
                Q5 = Q5r.next()
                ein = einr.next()
                shift_load(rr[:, :], rr.b, c0, 128, t0, TB, mu[:, hp:hp + 1], Xr.next(), dtl)
                shift_load(kp[:, :], kp.b, 1024 + c0, 128, t0, TB, mu[:, 8 + hp:9 + hp], Xr.next(), dtl)
                shift_load(Q5[:, 4, :], Q5.b, 2048 + c0, 128, t0, TB, mu[:, 16 + hp:17 + hp], Xr.next(), dtl)
                p = self.psum()
                mm(p[:], p.b, w2[:, c0:c0 + 128], w2.b, lw[:, t0:t0 + TB], lw.b)
                tk.op("act", lambda: nc.scalar.activation(out=logw[:], in_=p[:], func=AF.Sigmoid, bias=pc["rwkv_w0"][:, hp:hp + 1]),
                      reads=[p.b, pc["rwkv_w0"].b], writes=[logw.b])
                tk.op("pool", lambda: nc.gpsimd.tensor_scalar_mul(logw[:], logw[:], -math.exp(-0.5)), reads=[logw.b], writes=[logw.b])
                p = self.psum()
                mm(p[:], p.b, a2[:, c0:c0 + 128], a2.b, la[:, t0:t0 + TB], la.b)
                tk.op("act", lambda: nc.scalar.activation(out=aa[:], in_=p[:], func=AF.Sigmoid, bias=pc["rwkv_a0"][:, hp:hp + 1]),
                      reads=[p.b, pc["rwkv_a0"].b], writes=[aa.b])
                p = self.psum()
                mm(p[:], p.b, g2[:, 0, c0:c0 + 128], g2.b, lg[:, 0, t0:t0 + TB], lg.b, True, False)
                mm(p[:], p.b, g2[:32, 1, c0:c0 + 128], g2.b, lg[:32, 1, t0:t0 + TB], lg.b, False, True)
                tk.op("act", lambda: nc.scalar.copy(gg[:], p[:]), reads=[p.b], writes=[gg.b])
                tk.op("dve", lambda: nc.vector.tensor_scalar_mul(kkr[:], kp[:], pc["rwkv_kk"][:, hp:hp + 1]),
                      reads=[kp.b, pc["rwkv_kk"].b], writes=[kkr.b])
                tk.op("act", lambda: nc.scalar.activation(out=sq[:], in_=kkr[:], func=AF.Square), reads=[kkr.b], writes=[sq.b])
                p = self.psum()
                mm(p[:], p.b, bdones[:], bdones.b, sq[:], sq.b)
                tk.op("dve", lambda: nc.vector.tensor_scalar_max(tmp[:], p[:], 1e-24), reads=[p.b], writes=[tmp.b])
                self.rpow(tmp[:], tmp.b, -0.5)
                tk.op("dve", lambda: nc.vector.tensor_tensor(out=kkr[:], in0=kkr[:], in1=tmp[:], op=ALU.mult), reads=[kkr.b, tmp.b], writes=[kkr.b])
                tk.op("dve", lambda: nc.vector.tensor_scalar(kmod[:], aa[:], pc["rwkv_ka"][:, hp:hp + 1], omk[:, hp:hp + 1], ALU.mult, ALU.add),
                      reads=[aa.b, pc["rwkv_ka"].b, omk.b], writes=[kmod.b])
                tk.op("pool", lambda: nc.gpsimd.tensor_tensor(out=kmod[:], in0=kmod[:], in1=kp[:], op=ALU.mult), reads=[kmod.b, kp.b], writes=[kmod.b])
                tk.op("pool", lambda: nc.gpsimd.tensor_tensor(out=kb[:], in0=kkr[:], in1=aa[:], op=ALU.mult), reads=[kkr.b, aa.b], writes=[kb.b])
                tk.op("dve", lambda: nc.vector.scalar_tensor_tensor(out=tmp[:], in0=rr[:], scalar=pc["rwkv_rk"][:, hp:hp + 1], in1=kmod[:],
                                                                    op0=ALU.mult, op1=ALU.mult), reads=[rr.b, kmod.b, pc["rwkv_rk"].b], writes=[tmp.b])
                p = self.psum()
                mm(p[:], p.b, bdones[:], bdones.b, tmp[:], tmp.b)
                tk.op("dve", lambda: nc.vector.tensor_tensor(out=bonus[:], in0=p[:], in1=Q5[:, 4, :], op=ALU.mult), reads=[p.b, Q5.b], writes=[bonus.b])
                tk.op("dve", lambda: nc.vector.tensor_tensor_scan(out=cum[:], data0=segm[:], data1=logw[:], initial=0.0, op0=ALU.mult, op1=ALU.add),
                      reads=[segm.b, logw.b], writes=[cum.b])
                tk.op("pool", lambda: nc.gpsimd.tensor_tensor(out=cex[:], in0=cum[:], in1=logw[:], op=ALU.subtract), reads=[cum.b, logw.b], writes=[cex.b])
                tk.op("act", lambda: nc.scalar.activation(out=epv[:], in_=cex[:], func=AF.Exp), reads=[cex.b], writes=[epv.b])
                tk.op("act", lambda: nc.scalar.activation(out=ein[:], in_=cum[:], func=AF.Exp), reads=[cum.b], writes=[ein.b])
                tk.op("act", lambda: nc.scalar.activation(out=eng[:], in_=cum[:], func=AF.Exp, scale=-1.0), reads=[cum.b], writes=[eng.b])
                tk.op("dve", lambda: nc.vector.scalar_tensor_tensor(out=Q5[:, 0, :], in0=kkr[:], scalar=-1.0, in1=epv[:], op0=ALU.mult, op1=ALU.mult),
                      reads=[kkr.b, epv.b], writes=[Q5.b])
                tk.op("pool", lambda: nc.gpsimd.tensor_tensor(out=Q5[:, 1, :], in0=rr[:], in1=ein[:], op=ALU.mult), reads=[rr.b, ein.b], writes=[Q5.b])
                tk.op("dve", lambda: nc.vector.tensor_tensor(out=Q5[:, 2, :], in0=kb[:], in1=eng[:], op=ALU.mult), reads=[kb.b, eng.b], writes=[Q5.b])
                tk.op("pool", lambda: nc.gpsimd.tensor_tensor(out=Q5[:, 3, :], in0=kmod[:], in1=eng[:], op=ALU.mult), reads=[kmod.b, eng.b], writes=[Q5.b])
                if LIM <= 2:
                    return
                NBC = 2
                st = {}

                def batch_gen(chunks):
                    for c in chunks:
                        cs = slice(c * 64, (c + 1) * 64)
                        BD5 = BD5l[c % 4]
                        src = Q5[:, :, cs].unsqueeze(2).to_broadcast([128, 5, 2, 64])
                        tk.op("dve", lambda: nc.vector.tensor_tensor(out=BD5[:], in0=src, in1=bdm5[:, :].rearrange("p (a h b) -> p a h b", a=5, h=2),
                                                                     op=ALU.mult), reads=[Q5.b, bdm5.b], writes=[BD5.b])
                        bd = lambda j, BD5=BD5: BD5[:, j, :, :].rearrange("p h b -> p (h b)")
                        p = self.psum()
                        ar = BD5[:, 0:2, :, :].rearrange("p a h b -> p (a h b)")
                        mm(p[:, 0:256], p.b, bd(2), BD5.b, ar, BD5.b)
                        mm(p[:, 256:512], p.b, bd(3), BD5.b, ar, BD5.b)
                        GBK = GBKl[c % 4]
                        tk.op("dve", lambda: nc.vector.tensor_tensor(out=GBK[:], in0=p[:], in1=mask4[:], op=ALU.mult), reads=[p.b, mask4.b], writes=[GBK.b])
                        yield
                        p3 = self.psum()
                        for j in range(3):
                            tk.op("pe", lambda: nc.tensor.transpose(p3[:, j * 128:(j + 1) * 128], bd(2 + j), ident[:]), reads=[BD5.b, ident.b], writes=[p3.b])
                        TM3 = TM3l[c % 4]
                        tk.op("act", lambda: nc.scalar.copy(TM3[:], p3[:, 0:384]), reads=[p3.b], writes=[TM3.b])
                        st[c] = dict(BD5=BD5, bd=bd, GBK=GBK, TM3=TM3, cs=cs)
                        yield
                    for c in chunks:
                        d = st[c]
                        GBK = d["GBK"]
                        NT = NTl[c % 4]
                        p = self.psum()
                        tk.op("pe", lambda: nc.tensor.transpose(p[:, 0:128], GBK[:, 0:128], ident[:]), reads=[GBK.b, ident.b], writes=[p.b])
                        tk.op("act", lambda: nc.scalar.copy(NT[:], p[:, 0:128]), reads=[p.b], writes=[NT.b])
                        MX = MXl[c % 4][0]
                        tk.op("pool", lambda: nc.gpsimd.tensor_tensor(out=MX[:, 128:256], in0=ident[:], in1=GBK[:, 0:128], op=ALU.add),
                              reads=[ident.b, GBK.b], writes=[MX.b])
                        d["NT"] = NT
                        yield
                    for c in chunks:
                        d = st[c]
                        GBK, NT = d["GBK"], d["NT"]
                        MX, MT = MXl[c % 4][0], MTl[c % 4][0]
                        p = self.psum()
                        mm(p[:, 0:128], p.b, NT[:], NT.b, GBK[:, 0:128], GBK.b)
                        pt = self.psum()
                        mm(pt[:, 0:128], pt.b, GBK[:, 0:128], GBK.b, NT[:], NT.b)
                        tk.op("act", lambda: nc.scalar.copy(MX[:, 0:128], p[:, 0:128]), reads=[p.b], writes=[MX.b])
                        tk.op("dve", lambda: nc.vector.tensor_copy(MT[:], pt[:, 0:128]), reads=[pt.b], writes=[MT.b])
                        d["MX"], d["MT"], d["par"] = MX, MT, 0
                        yield
                    for j in range(2, 6):
                        for c in chunks:
                            d = st[c]
                            MX, MT = d["MX"], d["MT"]
                            par = 1 - d["par"]
                            MX2, MT2 = MXl[c % 4][par], MTl[c % 4][par]
                            pm = self.psum()
                            mm(pm[:, 0:128], pm.b, MT[:], MT.b, MX[:, 0:128], MX.b)
                            px = self.psum()
                            mm(px[:, 0:128], px.b, MT[:], MT.b, MX[:, 128:256], MX.b)
                            pt = self.psum()
                            mm(pt[:, 0:128], pt.b, MX[:, 0:128], MX.b, MT[:], MT.b)
                            tk.op("act", lambda: nc.scalar.copy(MX2[:, 0:128], pm[:, 0:128]), reads=[pm.b], writes=[MX2.b])
                            tk.op("dve", lambda: nc.vector.tensor_tensor(out=MX2[:, 128:256], in0=px[:, 0:128], in1=MX[:, 128:256], op=ALU.add),
                                  reads=[px.b, MX.b], writes=[MX2.b])
                            tk.op("act", lambda: nc.scalar.copy(MT2[:], pt[:, 0:128]), reads=[pt.b], writes=[MT2.b])
                            d["MX"], d["MT"], d["par"] = MX2, MT2, par
                            yield
                    for c in chunks:
                        d = st[c]
                        MX, MT = d["MX"], d["MT"]
                        p = self.psum()
                        mm(p[:, 0:128], p.b, MT[:], MT.b, MX[:, 128:256], MX.b)
                        TT = TTl[c % 4]
                        tk.op("dve", lambda: nc.vector.tensor_tensor(out=TT[:], in0=p[:, 0:128], in1=MX[:, 128:256], op=ALU.add), reads=[p.b, MX.b], writes=[TT.b])
                        d["TT"] = TT
                        yield

                Sh = [S]

                def chain_gen(chunks):
                    for c in chunks:
                        d = st[c]
                        bd, GBK, TM3, TT, cs = d["bd"], d["GBK"], d["TM3"], d["TT"], d["cs"]
                        BD5 = d["BD5"]
                        S = Sh[0]
                        PCc = ein[:, c * 64 + 63:c * 64 + 64]
                        SP = SPr.next()
                        tk.op("act", lambda: nc.scalar.activation(out=SP[:], in_=S[:], func=AF.Identity, scale=PCc), reads=[S.b, ein.b], writes=[SP.b])
                        p = self.psum()
                        mm(p[:, 0:128], p.b, bd(0), BD5.b, S[:], S.b, True, False)
                        mm(p[:, 0:128], p.b, GBK[:, 256:384], GBK.b, TM3[:, 256:384], TM3.b, False, True)
                        RH = RHr.next()
                        tk.op("act", lambda: nc.scalar.copy(RH[:], p[:, 0:128]), reads=[p.b], writes=[RH.b])
                        yield
                        p = self.psum()
                        mm(p[:, 0:128], p.b, TT[:], TT.b, RH[:], RH.b)
                        U = Ur.next()
                        tk.op("dve", lambda: nc.vector.tensor_copy(U[:], p[:, 0:128]), reads=[p.b], writes=[U.b])
                        yield
                        pS = self.psum()
                        mm(pS[:, 0:128], pS.b, TM3[:, 0:128], TM3.b, U[:], U.b, True, False)
                        mm(pS[:, 0:128], pS.b, TM3[:, 128:256], TM3.b, TM3[:, 256:384], TM3.b, False, True)
                        S2 = Sr.next()
                        tk.op("dve", lambda: nc.vector.scalar_tensor_tensor(out=S2[:], in0=pS[:, 0:128], scalar=PCc, in1=SP[:], op0=ALU.mult, op1=ALU.add),
                              reads=[pS.b, ein.b, SP.b], writes=[S2.b])
                        p = self.psum()
                        mm(p[:, 0:128], p.b, S[:], S.b, bd(1), BD5.b, True, False)
                        mm(p[:, 0:128], p.b, U[:], U.b, GBK[:, 128:256], GBK.b, False, False)
                        mm(p[:, 0:128], p.b, TM3[:, 256:384], TM3.b, GBK[:, 384:512], GBK.b, False, True)
                        tk.op("act", lambda: nc.scalar.copy(yfm[0:64, cs], p[0:64, 0:64]), reads=[p.b], writes=[yfm.b])
                        tk.op("act", lambda: nc.scalar.copy(yfm[64:128, cs], p[64:128, 64:128]), reads=[p.b], writes=[yfm.b])
                        Sh[0] = S2
                        yield

                nbt = TB // 64 // NBC
                batches = [list(range(k * NBC, (k + 1) * NBC)) for k in range(nbt)]
                for _ in batch_gen(batches[0]):
                    pass
                for k in range(nbt):
                    cg = chain_gen(batches[k])
                    bg = batch_gen(batches[k + 1]) if k + 1 < nbt else iter(())
                    done_b = done_c = False
                    while not (done_b and done_c):
                        for _ in range(3):
                            if not done_b:
                                try:
                                    next(bg)
                                except StopIteration:
                                    done_b = True
                        if not done_c:
                            try:
                                next(cg)
                            except StopIteration:
                                done_c = True
                S = Sh[0]
                if LIM <= 5:
                    return
                p = self.psum()
                mm(p[:], p.b, bdones[:], bdones.b, yfm[:], yfm.b)
                tk.op("dve", lambda: nc.vector.scalar_tensor_tensor(out=dd[:], in0=p[:], scalar=-1.0 / 64, in1=yfm[:], op0=ALU.mult, op1=ALU.add),
                      reads=[p.b, yfm.b], writes=[dd.b])
                tk.op("act", lambda: nc.scalar.activation(out=sq[:], in_=dd[:], func=AF.Square), reads=[dd.b], writes=[sq.b])
                p = self.psum()
                mm(p[:], p.b, bdones[:], bdones.b, sq[:], sq.b)
                tk.op("dve", lambda: nc.vector.tensor_scalar(tmp[:], p[:], 1.0 / 64, 64e-5, ALU.mult, ALU.add), reads=[p.b], writes=[tmp.b])
                self.rpow(tmp[:], tmp.b, -0.5)
                tk.op("dve", lambda: nc.vector.tensor_tensor(out=dd[:], in0=dd[:], in1=tmp[:], op=ALU.mult), reads=[dd.b, tmp.b], writes=[dd.b])
                tk.op("act", lambda: nc.scalar.activation(out=dd[:], in_=dd[:], func=AF.Identity, bias=pc["rwkv_lnx_b"][:, hp:hp + 1],
                                                          scale=pc["rwkv_lnx_w"][:, hp:hp + 1]),
                      reads=[dd.b, pc["rwkv_lnx_b"].b, pc["rwkv_lnx_w"].b], writes=[dd.b])
                tk.op("pool", lambda: nc.gpsimd.tensor_tensor(out=dd[:], in0=dd[:], in1=bonus[:], op=ALU.add), reads=[dd.b, bonus.b], writes=[dd.b])
                o = ob.next()
                tk.op("dve", lambda: nc.vector.tensor_tensor(out=o[:], in0=dd[:], in1=gg[:], op=ALU.mult), reads=[dd.b, gg.b], writes=[o.b])
                tk.dma("pool", self.din["yr_fm"].ap()[c0:c0 + 128, t0:t0 + TB], o[:], reads=[o.b], writes=[self.dbuf["yr_fm"]])
                if LIM <= 6:
                    return


Prog.phase_rwkv = _rwkv


def _nsa_bias(self):
    nc, tk = self.nc, self.tk
    self.scratch("bvec_c", [16, LVEC], BF16)
    self.scratch("bvec_w", [16, LVEC], BF16)
    with self.scope() as es:
        tab = self.sb(es, "nb_tab", [33, 16], F32)
        tk.op("pool", lambda: nc.gpsimd.memset(tab[:], 1.0), writes=[tab.b])
        tk.dma("sp", tab[0:32, :], self.din["rel_bias"].ap()[:, :], reads=[self.dbuf["rel_bias"]], writes=[tab.b])
        e33 = self.sb(es, "nb_e33", [33, LVEC], F32)
        ob = self.ring(es, "nb_o", 2, [16, 512], BF16)
        for cname, dname in (("c_e33c", "bvec_c"), ("c_e33w", "bvec_w")):
            tk.dma("sp", e33[:], self.din[cname].ap()[:, :], reads=[self.dbuf[cname]], writes=[e33.b])
            for j in range(LVEC // 512):
                p = self.psum()
                tk.op("pe", lambda: nc.tensor.matmul(p[:16, :], lhsT=tab[:], rhs=e33[:, j * 512:(j + 1) * 512], start=True, stop=True),
                      reads=[tab.b, e33.b], writes=[p.b])
                o = ob.next()
                tk.op("act", lambda: nc.scalar.copy(o[:], p[:16, :]), reads=[p.b], writes=[o.b])
                tk.dma("pool", self.din[dname].ap()[:, j * 512:(j + 1) * 512], o[:], reads=[o.b], writes=[self.dbuf[dname]])


def _nsa(self):
    nc, tk = self.nc, self.tk
    self.scratch("yn_fm", [1024, T], BF16)
    ngen = getattr(self, "ns_ngen", 5)
    gen = Ring(self.psr.tiles[0:ngen])
    accp = Ring(self.psr.tiles[ngen:8])

    def mm(p_ap, pbuf, lhsT, lb, rhs, rb, start=True, stop=True):
        tk.op("pe", lambda: nc.tensor.matmul(p_ap, lhsT=lhsT, rhs=rhs, start=start, stop=stop), reads=[lb, rb], writes=[pbuf])

    with self.scope() as es:
        Jb = self.load_const(es, "c_J", BF16, tmp_es=es)
        c2s_f = self.sb(es, "ns_c2sf", [128, 2, 64], F32)
        tk.dma("sp", c2s_f[:, :, :], self.dap("c_c2s", 0, [[64, 128], [128 * 64, 2], [1, 64]]), reads=[self.dbuf["c_c2s"]], writes=[c2s_f.b])
        c2s = self.sb(es, "ns_c2s", [128, 2, 64], BF16)
        tk.op("dve", lambda: nc.vector.tensor_copy(c2s[:], c2s_f[:]), reads=[c2s_f.b], writes=[c2s.b])
        ksX = self.sb(es, "ns_ksX", [128, T], BF16, dj=True)
        kwX = self.sb(es, "ns_kwX", [128, T], BF16, dj=True)
        with self.scope() as es0:
            exf = self.sb(es0, "ns_exf", [128, 4096], F32)
            tk.dma("sp", exf[64:128, :], self.din["c_expand"].ap()[:, :], reads=[self.dbuf["c_expand"]], writes=[exf.b])
            tk.op("dve", lambda: nc.vector.tensor_scalar_mul(ksX[64:128, :], exf[64:128, :], BIG), reads=[exf.b], writes=[ksX.b])
        tk.op("pool", lambda: nc.gpsimd.memset(kwX[64:128, :], 0.0), writes=[kwX.b])
        ones = self.sb(es, "ns_ones", [128, 64], BF16)
        tk.op("pool", lambda: nc.gpsimd.memset(ones[:], 1.0), writes=[ones.b])
        kgain = self.col_vec(es, "nsa_k_gain", 0, 0, 1, "ns_kg", p=64)
        ident, bdones = self.ident, self.bdones
        kcmpT = [self.sb(es, f"ns_kcT{g}", [128, 256], BF16) for g in range(4)]
        for g in range(4):
            tk.op("pool", lambda: nc.gpsimd.memset(kcmpT[g][64:128, :], 0.0), writes=[kcmpT[g].b])
        vcmp = [self.sb(es, f"ns_vc{g}", [128, 2, 64], BF16) for g in range(4)]
        with self.scope() as es2:
            kc2 = self.sb(es2, "ns_kc2", [128, T], BF16)
            w1t = self.sb(es2, "ns_w1", [128, 16, 256], BF16)
            w2t = self.sb(es2, "ns_w2", [128, 2, 64], BF16)
            hg_ = self.sb(es2, "ns_hg", [128, 2, 256], BF16)
            xx = self.sb(es2, "ns_x", [128, 256], F32)
            x2 = self.sb(es2, "ns_x2", [128, 256], F32)
            pvb = self.sb(es2, "ns_pvb", [128, 2], F32)
            t64 = self.sb(es2, "ns_t64", [64, 256], F32)
            t64b = self.sb(es2, "ns_t64b", [64, 256], F32)
            for kind in range(2):
                sfx = "_k" if kind == 0 else "_v"
                self.load_w(w1t, "cmp_w1" + sfx + "_bf", 0, 256, kchunks=16)
                self.load_w(w2t, "cmp_w2" + sfx + "_bf", 0, 64, kchunks=2)
                pe_f = self.col_vec(es2, "cmp_pe" + sfx, 0, 0, 16, "ns_pe" + sfx)
                pe_b = self.sb(es2, "ns_peb" + sfx, [128, 16], BF16)
                tk.op("dve", lambda: nc.vector.tensor_copy(pe_b[:], pe_f[:]), reads=[pe_f.b], writes=[pe_b.b])
                for ct in range(2):
                    p = gen.next()
                    for l2 in range(16):
                        mm(p[:, 0:1], p.b, w1t[:, l2, ct * 128:(ct + 1) * 128], w1t.b, pe_b[:, l2:l2 + 1], pe_b.b, l2 == 0, l2 == 15)
                    tk.op("dve", lambda: nc.vector.tensor_copy(pvb[:, ct:ct + 1], p[:, 0:1]), reads=[p.b], writes=[pvb.b])
                for g in range(4):
                    r0 = 256 * kind + 64 * g
                    tk.op("pool", lambda: nc.gpsimd.memset(kc2[64:128, T - 1:T], 0.0), writes=[kc2.b])
                    tk.dma("sp", kc2[0:64, :], self.din["kcvc_fm"].ap()[r0:r0 + 64, :], reads=[self.dbuf["kcvc_fm"]], writes=[kc2.b])
                    tk.dma("sp", kc2[64:128, 0:T - 1], self.din["kcvc_fm"].ap()[r0:r0 + 64, 1:T], reads=[self.dbuf["kcvc_fm"]], writes=[kc2.b])
                    tk.op("pool", lambda: nc.gpsimd.memset(hg_[:], 0.0), writes=[hg_.b])
                    for ct in range(2):
                        p = gen.next()
                        for l2 in range(16):
                            rhs = kc2[:, 2 * l2: 2 * l2 + 16 * 254 + 1: 16]
                            mm(p[:, 0:255], p.b, w1t[:, l2, ct * 128:(ct + 1) * 128], w1t.b, rhs, kc2.b, l2 == 0, l2 == 15)
                        tk.op("act", lambda: nc.scalar.activation(out=xx[:, 0:255], in_=p[:, 0:255], func=AF.Identity, bias=pvb[:, ct:ct + 1]),
                              reads=[p.b, pvb.b], writes=[xx.b])
                        tk.op("act", lambda: nc.scalar.activation(out=x2[:, 0:255], in_=xx[:, 0:255], func=AF.Square), reads=[xx.b], writes=[x2.b])
                        tk.op("dve", lambda: nc.vector.tensor_scalar(x2[:, 0:255], x2[:, 0:255], 0.044715, 1.0, ALU.mult, ALU.add), reads=[x2.b], writes=[x2.b])
                        tk.op("dve", lambda: nc.vector.tensor_tensor(out=x2[:, 0:255], in0=x2[:, 0:255], in1=xx[:, 0:255], op=ALU.mult), reads=[x2.b, xx.b], writes=[x2.b])
                        tk.op("act", lambda: nc.scalar.activation(out=x2[:, 0:255], in_=x2[:, 0:255], func=AF.Sigmoid, scale=1.5957691216057308),
                              reads=[x2.b], writes=[x2.b])
                        tk.op("dve", lambda: nc.vector.tensor_tensor(out=hg_[:, ct, 0:255], in0=x2[:, 0:255], in1=xx[:, 0:255], op=ALU.mult),
                              reads=[x2.b, xx.b], writes=[hg_.b])
                    if kind == 0:
                        p = gen.next()
                        for ct in range(2):
                            mm(p[0:64, 0:256], p.b, w2t[:, ct, :], w2t.b, hg_[:, ct, :], hg_.b, ct == 0, ct == 1)
                        tk.op("act", lambda: nc.scalar.activation(out=t64[:], in_=p[0:64, 0:256], func=AF.Square), reads=[p.b], writes=[t64.b])
                        p2 = gen.next()
                        mm(p2[0:64, 0:256], p2.b, bdones[0:64, 0:64], bdones.b, t64[:], t64.b)
                        tk.op("dve", lambda: nc.vector.tensor_scalar(t64[:], p2[0:64, 0:256], 1.0 / 64, 1e-6, ALU.mult, ALU.add), reads=[p2.b], writes=[t64.b])
                        self.rpow(t64[:], t64.b, -0.5)
                        tk.op("dve", lambda: nc.vector.tensor_tensor(out=t64b[:], in0=p[0:64, 0:256], in1=t64[:], op=ALU.mult), reads=[p.b, t64.b], writes=[t64b.b])
                        tk.op("dve", lambda: nc.vector.tensor_scalar_mul(kcmpT[g][0:64, :], t64b[:], kgain[:, 0:1]), reads=[t64b.b, kgain.b], writes=[kcmpT[g].b])
                    else:
                        for nt in range(2):
                            p = gen.next()
                            for ct in range(2):
                                mm(p[:, 0:64], p.b, hg_[:, ct, nt * 128:(nt + 1) * 128], hg_.b, w2t[:, ct, :], w2t.b, ct == 0, ct == 1)
                            tk.op("act", lambda: nc.scalar.copy(vcmp[g][:, nt, :], p[:, 0:64]), reads=[p.b], writes=[vcmp[g].b])
        if getattr(self, "ns_lim", 99) <= 1:
            return
        Vs = self.sb(es, "ns_Vs", [128, 32, 128], BF16)
        Vw = self.sb(es, "ns_Vw", [128, 32, 128], BF16)
        tk.op("pool", lambda: nc.gpsimd.memset(Vs[:], 1.0), writes=[Vs.b])
        tk.op("pool", lambda: nc.gpsimd.memset(Vw[:], 1.0), writes=[Vw.b])
        vco = [self.sb(es, f"ns_vco{g}", [128, 2, 128], BF16) for g in range(4)]
        for g in range(4):
            tk.op("pool", lambda: nc.gpsimd.memset(vco[g][:], 1.0), writes=[vco[g].b])
            tk.op("dve", lambda: nc.vector.tensor_copy(vco[g][:, :, 0:64], vcmp[g][:]), reads=[vcmp[g].b], writes=[vco[g].b])
        bfar = self.sb(es, "ns_bfar", [128, 16], F32)
        tk.dma("sp", bfar[:], self.din["rel_bias"].ap()[31:32, :].partition_broadcast(128), reads=[self.dbuf["rel_bias"]], writes=[bfar.b])
        qTr = self.ring(es, "ns_qT", 2, [128, 4, 512], BF16)
        for t_ in qTr.tiles:
            tk.op("pool", lambda: nc.gpsimd.memset(t_[64:128, :, :], 0.0), writes=[t_.b])
        gbr = self.ring(es, "ns_gb", 3, [64, 4, 512], F32)
        Hr = self.ring(es, "ns_H", 4, [128, 512], BF16)
        Er = self.ring(es, "ns_E", 4, [128, 512], BF16)
        E2r = self.ring(es, "ns_E2", 4, [128, 512], BF16)
        Ec = self.sb(es, "ns_Ec", [128, 8, 512], BF16, dj=True)
        accC = self.sb(es, "ns_accC", [64, 4, 8, 512], BF16, dj=True)
        QSr = self.ring(es, "ns_QS", 2, [128, T], BF16)
        for t_ in QSr.tiles:
            t_.b.dj = True
        acc = self.ring(es, "ns_acc", 2, [64, 512], F32)
        impa = self.sb(es, "ns_impa", [64, 512], F32)
        frc = self.ring(es, "ns_frc", 2, [64, 512], F32)
        rdr = self.ring(es, "ns_rd", 3, [64, 512], F32)
        t1r = self.ring(es, "ns_t1", 3, [64, 512], F32)
        impq = self.sb(es, "ns_impq", [128, 4, 64], F32)
        selq = self.sb(es, "ns_selq", [128, 4, 64], F32)
        wk = self.sb(es, "ns_wk", [128, 64], F32)
        m8 = self.sb(es, "ns_m8", [128, 16], F32)
        obr = self.ring(es, "ns_ob", 2, [64, 512], BF16)
        XBr = [[self.sb(es, f"ns_xb{k}_{i}", [128, 512], BF16) for i in range(13)] for k in range(1)]
        pend = []
        eng_alt = [0]

        def flush():
            while pend:
                pend.pop(0)()

        def hankel(vname, h, c, pstep):
            H = Hr.next()
            src = self.dap(vname, h * LVEC + c, [[pstep, 128], [1, 512]])
            tk.dma("sp", H[:], src, reads=[self.dbuf[vname]], writes=[H.b])
            return H

        def ratio(pn):
            rd = rdr.next()
            tk.op("dve", lambda: nc.vector.tensor_scalar_max(rd[:], pn[64:128, :], 1e-30), reads=[pn.b], writes=[rd.b])
            self.rpow(rd[:], rd.b, -1.0)
            t1 = t1r.next()
            tk.op("dve", lambda: nc.vector.tensor_tensor(out=t1[:], in0=pn[0:64, :], in1=rd[:], op=ALU.mult), reads=[pn.b, rd.b], writes=[t1.b])
            return rd, t1

        def key_tile(s_mms, e_ap, e_buf, act_bias, pv, mult=None):
            p = gen.next()
            for i, (lhsT, lb, rhs, rb) in enumerate(s_mms):
                mm(p[:], p.b, lhsT, lb, rhs, rb, i == 0, i == len(s_mms) - 1)
            if mult is None:
                if act_bias is None:
                    tk.op("act", lambda: nc.scalar.activation(out=e_ap, in_=p[:], func=AF.Exp), reads=[p.b], writes=[e_buf])
                else:
                    tk.op("act", lambda: nc.scalar.activation(out=e_ap, in_=p[:], func=AF.Exp, bias=act_bias), reads=[p.b, bfar.b], writes=[e_buf])
            else:
                E0 = E2r.next()
                tk.op("act", lambda: nc.scalar.activation(out=E0[:], in_=p[:], func=AF.Exp), reads=[p.b], writes=[E0.b])
                eng_alt[0] += 1
                if False:
                    tk.op("pool", lambda: nc.gpsimd.tensor_tensor(out=e_ap, in0=E0[:], in1=mult[:], op=ALU.mult), reads=[E0.b, mult.b], writes=[e_buf])
                else:
                    tk.op("dve", lambda: nc.vector.tensor_tensor(out=e_ap, in0=E0[:], in1=mult[:], op=ALU.mult), reads=[E0.b, mult.b], writes=[e_buf])
            while len(pend) >= SKEW:
                pend.pop(0)()
            pend.append(pv)

        SKEW = getattr(self, "ns_skew", 3)
        for g in range(4):
            flush()
            tk.dma("sp", ksX[0:64, :], self.din["ks_fm"].ap()[64 * g:64 * g + 64, :], reads=[self.dbuf["ks_fm"]], writes=[ksX.b])
            tk.dma("sp", kwX[0:64, :], self.din["kw_fm"].ap()[64 * g:64 * g + 64, :], reads=[self.dbuf["kw_fm"]], writes=[kwX.b])
            for k8 in range(4):
                tk.dma("sp", Vs[:, 8 * k8:8 * k8 + 8, 0:64], self.dap("vsw_tm", 64 * g + 8 * k8 * 128 * 512, [[512, 128], [128 * 512, 8], [1, 64]]),
                       reads=[self.dbuf["vsw_tm"]], writes=[Vs.b])
                tk.dma("sp", Vw[:, 8 * k8:8 * k8 + 8, 0:64], self.dap("vsw_tm", 256 + 64 * g + 8 * k8 * 128 * 512, [[512, 128], [128 * 512, 8], [1, 64]]),
                       reads=[self.dbuf["vsw_tm"]], writes=[Vw.b])
            for qt in range(T // 512):
                t0 = qt * 512
                qT = qTr.next()
                tk.dma("sp", qT[0:64, :, :], self.dap("q_fm", 256 * g * T + t0, [[T, 64], [64 * T, 4], [1, 512]]), reads=[self.dbuf["q_fm"]], writes=[qT.b])
                gb = gbr.next()
                for j in range(4):
                    row = 12 * g + 3 * j
                    tk.dma("sp", gb[:, j, :], self.din["gates_fm"].ap()[row:row + 1, t0:t0 + 512].partition_broadcast(64),
                           reads=[self.dbuf["gates_fm"]], writes=[gb.b])
                fr = frc.next()
                tk.dma("sp", fr[:], self.din["c_forced"].ap()[:, t0:t0 + 512], reads=[self.dbuf["c_forced"]], writes=[fr.b])
                nnt = 2 if t0 >= 2048 else 1
                for hg in range(4):
                    h = 4 * g + hg
                    pn, pi = accp.next(), accp.next()
                    for nt in range(nnt):
                        H = hankel("bvec_c", h, OFFC + t0 - 16 * 128 * nt - 2063, 16)
                        e_ap = Ec[:, hg * 2 + nt, :]

                        def pv(pn=pn, pi=pi, nt=nt, e_ap=e_ap, nnt=nnt):
                            mm(pn[:], pn.b, vco[g][:, nt, :], vco[g].b, e_ap, Ec.b, nt == 0, nt == nnt - 1)
                            mm(pi[0:64, :], pi.b, c2s[:, nt, :], c2s.b, e_ap, Ec.b, nt == 0, nt == nnt - 1)
                        key_tile([(kcmpT[g][:, nt * 128:(nt + 1) * 128], kcmpT[g].b, qT[:, hg, :], qT.b), (Jb[:], Jb.b, H[:], H.b)], e_ap, Ec.b, None, pv)

                    def fin(pn=pn, pi=pi, hg=hg, gb=gb, qt=qt):
                        rd, t1 = ratio(pn)
                        tk.op("pool", lambda: nc.gpsimd.tensor_tensor(out=accC[:, hg, qt, :], in0=t1[:], in1=gb[:, hg, :], op=ALU.mult),
                              reads=[t1.b, gb.b], writes=[accC.b])
                        if hg == 0:
                            tk.op("dve", lambda: nc.vector.tensor_tensor(out=impa[:], in0=pi[0:64, :], in1=rd[:], op=ALU.mult), reads=[pi.b, rd.b], writes=[impa.b])
                        else:
                            t2 = t1r.next()
                            tk.op("dve", lambda: nc.vector.tensor_tensor(out=t2[:], in0=pi[0:64, :], in1=rd[:], op=ALU.mult), reads=[pi.b, rd.b], writes=[t2.b])
                            tk.op("pool", lambda: nc.gpsimd.tensor_tensor(out=impa[:], in0=impa[:], in1=t2[:], op=ALU.add), reads=[impa.b, t2.b], writes=[impa.b])
                    pend.append(fin)
                flush()
                tk.op("dve", lambda: nc.vector.tensor_tensor(out=impa[:], in0=impa[:], in1=fr[:], op=ALU.max), reads=[impa.b, fr.b], writes=[impa.b])
                p = gen.next()
                for s4 in range(4):
                    tk.op("pe", lambda: nc.tensor.transpose(p[:, s4 * 64:(s4 + 1) * 64], impa[:, s4 * 128:(s4 + 1) * 128], ident[0:64, 0:64]),
                          reads=[impa.b, ident.b], writes=[p.b])
                tk.op("act", lambda: nc.scalar.copy(impq[:], p[:, 0:256].rearrange("p (a b) -> p a b", a=4)), reads=[p.b], writes=[impq.b])
                for s4 in range(4):
                    tk.op("dve", lambda: nc.vector.max(out=m8[:, 0:8], in_=impq[:, s4, :]), reads=[impq.b], writes=[m8.b])
                    tk.op("dve", lambda: nc.vector.match_replace(out=wk[:], in_to_replace=m8[:, 0:8], in_values=impq[:, s4, :], imm_value=-1e30),
                          reads=[impq.b, m8.b], writes=[wk.b])
                    tk.op("dve", lambda: nc.vector.max(out=m8[:, 8:16], in_=wk[:]), reads=[wk.b], writes=[m8.b])
                    tk.op("dve", lambda: nc.vector.tensor_scalar(selq[:, s4, :], impq[:, s4, :], m8[:, 15:16], 1.0, ALU.is_ge, ALU.subtract),
                          reads=[impq.b, m8.b], writes=[selq.b])
                p = gen.next()
                for s4 in range(4):
                    tk.op("pe", lambda: nc.tensor.transpose(p[0:64, s4 * 128:(s4 + 1) * 128], selq[:, s4, :], ident[:]),
                          reads=[selq.b, ident.b], writes=[p.b])
                tk.op("act", lambda: nc.scalar.copy(QSr.tiles[0][64:128, qt * 512:(qt + 1) * 512], p[0:64, :]), reads=[p.b], writes=[QSr.tiles[0].b])
                tk.op("dve", lambda: nc.vector.tensor_copy(QSr.tiles[1][64:128, qt * 512:(qt + 1) * 512], p[0:64, :]), reads=[p.b], writes=[QSr.tiles[1].b])
            for hg in range(4):
                h = 4 * g + hg
                flush()
                XB = XBr[0]
                xw, xs = {}, {}
                for i, (vname, d) in enumerate([("bvec_w", dd_) for dd_ in range(512, -385, -128)] + [("bvec_c", dd_) for dd_ in range(128, -385, -128)]):
                    H = hankel(vname, h, OFFC + d - 127, 1)
                    p = gen.next()
                    mm(p[:], p.b, Jb[:], Jb.b, H[:], H.b)
                    tk.op("act", lambda: nc.scalar.activation(out=XB[i][:], in_=p[:], func=AF.Exp), reads=[p.b], writes=[XB[i].b])
                    (xw if vname == "bvec_w" else xs)[d] = XB[i]
                q_ = QSr.next()
                tk.dma("sp", q_[0:64, :], self.din["q_fm"].ap()[64 * h:64 * h + 64, :], reads=[self.dbuf["q_fm"]], writes=[q_.b])
                for qt in range(T // 512):
                    t0 = qt * 512
                    qs = q_[:, t0:t0 + 512]
                    gb = gbr.next()
                    for j in range(2):
                        row = 3 * h + 1 + j
                        tk.dma("sp", gb[:, j, :], self.din["gates_fm"].ap()[row:row + 1, t0:t0 + 512].partition_broadcast(64),
                               reads=[self.dbuf["gates_fm"]], writes=[gb.b])
                    ac = acc.next()
                    kts = list(range(max(0, (t0 - 512) // 128), (t0 + 511) // 128 + 1))
                    pn = accp.next()
                    for i, kt in enumerate(kts):
                        E = Er.next()

                        def pv(pn=pn, kt=kt, E=E, first=(i == 0), last=(i == len(kts) - 1)):
                            mm(pn[:], pn.b, Vw[:, kt, :], Vw.b, E[:], E.b, first, last)
                        key_tile([(kwX[:, kt * 128:(kt + 1) * 128], kwX.b, qs, q_.b)], E[:], E.b, None, pv, mult=xw[t0 - 128 * kt])

                    def finw(pn=pn, gb=gb, ac=ac):
                        rd, t1 = ratio(pn)
                        tk.op("dve", lambda: nc.vector.tensor_tensor(out=ac[:], in0=t1[:], in1=gb[:, 1, :], op=ALU.mult), reads=[t1.b, gb.b], writes=[ac.b])
                    pend.append(finw)
                    kts = list(range(0, (t0 + 511) // 128 + 1))
                    pn = accp.next()
                    for i, kt in enumerate(kts):
                        far = (128 * kt <= t0 - 256)
                        E = Er.next()
                        s_mms = [(ksX[:, kt * 128:(kt + 1) * 128], ksX.b, qs, q_.b)]

                        def pv(pn=pn, kt=kt, E=E, first=(i == 0), last=(i == len(kts) - 1)):
                            mm(pn[:], pn.b, Vs[:, kt, :], Vs.b, E[:], E.b, first, last)
                        key_tile(s_mms, E[:], E.b, bfar[:, h:h + 1] if far else None, pv, mult=None if far else xs[t0 - 128 * kt])

                    def fins(pn=pn, gb=gb, ac=ac, hg=hg, h=h, t0=t0, qt=qt):
                        rd, t1 = ratio(pn)
                        tk.op("dve", lambda: nc.vector.tensor_tensor(out=t1[:], in0=t1[:], in1=gb[:, 0, :], op=ALU.mult), reads=[t1.b, gb.b], writes=[t1.b])
                        tk.op("dve", lambda: nc.vector.tensor_tensor(out=ac[:], in0=ac[:], in1=t1[:], op=ALU.add), reads=[t1.b, ac.b], writes=[ac.b])
                        o = obr.next()
                        tk.op("dve", lambda: nc.vector.tensor_tensor(out=o[:], in0=ac[:], in1=accC[:, hg, qt, :], op=ALU.add), reads=[ac.b, accC.b], writes=[o.b])
                        tk.dma("pool", self.din["yn_fm"].ap()[64 * h:64 * h + 64, t0:t0 + 512], o[:], reads=[o.b], writes=[self.dbuf["yn_fm"]])
                    pend.append(fins)
        flush()


Prog.nsa_bias = _nsa_bias
Prog.phase_nsa = _nsa


def _proj_tm_res(self, actT, tok0, ntok, kchunks, w, res_name, res_row0, dst_name, dst_row0, es):
    nc, tk = self.nc, self.tk
    xr = self.ring(es, "pt_x", 2, [128, D], F32)
    orr = self.ring(es, "pt_o", 2, [128, D], F32)
    for i in range(ntok // 128):
        x = xr.next()
        o = orr.next()
        tk.dma("sp", x[:], self.din[res_name].ap()[res_row0 + i * 128:res_row0 + (i + 1) * 128, :], reads=[self.dbuf[res_name]], writes=[x.b])
        for half in range(2):
            p = self.psum()
            for kc in range(kchunks):
                tk.op("pe", lambda: nc.tensor.matmul(p[:], lhsT=actT[:, kc, tok0 + i * 128:tok0 + (i + 1) * 128], rhs=w[:, kc, half * 512:(half + 1) * 512],
                                                      start=(kc == 0), stop=(kc == kchunks - 1)), reads=[actT.b, w.b], writes=[p.b])
            tk.op("dve", lambda: nc.vector.tensor_tensor(out=o[:, half * 512:(half + 1) * 512], in0=p[:], in1=x[:, half * 512:(half + 1) * 512], op=ALU.add),
                  reads=[p.b, x.b], writes=[o.b])
        tk.dma("pool", self.din[dst_name].ap()[dst_row0 + i * 128:dst_row0 + (i + 1) * 128, :], o[:], reads=[o.b], writes=[self.dbuf[dst_name]])


def _merge(self, bi):
    nc, tk = self.nc, self.tk
    self.scratch("h1", [T, D], F32)
    with self.scope() as es:
        mT = self.sb(es, "mg_mT", [128, 8, T], BF16, dj=True)
        with self.scope() as es2:
            wr = self.sb(es2, "mg_wr", [128, 8, 1024], BF16)
            wn = self.sb(es2, "mg_wn", [128, 8, 1024], BF16)
            self.load_w(wr, "w_branch_rwkv_bf", 0, 1024)
            self.load_w(wn, "w_branch_nsa_bf", 0, 1024)
            yr = self.ring(es2, "mg_yr", 2, [128, 8, 512], BF16)
            yn = self.ring(es2, "mg_yn", 2, [128, 8, 512], BF16)
            gr = self.ring(es2, "mg_g", 4, [128, 512], F32)
            tr = self.ring(es2, "mg_t", 4, [128, 512], F32)
            for tt in range(T // 512):
                a, b = yr.next(), yn.next()
                tk.dma("sp", a[:], self.dap("yr_fm", tt * 512, [[T, 128], [128 * T, 8], [1, 512]]), reads=[self.dbuf["yr_fm"]], writes=[a.b])
                tk.dma("sp", b[:], self.dap("yn_fm", tt * 512, [[T, 128], [128 * T, 8], [1, 512]]), reads=[self.dbuf["yn_fm"]], writes=[b.b])
                for ci in range(8):
                    g0, g1 = gr.next(), gr.next()
                    tk.dma("sp", g0[:], self.din["gm_fm"].ap()[ci * 128:(ci + 1) * 128, tt * 512:(tt + 1) * 512], reads=[self.dbuf["gm_fm"]], writes=[g0.b])
                    tk.dma("sp", g1[:], self.din["gm_fm"].ap()[1024 + ci * 128:1024 + (ci + 1) * 128, tt * 512:(tt + 1) * 512], reads=[self.dbuf["gm_fm"]], writes=[g1.b])
                    pr, pn = self.psum(), self.psum()
                    for kc in range(8):
                        tk.op("pe", lambda: nc.tensor.matmul(pr[:], lhsT=wr[:, kc, ci * 128:(ci + 1) * 128], rhs=a[:, kc, :], start=(kc == 0), stop=(kc == 7)),
                              reads=[wr.b, a.b], writes=[pr.b])
                    for kc in range(8):
                        tk.op("pe", lambda: nc.tensor.matmul(pn[:], lhsT=wn[:, kc, ci * 128:(ci + 1) * 128], rhs=b[:, kc, :], start=(kc == 0), stop=(kc == 7)),
                              reads=[wn.b, b.b], writes=[pn.b])
                    t0_, t1_ = tr.next(), tr.next()
                    tk.op("dve", lambda: nc.vector.tensor_tensor(out=t0_[:], in0=pr[:], in1=g0[:], op=ALU.mult), reads=[pr.b, g0.b], writes=[t0_.b])
                    tk.op("dve", lambda: nc.vector.tensor_tensor(out=t1_[:], in0=pn[:], in1=g1[:], op=ALU.mult), reads=[pn.b, g1.b], writes=[t1_.b])
                    tk.op("pool", lambda: nc.gpsimd.tensor_tensor(out=mT[:, ci, tt * 512:(tt + 1) * 512], in0=t0_[:], in1=t1_[:], op=ALU.add),
                          reads=[t0_.b, t1_.b], writes=[mT.b])
        with self.scope() as es3:
            wm = self.sb(es3, "mg_wm", [128, 8, 1024], BF16)
            self.load_w(wm, "w_mix_out_bf", 0, 1024)
            self.proj_tm_res(mT, 0, T, 8, wm, "x", bi * T, "h1", 0, es3)


def _cross(self, bi):
    nc, tk = self.nc, self.tk
    self.scratch("h2", [T, D], F32)
    HT = 2048
    with self.scope() as es:
        wq = self.sb(es, "ca_wq", [128, 8, 1024], BF16)
        wo = self.sb(es, "ca_wo", [128, 8, 1024], BF16)
        self.load_w(wq, "ca_wq_bf", 0, 1024)
        self.load_w(wo, "ca_wo_bf", 0, 1024)
        kT = self.sb(es, "ca_kT", [128, 8, NMEM], BF16, dj=True)
        Vc = self.sb(es, "ca_V", [128, 2, 1024], BF16, dj=True)
        qgain = self.col_vec(es, "ca_q_gain", 0, 0, 2, "ca_qg")
        kgain = self.col_vec(es, "ca_k_gain", 0, 0, 2, "ca_kg")
        ones_f = self.load_const(es, "c_ones")
        ones_b = self.sb(es, "ca_1b", [128, 128], BF16)
        tk.op("dve", lambda: nc.vector.tensor_copy(ones_b[:], ones_f[:]), reads=[ones_f.b], writes=[ones_b.b])
        sqr = self.ring(es, "ca_sq", 2, [128, 2, 512], F32)
        rr = self.ring(es, "ca_r", 4, [128, 512], F32)
        tmpr = self.ring(es, "ca_tmp", 2, [128, 512], F32)
        qh = self.ring(es, "ca_qh", 3, [128, 2, 512], BF16)
        Er = self.ring(es, "ca_E", 2, [128, 2, 512], BF16)

        def qk_norm(p0, p1, n, gain, scale, out_aps, out_buf):
            s = sqr.next()
            tk.op("act", lambda: nc.scalar.activation(out=s[:, 0, 0:n], in_=p0[:, 0:n], func=AF.Square), reads=[p0.b], writes=[s.b])
            tk.op("act", lambda: nc.scalar.activation(out=s[:, 1, 0:n], in_=p1[:, 0:n], func=AF.Square), reads=[p1.b], writes=[s.b])
            p2 = self.psum()
            for j in range(2):
                tk.op("pe", lambda: nc.tensor.matmul(p2[:, 0:n], lhsT=ones_f[:], rhs=s[:, j, 0:n], start=(j == 0), stop=(j == 1)),
                      reads=[ones_f.b, s.b], writes=[p2.b])
            r = rr.next()
            tk.op("dve", lambda: nc.vector.tensor_scalar(r[:, 0:n], p2[:, 0:n], 1.0 / 256, 1e-6, ALU.mult, ALU.add), reads=[p2.b], writes=[r.b])
            self.rpow(r[:, 0:n], r.b, -0.5)
            for j, pj in enumerate((p0, p1)):
                t = tmpr.next()
                tk.op("dve", lambda: nc.vector.tensor_tensor(out=t[:, 0:n], in0=pj[:, 0:n], in1=r[:, 0:n], op=ALU.mult), reads=[pj.b, r.b], writes=[t.b])
                tk.op("dve", lambda: nc.vector.tensor_scalar(out_aps[j], t[:, 0:n], gain[:, j:j + 1], scale, ALU.mult, ALU.mult),
                      reads=[t.b, gain.b], writes=[out_buf])

        with self.scope() as es2:
            mnT = self.sb(es2, "ca_mnT", [128, 8, NMEM], BF16, dj=True)
            self.norm_T("mem", bi * NMEM, NMEM, "norm_mem", mnT)
            wk = self.sb(es2, "ca_wk", [128, 8, 1024], BF16)
            wv = self.sb(es2, "ca_wv", [128, 8, 1024], BF16)
            self.load_w(wk, "ca_wkv_bf", 0, 1024)
            self.load_w(wv, "ca_wkv_bf", 1024, 1024)
            for h in range(4):
                ps_ = []
                for j in range(2):
                    p = self.psum()
                    ci = 2 * h + j
                    for kc in range(8):
                        tk.op("pe", lambda: nc.tensor.matmul(p[:, 0:NMEM], lhsT=wk[:, kc, ci * 128:(ci + 1) * 128], rhs=mnT[:, kc, :], start=(kc == 0), stop=(kc == 7)),
                              reads=[wk.b, mnT.b], writes=[p.b])
                    ps_.append(p)
                qk_norm(ps_[0], ps_[1], NMEM, kgain, 1.0, [kT[:, 2 * h, :], kT[:, 2 * h + 1, :]], kT.b)
            for mt in range(2):
                for half in range(2):
                    p = self.psum()
                    for kc in range(8):
                        tk.op("pe", lambda: nc.tensor.matmul(p[:], lhsT=mnT[:, kc, mt * 128:(mt + 1) * 128], rhs=wv[:, kc, half * 512:(half + 1) * 512],
                                                              start=(kc == 0), stop=(kc == 7)), reads=[mnT.b, wv.b], writes=[p.b])
                    tk.op("act", lambda: nc.scalar.copy(Vc[:, mt, half * 512:(half + 1) * 512], p[:]), reads=[p.b], writes=[Vc.b])
        for hf in range(T // HT):
            with self.scope() as es2:
                hnT = self.sb(es2, "ca_hnT", [128, 8, HT], BF16, dj=True)
                oT = self.sb(es2, "ca_oT", [128, 8, HT], BF16, dj=True)
                self.norm_T("h1", hf * HT, HT, "norm_cross", hnT)
                def stage_q(h, tt):
                    ps_ = []
                    for j in range(2):
                        p = self.psum()
                        ci = 2 * h + j
                        for kc in range(8):
                            tk.op("pe", lambda: nc.tensor.matmul(p[:], lhsT=wq[:, kc, ci * 128:(ci + 1) * 128], rhs=hnT[:, kc, tt * 512:(tt + 1) * 512],
                                                                  start=(kc == 0), stop=(kc == 7)), reads=[wq.b, hnT.b], writes=[p.b])
                        ps_.append(p)
                    q = qh.next()
                    qk_norm(ps_[0], ps_[1], 512, qgain, 1.0 / 16, [q[:, 0, :], q[:, 1, :]], q.b)
                    return q

                def stage_att(h, tt, q):
                    E = Er.next()
                    for mt in range(2):
                        p = self.psum()
                        for j in range(2):
                            tk.op("pe", lambda: nc.tensor.matmul(p[:], lhsT=kT[:, 2 * h + j, mt * 128:(mt + 1) * 128], rhs=q[:, j, :], start=(j == 0), stop=(j == 1)),
                                  reads=[kT.b, q.b], writes=[p.b])
                        tk.op("act", lambda: nc.scalar.activation(out=E[:, mt, :], in_=p[:], func=AF.Exp), reads=[p.b], writes=[E.b])
                    pd = self.psum()
                    for mt in range(2):
                        tk.op("pe", lambda: nc.tensor.matmul(pd[:], lhsT=ones_b[:], rhs=E[:, mt, :], start=(mt == 0), stop=(mt == 1)),
                              reads=[ones_b.b, E.b], writes=[pd.b])
                    r = rr.next()
                    tk.op("act", lambda: nc.scalar.activation(out=r[:], in_=pd[:], func=AF.Ln), reads=[pd.b], writes=[r.b])
                    tk.op("act", lambda: nc.scalar.activation(out=r[:], in_=r[:], func=AF.Exp, scale=-1.0), reads=[r.b], writes=[r.b])
                    for j in range(2):
                        pn = self.psum()
                        for mt in range(2):
                            tk.op("pe", lambda: nc.tensor.matmul(pn[:], lhsT=Vc[:, mt, h * 256 + j * 128:h * 256 + (j + 1) * 128], rhs=E[:, mt, :],
                                                                  start=(mt == 0), stop=(mt == 1)), reads=[Vc.b, E.b], writes=[pn.b])
                        tk.op("dve", lambda: nc.vector.tensor_tensor(out=oT[:, 2 * h + j, tt * 512:(tt + 1) * 512], in0=pn[:], in1=r[:], op=ALU.mult),
                              reads=[pn.b, r.b], writes=[oT.b])

                its = [(h, tt) for h in range(4) for tt in range(HT // 512)]
                prev = None
                for (h, tt) in its:
                    q = stage_q(h, tt)
                    if prev is not None:
                        stage_att(*prev)
                    prev = (h, tt, q)
                stage_att(*prev)
                self.proj_tm_res(oT, 0, HT, 8, wo, "h1", hf * HT, "h2", hf * HT, es2)


def _ffn(self, bi):
    nc, tk = self.nc, self.tk
    self.scratch("ff_fm", [DFF, T], BF16)
    NCT = DFF // 128
    with self.scope() as es:
        hnT = self.sb(es, "ff_hnT", [128, 8, T], BF16, dj=True)
        self.norm_T("h2", 0, T, "norm_ffn", hnT)
        cw = [self.col_vec(es, "ffn_conv", j, 0, NCT, f"ff_cw{j}") for j in range(3)]
        cb = self.col_vec(es, "ffn_conv_b", 0, 0, NCT, "ff_cb")
        wring = self.ring(es, "ff_w", 4, [128, 8, 128], BF16)
        atr = self.ring(es, "ff_a", 2, [128, T + 2], F32)
        btr = self.ring(es, "ff_b", 2, [128, T], F32)
        acc = self.sb(es, "ff_acc", [128, T], F32)
        ob = self.ring(es, "ff_ob", 2, [128, T], BF16)
        for at in atr.tiles:
            tk.op("pool", lambda: nc.gpsimd.memset(at[:, 0:2], 0.0), writes=[at.b])
        for ci in range(NCT):
            at, bt = atr.next(), btr.next()
            wa, wb = wring.next(), wring.next()
            self.load_w(wa, "ffn_up_bf", ci * 128, 128)
            self.load_w(wb, "ffn_up_bf", DFF + ci * 128, 128)
            for tt in range(T // 512):
                pa, pb = self.psum(), self.psum()
                for kc in range(8):
                    tk.op("pe", lambda: nc.tensor.matmul(pa[:], lhsT=wa[:, kc, :], rhs=hnT[:, kc, tt * 512:(tt + 1) * 512], start=(kc == 0), stop=(kc == 7)),
                          reads=[wa.b, hnT.b], writes=[pa.b])
                for kc in range(8):
                    tk.op("pe", lambda: nc.tensor.matmul(pb[:], lhsT=wb[:, kc, :], rhs=hnT[:, kc, tt * 512:(tt + 1) * 512], start=(kc == 0), stop=(kc == 7)),
                          reads=[wb.b, hnT.b], writes=[pb.b])
                tk.op("act", lambda: nc.scalar.copy(at[:, 2 + tt * 512:2 + (tt + 1) * 512], pa[:]), reads=[pa.b], writes=[at.b])
                tk.op("dve", lambda: nc.vector.tensor_copy(bt[:, tt * 512:(tt + 1) * 512], pb[:]), reads=[pb.b], writes=[bt.b])
            tk.op("dve", lambda: nc.vector.tensor_scalar(acc[:], at[:, 2:T + 2], cw[2][:, ci:ci + 1], cb[:, ci:ci + 1], ALU.mult, ALU.add),
                  reads=[at.b, cw[2].b, cb.b], writes=[acc.b])
            tk.op("dve", lambda: nc.vector.scalar_tensor_tensor(out=acc[:], in0=at[:, 1:T + 1], scalar=cw[1][:, ci:ci + 1], in1=acc[:], op0=ALU.mult, op1=ALU.add),
                  reads=[at.b, cw[1].b, acc.b], writes=[acc.b])
            tk.op("dve", lambda: nc.vector.scalar_tensor_tensor(out=acc[:], in0=at[:, 0:T], scalar=cw[0][:, ci:ci + 1], in1=acc[:], op0=ALU.mult, op1=ALU.add),
                  reads=[at.b, cw[0].b, acc.b], writes=[acc.b])
            tk.op("act", lambda: nc.scalar.activation(out=acc[:], in_=acc[:], func=AF.Silu), reads=[acc.b], writes=[acc.b])
            o = ob.next()
            tk.op("dve", lambda: nc.vector.tensor_tensor(out=o[:], in0=acc[:], in1=bt[:], op=ALU.mult), reads=[acc.b, bt.b], writes=[o.b])
            tk.dma("pool", self.din["ff_fm"].ap()[ci * 128:(ci + 1) * 128, :], o[:], reads=[o.b], writes=[self.dbuf["ff_fm"]])
    with self.scope() as es:
        wd = self.sb(es, "ff_wd", [128, NCT, 1024], BF16)
        self.load_w(wd, "ffn_down_bf", 0, 1024, kchunks=NCT)
        TBK = 1024
        for blk in range(T // TBK):
            with self.scope() as es2:
                fT = self.sb(es2, "ff_fT", [128, NCT, TBK], BF16)
                tk.dma("sp", fT[:], self.dap("ff_fm", blk * TBK, [[T, 128], [128 * T, NCT], [1, TBK]]), reads=[self.dbuf["ff_fm"]], writes=[fT.b])
                self.proj_tm_res(fT, 0, TBK, NCT, wd, "h2", blk * TBK, "out", bi * T + blk * TBK, es2)


Prog.proj_tm_res = _proj_tm_res
Prog.phase_merge = _merge
Prog.phase_cross = _cross
Prog.phase_ffn = _ffn
```

```python
import contextlib
import math
import numpy as np
import concourse.bass as bass
import concourse.mybir as mybir
from concourse.bass_utils import run_bass_kernel_spmd

F32 = mybir.dt.float32
BF16 = mybir.dt.bfloat16
AF = mybir.ActivationFunctionType
ALU = mybir.AluOpType
AX = mybir.AxisListType

NCORES = 8
NB = 2
T = 4096
D = 1024
NMEM = 256
DFF = 2816
IN_COLS = 8016
BIG = 30000.0
OFFC = 2176
LVEC = 7680
SCALE_NSA = 0.125


class Buf:
    __slots__ = ("w", "r", "name", "dj", "xr")

    def __init__(self, name="", dj=False):
        self.w = {}
        self.r = {}
        self.name = name
        self.dj = dj
        self.xr = False


class Tile:
    def __init__(self, t, name, dj=False):
        self.t = t
        self.b = Buf(name, dj)

    def __getitem__(self, k):
        return self.t[k]


class Ring:
    def __init__(self, tiles):
        self.tiles = tiles
        self.i = 0

    def next(self):
        t = self.tiles[self.i]
        self.i = (self.i + 1) % len(self.tiles)
        return t


class TK:
    EPOCH = 20000
    NDSEM = 10

    def __init__(self, nc, es):
        self.nc = nc
        self.es = es
        self.eng = {"pe": nc.tensor, "act": nc.scalar, "dve": nc.vector,
                    "pool": nc.gpsimd, "sp": nc.sync}
        self.cnt = {e: 0 for e in self.eng}
        self.esem = {e: [] for e in self.eng}
        self.seen = {e: {} for e in self.eng}
        self.dsem = {}
        self.dptr = {}
        self.nwait = 0
        self.fence = {}

    def _newsem(self, name):
        return self.es.enter_context(self.nc.semaphore(name))

    def _engsem(self, e, epoch):
        while len(self.esem[e]) <= epoch:
            self.esem[e].append(self._newsem(f"s_{e}_{len(self.esem[e])}"))
        return self.esem[e][epoch]

    def _wait(self, e, ts):
        sem, val, src = ts
        if src == "pe" and e == "pe":
            return
        k = id(sem)
        if self.seen[e].get(k, 0) >= val:
            return
        self.seen[e][k] = val
        self.eng[e].wait_ge(sem, val)
        self.nwait += 1

    def deps(self, e, reads, writes):
        for b in reads:
            for ts in b.w.values():
                self._wait(e, ts)
            if b.xr:
                for ts in b.r.values():
                    if ts[2] != e:
                        self._wait(e, ts)
        for b in writes:
            if not (b.dj and not b.r):
                for ts in b.w.values():
                    self._wait(e, ts)
            for ts in b.r.values():
                self._wait(e, ts)

    def mark(self, ts, reads, writes):
        k = id(ts[0])
        for b in reads:
            b.r[k] = ts
        for b in writes:
            if b.dj and not b.r:
                b.w[k] = ts
            else:
                b.w = {k: ts}
                b.r = {}

    def op(self, e, ins_fn, reads=(), writes=()):
        self.deps(e, reads, writes)
        n = self.cnt[e]
        sem = self._engsem(e, n // self.EPOCH)
        val = n % self.EPOCH + 1
        ins_fn().then_inc(sem, 1)
        self.cnt[e] = n + 1
        ts = (sem, val, e)
        self.mark(ts, reads, writes)
        return ts

    def dma(self, q, out_ap, in_ap, reads=(), writes=(), **kw):
        if q not in self.dsem:
            self.dsem[q] = [[self._newsem(f"d_{q}_{i}"), 0] for i in range(self.NDSEM)]
            self.dptr[q] = 0
        slot = self.dsem[q][self.dptr[q]]
        self.dptr[q] = (self.dptr[q] + 1) % self.NDSEM
        sem, issued = slot
        if issued:
            self._wait(q, (sem, 16 * issued, None))
        self.deps(q, reads, writes)
        self.eng[q].dma_start(out=out_ap, in_=in_ap, **kw).then_inc(sem, 16)
        slot[1] = issued + 1
        ts = (sem, 16 * (issued + 1), None)
        self.mark(ts, reads, writes)
        return ts

    def update_fence(self):
        f = {}
        for e in self.eng:
            n = self.cnt[e]
            if n:
                sem = self.esem[e][(n - 1) // self.EPOCH]
                f[id(sem)] = (sem, (n - 1) % self.EPOCH + 1, e)
        for q in self.dsem:
            for sem, issued in self.dsem[q]:
                if issued:
                    f[id(sem)] = (sem, 16 * issued, None)
        self.fence = f

    def drain(self):
        for q in self.dsem:
            for sem, issued in self.dsem[q]:
                if issued:
                    self._wait(q, (sem, 16 * issued, None))


def _t5_bucket_np(dist):
    n = np.maximum(dist, 0)
    nf = np.maximum(n, 1).astype(np.float64)
    large = 16 + (np.log(nf / 16) / math.log(128 / 16) * 16).astype(np.int64)
    large = np.minimum(large, 31)
    return np.where(n < 16, n, large)


def host_consts():
    c = {}
    c["c_ident"] = np.eye(128, dtype=np.float32)
    c["c_J"] = np.ascontiguousarray(np.eye(128, dtype=np.float32)[::-1])
    hb = np.arange(128) // 64
    bd = (hb[:, None] == hb[None, :]).astype(np.float32)
    c["c_bdones"] = bd
    c["c_ones"] = np.ones((128, 128), np.float32)
    s = np.arange(128) % 64
    strict = bd * (s[:, None] < s[None, :])
    incl = bd * (s[:, None] <= s[None, :])
    c["c_mask2"] = np.concatenate([strict, incl], axis=1).astype(np.float32)
    bd5 = np.zeros((128, 5, 2, 64), np.float32)
    for h in range(2):
        bd5[64 * h:64 * h + 64, :, h, :] = 1.0
    c["c_bdmask5"] = bd5.reshape(128, 640)
    seg = np.ones((128, 1024), np.float32)
    seg[:, ::64] = 0.0
    c["c_segmask"] = seg
    dist = np.arange(LVEC) - OFFC
    bk = _t5_bucket_np(dist)
    oh = np.zeros((33, LVEC), np.float32)
    oh[bk, np.arange(LVEC)] = 1.0
    ec = oh.copy()
    ec[32] = np.where(dist >= 0, 0.0, -BIG)
    ec[:32, dist < 0] = 0.0
    ew = oh.copy()
    ok = (dist >= 0) & (dist < 512)
    ew[32] = np.where(ok, 0.0, -BIG)
    ew[:32, ~ok] = 0.0
    c["c_e33c"] = ec
    c["c_e33w"] = ew
    t = np.arange(T)
    cur = t // 64
    blk = np.arange(64)
    forced = (blk[:, None] == 0) | (blk[:, None] == cur[None, :]) | (blk[:, None] == cur[None, :] - 1)
    c["c_forced"] = np.where(forced, 1e4, 0.0).astype(np.float32)
    ex = np.zeros((64, 32, 128), np.float32)
    for kt in range(32):
        for p in range(128):
            ex[2 * kt + p // 64, kt, p] = 1.0
    c["c_expand"] = ex.reshape(64, 32 * 128)
    ncmp = 255
    ci = np.arange(256)[:, None] * 16
    sj = np.arange(64)[None, :] * 64
    c2s = ((ci <= sj + 63) & (ci + 31 >= sj)).astype(np.float32)
    c2s[ncmp:] = 0.0
    c["c_c2s"] = c2s
    return c


CONST_SHAPES = {k: v.shape for k, v in host_consts().items()}

W_SPECS = [
    ("w_in", 1024, IN_COLS), ("rwkv_w2", 64, 1024), ("rwkv_a2", 64, 1024), ("rwkv_g2", 160, 1024),
    ("cmp_w1_k", 2048, 256), ("cmp_w2_k", 256, 64), ("cmp_w1_v", 2048, 256), ("cmp_w2_v", 256, 64),
    ("w_branch_rwkv", 1024, 1024), ("w_branch_nsa", 1024, 1024), ("w_mix_out", 1024, 1024),
    ("ca_wq", 1024, 1024), ("ca_wkv", 1024, 2048), ("ca_wo", 1024, 1024),
    ("ffn_up", 1024, 2 * DFF), ("ffn_down", DFF, 1024),
]
V_SPECS = [
    ("rel_bias", (32, 16)), ("norm_mix", (1, 1024)), ("rwkv_mu", (1, 3360)), ("rwkv_w0", (1, 1024)),
    ("rwkv_a0", (1, 1024)), ("rwkv_kk", (1, 1024)), ("rwkv_ka", (1, 1024)), ("rwkv_rk", (1, 1024)),
    ("rwkv_lnx_w", (1, 1024)), ("rwkv_lnx_b", (1, 1024)), ("nsa_q_gain", (1, 64)), ("nsa_k_gain", (3, 64)),
    ("cmp_pe_k", (1, 2048)), ("cmp_pe_v", (1, 2048)), ("norm_cross", (1, 1024)), ("norm_mem", (1, 1024)),
    ("ca_q_gain", (1, 256)), ("ca_k_gain", (1, 256)), ("norm_ffn", (1, 1024)),
    ("ffn_conv", (3, DFF)), ("ffn_conv_b", (1, DFF)),
]


class Prog:
    def __init__(self, upto="all", dbg=()):
        self.upto = upto
        self.dbg = set(dbg)
        nc = self.nc = bass.Bass("TRN2", target_bir_lowering=False)
        self.es = contextlib.ExitStack()
        self.tk = TK(nc, self.es)
        self.din = {}
        self.dbuf = {}

    def dram_in(self, name, shape):
        self.din[name] = self.nc.dram_tensor(name, list(shape), F32, kind="ExternalInput")
        self.dbuf[name] = Buf(name, dj=True)
        return self.din[name]

    def scratch(self, name, shape, dt):
        if name in self.din:
            return self.din[name]
        kind = "ExternalOutput" if name in self.dbg else "Internal"
        self.din[name] = self.nc.dram_tensor(name, list(shape), dt, kind=kind)
        self.dbuf[name] = Buf(name, dj=True)
        return self.din[name]

    def sb(self, es, name, shape, dt, dj=False):
        self.uid = getattr(self, "uid", 0) + 1
        name = f"{name}_{self.uid}"
        t = Tile(es.enter_context(self.nc.sbuf_tensor(name, list(shape), dt)), name, dj)
        t.b.r = dict(self.tk.fence)
        return t

    @contextlib.contextmanager
    def scope(self):
        with contextlib.ExitStack() as es:
            yield es
        self.tk.update_fence()

    def ring(self, es, name, n, shape, dt):
        return Ring([self.sb(es, f"{name}{i}", shape, dt) for i in range(n)])

    def psum(self):
        return self.psr.next()

    def rpow(self, ap, buf, power):
        nc, tk = self.nc, self.tk
        tk.op("act", lambda: nc.scalar.activation(out=ap, in_=ap, func=AF.Ln), reads=[buf], writes=[buf])
        tk.op("act", lambda: nc.scalar.activation(out=ap, in_=ap, func=AF.Exp, scale=float(power)), reads=[buf], writes=[buf])

    def dap(self, name, offset, ap):
        return bass.AP(tensor=self.din[name], offset=offset, ap=[list(x) for x in ap])

    def load_const(self, es, name, dt=F32, tmp_es=None):
        nc, tk = self.nc, self.tk
        shp = CONST_SHAPES[name]
        t32 = self.sb(es if dt == F32 else tmp_es, name + "_f", shp, F32)
        tk.dma("sp", t32[:], self.din[name].ap()[:, :], reads=[self.dbuf[name]], writes=[t32.b])
        if dt == F32:
            return t32
        t16 = self.sb(es, name + "_h", shp, BF16)
        tk.op("dve", lambda: nc.vector.tensor_copy(t16[:], t32[:]), reads=[t32.b], writes=[t16.b])
        return t16

    def bcast_vec(self, es, name, row, c0, n, tname):
        t = self.sb(es, tname, [128, n], F32)
        src = self.din[name].ap()[row:row + 1, c0:c0 + n].partition_broadcast(128)
        self.tk.dma("sp", t[:], src, reads=[self.dbuf[name]], writes=[t.b])
        return t

    def col_vec(self, es, name, row, c0, nchunk, tname, p=128):
        nc, tk = self.nc, self.tk
        t = self.sb(es, tname, [p, nchunk], F32)
        ncols = self.din[name].shape[1]
        with self.scope() as es2:
            raw = self.sb(es2, tname + "_raw", [nchunk, p], F32)
            tk.dma("sp", raw[:], self.dap(name, row * ncols + c0, [[p, nchunk], [1, p]]), reads=[self.dbuf[name]], writes=[raw.b])
            ps_ = self.psum()
            tk.op("pe", lambda: nc.tensor.transpose(ps_[:p, 0:nchunk], raw[:], self.ident[:nchunk, :nchunk]),
                  reads=[raw.b, self.ident.b], writes=[ps_.b])
            tk.op("dve", lambda: nc.vector.tensor_copy(t[:], ps_[:p, 0:nchunk]), reads=[ps_.b], writes=[t.b])
        return t

    def phase_w(self):
        nc, tk = self.nc, self.tk
        with self.scope() as es:
            st = self.ring(es, "wst", 3, [128, 2048], F32)
            sh = self.ring(es, "wsh", 3, [128, 2048], BF16)
            k = 0
            for name, R, C in W_SPECS:
                dst = self.scratch(name + "_bf", [R, C], BF16)
                src = self.din[name].ap()
                for r0 in range(0, R, 128):
                    rr = min(128, R - r0)
                    for c0 in range(0, C, 2048):
                        cc = min(2048, C - c0)
                        a = st.next()
                        h = sh.next()
                        tk.dma("sp", a[:rr, :cc], src[r0:r0 + rr, c0:c0 + cc], reads=[self.dbuf[name]], writes=[a.b])
                        e = ("dve", "pool", "act")[k % 3]
                        k += 1
                        if e == "act":
                            tk.op(e, lambda: nc.scalar.copy(h[:rr, :cc], a[:rr, :cc]), reads=[a.b], writes=[h.b])
                        elif e == "dve":
                            tk.op(e, lambda: nc.vector.tensor_copy(h[:rr, :cc], a[:rr, :cc]), reads=[a.b], writes=[h.b])
                        else:
                            tk.op(e, lambda: nc.gpsimd.tensor_copy(h[:rr, :cc], a[:rr, :cc]), reads=[a.b], writes=[h.b])
                        tk.dma("pool", dst.ap()[r0:r0 + rr, c0:c0 + cc], h[:rr, :cc], reads=[h.b],
                               writes=[self.dbuf[name + "_bf"]])

    def norm_T(self, src_name, src_row0, ntok, gname, dstT):
        nc, tk = self.nc, self.tk
        with self.scope() as es:
            gbc = self.bcast_vec(es, gname, 0, 0, D, "nt_g")
            xr = self.ring(es, "nt_x", 2, [128, D], F32)
            xs = self.ring(es, "nt_xs", 2, [128, D], F32)
            junk = self.sb(es, "nt_junk", [128, D], BF16)
            st = self.ring(es, "nt_st", 2, [128, 4], F32)
            src = self.din[src_name].ap()
            for i in range(ntok // 128):
                x = xr.next()
                s = st.next()
                y = xs.next()
                tk.dma("sp", x[:], src[src_row0 + i * 128: src_row0 + (i + 1) * 128, :],
                       reads=[self.dbuf[src_name]], writes=[x.b])
                tk.op("act", lambda: nc.scalar.activation(out=junk[:], in_=x[:], func=AF.Square, accum_out=s[:, 0:1]),
                      reads=[x.b], writes=[junk.b, s.b])
                tk.op("dve", lambda: nc.vector.tensor_scalar(s[:, 1:2], s[:, 0:1], 1.0 / D, 1e-6, ALU.mult, ALU.add),
                      reads=[s.b], writes=[s.b])
                tk.op("act", lambda: nc.scalar.sqrt(s[:, 2:3], s[:, 1:2]), reads=[s.b], writes=[s.b])
                tk.op("dve", lambda: nc.vector.reciprocal(s[:, 3:4], s[:, 2:3]), reads=[s.b], writes=[s.b])
                tk.op("dve", lambda: nc.vector.scalar_tensor_tensor(out=y[:], in0=x[:], scalar=s[:, 3:4], in1=gbc[:],
                                                                    op0=ALU.mult, op1=ALU.mult),
                      reads=[x.b, s.b, gbc.b], writes=[y.b])
                for half in range(2):
                    p = self.psum()
                    for j in range(4):
                        kc = half * 4 + j
                        tk.op("pe", lambda: nc.tensor.transpose(p[:, j * 128:(j + 1) * 128], y[:, kc * 128:(kc + 1) * 128],
                                                                self.ident[:]),
                              reads=[y.b, self.ident.b], writes=[p.b])
                    o = dstT[:, half * 4:half * 4 + 4, i * 128:(i + 1) * 128]
                    pin = p[:, :].rearrange("p (a b) -> p a b", a=4)
                    if half == 0:
                        tk.op("act", lambda: nc.scalar.copy(o, pin), reads=[p.b], writes=[dstT.b])
                    else:
                        tk.op("dve", lambda: nc.vector.tensor_copy(o, pin), reads=[p.b], writes=[dstT.b])

    def load_w(self, tile, wname, c0, ncols, kchunks=8, r0=0):
        C = self.din[wname].shape[1]
        src = self.dap(wname, r0 * C + c0, [[C, 128], [128 * C, kchunks], [1, ncols]])
        self.tk.dma("sp", tile[:, 0:kchunks, 0:ncols], src, reads=[self.dbuf[wname]], writes=[tile.b])

    def proj_fm(self, wname, c0, ncols_total, actT, ntok, epi, kchunks=8, wring=None):
        nc, tk = self.nc, self.tk
        nct = (ncols_total + 127) // 128
        for ci in range(nct):
            cc = min(128, ncols_total - ci * 128)
            w = wring.next()
            self.load_w(w, wname, c0 + ci * 128, cc, kchunks)
            for tt in range(ntok // 512):
                p = self.psum()
                for kc in range(kchunks):
                    tk.op("pe", lambda: nc.tensor.matmul(p[:cc, :], lhsT=w[:, kc, 0:cc],
                                                          rhs=actT[:, kc, tt * 512:(tt + 1) * 512],
                                                          start=(kc == 0), stop=(kc == kchunks - 1)),
                          reads=[w.b, actT.b], writes=[p.b])
                epi(p, ci, tt, cc)

    def phase_b(self, xT):
        nc, tk = self.nc, self.tk
        self.scratch("zr_fm", [3360, T], F32)
        self.scratch("q_fm", [1024, T], BF16)
        self.scratch("kcvc_fm", [512, T], BF16)
        self.scratch("ks_fm", [256, T], BF16)
        self.scratch("kw_fm", [256, T], BF16)
        self.scratch("vsw_tm", [T, 512], BF16)
        self.scratch("gates_fm", [48, T], F32)
        self.scratch("gm_fm", [2048, T], F32)
        with self.scope() as es:
            wring = self.ring(es, "pb_w", 2, [128, 8, 128], BF16)
            o32 = self.ring(es, "pb_o32", 3, [128, 512], F32)
            o16 = self.ring(es, "pb_o16", 3, [128, 512], BF16)
            sq = self.ring(es, "pb_sq", 2, [128, 512], F32)
            qg = self.sb(es, "pb_qg", [128, 4], F32)
            eps = self.sb(es, "pb_eps", [128, 1], F32)
            tk.op("pool", lambda: nc.gpsimd.memset(eps[:], 1e-6), writes=[eps.b])
            for h in range(2):
                tk.dma("sp", qg[64 * h:64 * h + 64, 0:1], self.dap("nsa_q_gain", 0, [[1, 64], [1, 1]]),
                       reads=[self.dbuf["nsa_q_gain"]], writes=[qg.b])
                for j in (1, 2):
                    tk.dma("sp", qg[64 * h:64 * h + 64, j + 1:j + 2], self.dap("nsa_k_gain", 64 * j, [[1, 64], [1, 1]]),
                           reads=[self.dbuf["nsa_k_gain"]], writes=[qg.b])
            cnt = [0]

            def store(dname, row0, t0, tile, rows):
                tk.dma("pool", self.din[dname].ap()[row0:row0 + rows, t0:t0 + 512], tile[:rows, :], reads=[tile.b],
                       writes=[self.dbuf[dname]])

            def epi_copy(dname, row_base, dt):
                def f(p, ci, tt, cc):
                    o = (o32 if dt == F32 else o16).next()
                    cnt[0] += 1
                    if cnt[0] % 2:
                        tk.op("act", lambda: nc.scalar.copy(o[:cc, :], p[:cc, :]), reads=[p.b], writes=[o.b])
                    else:
                        tk.op("dve", lambda: nc.vector.tensor_copy(o[:cc, :], p[:cc, :]), reads=[p.b], writes=[o.b])
                    store(dname, row_base + ci * 128, tt * 512, o, cc)
                return f

            def epi_sig(dname, row_base):
                def f(p, ci, tt, cc):
                    o = o32.next()
                    tk.op("act", lambda: nc.scalar.activation(out=o[:cc, :], in_=p[:cc, :], func=AF.Sigmoid),
                          reads=[p.b], writes=[o.b])
                    store(dname, row_base + ci * 128, tt * 512, o, cc)
                return f

            def epi_norm(dname, row_base, gcol, scale):
                def f(p, ci, tt, cc):
                    s = sq.next()
                    tk.op("act", lambda: nc.scalar.activation(out=s[:], in_=p[:], func=AF.Square), reads=[p.b], writes=[s.b])
                    p2 = self.psum()
                    tk.op("pe", lambda: nc.tensor.matmul(p2[:], lhsT=self.bdones[:], rhs=s[:], start=True, stop=True),
                          reads=[self.bdones.b, s.b], writes=[p2.b])
                    r = o32.next()
                    tk.op("dve", lambda: nc.vector.tensor_scalar(r[:], p2[:], 1.0 / 64, 1e-6, ALU.mult, ALU.add),
                          reads=[p2.b], writes=[r.b])
                    self.rpow(r[:], r.b, -0.5)
                    tk.op("dve", lambda: nc.vector.tensor_tensor(out=r[:], in0=p[:], in1=r[:], op=ALU.mult),
                          reads=[p.b, r.b], writes=[r.b])
                    o = o16.next()
                    tk.op("dve", lambda: nc.vector.tensor_scalar(o[:], r[:], qg[:, gcol:gcol + 1], scale, ALU.mult, ALU.mult),
                          reads=[r.b, qg.b], writes=[o.b])
                    store(dname, row_base + ci * 128, tt * 512, o, cc)
                return f

            segs = [
                (0, 3360, epi_copy("zr_fm", 0, F32)),
                (3360, 1024, epi_norm("q_fm", 0, 0, SCALE_NSA)),
                (4384, 512, epi_copy("kcvc_fm", 0, BF16)),
                (4896, 256, epi_norm("ks_fm", 0, 2, 1.0)),
                (5408, 256, epi_norm("kw_fm", 0, 3, 1.0)),
                (5920, 48, epi_sig("gates_fm", 0)),
                (5968, 2048, epi_sig("gm_fm", 0)),
            ]
            for c0, n, epi in segs:
                self.proj_fm("w_in_bf", c0, n, xT, T, epi, wring=wring)
            wv = self.sb(es, "pb_wv", [128, 8, 512], BF16)
            self.load_w(wv, "w_in_bf", 5152, 256)
            C = IN_COLS
            tk.dma("sp", wv[:, :, 256:512], self.dap("w_in_bf", 5664, [[C, 128], [128 * C, 8], [1, 256]]),
                   reads=[self.dbuf["w_in_bf"]], writes=[wv.b])
            for i in range(T // 128):
                p = self.psum()
                for kc in range(8):
                    tk.op("pe", lambda: nc.tensor.matmul(p[:], lhsT=xT[:, kc, i * 128:(i + 1) * 128], rhs=wv[:, kc, :],
                                                          start=(kc == 0), stop=(kc == 7)), reads=[xT.b, wv.b], writes=[p.b])
                o = o16.next()
                tk.op("act", lambda: nc.scalar.copy(o[:], p[:]), reads=[p.b], writes=[o.b])
                tk.dma("pool", self.din["vsw_tm"].ap()[i * 128:(i + 1) * 128, :], o[:], reads=[o.b], writes=[self.dbuf["vsw_tm"]])

    def build(self):
        nc, tk = self.nc, self.tk
        self.dram_in("x", [NB * T, D])
        self.dram_in("mem", [NB * NMEM, D])
        for name, R, C in W_SPECS:
            self.dram_in(name, [R, C])
        for name, shp in V_SPECS:
            self.dram_in(name, shp)
        for name, shp in CONST_SHAPES.items():
            self.dram_in(name, shp)
        self.out = self.nc.dram_tensor("out", [NB * T, D], F32, kind="ExternalOutput")
        self.din["out"] = self.out
        self.dbuf["out"] = Buf("out", dj=True)
        es = self.es
        self.psr = Ring([Tile(es.enter_context(nc.psum_tensor(f"ps{i}", [128, 512], F32)), f"ps{i}") for i in range(8)])
        for t_ in self.psr.tiles:
            t_.b.xr = True
        self.ident = self.load_const(es, "c_ident")
        self.bdones = self.load_const(es, "c_bdones")
        self.phase_w()
        if self.upto == "w":
            return self.finish()
        self.nsa_bias()
        for bi in range(NB):
            self.seq(bi)
            if self.upto != "all":
                break
        return self.finish()

    def seq(self, bi):
        tk = self.tk
        with self.scope() as es1:
            xT = self.sb(es1, "xT", [128, 8, T], BF16, dj=True)
            self.norm_T("x", bi * T, T, "norm_mix", xT)
            if bi == 0 and "xT_dbg" in self.dbg:
                d = self.scratch("xT_dbg", [128, 8 * T], BF16)
                tk.dma("sp", d.ap()[:, :], xT[:, :, :].rearrange("p a b -> p (a b)"), reads=[xT.b], writes=[self.dbuf["xT_dbg"]])
            if self.upto == "a":
                return
            self.phase_b(xT)
        if self.upto == "b":
            return
        if not getattr(self, "skip_rwkv", False):
            self.phase_rwkv()
        if self.upto == "rwkv":
            return
        self.phase_nsa()
        if self.upto == "nsa":
            return
        self.phase_merge(bi)
        if self.upto == "merge":
            return
        self.phase_cross(bi)
        if self.upto == "cross":
            return
        self.phase_ffn(bi)

    def finish(self):
        self.tk.drain()
        self.es.close()
        return self.nc


def make_in_maps(inputs, cores=range(NCORES)):
    consts = host_consts()
    shared = {}
    for name, R, C in W_SPECS:
        shared[name] = np.ascontiguousarray(np.asarray(inputs[name], np.float32).reshape(R, C))
    for name, shp in V_SPECS:
        shared[name] = np.ascontiguousarray(np.asarray(inputs[name], np.float32).reshape(shp))
    shared.update(consts)
    x = np.asarray(inputs["x"], np.float32)
    mem = np.asarray(inputs["mem"], np.float32)
    maps = []
    for c in cores:
        m = dict(shared)
        m["x"] = np.ascontiguousarray(x[NB * c:NB * c + NB].reshape(NB * T, D))
        m["mem"] = np.ascontiguousarray(mem[NB * c:NB * c + NB].reshape(NB * NMEM, D))
        maps.append(m)
    return maps


def kernel(**inputs):
    prog = Prog()
    nc = prog.build()
    maps = make_in_maps(inputs)
    res = run_bass_kernel_spmd(nc, maps, core_ids=list(range(NCORES)))
    outs = [np.asarray(r["out"]).reshape(NB, T, D) for r in res.results]
    return np.concatenate(outs, axis=0).astype(np.float32)


def _rwkv(self):
    nc, tk = self.nc, self.tk
    TB = 512
    self.scratch("yr_fm", [1024, T], BF16)
    zr = self.din["zr_fm"].ap()
    zb = self.dbuf["zr_fm"]

    def shift_load(dst_ap, dst_buf, r0, nrows, t0, nt, mucol, X, dtile):
        if t0 == 0:
            tk.op("pool", lambda: nc.gpsimd.memset(X[:nrows, 0:1], 0.0), writes=[X.b])
            tk.dma("sp", X[:nrows, 1:nt + 1], zr[r0:r0 + nrows, 0:nt], reads=[zb], writes=[X.b])
        else:
            tk.dma("sp", X[:nrows, 0:nt + 1], zr[r0:r0 + nrows, t0 - 1:t0 + nt], reads=[zb], writes=[X.b])
        tk.op("pool", lambda: nc.gpsimd.tensor_tensor(out=dtile[:nrows, :nt], in0=X[:nrows, 0:nt], in1=X[:nrows, 1:nt + 1],
                                                      op=ALU.subtract), reads=[X.b], writes=[dtile.b])
        tk.op("dve", lambda: nc.vector.scalar_tensor_tensor(out=dst_ap, in0=dtile[:nrows, :nt], scalar=mucol,
                                                             in1=X[:nrows, 1:nt + 1], op0=ALU.mult, op1=ALU.add),
              reads=[dtile.b, X.b], writes=[dst_buf])

    with self.scope() as es:
        mask4 = self.sb(es, "rk_mask4", [128, 512], F32)
        for j in range(2):
            tk.dma("sp", mask4[:, j * 256:(j + 1) * 256], self.din["c_mask2"].ap()[:, :], reads=[self.dbuf["c_mask2"]], writes=[mask4.b])
        bdm5 = self.load_const(es, "c_bdmask5")
        segm = self.sb(es, "rk_seg", [128, TB], F32)
        tk.dma("sp", segm[:], self.din["c_segmask"].ap()[:, 0:TB], reads=[self.dbuf["c_segmask"]], writes=[segm.b])
        lw = self.sb(es, "rk_lw", [64, T], BF16)
        la = self.sb(es, "rk_la", [64, T], BF16)
        lg = self.sb(es, "rk_lg", [128, 2, T], BF16)
        w2 = self.sb(es, "rk_w2", [64, 1024], BF16)
        a2 = self.sb(es, "rk_a2", [64, 1024], BF16)
        g2 = self.sb(es, "rk_g2", [128, 2, 1024], BF16)
        tk.dma("sp", w2[:], self.din["rwkv_w2_bf"].ap()[:, :], reads=[self.dbuf["rwkv_w2_bf"]], writes=[w2.b])
        tk.dma("sp", a2[:], self.din["rwkv_a2_bf"].ap()[:, :], reads=[self.dbuf["rwkv_a2_bf"]], writes=[a2.b])
        tk.dma("sp", g2[:, 0, :], self.din["rwkv_g2_bf"].ap()[0:128, :], reads=[self.dbuf["rwkv_g2_bf"]], writes=[g2.b])
        tk.dma("sp", g2[0:32, 1, :], self.din["rwkv_g2_bf"].ap()[128:160, :], reads=[self.dbuf["rwkv_g2_bf"]], writes=[g2.b])
        pc = {}
        for nm in ("rwkv_w0", "rwkv_a0", "rwkv_kk", "rwkv_ka", "rwkv_rk", "rwkv_lnx_w", "rwkv_lnx_b"):
            pc[nm] = self.col_vec(es, nm, 0, 0, 8, "rk_" + nm)
        mu = self.col_vec(es, "rwkv_mu", 0, 0, 24, "rk_mu")
        omk = self.sb(es, "rk_omk", [128, 8], F32)
        tk.op("dve", lambda: nc.vector.tensor_scalar(omk[:], pc["rwkv_ka"][:], -1.0, 1.0, ALU.mult, ALU.add),
              reads=[pc["rwkv_ka"].b], writes=[omk.b])
        with self.scope() as es2:
            X = self.sb(es2, "rk_LX", [128, T + 1], F32)
            dt_ = self.sb(es2, "rk_Ld", [128, T], F32)
            zt = self.sb(es2, "rk_Lz", [128, T], F32)
            for (r0, nrows, kind) in ((3072, 64, "w"), (3136, 64, "a"), (3200, 128, "g0"), (3328, 32, "g1")):
                mucol = self.sb(es2, "rk_Lmu" + kind, [128, 1], F32)
                tk.dma("sp", mucol[:nrows, :], self.dap("rwkv_mu", r0, [[1, nrows], [1, 1]]), reads=[self.dbuf["rwkv_mu"]], writes=[mucol.b])
                shift_load(zt[:nrows, :], zt.b, r0, nrows, 0, T, mucol[:nrows, 0:1], X, dt_)
                if kind == "w":
                    tk.op("act", lambda: nc.scalar.activation(out=lw[:, :], in_=zt[:64, :], func=AF.Tanh), reads=[zt.b], writes=[lw.b])
                elif kind == "a":
                    tk.op("act", lambda: nc.scalar.copy(la[:, :], zt[:64, :]), reads=[zt.b], writes=[la.b])
                elif kind == "g0":
                    tk.op("act", lambda: nc.scalar.activation(out=lg[:, 0, :], in_=zt[:, :], func=AF.Sigmoid), reads=[zt.b], writes=[lg.b])
                else:
                    tk.op("act", lambda: nc.scalar.activation(out=lg[:32, 1, :], in_=zt[:32, :], func=AF.Sigmoid), reads=[zt.b], writes=[lg.b])
        LIM = getattr(self, "rk_lim", 99)
        if LIM <= 1:
            return
        f = lambda n: self.sb(es, n, [128, TB], F32)
        Xr = self.ring(es, "rk_X", 2, [128, TB + 1], F32)
        dtl = f("rk_d")
        rr, kp, logw, aa, gg, kkr, sq, kmod, kb, cum, cex, epv, eng, bonus, tmp = [f("rk_t%d" % i) for i in range(15)]
        einr = self.ring(es, "rk_ein", 2, [128, TB], F32)
        Q5r = self.ring(es, "rk_Q5", 2, [128, 5, TB], F32)
        yfm = f("rk_yfm")
        dd = f("rk_dd")
        ob = self.ring(es, "rk_ob", 2, [128, TB], BF16)
        BD5l = [self.sb(es, f"rk_BD5{i}", [128, 5, 2, 64], F32) for i in range(4)]
        GBKl = [self.sb(es, f"rk_GBK{i}", [128, 512], F32) for i in range(4)]
        NTl = [self.sb(es, f"rk_NT{i}", [128, 128], F32) for i in range(4)]
        MXl = [[self.sb(es, f"rk_MX{i}{k}", [128, 256], F32) for k in range(2)] for i in range(4)]
        XXl = [[self.sb(es, f"rk_XX{i}{k}", [128, 128], F32) for k in range(2)] for i in range(4)]
        TTl = [self.sb(es, f"rk_TT{i}", [128, 128], F32) for i in range(4)]
        TM3l = [self.sb(es, f"rk_TM3{i}", [128, 384], F32) for i in range(4)]
        RHr = self.ring(es, "rk_RH", 2, [128, 128], F32)
        Ur = self.ring(es, "rk_U", 2, [128, 128], F32)
        Sr = self.ring(es, "rk_S", 2, [128, 128], F32)
        SPr = self.ring(es, "rk_SP", 2, [128, 128], F32)
        ident, bdones = self.ident, self.bdones

        def mm(p_ap, pbuf, lhsT, lb, rhs, rb, start=True, stop=True):
            tk.op("pe", lambda: nc.tensor.matmul(p_ap, lhsT=lhsT, rhs=rhs, start=start, stop=stop), reads=[lb, rb], writes=[pbuf])

        bonr = self.ring(es, "rk_bon", 2, [128, TB], F32)
        ggr = self.ring(es, "rk_ggr", 2, [128, TB], F32)

        def prep(hp, tb, out):
            c0 = 128 * hp
            if True:
                t0 = tb * TB
                Q5 = Q5r.next()
                ein = einr.next()
                bonus = bonr.next()
                gg = ggr.next()
                out.update(Q5=Q5, ein=ein, bonus=bonus, gg=gg)
                shift_load(rr[:, :], rr.b, c0, 128, t0, TB, mu[:, hp:hp + 1], Xr.next(), dtl)
                yield
                shift_load(kp[:, :], kp.b, 1024 + c0, 128, t0, TB, mu[:, 8 + hp:9 + hp], Xr.next(), dtl)
                yield
                shift_load(Q5[:, 4, :], Q5.b, 2048 + c0, 128, t0, TB, mu[:, 16 + hp:17 + hp], Xr.next(), dtl)
                p = self.psum()
                mm(p[:], p.b, w2[:, c0:c0 + 128], w2.b, lw[:, t0:t0 + TB], lw.b)
                tk.op("act", lambda: nc.scalar.activation(out=logw[:], in_=p[:], func=AF.Sigmoid, bias=pc["rwkv_w0"][:, hp:hp + 1]),
                      reads=[p.b, pc["rwkv_w0"].b], writes=[logw.b])
                yield
                tk.op("pool", lambda: nc.gpsimd.tensor_scalar_mul(logw[:], logw[:], -math.exp(-0.5)), reads=[logw.b], writes=[logw.b])
                p = self.psum()
                mm(p[:], p.b, a2[:, c0:c0 + 128], a2.b, la[:, t0:t0 + TB], la.b)
                tk.op("act", lambda: nc.scalar.activation(out=aa[:], in_=p[:], func=AF.Sigmoid, bias=pc["rwkv_a0"][:, hp:hp + 1]),
                      reads=[p.b, pc["rwkv_a0"].b], writes=[aa.b])
                p = self.psum()
                mm(p[:], p.b, g2[:, 0, c0:c0 + 128], g2.b, lg[:, 0, t0:t0 + TB], lg.b, True, False)
                mm(p[:], p.b, g2[:32, 1, c0:c0 + 128], g2.b, lg[:32, 1, t0:t0 + TB], lg.b, False, True)
                tk.op("act", lambda: nc.scalar.copy(gg[:], p[:]), reads=[p.b], writes=[gg.b])
                yield
                tk.op("dve", lambda: nc.vector.tensor_scalar_mul(kkr[:], kp[:], pc["rwkv_kk"][:, hp:hp + 1]),
                      reads=[kp.b, pc["rwkv_kk"].b], writes=[kkr.b])
                yield
                tk.op("act", lambda: nc.scalar.activation(out=sq[:], in_=kkr[:], func=AF.Square), reads=[kkr.b], writes=[sq.b])
                p = self.psum()
                mm(p[:], p.b, bdones[:], bdones.b, sq[:], sq.b)
                tk.op("dve", lambda: nc.vector.tensor_scalar_max(tmp[:], p[:], 1e-24), reads=[p.b], writes=[tmp.b])
                yield
                self.rpow(tmp[:], tmp.b, -0.5)
                yield
                tk.op("dve", lambda: nc.vector.tensor_tensor(out=kkr[:], in0=kkr[:], in1=tmp[:], op=ALU.mult), reads=[kkr.b, tmp.b], writes=[kkr.b])
                yield
                tk.op("dve", lambda: nc.vector.tensor_scalar(kmod[:], aa[:], pc["rwkv_ka"][:, hp:hp + 1], omk[:, hp:hp + 1], ALU.mult, ALU.add),
                      reads=[aa.b, pc["rwkv_ka"].b, omk.b], writes=[kmod.b])
                yield
                tk.op("pool", lambda: nc.gpsimd.tensor_tensor(out=kmod[:], in0=kmod[:], in1=kp[:], op=ALU.mult), reads=[kmod.b, kp.b], writes=[kmod.b])
                yield
                tk.op("pool", lambda: nc.gpsimd.tensor_tensor(out=kb[:], in0=kkr[:], in1=aa[:], op=ALU.mult), reads=[kkr.b, aa.b], writes=[kb.b])
                yield
                tk.op("dve", lambda: nc.vector.scalar_tensor_tensor(out=tmp[:], in0=rr[:], scalar=pc["rwkv_rk"][:, hp:hp + 1], in1=kmod[:],
                                                                    op0=ALU.mult, op1=ALU.mult), reads=[rr.b, kmod.b, pc["rwkv_rk"].b], writes=[tmp.b])
                p = self.psum()
                mm(p[:], p.b, bdones[:], bdones.b, tmp[:], tmp.b)
                tk.op("dve", lambda: nc.vector.tensor_tensor(out=bonus[:], in0=p[:], in1=Q5[:, 4, :], op=ALU.mult), reads=[p.b, Q5.b], writes=[bonus.b])
                yield
                tk.op("dve", lambda: nc.vector.tensor_tensor_scan(out=cum[:], data0=segm[:], data1=logw[:], initial=0.0, op0=ALU.mult, op1=ALU.add),
                      reads=[segm.b, logw.b], writes=[cum.b])
                yield
                tk.op("pool", lambda: nc.gpsimd.tensor_tensor(out=cex[:], in0=cum[:], in1=logw[:], op=ALU.subtract), reads=[cum.b, logw.b], writes=[cex.b])
                yield
                tk.op("act", lambda: nc.scalar.activation(out=epv[:], in_=cex[:], func=AF.Exp), reads=[cex.b], writes=[epv.b])
                yield
                tk.op("act", lambda: nc.scalar.activation(out=ein[:], in_=cum[:], func=AF.Exp), reads=[cum.b], writes=[ein.b])
                yield
                tk.op("act", lambda: nc.scalar.activation(out=eng[:], in_=cum[:], func=AF.Exp, scale=-1.0), reads=[cum.b], writes=[eng.b])
                yield
                tk.op("dve", lambda: nc.vector.scalar_tensor_tensor(out=Q5[:, 0, :], in0=kkr[:], scalar=-1.0, in1=epv[:], op0=ALU.mult, op1=ALU.mult),
                      reads=[kkr.b, epv.b], writes=[Q5.b])
                yield
                tk.op("pool", lambda: nc.gpsimd.tensor_tensor(out=Q5[:, 1, :], in0=rr[:], in1=ein[:], op=ALU.mult), reads=[rr.b, ein.b], writes=[Q5.b])
                yield
                tk.op("dve", lambda: nc.vector.tensor_tensor(out=Q5[:, 2, :], in0=kb[:], in1=eng[:], op=ALU.mult), reads=[kb.b, eng.b], writes=[Q5.b])
                yield
                tk.op("pool", lambda: nc.gpsimd.tensor_tensor(out=Q5[:, 3, :], in0=kmod[:], in1=eng[:], op=ALU.mult), reads=[kmod.b, eng.b], writes=[Q5.b])
                yield

        blocks = [(hp, tb) for hp in range(8) for tb in range(T // TB)]
        nxt = {}
        for _ in prep(*blocks[0], nxt):
            pass
        S = None
        epi_g = None
        for bi_, (hp, tb) in enumerate(blocks):
            c0 = 128 * hp
            if True:
                t0 = tb * TB
                Q5, ein, bonus, gg = nxt["Q5"], nxt["ein"], nxt["bonus"], nxt["gg"]
                if tb == 0:
                    S = Sr.next()
                    tk.op("pool", lambda: nc.gpsimd.memset(S[:], 0.0), writes=[S.b])
                NBC = 2
                st = {}

                def batch_gen(chunks):
                    for c in chunks:
                        cs = slice(c * 64, (c + 1) * 64)
                        BD5 = BD5l[c % 4]
                        src = Q5[:, :, cs].unsqueeze(2).to_broadcast([128, 5, 2, 64])
                        tk.op("dve", lambda: nc.vector.tensor_tensor(out=BD5[:], in0=src, in1=bdm5[:, :].rearrange("p (a h b) -> p a h b", a=5, h=2),
                                                                     op=ALU.mult), reads=[Q5.b, bdm5.b], writes=[BD5.b])
                        bd = lambda j, BD5=BD5: BD5[:, j, :, :].rearrange("p h b -> p (h b)")
                        p = self.psum()
                        ar = BD5[:, 0:2, :, :].rearrange("p a h b -> p (a h b)")
                        mm(p[:, 0:256], p.b, bd(2), BD5.b, ar, BD5.b)
                        mm(p[:, 256:512], p.b, bd(3), BD5.b, ar, BD5.b)
                        GBK = GBKl[c % 4]
                        tk.op("dve", lambda: nc.vector.tensor_tensor(out=GBK[:], in0=p[:], in1=mask4[:], op=ALU.mult), reads=[p.b, mask4.b], writes=[GBK.b])
                        yield
                        p3 = self.psum()
                        for j in range(3):
                            tk.op("pe", lambda: nc.tensor.transpose(p3[:, j * 128:(j + 1) * 128], bd(2 + j), ident[:]), reads=[BD5.b, ident.b], writes=[p3.b])
                        TM3 = TM3l[c % 4]
                        tk.op("act", lambda: nc.scalar.copy(TM3[:], p3[:, 0:384]), reads=[p3.b], writes=[TM3.b])
                        st[c] = dict(BD5=BD5, bd=bd, GBK=GBK, TM3=TM3, cs=cs)
                        yield
                    for c in chunks:
                        d = st[c]
                        GBK = d["GBK"]
                        NT = NTl[c % 4]
                        p = self.psum()
                        tk.op("pe", lambda: nc.tensor.transpose(p[:, 0:128], GBK[:, 0:128], ident[:]), reads=[GBK.b, ident.b], writes=[p.b])
                        tk.op("act", lambda: nc.scalar.copy(NT[:], p[:, 0:128]), reads=[p.b], writes=[NT.b])
                        X = XXl[c % 4][0]
                        tk.op("pool", lambda: nc.gpsimd.tensor_tensor(out=X[:], in0=ident[:], in1=GBK[:, 0:128], op=ALU.add),
                              reads=[ident.b, GBK.b], writes=[X.b])
                        d["NT"], d["X"] = NT, X
                        yield
                    for c in chunks:
                        d = st[c]
                        GBK, NT = d["GBK"], d["NT"]
                        MM = MXl[c % 4][0]
                        p = self.psum()
                        mm(p[:, 0:128], p.b, NT[:], NT.b, GBK[:, 0:128], GBK.b)
                        mm(p[:, 128:256], p.b, GBK[:, 0:128], GBK.b, NT[:], NT.b)
                        tk.op("act", lambda: nc.scalar.copy(MM[:], p[:, 0:256]), reads=[p.b], writes=[MM.b])
                        d["MM"], d["par"] = MM, 0
                        yield
                    for j in range(2, 6):
                        for c in chunks:
                            d = st[c]
                            MM, X = d["MM"], d["X"]
                            par = 1 - d["par"]
                            MM2, X2 = MXl[c % 4][par], XXl[c % 4][par]
                            px = self.psum()
                            mm(px[:, 0:128], px.b, MM[:, 128:256], MM.b, X[:], X.b)
                            tk.op("dve", lambda: nc.vector.tensor_tensor(out=X2[:], in0=px[:, 0:128], in1=X[:], op=ALU.add), reads=[px.b, X.b], writes=[X2.b])
                            pm = self.psum()
                            mm(pm[:, 0:128], pm.b, MM[:, 128:256], MM.b, MM[:, 0:128], MM.b)
                            mm(pm[:, 128:256], pm.b, MM[:, 0:128], MM.b, MM[:, 128:256], MM.b)
                            tk.op("act", lambda: nc.scalar.copy(MM2[:], pm[:, 0:256]), reads=[pm.b], writes=[MM2.b])
                            d["MM"], d["X"], d["par"] = MM2, X2, par
                            yield
                    for c in chunks:
                        d = st[c]
                        MM, X = d["MM"], d["X"]
                        p = self.psum()
                        mm(p[:, 0:128], p.b, MM[:, 128:256], MM.b, X[:], X.b)
                        TT = TTl[c % 4]
                        tk.op("dve", lambda: nc.vector.tensor_tensor(out=TT[:], in0=p[:, 0:128], in1=X[:], op=ALU.add), reads=[p.b, X.b], writes=[TT.b])
                        d["TT"] = TT
                        yield

                Sh = [S]

                def chain_gen(chunks):
                    for c in chunks:
                        d = st[c]
                        bd, GBK, TM3, TT, cs = d["bd"], d["GBK"], d["TM3"], d["TT"], d["cs"]
                        BD5 = d["BD5"]
                        S = Sh[0]
                        PCc = ein[:, c * 64 + 63:c * 64 + 64]
                        SP = SPr.next()
                        tk.op("act", lambda: nc.scalar.activation(out=SP[:], in_=S[:], func=AF.Identity, scale=PCc), reads=[S.b, ein.b], writes=[SP.b])
                        p = self.psum()
                        mm(p[:, 0:128], p.b, bd(0), BD5.b, S[:], S.b, True, False)
                        mm(p[:, 0:128], p.b, GBK[:, 256:384], GBK.b, TM3[:, 256:384], TM3.b, False, True)
                        RH = RHr.next()
                        tk.op("act", lambda: nc.scalar.copy(RH[:], p[:, 0:128]), reads=[p.b], writes=[RH.b])
                        yield
                        p = self.psum()
                        mm(p[:, 0:128], p.b, TT[:], TT.b, RH[:], RH.b)
                        U = Ur.next()
                        tk.op("dve", lambda: nc.vector.tensor_copy(U[:], p[:, 0:128]), reads=[p.b], writes=[U.b])
                        yield
                        pS = self.psum()
                        mm(pS[:, 0:128], pS.b, TM3[:, 0:128], TM3.b, U[:], U.b, True, False)
                        mm(pS[:, 0:128], pS.b, TM3[:, 128:256], TM3.b, TM3[:, 256:384], TM3.b, False, True)
                        S2 = Sr.next()
                        tk.op("dve", lambda: nc.vector.scalar_tensor_tensor(out=S2[:], in0=pS[:, 0:128], scalar=PCc, in1=SP[:], op0=ALU.mult, op1=ALU.add),
                              reads=[pS.b, ein.b, SP.b], writes=[S2.b])
                        p = self.psum()
                        mm(p[:, 0:128], p.b, S[:], S.b, bd(1), BD5.b, True, False)
                        mm(p[:, 0:128], p.b, U[:], U.b, GBK[:, 128:256], GBK.b, False, False)
                        mm(p[:, 0:128], p.b, TM3[:, 256:384], TM3.b, GBK[:, 384:512], GBK.b, False, True)
                        tk.op("act", lambda: nc.scalar.copy(yfm[0:64, cs], p[0:64, 0:64]), reads=[p.b], writes=[yfm.b])
                        tk.op("act", lambda: nc.scalar.copy(yfm[64:128, cs], p[64:128, 64:128]), reads=[p.b], writes=[yfm.b])
                        Sh[0] = S2
                        yield

                nbt = TB // 64 // NBC
                batches = [list(range(k * NBC, (k + 1) * NBC)) for k in range(nbt)]
                for _ in batch_gen(batches[0]):
                    if epi_g is not None:
                        try:
                            next(epi_g)
                        except StopIteration:
                            epi_g = None
                if epi_g is not None:
                    for _ in epi_g:
                        pass
                    epi_g = None
                nxt = {}
                prep_g = prep(*blocks[bi_ + 1], nxt) if bi_ + 1 < len(blocks) else iter(())
                next(prep_g, None)
                for k in range(nbt):
                    cg = chain_gen(batches[k])
                    bg = batch_gen(batches[k + 1]) if k + 1 < nbt else iter(())
                    done_b = done_c = False
                    while not (done_b and done_c):
                        for _ in range(3):
                            if not done_b:
                                try:
                                    next(bg)
                                except StopIteration:
                                    done_b = True
                        if not done_c:
                            try:
                                next(cg)
                            except StopIteration:
                                done_c = True
                        next(prep_g, None)
                        next(prep_g, None)
                for _ in prep_g:
                    pass
                S = Sh[0]
                def epilogue(hp=hp, c0=c0, t0=t0, bonus=bonus, gg=gg):
                    p = self.psum()
                    mm(p[:], p.b, bdones[:], bdones.b, yfm[:], yfm.b)
                    tk.op("dve", lambda: nc.vector.scalar_tensor_tensor(out=dd[:], in0=p[:], scalar=-1.0 / 64, in1=yfm[:], op0=ALU.mult, op1=ALU.add),
                          reads=[p.b, yfm.b], writes=[dd.b])
                    yield
                    tk.op("act", lambda: nc.scalar.activation(out=sq[:], in_=dd[:], func=AF.Square), reads=[dd.b], writes=[sq.b])
                    p = self.psum()
                    mm(p[:], p.b, bdones[:], bdones.b, sq[:], sq.b)
                    tk.op("dve", lambda: nc.vector.tensor_scalar(tmp[:], p[:], 1.0 / 64, 64e-5, ALU.mult, ALU.add), reads=[p.b], writes=[tmp.b])
                    yield
                    self.rpow(tmp[:], tmp.b, -0.5)
                    yield
                    tk.op("dve", lambda: nc.vector.tensor_tensor(out=dd[:], in0=dd[:], in1=tmp[:], op=ALU.mult), reads=[dd.b, tmp.b], writes=[dd.b])
                    yield
                    tk.op("act", lambda: nc.scalar.activation(out=dd[:], in_=dd[:], func=AF.Identity, bias=pc["rwkv_lnx_b"][:, hp:hp + 1],
                                                              scale=pc["rwkv_lnx_w"][:, hp:hp + 1]),
                          reads=[dd.b, pc["rwkv_lnx_b"].b, pc["rwkv_lnx_w"].b], writes=[dd.b])
                    yield
                    tk.op("pool", lambda: nc.gpsimd.tensor_tensor(out=dd[:], in0=dd[:], in1=bonus[:], op=ALU.add), reads=[dd.b, bonus.b], writes=[dd.b])
                    o = ob.next()
                    yield
                    tk.op("dve", lambda: nc.vector.tensor_tensor(out=o[:], in0=dd[:], in1=gg[:], op=ALU.mult), reads=[dd.b, gg.b], writes=[o.b])
                    yield
                    tk.dma("pool", self.din["yr_fm"].ap()[c0:c0 + 128, t0:t0 + TB], o[:], reads=[o.b], writes=[self.dbuf["yr_fm"]])

                epi_g = epilogue()
        for _ in epi_g:
            pass


Prog.phase_rwkv = _rwkv


def _nsa_bias(self):
    nc, tk = self.nc, self.tk
    self.scratch("bvec_c", [16, LVEC], BF16)
    self.scratch("bvec_w", [16, LVEC], BF16)
    with self.scope() as es:
        tab = self.sb(es, "nb_tab", [33, 16], F32)
        tk.op("pool", lambda: nc.gpsimd.memset(tab[:], 1.0), writes=[tab.b])
        tk.dma("sp", tab[0:32, :], self.din["rel_bias"].ap()[:, :], reads=[self.dbuf["rel_bias"]], writes=[tab.b])
        e33 = self.sb(es, "nb_e33", [33, LVEC], F32)
        ob = self.ring(es, "nb_o", 2, [16, 512], BF16)
        for cname, dname in (("c_e33c", "bvec_c"), ("c_e33w", "bvec_w")):
            tk.dma("sp", e33[:], self.din[cname].ap()[:, :], reads=[self.dbuf[cname]], writes=[e33.b])
            for j in range(LVEC // 512):
                p = self.psum()
                tk.op("pe", lambda: nc.tensor.matmul(p[:16, :], lhsT=tab[:], rhs=e33[:, j * 512:(j + 1) * 512], start=True, stop=True),
                      reads=[tab.b, e33.b], writes=[p.b])
                o = ob.next()
                tk.op("act", lambda: nc.scalar.copy(o[:], p[:16, :]), reads=[p.b], writes=[o.b])
                tk.dma("pool", self.din[dname].ap()[:, j * 512:(j + 1) * 512], o[:], reads=[o.b], writes=[self.dbuf[dname]])


def _nsa(self):
    nc, tk = self.nc, self.tk
    self.scratch("yn_fm", [1024, T], BF16)
    ngen = getattr(self, "ns_ngen", 5)
    gen = Ring(self.psr.tiles[0:ngen])
    accp = Ring(self.psr.tiles[ngen:8])

    def mm(p_ap, pbuf, lhsT, lb, rhs, rb, start=True, stop=True):
        tk.op("pe", lambda: nc.tensor.matmul(p_ap, lhsT=lhsT, rhs=rhs, start=start, stop=stop), reads=[lb, rb], writes=[pbuf])

    with self.scope() as es:
        Jb = self.load_const(es, "c_J", BF16, tmp_es=es)
        c2s_f = self.sb(es, "ns_c2sf", [128, 2, 64], F32)
        tk.dma("sp", c2s_f[:, :, :], self.dap("c_c2s", 0, [[64, 128], [128 * 64, 2], [1, 64]]), reads=[self.dbuf["c_c2s"]], writes=[c2s_f.b])
        c2s = self.sb(es, "ns_c2s", [128, 2, 64], BF16)
        tk.op("dve", lambda: nc.vector.tensor_copy(c2s[:], c2s_f[:]), reads=[c2s_f.b], writes=[c2s.b])
        ksX = self.sb(es, "ns_ksX", [128, T], BF16, dj=True)
        kwX = self.sb(es, "ns_kwX", [128, T], BF16, dj=True)
        with self.scope() as es0:
            exf = self.sb(es0, "ns_exf", [128, 4096], F32)
            tk.dma("sp", exf[64:128, :], self.din["c_expand"].ap()[:, :], reads=[self.dbuf["c_expand"]], writes=[exf.b])
            tk.op("dve", lambda: nc.vector.tensor_scalar_mul(ksX[64:128, :], exf[64:128, :], BIG), reads=[exf.b], writes=[ksX.b])
        tk.op("pool", lambda: nc.gpsimd.memset(kwX[64:128, :], 0.0), writes=[kwX.b])
        ones = self.sb(es, "ns_ones", [128, 64], BF16)
        tk.op("pool", lambda: nc.gpsimd.memset(ones[:], 1.0), writes=[ones.b])
        kgain = self.col_vec(es, "nsa_k_gain", 0, 0, 1, "ns_kg", p=64)
        ident, bdones = self.ident, self.bdones
        kcmpT = [self.sb(es, f"ns_kcT{g}", [128, 256], BF16) for g in range(4)]
        for g in range(4):
            tk.op("pool", lambda: nc.gpsimd.memset(kcmpT[g][64:128, :], 0.0), writes=[kcmpT[g].b])
        vcmp = [self.sb(es, f"ns_vc{g}", [128, 2, 64], BF16) for g in range(4)]
        with self.scope() as es2:
            kc2 = self.sb(es2, "ns_kc2", [128, T], BF16)
            w1t = self.sb(es2, "ns_w1", [128, 16, 256], BF16)
            w2t = self.sb(es2, "ns_w2", [128, 2, 64], BF16)
            hg_ = self.sb(es2, "ns_hg", [128, 2, 256], BF16)
            xx = self.sb(es2, "ns_x", [128, 256], F32)
            x2 = self.sb(es2, "ns_x2", [128, 256], F32)
            pvb = self.sb(es2, "ns_pvb", [128, 2], F32)
            t64 = self.sb(es2, "ns_t64", [64, 256], F32)
            t64b = self.sb(es2, "ns_t64b", [64, 256], F32)
            for kind in range(2):
                sfx = "_k" if kind == 0 else "_v"
                self.load_w(w1t, "cmp_w1" + sfx + "_bf", 0, 256, kchunks=16)
                self.load_w(w2t, "cmp_w2" + sfx + "_bf", 0, 64, kchunks=2)
                pe_f = self.col_vec(es2, "cmp_pe" + sfx, 0, 0, 16, "ns_pe" + sfx)
                pe_b = self.sb(es2, "ns_peb" + sfx, [128, 16], BF16)
                tk.op("dve", lambda: nc.vector.tensor_copy(pe_b[:], pe_f[:]), reads=[pe_f.b], writes=[pe_b.b])
                for ct in range(2):
                    p = gen.next()
                    for l2 in range(16):
                        mm(p[:, 0:1], p.b, w1t[:, l2, ct * 128:(ct + 1) * 128], w1t.b, pe_b[:, l2:l2 + 1], pe_b.b, l2 == 0, l2 == 15)
                    tk.op("dve", lambda: nc.vector.tensor_copy(pvb[:, ct:ct + 1], p[:, 0:1]), reads=[p.b], writes=[pvb.b])
                for g in range(4):
                    r0 = 256 * kind + 64 * g
                    tk.op("pool", lambda: nc.gpsimd.memset(kc2[64:128, T - 1:T], 0.0), writes=[kc2.b])
                    tk.dma("sp", kc2[0:64, :], self.din["kcvc_fm"].ap()[r0:r0 + 64, :], reads=[self.dbuf["kcvc_fm"]], writes=[kc2.b])
                    tk.dma("sp", kc2[64:128, 0:T - 1], self.din["kcvc_fm"].ap()[r0:r0 + 64, 1:T], reads=[self.dbuf["kcvc_fm"]], writes=[kc2.b])
                    tk.op("pool", lambda: nc.gpsimd.memset(hg_[:], 0.0), writes=[hg_.b])
                    for ct in range(2):
                        p = gen.next()
                        for l2 in range(16):
                            rhs = kc2[:, 2 * l2: 2 * l2 + 16 * 254 + 1: 16]
                            mm(p[:, 0:255], p.b, w1t[:, l2, ct * 128:(ct + 1) * 128], w1t.b, rhs, kc2.b, l2 == 0, l2 == 15)
                        tk.op("act", lambda: nc.scalar.activation(out=xx[:, 0:255], in_=p[:, 0:255], func=AF.Identity, bias=pvb[:, ct:ct + 1]),
                              reads=[p.b, pvb.b], writes=[xx.b])
                        tk.op("act", lambda: nc.scalar.activation(out=x2[:, 0:255], in_=xx[:, 0:255], func=AF.Square), reads=[xx.b], writes=[x2.b])
                        tk.op("dve", lambda: nc.vector.tensor_scalar(x2[:, 0:255], x2[:, 0:255], 0.044715, 1.0, ALU.mult, ALU.add), reads=[x2.b], writes=[x2.b])
                        tk.op("dve", lambda: nc.vector.tensor_tensor(out=x2[:, 0:255], in0=x2[:, 0:255], in1=xx[:, 0:255], op=ALU.mult), reads=[x2.b, xx.b], writes=[x2.b])
                        tk.op("act", lambda: nc.scalar.activation(out=x2[:, 0:255], in_=x2[:, 0:255], func=AF.Sigmoid, scale=1.5957691216057308),
                              reads=[x2.b], writes=[x2.b])
                        tk.op("dve", lambda: nc.vector.tensor_tensor(out=hg_[:, ct, 0:255], in0=x2[:, 0:255], in1=xx[:, 0:255], op=ALU.mult),
                              reads=[x2.b, xx.b], writes=[hg_.b])
                    if kind == 0:
                        p = gen.next()
                        for ct in range(2):
                            mm(p[0:64, 0:256], p.b, w2t[:, ct, :], w2t.b, hg_[:, ct, :], hg_.b, ct == 0, ct == 1)
                        tk.op("act", lambda: nc.scalar.activation(out=t64[:], in_=p[0:64, 0:256], func=AF.Square), reads=[p.b], writes=[t64.b])
                        p2 = gen.next()
                        mm(p2[0:64, 0:256], p2.b, bdones[0:64, 0:64], bdones.b, t64[:], t64.b)
                        tk.op("dve", lambda: nc.vector.tensor_scalar(t64[:], p2[0:64, 0:256], 1.0 / 64, 1e-6, ALU.mult, ALU.add), reads=[p2.b], writes=[t64.b])
                        self.rpow(t64[:], t64.b, -0.5)
                        tk.op("dve", lambda: nc.vector.tensor_tensor(out=t64b[:], in0=p[0:64, 0:256], in1=t64[:], op=ALU.mult), reads=[p.b, t64.b], writes=[t64b.b])
                        tk.op("dve", lambda: nc.vector.tensor_scalar_mul(kcmpT[g][0:64, :], t64b[:], kgain[:, 0:1]), reads=[t64b.b, kgain.b], writes=[kcmpT[g].b])
                    else:
                        for nt in range(2):
                            p = gen.next()
                            for ct in range(2):
                                mm(p[:, 0:64], p.b, hg_[:, ct, nt * 128:(nt + 1) * 128], hg_.b, w2t[:, ct, :], w2t.b, ct == 0, ct == 1)
                            tk.op("act", lambda: nc.scalar.copy(vcmp[g][:, nt, :], p[:, 0:64]), reads=[p.b], writes=[vcmp[g].b])
        if getattr(self, "ns_lim", 99) <= 1:
            return
        Vs = self.sb(es, "ns_Vs", [128, 32, 128], BF16)
        Vw = self.sb(es, "ns_Vw", [128, 32, 128], BF16)
        tk.op("pool", lambda: nc.gpsimd.memset(Vs[:], 1.0), writes=[Vs.b])
        tk.op("pool", lambda: nc.gpsimd.memset(Vw[:], 1.0), writes=[Vw.b])
        vco = [self.sb(es, f"ns_vco{g}", [128, 2, 128], BF16) for g in range(4)]
        for g in range(4):
            tk.op("pool", lambda: nc.gpsimd.memset(vco[g][:], 1.0), writes=[vco[g].b])
            tk.op("dve", lambda: nc.vector.tensor_copy(vco[g][:, :, 0:64], vcmp[g][:]), reads=[vcmp[g].b], writes=[vco[g].b])
        bfar = self.sb(es, "ns_bfar", [128, 16], F32)
        tk.dma("sp", bfar[:], self.din["rel_bias"].ap()[31:32, :].partition_broadcast(128), reads=[self.dbuf["rel_bias"]], writes=[bfar.b])
        qTr = self.ring(es, "ns_qT", 2, [128, 4, 512], BF16)
        for t_ in qTr.tiles:
            tk.op("pool", lambda: nc.gpsimd.memset(t_[64:128, :, :], 0.0), writes=[t_.b])
        gbr = self.ring(es, "ns_gb", 3, [64, 4, 512], F32)
        Hr = self.ring(es, "ns_H", 4, [128, 512], BF16)
        Er = self.ring(es, "ns_E", 4, [128, 512], BF16)
        E2r = self.ring(es, "ns_E2", 4, [128, 512], BF16)
        Ec = self.sb(es, "ns_Ec", [128, 8, 512], BF16, dj=True)
        accC = self.sb(es, "ns_accC", [64, 4, 8, 512], BF16, dj=True)
        QSr = self.ring(es, "ns_QS", 2, [128, T], BF16)
        for t_ in QSr.tiles:
            t_.b.dj = True
        acc = self.ring(es, "ns_acc", 2, [64, 512], F32)
        impa = self.sb(es, "ns_impa", [64, 512], F32)
        frc = self.ring(es, "ns_frc", 2, [64, 512], F32)
        rdr = self.ring(es, "ns_rd", 3, [64, 512], F32)
        t1r = self.ring(es, "ns_t1", 3, [64, 512], F32)
        impq = self.sb(es, "ns_impq", [128, 4, 64], F32)
        selq = self.sb(es, "ns_selq", [128, 4, 64], F32)
        wk = self.sb(es, "ns_wk", [128, 64], F32)
        m8 = self.sb(es, "ns_m8", [128, 16], F32)
        obr = self.ring(es, "ns_ob", 2, [64, 512], BF16)
        XBr = [[self.sb(es, f"ns_xb{k}_{i}", [128, 512], BF16) for i in range(13)] for k in range(1)]
        pend = []
        eng_alt = [0]

        def flush():
            while pend:
                pend.pop(0)()

        def hankel(vname, h, c, pstep):
            H = Hr.next()
            src = self.dap(vname, h * LVEC + c, [[pstep, 128], [1, 512]])
            tk.dma("sp", H[:], src, reads=[self.dbuf[vname]], writes=[H.b])
            return H

        def ratio(pn):
            rd = rdr.next()
            tk.op("dve", lambda: nc.vector.tensor_scalar_max(rd[:], pn[64:128, :], 1e-30), reads=[pn.b], writes=[rd.b])
            self.rpow(rd[:], rd.b, -1.0)
            t1 = t1r.next()
            tk.op("dve", lambda: nc.vector.tensor_tensor(out=t1[:], in0=pn[0:64, :], in1=rd[:], op=ALU.mult), reads=[pn.b, rd.b], writes=[t1.b])
            return rd, t1

        def key_tile(s_mms, e_ap, e_buf, act_bias, pv, mult=None):
            p = gen.next()
            for i, (lhsT, lb, rhs, rb) in enumerate(s_mms):
                mm(p[:], p.b, lhsT, lb, rhs, rb, i == 0, i == len(s_mms) - 1)
            if mult is None:
                if act_bias is None:
                    tk.op("act", lambda: nc.scalar.activation(out=e_ap, in_=p[:], func=AF.Exp), reads=[p.b], writes=[e_buf])
                else:
                    tk.op("act", lambda: nc.scalar.activation(out=e_ap, in_=p[:], func=AF.Exp, bias=act_bias), reads=[p.b, bfar.b], writes=[e_buf])
            else:
                E0 = E2r.next()
                tk.op("act", lambda: nc.scalar.activation(out=E0[:], in_=p[:], func=AF.Exp), reads=[p.b], writes=[E0.b])
                eng_alt[0] += 1
                if False:
                    tk.op("pool", lambda: nc.gpsimd.tensor_tensor(out=e_ap, in0=E0[:], in1=mult[:], op=ALU.mult), reads=[E0.b, mult.b], writes=[e_buf])
                else:
                    tk.op("dve", lambda: nc.vector.tensor_tensor(out=e_ap, in0=E0[:], in1=mult[:], op=ALU.mult), reads=[E0.b, mult.b], writes=[e_buf])
            while len(pend) >= SKEW:
                pend.pop(0)()
            pend.append(pv)

        SKEW = getattr(self, "ns_skew", 3)
        for g in range(4):
            flush()
            tk.dma("sp", ksX[0:64, :], self.din["ks_fm"].ap()[64 * g:64 * g + 64, :], reads=[self.dbuf["ks_fm"]], writes=[ksX.b])
            tk.dma("sp", kwX[0:64, :], self.din["kw_fm"].ap()[64 * g:64 * g + 64, :], reads=[self.dbuf["kw_fm"]], writes=[kwX.b])
            for k8 in range(4):
                tk.dma("sp", Vs[:, 8 * k8:8 * k8 + 8, 0:64], self.dap("vsw_tm", 64 * g + 8 * k8 * 128 * 512, [[512, 128], [128 * 512, 8], [1, 64]]),
                       reads=[self.dbuf["vsw_tm"]], writes=[Vs.b])
                tk.dma("sp", Vw[:, 8 * k8:8 * k8 + 8, 0:64], self.dap("vsw_tm", 256 + 64 * g + 8 * k8 * 128 * 512, [[512, 128], [128 * 512, 8], [1, 64]]),
                       reads=[self.dbuf["vsw_tm"]], writes=[Vw.b])
            for qt in range(T // 512):
                t0 = qt * 512
                qT = qTr.next()
                tk.dma("sp", qT[0:64, :, :], self.dap("q_fm", 256 * g * T + t0, [[T, 64], [64 * T, 4], [1, 512]]), reads=[self.dbuf["q_fm"]], writes=[qT.b])
                gb = gbr.next()
                for j in range(4):
                    row = 12 * g + 3 * j
                    tk.dma("sp", gb[:, j, :], self.din["gates_fm"].ap()[row:row + 1, t0:t0 + 512].partition_broadcast(64),
                           reads=[self.dbuf["gates_fm"]], writes=[gb.b])
                fr = frc.next()
                tk.dma("sp", fr[:], self.din["c_forced"].ap()[:, t0:t0 + 512], reads=[self.dbuf["c_forced"]], writes=[fr.b])
                nnt = 2 if t0 >= 2048 else 1
                for hg in range(4):
                    h = 4 * g + hg
                    pn, pi = accp.next(), accp.next()
                    for nt in range(nnt):
                        H = hankel("bvec_c", h, OFFC + t0 - 16 * 128 * nt - 2063, 16)
                        e_ap = Ec[:, hg * 2 + nt, :]

                        def pv(pn=pn, pi=pi, nt=nt, e_ap=e_ap, nnt=nnt):
                            mm(pn[:], pn.b, vco[g][:, nt, :], vco[g].b, e_ap, Ec.b, nt == 0, nt == nnt - 1)
                            mm(pi[0:64, :], pi.b, c2s[:, nt, :], c2s.b, e_ap, Ec.b, nt == 0, nt == nnt - 1)
                        key_tile([(kcmpT[g][:, nt * 128:(nt + 1) * 128], kcmpT[g].b, qT[:, hg, :], qT.b), (Jb[:], Jb.b, H[:], H.b)], e_ap, Ec.b, None, pv)

                    def fin(pn=pn, pi=pi, hg=hg, gb=gb, qt=qt):
                        rd, t1 = ratio(pn)
                        tk.op("pool", lambda: nc.gpsimd.tensor_tensor(out=accC[:, hg, qt, :], in0=t1[:], in1=gb[:, hg, :], op=ALU.mult),
                              reads=[t1.b, gb.b], writes=[accC.b])
                        if hg == 0:
                            tk.op("dve", lambda: nc.vector.tensor_tensor(out=impa[:], in0=pi[0:64, :], in1=rd[:], op=ALU.mult), reads=[pi.b, rd.b], writes=[impa.b])
                        else:
                            t2 = t1r.next()
                            tk.op("dve", lambda: nc.vector.tensor_tensor(out=t2[:], in0=pi[0:64, :], in1=rd[:], op=ALU.mult), reads=[pi.b, rd.b], writes=[t2.b])
                            tk.op("pool", lambda: nc.gpsimd.tensor_tensor(out=impa[:], in0=impa[:], in1=t2[:], op=ALU.add), reads=[impa.b, t2.b], writes=[impa.b])
                    pend.append(fin)
                flush()
                tk.op("dve", lambda: nc.vector.tensor_tensor(out=impa[:], in0=impa[:], in1=fr[:], op=ALU.max), reads=[impa.b, fr.b], writes=[impa.b])
                p = gen.next()
                for s4 in range(4):
                    tk.op("pe", lambda: nc.tensor.transpose(p[:, s4 * 64:(s4 + 1) * 64], impa[:, s4 * 128:(s4 + 1) * 128], ident[0:64, 0:64]),
                          reads=[impa.b, ident.b], writes=[p.b])
                tk.op("act", lambda: nc.scalar.copy(impq[:], p[:, 0:256].rearrange("p (a b) -> p a b", a=4)), reads=[p.b], writes=[impq.b])
                for s4 in range(4):
                    tk.op("dve", lambda: nc.vector.max(out=m8[:, 0:8], in_=impq[:, s4, :]), reads=[impq.b], writes=[m8.b])
                    tk.op("dve", lambda: nc.vector.match_replace(out=wk[:], in_to_replace=m8[:, 0:8], in_values=impq[:, s4, :], imm_value=-1e30),
                          reads=[impq.b, m8.b], writes=[wk.b])
                    tk.op("dve", lambda: nc.vector.max(out=m8[:, 8:16], in_=wk[:]), reads=[wk.b], writes=[m8.b])
                    tk.op("dve", lambda: nc.vector.tensor_scalar(selq[:, s4, :], impq[:, s4, :], m8[:, 15:16], 1.0, ALU.is_ge, ALU.subtract),
                          reads=[impq.b, m8.b], writes=[selq.b])
                p = gen.next()
                for s4 in range(4):
                    tk.op("pe", lambda: nc.tensor.transpose(p[0:64, s4 * 128:(s4 + 1) * 128], selq[:, s4, :], ident[:]),
                          reads=[selq.b, ident.b], writes=[p.b])
                tk.op("act", lambda: nc.scalar.copy(QSr.tiles[0][64:128, qt * 512:(qt + 1) * 512], p[0:64, :]), reads=[p.b], writes=[QSr.tiles[0].b])
                tk.op("dve", lambda: nc.vector.tensor_copy(QSr.tiles[1][64:128, qt * 512:(qt + 1) * 512], p[0:64, :]), reads=[p.b], writes=[QSr.tiles[1].b])
            for hg in range(4):
                h = 4 * g + hg
                flush()
                XB = XBr[0]
                xw, xs = {}, {}
                for i, (vname, d) in enumerate([("bvec_w", dd_) for dd_ in range(512, -385, -128)] + [("bvec_c", dd_) for dd_ in range(128, -385, -128)]):
                    H = hankel(vname, h, OFFC + d - 127, 1)
                    p = gen.next()
                    mm(p[:], p.b, Jb[:], Jb.b, H[:], H.b)
                    tk.op("act", lambda: nc.scalar.activation(out=XB[i][:], in_=p[:], func=AF.Exp), reads=[p.b], writes=[XB[i].b])
                    (xw if vname == "bvec_w" else xs)[d] = XB[i]
                q_ = QSr.next()
                tk.dma("sp", q_[0:64, :], self.din["q_fm"].ap()[64 * h:64 * h + 64, :], reads=[self.dbuf["q_fm"]], writes=[q_.b])
                for qt in range(T // 512):
                    t0 = qt * 512
                    qs = q_[:, t0:t0 + 512]
                    gb = gbr.next()
                    for j in range(2):
                        row = 3 * h + 1 + j
                        tk.dma("sp", gb[:, j, :], self.din["gates_fm"].ap()[row:row + 1, t0:t0 + 512].partition_broadcast(64),
                               reads=[self.dbuf["gates_fm"]], writes=[gb.b])
                    ac = acc.next()
                    kts = list(range(max(0, (t0 - 512) // 128), (t0 + 511) // 128 + 1))
                    pn = accp.next()
                    for i, kt in enumerate(kts):
                        E = Er.next()

                        def pv(pn=pn, kt=kt, E=E, first=(i == 0), last=(i == len(kts) - 1)):
                            mm(pn[:], pn.b, Vw[:, kt, :], Vw.b, E[:], E.b, first, last)
                        key_tile([(kwX[:, kt * 128:(kt + 1) * 128], kwX.b, qs, q_.b)], E[:], E.b, None, pv, mult=xw[t0 - 128 * kt])

                    def finw(pn=pn, gb=gb, ac=ac):
                        rd, t1 = ratio(pn)
                        tk.op("dve", lambda: nc.vector.tensor_tensor(out=ac[:], in0=t1[:], in1=gb[:, 1, :], op=ALU.mult), reads=[t1.b, gb.b], writes=[ac.b])
                    pend.append(finw)
                    kts = list(range(0, (t0 + 511) // 128 + 1))
                    pn = accp.next()
                    for i, kt in enumerate(kts):
                        far = (128 * kt <= t0 - 256)
                        E = Er.next()
                        s_mms = [(ksX[:, kt * 128:(kt + 1) * 128], ksX.b, qs, q_.b)]

                        def pv(pn=pn, kt=kt, E=E, first=(i == 0), last=(i == len(kts) - 1)):
                            mm(pn[:], pn.b, Vs[:, kt, :], Vs.b, E[:], E.b, first, last)
                        key_tile(s_mms, E[:], E.b, bfar[:, h:h + 1] if far else None, pv, mult=None if far else xs[t0 - 128 * kt])

                    def fins(pn=pn, gb=gb, ac=ac, hg=hg, h=h, t0=t0, qt=qt):
                        rd, t1 = ratio(pn)
                        tk.op("dve", lambda: nc.vector.tensor_tensor(out=t1[:], in0=t1[:], in1=gb[:, 0, :], op=ALU.mult), reads=[t1.b, gb.b], writes=[t1.b])
                        tk.op("dve", lambda: nc.vector.tensor_tensor(out=ac[:], in0=ac[:], in1=t1[:], op=ALU.add), reads=[t1.b, ac.b], writes=[ac.b])
                        o = obr.next()
                        tk.op("dve", lambda: nc.vector.tensor_tensor(out=o[:], in0=ac[:], in1=accC[:, hg, qt, :], op=ALU.add), reads=[ac.b, accC.b], writes=[o.b])
                        tk.dma("pool", self.din["yn_fm"].ap()[64 * h:64 * h + 64, t0:t0 + 512], o[:], reads=[o.b], writes=[self.dbuf["yn_fm"]])
                    pend.append(fins)
        flush()


Prog.nsa_bias = _nsa_bias
Prog.phase_nsa = _nsa


def _proj_tm_res(self, actT, tok0, ntok, kchunks, w, res_name, res_row0, dst_name, dst_row0, es):
    nc, tk = self.nc, self.tk
    xr = self.ring(es, "pt_x", 2, [128, D], F32)
    orr = self.ring(es, "pt_o", 2, [128, D], F32)
    for i in range(ntok // 128):
        x = xr.next()
        o = orr.next()
        tk.dma("sp", x[:], self.din[res_name].ap()[res_row0 + i * 128:res_row0 + (i + 1) * 128, :], reads=[self.dbuf[res_name]], writes=[x.b])
        for half in range(2):
            p = self.psum()
            for kc in range(kchunks):
                tk.op("pe", lambda: nc.tensor.matmul(p[:], lhsT=actT[:, kc, tok0 + i * 128:tok0 + (i + 1) * 128], rhs=w[:, kc, half * 512:(half + 1) * 512],
                                                      start=(kc == 0), stop=(kc == kchunks - 1)), reads=[actT.b, w.b], writes=[p.b])
            tk.op("dve", lambda: nc.vector.tensor_tensor(out=o[:, half * 512:(half + 1) * 512], in0=p[:], in1=x[:, half * 512:(half + 1) * 512], op=ALU.add),
                  reads=[p.b, x.b], writes=[o.b])
        tk.dma("pool", self.din[dst_name].ap()[dst_row0 + i * 128:dst_row0 + (i + 1) * 128, :], o[:], reads=[o.b], writes=[self.dbuf[dst_name]])


def _merge(self, bi):
    nc, tk = self.nc, self.tk
    self.scratch("h1", [T, D], F32)
    with self.scope() as es:
        mT = self.sb(es, "mg_mT", [128, 8, T], BF16, dj=True)
        with self.scope() as es2:
            wr = self.sb(es2, "mg_wr", [128, 8, 1024], BF16)
            wn = self.sb(es2, "mg_wn", [128, 8, 1024], BF16)
            self.load_w(wr, "w_branch_rwkv_bf", 0, 1024)
            self.load_w(wn, "w_branch_nsa_bf", 0, 1024)
            yr = self.ring(es2, "mg_yr", 2, [128, 8, 512], BF16)
            yn = self.ring(es2, "mg_yn", 2, [128, 8, 512], BF16)
            gr = self.ring(es2, "mg_g", 4, [128, 512], F32)
            tr = self.ring(es2, "mg_t", 4, [128, 512], F32)
            for tt in range(T // 512):
                a, b = yr.next(), yn.next()
                tk.dma("sp", a[:], self.dap("yr_fm", tt * 512, [[T, 128], [128 * T, 8], [1, 512]]), reads=[self.dbuf["yr_fm"]], writes=[a.b])
                tk.dma("sp", b[:], self.dap("yn_fm", tt * 512, [[T, 128], [128 * T, 8], [1, 512]]), reads=[self.dbuf["yn_fm"]], writes=[b.b])
                for ci in range(8):
                    g0, g1 = gr.next(), gr.next()
                    tk.dma("sp", g0[:], self.din["gm_fm"].ap()[ci * 128:(ci + 1) * 128, tt * 512:(tt + 1) * 512], reads=[self.dbuf["gm_fm"]], writes=[g0.b])
                    tk.dma("sp", g1[:], self.din["gm_fm"].ap()[1024 + ci * 128:1024 + (ci + 1) * 128, tt * 512:(tt + 1) * 512], reads=[self.dbuf["gm_fm"]], writes=[g1.b])
                    pr, pn = self.psum(), self.psum()
                    for kc in range(8):
                        tk.op("pe", lambda: nc.tensor.matmul(pr[:], lhsT=wr[:, kc, ci * 128:(ci + 1) * 128], rhs=a[:, kc, :], start=(kc == 0), stop=(kc == 7)),
                              reads=[wr.b, a.b], writes=[pr.b])
                    for kc in range(8):
                        tk.op("pe", lambda: nc.tensor.matmul(pn[:], lhsT=wn[:, kc, ci * 128:(ci + 1) * 128], rhs=b[:, kc, :], start=(kc == 0), stop=(kc == 7)),
                              reads=[wn.b, b.b], writes=[pn.b])
                    t0_, t1_ = tr.next(), tr.next()
                    tk.op("dve", lambda: nc.vector.tensor_tensor(out=t0_[:], in0=pr[:], in1=g0[:], op=ALU.mult), reads=[pr.b, g0.b], writes=[t0_.b])
                    tk.op("dve", lambda: nc.vector.tensor_tensor(out=t1_[:], in0=pn[:], in1=g1[:], op=ALU.mult), reads=[pn.b, g1.b], writes=[t1_.b])
                    tk.op("pool", lambda: nc.gpsimd.tensor_tensor(out=mT[:, ci, tt * 512:(tt + 1) * 512], in0=t0_[:], in1=t1_[:], op=ALU.add),
                          reads=[t0_.b, t1_.b], writes=[mT.b])
        with self.scope() as es3:
            wm = self.sb(es3, "mg_wm", [128, 8, 1024], BF16)
            self.load_w(wm, "w_mix_out_bf", 0, 1024)
            self.proj_tm_res(mT, 0, T, 8, wm, "x", bi * T, "h1", 0, es3)


def _cross(self, bi):
    nc, tk = self.nc, self.tk
    self.scratch("h2", [T, D], F32)
    HT = 2048
    with self.scope() as es:
        wq = self.sb(es, "ca_wq", [128, 8, 1024], BF16)
        wo = self.sb(es, "ca_wo", [128, 8, 1024], BF16)
        self.load_w(wq, "ca_wq_bf", 0, 1024)
        self.load_w(wo, "ca_wo_bf", 0, 1024)
        kT = self.sb(es, "ca_kT", [128, 8, NMEM], BF16, dj=True)
        Vc = self.sb(es, "ca_V", [128, 2, 1024], BF16, dj=True)
        qgain = self.col_vec(es, "ca_q_gain", 0, 0, 2, "ca_qg")
        kgain = self.col_vec(es, "ca_k_gain", 0, 0, 2, "ca_kg")
        ones_f = self.load_const(es, "c_ones")
        ones_b = self.sb(es, "ca_1b", [128, 128], BF16)
        tk.op("dve", lambda: nc.vector.tensor_copy(ones_b[:], ones_f[:]), reads=[ones_f.b], writes=[ones_b.b])
        sqr = self.ring(es, "ca_sq", 2, [128, 2, 512], F32)
        rr = self.ring(es, "ca_r", 4, [128, 512], F32)
        tmpr = self.ring(es, "ca_tmp", 2, [128, 512], F32)
        qh = self.ring(es, "ca_qh", 3, [128, 2, 512], BF16)
        Er = self.ring(es, "ca_E", 2, [128, 2, 512], BF16)

        def qk_norm(p0, p1, n, gain, scale, out_aps, out_buf):
            s = sqr.next()
            tk.op("act", lambda: nc.scalar.activation(out=s[:, 0, 0:n], in_=p0[:, 0:n], func=AF.Square), reads=[p0.b], writes=[s.b])
            tk.op("act", lambda: nc.scalar.activation(out=s[:, 1, 0:n], in_=p1[:, 0:n], func=AF.Square), reads=[p1.b], writes=[s.b])
            p2 = self.psum()
            for j in range(2):
                tk.op("pe", lambda: nc.tensor.matmul(p2[:, 0:n], lhsT=ones_f[:], rhs=s[:, j, 0:n], start=(j == 0), stop=(j == 1)),
                      reads=[ones_f.b, s.b], writes=[p2.b])
            r = rr.next()
            tk.op("dve", lambda: nc.vector.tensor_scalar(r[:, 0:n], p2[:, 0:n], 1.0 / 256, 1e-6, ALU.mult, ALU.add), reads=[p2.b], writes=[r.b])
            self.rpow(r[:, 0:n], r.b, -0.5)
            for j, pj in enumerate((p0, p1)):
                t = tmpr.next()
                tk.op("dve", lambda: nc.vector.tensor_tensor(out=t[:, 0:n], in0=pj[:, 0:n], in1=r[:, 0:n], op=ALU.mult), reads=[pj.b, r.b], writes=[t.b])
                tk.op("dve", lambda: nc.vector.tensor_scalar(out_aps[j], t[:, 0:n], gain[:, j:j + 1], scale, ALU.mult, ALU.mult),
                      reads=[t.b, gain.b], writes=[out_buf])

        with self.scope() as es2:
            mnT = self.sb(es2, "ca_mnT", [128, 8, NMEM], BF16, dj=True)
            self.norm_T("mem", bi * NMEM, NMEM, "norm_mem", mnT)
            wk = self.sb(es2, "ca_wk", [128, 8, 1024], BF16)
            wv = self.sb(es2, "ca_wv", [128, 8, 1024], BF16)
            self.load_w(wk, "ca_wkv_bf", 0, 1024)
            self.load_w(wv, "ca_wkv_bf", 1024, 1024)
            for h in range(4):
                ps_ = []
                for j in range(2):
                    p = self.psum()
                    ci = 2 * h + j
                    for kc in range(8):
                        tk.op("pe", lambda: nc.tensor.matmul(p[:, 0:NMEM], lhsT=wk[:, kc, ci * 128:(ci + 1) * 128], rhs=mnT[:, kc, :], start=(kc == 0), stop=(kc == 7)),
                              reads=[wk.b, mnT.b], writes=[p.b])
                    ps_.append(p)
                qk_norm(ps_[0], ps_[1], NMEM, kgain, 1.0, [kT[:, 2 * h, :], kT[:, 2 * h + 1, :]], kT.b)
            for mt in range(2):
                for half in range(2):
                    p = self.psum()
                    for kc in range(8):
                        tk.op("pe", lambda: nc.tensor.matmul(p[:], lhsT=mnT[:, kc, mt * 128:(mt + 1) * 128], rhs=wv[:, kc, half * 512:(half + 1) * 512],
                                                              start=(kc == 0), stop=(kc == 7)), reads=[mnT.b, wv.b], writes=[p.b])
                    tk.op("act", lambda: nc.scalar.copy(Vc[:, mt, half * 512:(half + 1) * 512], p[:]), reads=[p.b], writes=[Vc.b])
        for hf in range(T // HT):
            with self.scope() as es2:
                hnT = self.sb(es2, "ca_hnT", [128, 8, HT], BF16, dj=True)
                oT = self.sb(es2, "ca_oT", [128, 8, HT], BF16, dj=True)
                self.norm_T("h1", hf * HT, HT, "norm_cross", hnT)
                def stage_q(h, tt):
                    ps_ = []
                    for j in range(2):
                        p = self.psum()
                        ci = 2 * h + j
                        for kc in range(8):
                            tk.op("pe", lambda: nc.tensor.matmul(p[:], lhsT=wq[:, kc, ci * 128:(ci + 1) * 128], rhs=hnT[:, kc, tt * 512:(tt + 1) * 512],
                                                                  start=(kc == 0), stop=(kc == 7)), reads=[wq.b, hnT.b], writes=[p.b])
                        ps_.append(p)
                    q = qh.next()
                    qk_norm(ps_[0], ps_[1], 512, qgain, 1.0 / 16, [q[:, 0, :], q[:, 1, :]], q.b)
                    return q

                def stage_att(h, tt, q):
                    E = Er.next()
                    for mt in range(2):
                        p = self.psum()
                        for j in range(2):
                            tk.op("pe", lambda: nc.tensor.matmul(p[:], lhsT=kT[:, 2 * h + j, mt * 128:(mt + 1) * 128], rhs=q[:, j, :], start=(j == 0), stop=(j == 1)),
                                  reads=[kT.b, q.b], writes=[p.b])
                        tk.op("act", lambda: nc.scalar.activation(out=E[:, mt, :], in_=p[:], func=AF.Exp), reads=[p.b], writes=[E.b])
                    pd = self.psum()
                    for mt in range(2):
                        tk.op("pe", lambda: nc.tensor.matmul(pd[:], lhsT=ones_b[:], rhs=E[:, mt, :], start=(mt == 0), stop=(mt == 1)),
                              reads=[ones_b.b, E.b], writes=[pd.b])
                    r = rr.next()
                    tk.op("act", lambda: nc.scalar.activation(out=r[:], in_=pd[:], func=AF.Ln), reads=[pd.b], writes=[r.b])
                    tk.op("act", lambda: nc.scalar.activation(out=r[:], in_=r[:], func=AF.Exp, scale=-1.0), reads=[r.b], writes=[r.b])
                    for j in range(2):
                        pn = self.psum()
                        for mt in range(2):
                            tk.op("pe", lambda: nc.tensor.matmul(pn[:], lhsT=Vc[:, mt, h * 256 + j * 128:h * 256 + (j + 1) * 128], rhs=E[:, mt, :],
                                                                  start=(mt == 0), stop=(mt == 1)), reads=[Vc.b, E.b], writes=[pn.b])
                        tk.op("dve", lambda: nc.vector.tensor_tensor(out=oT[:, 2 * h + j, tt * 512:(tt + 1) * 512], in0=pn[:], in1=r[:], op=ALU.mult),
                              reads=[pn.b, r.b], writes=[oT.b])

                its = [(h, tt) for h in range(4) for tt in range(HT // 512)]
                prev = None
                for (h, tt) in its:
                    q = stage_q(h, tt)
                    if prev is not None:
                        stage_att(*prev)
                    prev = (h, tt, q)
                stage_att(*prev)
                self.proj_tm_res(oT, 0, HT, 8, wo, "h1", hf * HT, "h2", hf * HT, es2)


def _ffn(self, bi):
    nc, tk = self.nc, self.tk
    self.scratch("ff_fm", [DFF, T], BF16)
    NCT = DFF // 128
    with self.scope() as es:
        hnT = self.sb(es, "ff_hnT", [128, 8, T], BF16, dj=True)
        self.norm_T("h2", 0, T, "norm_ffn", hnT)
        cw = [self.col_vec(es, "ffn_conv", j, 0, NCT, f"ff_cw{j}") for j in range(3)]
        cb = self.col_vec(es, "ffn_conv_b", 0, 0, NCT, "ff_cb")
        wring = self.ring(es, "ff_w", 4, [128, 8, 128], BF16)
        atr = self.ring(es, "ff_a", 2, [128, T + 2], F32)
        btr = self.ring(es, "ff_b", 2, [128, T], F32)
        acc = self.sb(es, "ff_acc", [128, T], F32)
        ob = self.ring(es, "ff_ob", 2, [128, T], BF16)
        for at in atr.tiles:
            tk.op("pool", lambda: nc.gpsimd.memset(at[:, 0:2], 0.0), writes=[at.b])
        for ci in range(NCT):
            at, bt = atr.next(), btr.next()
            wa, wb = wring.next(), wring.next()
            self.load_w(wa, "ffn_up_bf", ci * 128, 128)
            self.load_w(wb, "ffn_up_bf", DFF + ci * 128, 128)
            for tt in range(T // 512):
                pa, pb = self.psum(), self.psum()
                for kc in range(8):
                    tk.op("pe", lambda: nc.tensor.matmul(pa[:], lhsT=wa[:, kc, :], rhs=hnT[:, kc, tt * 512:(tt + 1) * 512], start=(kc == 0), stop=(kc == 7)),
                          reads=[wa.b, hnT.b], writes=[pa.b])
                for kc in range(8):
                    tk.op("pe", lambda: nc.tensor.matmul(pb[:], lhsT=wb[:, kc, :], rhs=hnT[:, kc, tt * 512:(tt + 1) * 512], start=(kc == 0), stop=(kc == 7)),
                          reads=[wb.b, hnT.b], writes=[pb.b])
                tk.op("act", lambda: nc.scalar.copy(at[:, 2 + tt * 512:2 + (tt + 1) * 512], pa[:]), reads=[pa.b], writes=[at.b])
                tk.op("dve", lambda: nc.vector.tensor_copy(bt[:, tt * 512:(tt + 1) * 512], pb[:]), reads=[pb.b], writes=[bt.b])
            tk.op("dve", lambda: nc.vector.tensor_scalar(acc[:], at[:, 2:T + 2], cw[2][:, ci:ci + 1], cb[:, ci:ci + 1], ALU.mult, ALU.add),
                  reads=[at.b, cw[2].b, cb.b], writes=[acc.b])
            tk.op("dve", lambda: nc.vector.scalar_tensor_tensor(out=acc[:], in0=at[:, 1:T + 1], scalar=cw[1][:, ci:ci + 1], in1=acc[:], op0=ALU.mult, op1=ALU.add),
                  reads=[at.b, cw[1].b, acc.b], writes=[acc.b])
            tk.op("dve", lambda: nc.vector.scalar_tensor_tensor(out=acc[:], in0=at[:, 0:T], scalar=cw[0][:, ci:ci + 1], in1=acc[:], op0=ALU.mult, op1=ALU.add),
                  reads=[at.b, cw[0].b, acc.b], writes=[acc.b])
            tk.op("act", lambda: nc.scalar.activation(out=acc[:], in_=acc[:], func=AF.Silu), reads=[acc.b], writes=[acc.b])
            o = ob.next()
            tk.op("dve", lambda: nc.vector.tensor_tensor(out=o[:], in0=acc[:], in1=bt[:], op=ALU.mult), reads=[acc.b, bt.b], writes=[o.b])
            tk.dma("pool", self.din["ff_fm"].ap()[ci * 128:(ci + 1) * 128, :], o[:], reads=[o.b], writes=[self.dbuf["ff_fm"]])
    with self.scope() as es:
        wd = self.sb(es, "ff_wd", [128, NCT, 1024], BF16)
        self.load_w(wd, "ffn_down_bf", 0, 1024, kchunks=NCT)
        TBK = 1024
        for blk in range(T // TBK):
            with self.scope() as es2:
                fT = self.sb(es2, "ff_fT", [128, NCT, TBK], BF16)
                tk.dma("sp", fT[:], self.dap("ff_fm", blk * TBK, [[T, 128], [128 * T, NCT], [1, TBK]]), reads=[self.dbuf["ff_fm"]], writes=[fT.b])
                self.proj_tm_res(fT, 0, TBK, NCT, wd, "h2", blk * TBK, "out", bi * T + blk * TBK, es2)


Prog.proj_tm_res = _proj_tm_res
Prog.phase_merge = _merge
Prog.phase_cross = _cross
Prog.phase_ffn = _ffn
```

```python
import contextlib
import math
import numpy as np
import concourse.bass as bass
import concourse.mybir as mybir
from concourse.bass_utils import run_bass_kernel_spmd

F32 = mybir.dt.float32
BF16 = mybir.dt.bfloat16
AF = mybir.ActivationFunctionType
ALU = mybir.AluOpType
AX = mybir.AxisListType

NCORES = 8
NB = 2
T = 4096
D = 1024
NMEM = 256
DFF = 2816
IN_COLS = 8016
BIG = 30000.0
OFFC = 2176
LVEC = 7680
SCALE_NSA = 0.125


class Buf:
    __slots__ = ("w", "r", "name", "dj", "xr")

    def __init__(self, name="", dj=False):
        self.w = {}
        self.r = {}
        self.name = name
        self.dj = dj
        self.xr = False


class Tile:
    def __init__(self, t, name, dj=False):
        self.t = t
        self.b = Buf(name, dj)

    def __getitem__(self, k):
        return self.t[k]


class Ring:
    def __init__(self, tiles):
        self.tiles = tiles
        self.i = 0

    def next(self):
        t = self.tiles[self.i]
        self.i = (self.i + 1) % len(self.tiles)
        return t


class TK:
    EPOCH = 20000
    NDSEM = 10

    def __init__(self, nc, es):
        self.nc = nc
        self.es = es
        self.eng = {"pe": nc.tensor, "act": nc.scalar, "dve": nc.vector,
                    "pool": nc.gpsimd, "sp": nc.sync}
        self.cnt = {e: 0 for e in self.eng}
        self.esem = {e: [] for e in self.eng}
        self.seen = {e: {} for e in self.eng}
        self.dsem = {}
        self.dptr = {}
        self.nwait = 0
        self.fence = {}

    def _newsem(self, name):
        return self.es.enter_context(self.nc.semaphore(name))

    def _engsem(self, e, epoch):
        while len(self.esem[e]) <= epoch:
            self.esem[e].append(self._newsem(f"s_{e}_{len(self.esem[e])}"))
        return self.esem[e][epoch]

    def _wait(self, e, ts):
        sem, val, src = ts
        if src == "pe" and e == "pe":
            return
        k = id(sem)
        if self.seen[e].get(k, 0) >= val:
            return
        self.seen[e][k] = val
        self.eng[e].wait_ge(sem, val)
        self.nwait += 1

    def deps(self, e, reads, writes):
        for b in reads:
            for ts in b.w.values():
                self._wait(e, ts)
            if b.xr:
                for ts in b.r.values():
                    if ts[2] != e:
                        self._wait(e, ts)
        for b in writes:
            if not (b.dj and not b.r):
                for ts in b.w.values():
                    self._wait(e, ts)
            for ts in b.r.values():
                self._wait(e, ts)

    def mark(self, ts, reads, writes):
        k = id(ts[0])
        for b in reads:
            b.r[k] = ts
        for b in writes:
            if b.dj and not b.r:
                b.w[k] = ts
            else:
                b.w = {k: ts}
                b.r = {}

    def op(self, e, ins_fn, reads=(), writes=()):
        self.deps(e, reads, writes)
        n = self.cnt[e]
        sem = self._engsem(e, n // self.EPOCH)
        val = n % self.EPOCH + 1
        ins_fn().then_inc(sem, 1)
        self.cnt[e] = n + 1
        ts = (sem, val, e)
        self.mark(ts, reads, writes)
        return ts

    def dma(self, q, out_ap, in_ap, reads=(), writes=(), **kw):
        if q not in self.dsem:
            self.dsem[q] = [[self._newsem(f"d_{q}_{i}"), 0] for i in range(self.NDSEM)]
            self.dptr[q] = 0
        slot = self.dsem[q][self.dptr[q]]
        self.dptr[q] = (self.dptr[q] + 1) % self.NDSEM
        sem, issued = slot
        if issued:
            self._wait(q, (sem, 16 * issued, None))
        self.deps(q, reads, writes)
        self.eng[q].dma_start(out=out_ap, in_=in_ap, **kw).then_inc(sem, 16)
        slot[1] = issued + 1
        ts = (sem, 16 * (issued + 1), None)
        self.mark(ts, reads, writes)
        return ts

    def update_fence(self):
        f = {}
        for e in self.eng:
            n = self.cnt[e]
            if n:
                sem = self.esem[e][(n - 1) // self.EPOCH]
                f[id(sem)] = (sem, (n - 1) % self.EPOCH + 1, e)
        for q in self.dsem:
            for sem, issued in self.dsem[q]:
                if issued:
                    f[id(sem)] = (sem, 16 * issued, None)
        self.fence = f

    def drain(self):
        for q in self.dsem:
            for sem, issued in self.dsem[q]:
                if issued:
                    self._wait(q, (sem, 16 * issued, None))


def _t5_bucket_np(dist):
    n = np.maximum(dist, 0)
    nf = np.maximum(n, 1).astype(np.float64)
    large = 16 + (np.log(nf / 16) / math.log(128 / 16) * 16).astype(np.int64)
    large = np.minimum(large, 31)
    return np.where(n < 16, n, large)


def host_consts():
    c = {}
    c["c_ident"] = np.eye(128, dtype=np.float32)
    c["c_J"] = np.ascontiguousarray(np.eye(128, dtype=np.float32)[::-1])
    hb = np.arange(128) // 64
    bd = (hb[:, None] == hb[None, :]).astype(np.float32)
    c["c_bdones"] = bd
    c["c_ones"] = np.ones((128, 128), np.float32)
    s = np.arange(128) % 64
    strict = bd * (s[:, None] < s[None, :])
    incl = bd * (s[:, None] <= s[None, :])
    c["c_mask2"] = np.concatenate([strict, incl], axis=1).astype(np.float32)
    bd5 = np.zeros((128, 5, 2, 64), np.float32)
    for h in range(2):
        bd5[64 * h:64 * h + 64, :, h, :] = 1.0
    c["c_bdmask5"] = bd5.reshape(128, 640)
    seg = np.ones((128, 1024), np.float32)
    seg[:, ::64] = 0.0
    c["c_segmask"] = seg
    dist = np.arange(LVEC) - OFFC
    bk = _t5_bucket_np(dist)
    oh = np.zeros((33, LVEC), np.float32)
    oh[bk, np.arange(LVEC)] = 1.0
    ec = oh.copy()
    ec[32] = np.where(dist >= 0, 0.0, -BIG)
    ec[:32, dist < 0] = 0.0
    ew = oh.copy()
    ok = (dist >= 0) & (dist < 512)
    ew[32] = np.where(ok, 0.0, -BIG)
    ew[:32, ~ok] = 0.0
    c["c_e33c"] = ec
    c["c_e33w"] = ew
    t = np.arange(T)
    cur = t // 64
    blk = np.arange(64)
    forced = (blk[:, None] == 0) | (blk[:, None] == cur[None, :]) | (blk[:, None] == cur[None, :] - 1)
    c["c_forced"] = np.where(forced, 1e4, 0.0).astype(np.float32)
    ex = np.zeros((64, 32, 128), np.float32)
    for kt in range(32):
        for p in range(128):
            ex[2 * kt + p // 64, kt, p] = 1.0
    c["c_expand"] = ex.reshape(64, 32 * 128)
    ncmp = 255
    ci = np.arange(256)[:, None] * 16
    sj = np.arange(64)[None, :] * 64
    c2s = ((ci <= sj + 63) & (ci + 31 >= sj)).astype(np.float32)
    c2s[ncmp:] = 0.0
    c["c_c2s"] = c2s
    return c


CONST_SHAPES = {k: v.shape for k, v in host_consts().items()}

W_SPECS = [
    ("w_in", 1024, IN_COLS), ("rwkv_w2", 64, 1024), ("rwkv_a2", 64, 1024), ("rwkv_g2", 160, 1024),
    ("cmp_w1_k", 2048, 256), ("cmp_w2_k", 256, 64), ("cmp_w1_v", 2048, 256), ("cmp_w2_v", 256, 64),
    ("w_branch_rwkv", 1024, 1024), ("w_branch_nsa", 1024, 1024), ("w_mix_out", 1024, 1024),
    ("ca_wq", 1024, 1024), ("ca_wkv", 1024, 2048), ("ca_wo", 1024, 1024),
    ("ffn_up", 1024, 2 * DFF), ("ffn_down", DFF, 1024),
]
V_SPECS = [
    ("rel_bias", (32, 16)), ("norm_mix", (1, 1024)), ("rwkv_mu", (1, 3360)), ("rwkv_w0", (1, 1024)),
    ("rwkv_a0", (1, 1024)), ("rwkv_kk", (1, 1024)), ("rwkv_ka", (1, 1024)), ("rwkv_rk", (1, 1024)),
    ("rwkv_lnx_w", (1, 1024)), ("rwkv_lnx_b", (1, 1024)), ("nsa_q_gain", (1, 64)), ("nsa_k_gain", (3, 64)),
    ("cmp_pe_k", (1, 2048)), ("cmp_pe_v", (1, 2048)), ("norm_cross", (1, 1024)), ("norm_mem", (1, 1024)),
    ("ca_q_gain", (1, 256)), ("ca_k_gain", (1, 256)), ("norm_ffn", (1, 1024)),
    ("ffn_conv", (3, DFF)), ("ffn_conv_b", (1, DFF)),
]


class Prog:
    def __init__(self, upto="all", dbg=()):
        self.upto = upto
        self.dbg = set(dbg)
        nc = self.nc = bass.Bass("TRN2", target_bir_lowering=False)
        self.es = contextlib.ExitStack()
        self.tk = TK(nc, self.es)
        self.din = {}
        self.dbuf = {}

    def dram_in(self, name, shape):
        self.din[name] = self.nc.dram_tensor(name, list(shape), F32, kind="ExternalInput")
        self.dbuf[name] = Buf(name, dj=True)
        return self.din[name]

    def scratch(self, name, shape, dt):
        if name in self.din:
            return self.din[name]
        kind = "ExternalOutput" if name in self.dbg else "Internal"
        self.din[name] = self.nc.dram_tensor(name, list(shape), dt, kind=kind)
        self.dbuf[name] = Buf(name, dj=True)
        return self.din[name]

    def sb(self, es, name, shape, dt, dj=False):
        self.uid = getattr(self, "uid", 0) + 1
        name = f"{name}_{self.uid}"
        t = Tile(es.enter_context(self.nc.sbuf_tensor(name, list(shape), dt)), name, dj)
        t.b.r = dict(self.tk.fence)
        return t

    @contextlib.contextmanager
    def scope(self):
        with contextlib.ExitStack() as es:
            yield es
        self.tk.update_fence()

    def ring(self, es, name, n, shape, dt):
        return Ring([self.sb(es, f"{name}{i}", shape, dt) for i in range(n)])

    def psum(self):
        return self.psr.next()

    def rpow(self, ap, buf, power):
        nc, tk = self.nc, self.tk
        tk.op("act", lambda: nc.scalar.activation(out=ap, in_=ap, func=AF.Ln), reads=[buf], writes=[buf])
        tk.op("act", lambda: nc.scalar.activation(out=ap, in_=ap, func=AF.Exp, scale=float(power)), reads=[buf], writes=[buf])

    def dap(self, name, offset, ap):
        return bass.AP(tensor=self.din[name], offset=offset, ap=[list(x) for x in ap])

    def load_const(self, es, name, dt=F32, tmp_es=None):
        nc, tk = self.nc, self.tk
        shp = CONST_SHAPES[name]
        t32 = self.sb(es if dt == F32 else tmp_es, name + "_f", shp, F32)
        tk.dma("sp", t32[:], self.din[name].ap()[:, :], reads=[self.dbuf[name]], writes=[t32.b])
        if dt == F32:
            return t32
        t16 = self.sb(es, name + "_h", shp, BF16)
        tk.op("dve", lambda: nc.vector.tensor_copy(t16[:], t32[:]), reads=[t32.b], writes=[t16.b])
        return t16

    def bcast_vec(self, es, name, row, c0, n, tname):
        t = self.sb(es, tname, [128, n], F32)
        src = self.din[name].ap()[row:row + 1, c0:c0 + n].partition_broadcast(128)
        self.tk.dma("sp", t[:], src, reads=[self.dbuf[name]], writes=[t.b])
        return t

    def col_vec(self, es, name, row, c0, nchunk, tname, p=128):
        nc, tk = self.nc, self.tk
        t = self.sb(es, tname, [p, nchunk], F32)
        ncols = self.din[name].shape[1]
        with self.scope() as es2:
            raw = self.sb(es2, tname + "_raw", [nchunk, p], F32)
            tk.dma("sp", raw[:], self.dap(name, row * ncols + c0, [[p, nchunk], [1, p]]), reads=[self.dbuf[name]], writes=[raw.b])
            ps_ = self.psum()
            tk.op("pe", lambda: nc.tensor.transpose(ps_[:p, 0:nchunk], raw[:], self.ident[:nchunk, :nchunk]),
                  reads=[raw.b, self.ident.b], writes=[ps_.b])
            tk.op("dve", lambda: nc.vector.tensor_copy(t[:], ps_[:p, 0:nchunk]), reads=[ps_.b], writes=[t.b])
        return t

    def phase_w(self):
        nc, tk = self.nc, self.tk
        with self.scope() as es:
            st = self.ring(es, "wst", 3, [128, 2048], F32)
            sh = self.ring(es, "wsh", 3, [128, 2048], BF16)
            k = 0
            for name, R, C in W_SPECS:
                dst = self.scratch(name + "_bf", [R, C], BF16)
                src = self.din[name].ap()
                for r0 in range(0, R, 128):
                    rr = min(128, R - r0)
                    for c0 in range(0, C, 2048):
                        cc = min(2048, C - c0)
                        a = st.next()
                        h = sh.next()
                        tk.dma("sp", a[:rr, :cc], src[r0:r0 + rr, c0:c0 + cc], reads=[self.dbuf[name]], writes=[a.b])
                        e = ("dve", "pool", "act")[k % 3]
                        k += 1
                        if e == "act":
                            tk.op(e, lambda: nc.scalar.copy(h[:rr, :cc], a[:rr, :cc]), reads=[a.b], writes=[h.b])
                        elif e == "dve":
                            tk.op(e, lambda: nc.vector.tensor_copy(h[:rr, :cc], a[:rr, :cc]), reads=[a.b], writes=[h.b])
                        else:
                            tk.op(e, lambda: nc.gpsimd.tensor_copy(h[:rr, :cc], a[:rr, :cc]), reads=[a.b], writes=[h.b])
                        tk.dma("pool", dst.ap()[r0:r0 + rr, c0:c0 + cc], h[:rr, :cc], reads=[h.b],
                               writes=[self.dbuf[name + "_bf"]])

    def norm_T(self, src_name, src_row0, ntok, gname, dstT):
        nc, tk = self.nc, self.tk
        with self.scope() as es:
            gbc = self.bcast_vec(es, gname, 0, 0, D, "nt_g")
            xr = self.ring(es, "nt_x", 2, [128, D], F32)
            xs = self.ring(es, "nt_xs", 2, [128, D], F32)
            junk = self.sb(es, "nt_junk", [128, D], BF16)
            st = self.ring(es, "nt_st", 2, [128, 4], F32)
            src = self.din[src_name].ap()
            for i in range(ntok // 128):
                x = xr.next()
                s = st.next()
                y = xs.next()
                tk.dma("sp", x[:], src[src_row0 + i * 128: src_row0 + (i + 1) * 128, :],
                       reads=[self.dbuf[src_name]], writes=[x.b])
                tk.op("act", lambda: nc.scalar.activation(out=junk[:], in_=x[:], func=AF.Square, accum_out=s[:, 0:1]),
                      reads=[x.b], writes=[junk.b, s.b])
                tk.op("dve", lambda: nc.vector.tensor_scalar(s[:, 1:2], s[:, 0:1], 1.0 / D, 1e-6, ALU.mult, ALU.add),
                      reads=[s.b], writes=[s.b])
                tk.op("act", lambda: nc.scalar.sqrt(s[:, 2:3], s[:, 1:2]), reads=[s.b], writes=[s.b])
                tk.op("dve", lambda: nc.vector.reciprocal(s[:, 3:4], s[:, 2:3]), reads=[s.b], writes=[s.b])
                tk.op("dve", lambda: nc.vector.scalar_tensor_tensor(out=y[:], in0=x[:], scalar=s[:, 3:4], in1=gbc[:],
                                                                    op0=ALU.mult, op1=ALU.mult),
                      reads=[x.b, s.b, gbc.b], writes=[y.b])
                for half in range(2):
                    p = self.psum()
                    for j in range(4):
                        kc = half * 4 + j
                        tk.op("pe", lambda: nc.tensor.transpose(p[:, j * 128:(j + 1) * 128], y[:, kc * 128:(kc + 1) * 128],
                                                                self.ident[:]),
                              reads=[y.b, self.ident.b], writes=[p.b])
                    o = dstT[:, half * 4:half * 4 + 4, i * 128:(i + 1) * 128]
                    pin = p[:, :].rearrange("p (a b) -> p a b", a=4)
                    if half == 0:
                        tk.op("act", lambda: nc.scalar.copy(o, pin), reads=[p.b], writes=[dstT.b])
                    else:
                        tk.op("dve", lambda: nc.vector.tensor_copy(o, pin), reads=[p.b], writes=[dstT.b])

    def load_w(self, tile, wname, c0, ncols, kchunks=8, r0=0):
        C = self.din[wname].shape[1]
        src = self.dap(wname, r0 * C + c0, [[C, 128], [128 * C, kchunks], [1, ncols]])
        self.tk.dma("sp", tile[:, 0:kchunks, 0:ncols], src, reads=[self.dbuf[wname]], writes=[tile.b])

    def proj_fm(self, wname, c0, ncols_total, actT, ntok, epi, kchunks=8, wring=None):
        nc, tk = self.nc, self.tk
        nct = (ncols_total + 127) // 128
        for ci in range(nct):
            cc = min(128, ncols_total - ci * 128)
            w = wring.next()
            self.load_w(w, wname, c0 + ci * 128, cc, kchunks)
            for tt in range(ntok // 512):
                p = self.psum()
                for kc in range(kchunks):
                    tk.op("pe", lambda: nc.tensor.matmul(p[:cc, :], lhsT=w[:, kc, 0:cc],
                                                          rhs=actT[:, kc, tt * 512:(tt + 1) * 512],
                                                          start=(kc == 0), stop=(kc == kchunks - 1)),
                          reads=[w.b, actT.b], writes=[p.b])
                epi(p, ci, tt, cc)

    def phase_b(self, xT):
        nc, tk = self.nc, self.tk
        self.scratch("zr_fm", [3360, T], F32)
        self.scratch("q_fm", [1024, T], BF16)
        self.scratch("kcvc_fm", [512, T], BF16)
        self.scratch("ks_fm", [256, T], BF16)
        self.scratch("kw_fm", [256, T], BF16)
        self.scratch("vsw_tm", [T, 512], BF16)
        self.scratch("gates_fm", [48, T], F32)
        self.scratch("gm_fm", [2048, T], F32)
        with self.scope() as es:
            wring = self.ring(es, "pb_w", 2, [128, 8, 128], BF16)
            o32 = self.ring(es, "pb_o32", 3, [128, 512], F32)
            o16 = self.ring(es, "pb_o16", 3, [128, 512], BF16)
            sq = self.ring(es, "pb_sq", 2, [128, 512], F32)
            qg = self.sb(es, "pb_qg", [128, 4], F32)
            eps = self.sb(es, "pb_eps", [128, 1], F32)
            tk.op("pool", lambda: nc.gpsimd.memset(eps[:], 1e-6), writes=[eps.b])
            for h in range(2):
                tk.dma("sp", qg[64 * h:64 * h + 64, 0:1], self.dap("nsa_q_gain", 0, [[1, 64], [1, 1]]),
                       reads=[self.dbuf["nsa_q_gain"]], writes=[qg.b])
                for j in (1, 2):
                    tk.dma("sp", qg[64 * h:64 * h + 64, j + 1:j + 2], self.dap("nsa_k_gain", 64 * j, [[1, 64], [1, 1]]),
                           reads=[self.dbuf["nsa_k_gain"]], writes=[qg.b])
            cnt = [0]

            def store(dname, row0, t0, tile, rows):
                tk.dma("pool", self.din[dname].ap()[row0:row0 + rows, t0:t0 + 512], tile[:rows, :], reads=[tile.b],
                       writes=[self.dbuf[dname]])

            def epi_copy(dname, row_base, dt):
                def f(p, ci, tt, cc):
                    o = (o32 if dt == F32 else o16).next()
                    cnt[0] += 1
                    if cnt[0] % 2:
                        tk.op("act", lambda: nc.scalar.copy(o[:cc, :], p[:cc, :]), reads=[p.b], writes=[o.b])
                    else:
                        tk.op("dve", lambda: nc.vector.tensor_copy(o[:cc, :], p[:cc, :]), reads=[p.b], writes=[o.b])
                    store(dname, row_base + ci * 128, tt * 512, o, cc)
                return f

            def epi_sig(dname, row_base):
                def f(p, ci, tt, cc):
                    o = o32.next()
                    tk.op("act", lambda: nc.scalar.activation(out=o[:cc, :], in_=p[:cc, :], func=AF.Sigmoid),
                          reads=[p.b], writes=[o.b])
                    store(dname, row_base + ci * 128, tt * 512, o, cc)
                return f

            def epi_norm(dname, row_base, gcol, scale):
                def f(p, ci, tt, cc):
                    s = sq.next()
                    tk.op("act", lambda: nc.scalar.activation(out=s[:], in_=p[:], func=AF.Square), reads=[p.b], writes=[s.b])
                    p2 = self.psum()
                    tk.op("pe", lambda: nc.tensor.matmul(p2[:], lhsT=self.bdones[:], rhs=s[:], start=True, stop=True),
                          reads=[self.bdones.b, s.b], writes=[p2.b])
                    r = o32.next()
                    tk.op("dve", lambda: nc.vector.tensor_scalar(r[:], p2[:], 1.0 / 64, 1e-6, ALU.mult, ALU.add),
                          reads=[p2.b], writes=[r.b])
                    self.rpow(r[:], r.b, -0.5)
                    tk.op("dve", lambda: nc.vector.tensor_tensor(out=r[:], in0=p[:], in1=r[:], op=ALU.mult),
                          reads=[p.b, r.b], writes=[r.b])
                    o = o16.next()
                    tk.op("dve", lambda: nc.vector.tensor_scalar(o[:], r[:], qg[:, gcol:gcol + 1], scale, ALU.mult, ALU.mult),
                          reads=[r.b, qg.b], writes=[o.b])
                    store(dname, row_base + ci * 128, tt * 512, o, cc)
                return f

            segs = [
                (0, 3360, epi_copy("zr_fm", 0, F32)),
                (3360, 1024, epi_norm("q_fm", 0, 0, SCALE_NSA)),
                (4384, 512, epi_copy("kcvc_fm", 0, BF16)),
                (4896, 256, epi_norm("ks_fm", 0, 2, 1.0)),
                (5408, 256, epi_norm("kw_fm", 0, 3, 1.0)),
                (5920, 48, epi_sig("gates_fm", 0)),
                (5968, 2048, epi_sig("gm_fm", 0)),
            ]
            for c0, n, epi in segs:
                self.proj_fm("w_in_bf", c0, n, xT, T, epi, wring=wring)
            wv = self.sb(es, "pb_wv", [128, 8, 512], BF16)
            self.load_w(wv, "w_in_bf", 5152, 256)
            C = IN_COLS
            tk.dma("sp", wv[:, :, 256:512], self.dap("w_in_bf", 5664, [[C, 128], [128 * C, 8], [1, 256]]),
                   reads=[self.dbuf["w_in_bf"]], writes=[wv.b])
            for i in range(T // 128):
                p = self.psum()
                for kc in range(8):
                    tk.op("pe", lambda: nc.tensor.matmul(p[:], lhsT=xT[:, kc, i * 128:(i + 1) * 128], rhs=wv[:, kc, :],
                                                          start=(kc == 0), stop=(kc == 7)), reads=[xT.b, wv.b], writes=[p.b])
                o = o16.next()
                tk.op("act", lambda: nc.scalar.copy(o[:], p[:]), reads=[p.b], writes=[o.b])
                tk.dma("pool", self.din["vsw_tm"].ap()[i * 128:(i + 1) * 128, :], o[:], reads=[o.b], writes=[self.dbuf["vsw_tm"]])

    def build(self):
        nc, tk = self.nc, self.tk
        self.dram_in("x", [NB * T, D])
        self.dram_in("mem", [NB * NMEM, D])
        for name, R, C in W_SPECS:
            self.dram_in(name, [R, C])
        for name, shp in V_SPECS:
            self.dram_in(name, shp)
        for name, shp in CONST_SHAPES.items():
            self.dram_in(name, shp)
        self.out = self.nc.dram_tensor("out", [NB * T, D], F32, kind="ExternalOutput")
        self.din["out"] = self.out
        self.dbuf["out"] = Buf("out", dj=True)
        es = self.es
        self.psr = Ring([Tile(es.enter_context(nc.psum_tensor(f"ps{i}", [128, 512], F32)), f"ps{i}") for i in range(8)])
        for t_ in self.psr.tiles:
            t_.b.xr = True
        self.ident = self.load_const(es, "c_ident")
        self.bdones = self.load_const(es, "c_bdones")
        self.phase_w()
        if self.upto == "w":
            return self.finish()
        self.nsa_bias()
        for bi in range(NB):
            self.seq(bi)
            if self.upto != "all":
                break
        return self.finish()

    def seq(self, bi):
        tk = self.tk
        with self.scope() as es1:
            xT = self.sb(es1, "xT", [128, 8, T], BF16, dj=True)
            self.norm_T("x", bi * T, T, "norm_mix", xT)
            if bi == 0 and "xT_dbg" in self.dbg:
                d = self.scratch("xT_dbg", [128, 8 * T], BF16)
                tk.dma("sp", d.ap()[:, :], xT[:, :, :].rearrange("p a b -> p (a b)"), reads=[xT.b], writes=[self.dbuf["xT_dbg"]])
            if self.upto == "a":
                return
            self.phase_b(xT)
        if self.upto == "b":
            return
        if not getattr(self, "skip_rwkv", False):
            self.phase_rwkv()
        if self.upto == "rwkv":
            return
        self.phase_nsa()
        if self.upto == "nsa":
            return
        self.phase_merge(bi)
        if self.upto == "merge":
            return
        self.phase_cross(bi)
        if self.upto == "cross":
            return
        self.phase_ffn(bi)

    def finish(self):
        self.tk.drain()
        self.es.close()
        return self.nc


def make_in_maps(inputs, cores=range(NCORES)):
    consts = host_consts()
    shared = {}
    for name, R, C in W_SPECS:
        shared[name] = np.ascontiguousarray(np.asarray(inputs[name], np.float32).reshape(R, C))
    for name, shp in V_SPECS:
        shared[name] = np.ascontiguousarray(np.asarray(inputs[name], np.float32).reshape(shp))
    shared.update(consts)
    x = np.asarray(inputs["x"], np.float32)
    mem = np.asarray(inputs["mem"], np.float32)
    maps = []
    for c in cores:
        m = dict(shared)
        m["x"] = np.ascontiguousarray(x[NB * c:NB * c + NB].reshape(NB * T, D))
        m["mem"] = np.ascontiguousarray(mem[NB * c:NB * c + NB].reshape(NB * NMEM, D))
        maps.append(m)
    return maps


def kernel(**inputs):
    prog = Prog()
    nc = prog.build()
    maps = make_in_maps(inputs)
    res = run_bass_kernel_spmd(nc, maps, core_ids=list(range(NCORES)))
    outs = [np.asarray(r["out"]).reshape(NB, T, D) for r in res.results]
    return np.concatenate(outs, axis=0).astype(np.float32)


def _rwkv(self):
    nc, tk = self.nc, self.tk
    TB = 512
    self.scratch("yr_fm", [1024, T], BF16)
    zr = self.din["zr_fm"].ap()
    zb = self.dbuf["zr_fm"]

    def shift_load(dst_ap, dst_buf, r0, nrows, t0, nt, mucol, X, dtile):
        if t0 == 0:
            tk.op("pool", lambda: nc.gpsimd.memset(X[:nrows, 0:1], 0.0), writes=[X.b])
            tk.dma("sp", X[:nrows, 1:nt + 1], zr[r0:r0 + nrows, 0:nt], reads=[zb], writes=[X.b])
        else:
            tk.dma("sp", X[:nrows, 0:nt + 1], zr[r0:r0 + nrows, t0 - 1:t0 + nt], reads=[zb], writes=[X.b])
        tk.op("pool", lambda: nc.gpsimd.tensor_tensor(out=dtile[:nrows, :nt], in0=X[:nrows, 0:nt], in1=X[:nrows, 1:nt + 1],
                                                      op=ALU.subtract), reads=[X.b], writes=[dtile.b])
        tk.op("dve", lambda: nc.vector.scalar_tensor_tensor(out=dst_ap, in0=dtile[:nrows, :nt], scalar=mucol,
                                                             in1=X[:nrows, 1:nt + 1], op0=ALU.mult, op1=ALU.add),
              reads=[dtile.b, X.b], writes=[dst_buf])

    with self.scope() as es:
        mask4 = self.sb(es, "rk_mask4", [128, 512], F32)
        for j in range(2):
            tk.dma("sp", mask4[:, j * 256:(j + 1) * 256], self.din["c_mask2"].ap()[:, :], reads=[self.dbuf["c_mask2"]], writes=[mask4.b])
        bdm5 = self.load_const(es, "c_bdmask5")
        segm = self.sb(es, "rk_seg", [128, TB], F32)
        tk.dma("sp", segm[:], self.din["c_segmask"].ap()[:, 0:TB], reads=[self.dbuf["c_segmask"]], writes=[segm.b])
        lw = self.sb(es, "rk_lw", [64, T], BF16)
        la = self.sb(es, "rk_la", [64, T], BF16)
        lg = self.sb(es, "rk_lg", [128, 2, T], BF16)
        w2 = self.sb(es, "rk_w2", [64, 1024], BF16)
        a2 = self.sb(es, "rk_a2", [64, 1024], BF16)
        g2 = self.sb(es, "rk_g2", [128, 2, 1024], BF16)
        tk.dma("sp", w2[:], self.din["rwkv_w2_bf"].ap()[:, :], reads=[self.dbuf["rwkv_w2_bf"]], writes=[w2.b])
        tk.dma("sp", a2[:], self.din["rwkv_a2_bf"].ap()[:, :], reads=[self.dbuf["rwkv_a2_bf"]], writes=[a2.b])
        tk.dma("sp", g2[:, 0, :], self.din["rwkv_g2_bf"].ap()[0:128, :], reads=[self.dbuf["rwkv_g2_bf"]], writes=[g2.b])
        tk.dma("sp", g2[0:32, 1, :], self.din["rwkv_g2_bf"].ap()[128:160, :], reads=[self.dbuf["rwkv_g2_bf"]], writes=[g2.b])
        pc = {}
        for nm in ("rwkv_w0", "rwkv_a0", "rwkv_kk", "rwkv_ka", "rwkv_rk", "rwkv_lnx_w", "rwkv_lnx_b"):
            pc[nm] = self.col_vec(es, nm, 0, 0, 8, "rk_" + nm)
        mu = self.col_vec(es, "rwkv_mu", 0, 0, 24, "rk_mu")
        omk = self.sb(es, "rk_omk", [128, 8], F32)
        tk.op("dve", lambda: nc.vector.tensor_scalar(omk[:], pc["rwkv_ka"][:], -1.0, 1.0, ALU.mult, ALU.add),
              reads=[pc["rwkv_ka"].b], writes=[omk.b])
        with self.scope() as es2:
            X = self.sb(es2, "rk_LX", [128, T + 1], F32)
            dt_ = self.sb(es2, "rk_Ld", [128, T], F32)
            zt = self.sb(es2, "rk_Lz", [128, T], F32)
            for (r0, nrows, kind) in ((3072, 64, "w"), (3136, 64, "a"), (3200, 128, "g0"), (3328, 32, "g1")):
                mucol = self.sb(es2, "rk_Lmu" + kind, [128, 1], F32)
                tk.dma("sp", mucol[:nrows, :], self.dap("rwkv_mu", r0, [[1, nrows], [1, 1]]), reads=[self.dbuf["rwkv_mu"]], writes=[mucol.b])
                shift_load(zt[:nrows, :], zt.b, r0, nrows, 0, T, mucol[:nrows, 0:1], X, dt_)
                if kind == "w":
                    tk.op("act", lambda: nc.scalar.activation(out=lw[:, :], in_=zt[:64, :], func=AF.Tanh), reads=[zt.b], writes=[lw.b])
                elif kind == "a":
                    tk.op("act", lambda: nc.scalar.copy(la[:, :], zt[:64, :]), reads=[zt.b], writes=[la.b])
                elif kind == "g0":
                    tk.op("act", lambda: nc.scalar.activation(out=lg[:, 0, :], in_=zt[:, :], func=AF.Sigmoid), reads=[zt.b], writes=[lg.b])
                else:
                    tk.op("act", lambda: nc.scalar.activation(out=lg[:32, 1, :], in_=zt[:32, :], func=AF.Sigmoid), reads=[zt.b], writes=[lg.b])
        LIM = getattr(self, "rk_lim", 99)
        if LIM <= 1:
            return
        f = lambda n: self.sb(es, n, [128, TB], F32)
        Xr = self.ring(es, "rk_X", 2, [128, TB + 1], F32)
        dtl = f("rk_d")
        rr, kp, logw, aa, gg, kkr, sq, kmod, kb, cum, cex, epv, eng, bonus, tmp = [f("rk_t%d" % i) for i in range(15)]
        einr = self.ring(es, "rk_ein", 2, [128, TB], F32)
        Q5r = self.ring(es, "rk_Q5", 2, [128, 5, TB], F32)
        yfm = f("rk_yfm")
        dd = f("rk_dd")
        ob = self.ring(es, "rk_ob", 2, [128, TB], BF16)
        BD5l = [self.sb(es, f"rk_BD5{i}", [128, 5, 2, 64], F32) for i in range(4)]
        GBKl = [self.sb(es, f"rk_GBK{i}", [128, 512], F32) for i in range(4)]
        NTl = [self.sb(es, f"rk_NT{i}", [128, 128], F32) for i in range(4)]
        MXl = [[self.sb(es, f"rk_MX{i}{k}", [128, 256], F32) for k in range(2)] for i in range(4)]
        XXl = [[self.sb(es, f"rk_XX{i}{k}", [128, 128], F32) for k in range(2)] for i in range(4)]
        TTl = [self.sb(es, f"rk_TT{i}", [128, 128], F32) for i in range(4)]
        TM3l = [self.sb(es, f"rk_TM3{i}", [128, 384], F32) for i in range(4)]
        RHr = self.ring(es, "rk_RH", 2, [128, 128], F32)
        Ur = self.ring(es, "rk_U", 2, [128, 128], F32)
        Sr = self.ring(es, "rk_S", 2, [128, 128], F32)
        SPr = self.ring(es, "rk_SP", 2, [128, 128], F32)
        ident, bdones = self.ident, self.bdones

        def mm(p_ap, pbuf, lhsT, lb, rhs, rb, start=True, stop=True):
            tk.op("pe", lambda: nc.tensor.matmul(p_ap, lhsT=lhsT, rhs=rhs, start=start, stop=stop), reads=[lb, rb], writes=[pbuf])

        bonr = self.ring(es, "rk_bon", 2, [128, TB], F32)
        ggr = self.ring(es, "rk_ggr", 2, [128, TB], F32)

        def prep(hp, tb, out):
            c0 = 128 * hp
            if True:
                t0 = tb * TB
                Q5 = Q5r.next()
                ein = einr.next()
                bonus = bonr.next()
                gg = ggr.next()
                out.update(Q5=Q5, ein=ein, bonus=bonus, gg=gg)
                shift_load(rr[:, :], rr.b, c0, 128, t0, TB, mu[:, hp:hp + 1], Xr.next(), dtl)
                yield
                shift_load(kp[:, :], kp.b, 1024 + c0, 128, t0, TB, mu[:, 8 + hp:9 + hp], Xr.next(), dtl)
                yield
                shift_load(Q5[:, 4, :], Q5.b, 2048 + c0, 128, t0, TB, mu[:, 16 + hp:17 + hp], Xr.next(), dtl)
                p = self.psum()
                mm(p[:], p.b, w2[:, c0:c0 + 128], w2.b, lw[:, t0:t0 + TB], lw.b)
                tk.op("act", lambda: nc.scalar.activation(out=logw[:], in_=p[:], func=AF.Sigmoid, bias=pc["rwkv_w0"][:, hp:hp + 1]),
                      reads=[p.b, pc["rwkv_w0"].b], writes=[logw.b])
                yield
                tk.op("pool", lambda: nc.gpsimd.tensor_scalar_mul(logw[:], logw[:], -math.exp(-0.5)), reads=[logw.b], writes=[logw.b])
                p = self.psum()
                mm(p[:], p.b, a2[:, c0:c0 + 128], a2.b, la[:, t0:t0 + TB], la.b)
                tk.op("act", lambda: nc.scalar.activation(out=aa[:], in_=p[:], func=AF.Sigmoid, bias=pc["rwkv_a0"][:, hp:hp + 1]),
                      reads=[p.b, pc["rwkv_a0"].b], writes=[aa.b])
                p = self.psum()
                mm(p[:], p.b, g2[:, 0, c0:c0 + 128], g2.b, lg[:, 0, t0:t0 + TB], lg.b, True, False)
                mm(p[:], p.b, g2[:32, 1, c0:c0 + 128], g2.b, lg[:32, 1, t0:t0 + TB], lg.b, False, True)
                tk.op("act", lambda: nc.scalar.copy(gg[:], p[:]), reads=[p.b], writes=[gg.b])
                yield
                tk.op("dve", lambda: nc.vector.tensor_scalar_mul(kkr[:], kp[:], pc["rwkv_kk"][:, hp:hp + 1]),
                      reads=[kp.b, pc["rwkv_kk"].b], writes=[kkr.b])
                yield
                tk.op("act", lambda: nc.scalar.activation(out=sq[:], in_=kkr[:], func=AF.Square), reads=[kkr.b], writes=[sq.b])
                p = self.psum()
                mm(p[:], p.b, bdones[:], bdones.b, sq[:], sq.b)
                tk.op("dve", lambda: nc.vector.tensor_scalar_max(tmp[:], p[:], 1e-24), reads=[p.b], writes=[tmp.b])
                yield
                self.rpow(tmp[:], tmp.b, -0.5)
                yield
                tk.op("dve", lambda: nc.vector.tensor_tensor(out=kkr[:], in0=kkr[:], in1=tmp[:], op=ALU.mult), reads=[kkr.b, tmp.b], writes=[kkr.b])
                yield
                tk.op("dve", lambda: nc.vector.tensor_scalar(kmod[:], aa[:], pc["rwkv_ka"][:, hp:hp + 1], omk[:, hp:hp + 1], ALU.mult, ALU.add),
                      reads=[aa.b, pc["rwkv_ka"].b, omk.b], writes=[kmod.b])
                yield
                tk.op("pool", lambda: nc.gpsimd.tensor_tensor(out=kmod[:], in0=kmod[:], in1=kp[:], op=ALU.mult), reads=[kmod.b, kp.b], writes=[kmod.b])
                yield
                tk.op("pool", lambda: nc.gpsimd.tensor_tensor(out=kb[:], in0=kkr[:], in1=aa[:], op=ALU.mult), reads=[kkr.b, aa.b], writes=[kb.b])
                yield
                tk.op("dve", lambda: nc.vector.scalar_tensor_tensor(out=tmp[:], in0=rr[:], scalar=pc["rwkv_rk"][:, hp:hp + 1], in1=kmod[:],
                                                                    op0=ALU.mult, op1=ALU.mult), reads=[rr.b, kmod.b, pc["rwkv_rk"].b], writes=[tmp.b])
                p = self.psum()
                mm(p[:], p.b, bdones[:], bdones.b, tmp[:], tmp.b)
                tk.op("dve", lambda: nc.vector.tensor_tensor(out=bonus[:], in0=p[:], in1=Q5[:, 4, :], op=ALU.mult), reads=[p.b, Q5.b], writes=[bonus.b])
                yield
                tk.op("dve", lambda: nc.vector.tensor_tensor_scan(out=cum[:], data0=segm[:], data1=logw[:], initial=0.0, op0=ALU.mult, op1=ALU.add),
                      reads=[segm.b, logw.b], writes=[cum.b])
                yield
                tk.op("pool", lambda: nc.gpsimd.tensor_tensor(out=cex[:], in0=cum[:], in1=logw[:], op=ALU.subtract), reads=[cum.b, logw.b], writes=[cex.b])
                yield
                tk.op("act", lambda: nc.scalar.activation(out=epv[:], in_=cex[:], func=AF.Exp), reads=[cex.b], writes=[epv.b])
                yield
                tk.op("act", lambda: nc.scalar.activation(out=ein[:], in_=cum[:], func=AF.Exp), reads=[cum.b], writes=[ein.b])
                yield
                tk.op("act", lambda: nc.scalar.activation(out=eng[:], in_=cum[:], func=AF.Exp, scale=-1.0), reads=[cum.b], writes=[eng.b])
                yield
                tk.op("dve", lambda: nc.vector.scalar_tensor_tensor(out=Q5[:, 0, :], in0=kkr[:], scalar=-1.0, in1=epv[:], op0=ALU.mult, op1=ALU.mult),
                      reads=[kkr.b, epv.b], writes=[Q5.b])
                yield
                tk.op("pool", lambda: nc.gpsimd.tensor_tensor(out=Q5[:, 1, :], in0=rr[:], in1=ein[:], op=ALU.mult), reads=[rr.b, ein.b], writes=[Q5.b])
                yield
                tk.op("dve", lambda: nc.vector.tensor_tensor(out=Q5[:, 2, :], in0=kb[:], in1=eng[:], op=ALU.mult), reads=[kb.b, eng.b], writes=[Q5.b])
                yield
                tk.op("pool", lambda: nc.gpsimd.tensor_tensor(out=Q5[:, 3, :], in0=kmod[:], in1=eng[:], op=ALU.mult), reads=[kmod.b, eng.b], writes=[Q5.b])
                yield

        blocks = [(hp, tb) for hp in range(8) for tb in range(T // TB)]
        nxt = {}
        for _ in prep(*blocks[0], nxt):
            pass
        S = None
        epi_g = None
        for bi_, (hp, tb) in enumerate(blocks):
            c0 = 128 * hp
            if True:
                t0 = tb * TB
                Q5, ein, bonus, gg = nxt["Q5"], nxt["ein"], nxt["bonus"], nxt["gg"]
                if tb == 0:
                    S = Sr.next()
                    tk.op("pool", lambda: nc.gpsimd.memset(S[:], 0.0), writes=[S.b])
                NBC = 2
                st = {}

                def batch_gen(chunks):
                    for c in chunks:
                        cs = slice(c * 64, (c + 1) * 64)
                        BD5 = BD5l[c % 4]
                        src = Q5[:, :, cs].unsqueeze(2).to_broadcast([128, 5, 2, 64])
                        tk.op("dve", lambda: nc.vector.tensor_tensor(out=BD5[:], in0=src, in1=bdm5[:, :].rearrange("p (a h b) -> p a h b", a=5, h=2),
                                                                     op=ALU.mult), reads=[Q5.b, bdm5.b], writes=[BD5.b])
                        bd = lambda j, BD5=BD5: BD5[:, j, :, :].rearrange("p h b -> p (h b)")
                        p = self.psum()
                        ar = BD5[:, 0:2, :, :].rearrange("p a h b -> p (a h b)")
                        mm(p[:, 0:256], p.b, bd(2), BD5.b, ar, BD5.b)
                        mm(p[:, 256:512], p.b, bd(3), BD5.b, ar, BD5.b)
                        GBK = GBKl[c % 4]
                        tk.op("dve", lambda: nc.vector.tensor_tensor(out=GBK[:], in0=p[:], in1=mask4[:], op=ALU.mult), reads=[p.b, mask4.b], writes=[GBK.b])
                        yield
                        p3 = self.psum()
                        for j in range(3):
                            tk.op("pe", lambda: nc.tensor.transpose(p3[:, j * 128:(j + 1) * 128], bd(2 + j), ident[:]), reads=[BD5.b, ident.b], writes=[p3.b])
                        TM3 = TM3l[c % 4]
                        tk.op("act", lambda: nc.scalar.copy(TM3[:], p3[:, 0:384]), reads=[p3.b], writes=[TM3.b])
                        st[c] = dict(BD5=BD5, bd=bd, GBK=GBK, TM3=TM3, cs=cs)
                        yield
                    for c in chunks:
                        d = st[c]
                        GBK = d["GBK"]
                        NT = NTl[c % 4]
                        p = self.psum()
                        tk.op("pe", lambda: nc.tensor.transpose(p[:, 0:128], GBK[:, 0:128], ident[:]), reads=[GBK.b, ident.b], writes=[p.b])
                        tk.op("act", lambda: nc.scalar.copy(NT[:], p[:, 0:128]), reads=[p.b], writes=[NT.b])
                        X = XXl[c % 4][0]
                        tk.op("pool", lambda: nc.gpsimd.tensor_tensor(out=X[:], in0=ident[:], in1=GBK[:, 0:128], op=ALU.add),
                              reads=[ident.b, GBK.b], writes=[X.b])
                        d["NT"], d["X"] = NT, X
                        yield
                    for c in chunks:
                        d = st[c]
                        GBK, NT = d["GBK"], d["NT"]
                        MM = MXl[c % 4][0]
                        p = self.psum()
                        mm(p[:, 0:128], p.b, NT[:], NT.b, GBK[:, 0:128], GBK.b)
                        mm(p[:, 128:256], p.b, GBK[:, 0:128], GBK.b, NT[:], NT.b)
                        tk.op("act", lambda: nc.scalar.copy(MM[:], p[:, 0:256]), reads=[p.b], writes=[MM.b])
                        d["MM"], d["par"] = MM, 0
                        yield
                    for j in range(2, 6):
                        for c in chunks:
                            d = st[c]
                            MM, X = d["MM"], d["X"]
                            par = 1 - d["par"]
                            MM2, X2 = MXl[c % 4][par], XXl[c % 4][par]
                            px = self.psum()
                            mm(px[:, 0:128], px.b, MM[:, 128:256], MM.b, X[:], X.b)
                            tk.op("dve", lambda: nc.vector.tensor_tensor(out=X2[:], in0=px[:, 0:128], in1=X[:], op=ALU.add), reads=[px.b, X.b], writes=[X2.b])
                            pm = self.psum()
                            mm(pm[:, 0:128], pm.b, MM[:, 128:256], MM.b, MM[:, 0:128], MM.b)
                            mm(pm[:, 128:256], pm.b, MM[:, 0:128], MM.b, MM[:, 128:256], MM.b)
                            tk.op("act", lambda: nc.scalar.copy(MM2[:], pm[:, 0:256]), reads=[pm.b], writes=[MM2.b])
                            d["MM"], d["X"], d["par"] = MM2, X2, par
                            yield
                    for c in chunks:
                        d = st[c]
                        MM, X = d["MM"], d["X"]
                        p = self.psum()
                        mm(p[:, 0:128], p.b, MM[:, 128:256], MM.b, X[:], X.b)
                        TT = TTl[c % 4]
                        tk.op("dve", lambda: nc.vector.tensor_tensor(out=TT[:], in0=p[:, 0:128], in1=X[:], op=ALU.add), reads=[p.b, X.b], writes=[TT.b])
                        d["TT"] = TT
                        yield

                Sh = [S]

                def chain_gen(chunks):
                    for c in chunks:
                        d = st[c]
                        bd, GBK, TM3, TT, cs = d["bd"], d["GBK"], d["TM3"], d["TT"], d["cs"]
                        BD5 = d["BD5"]
                        S = Sh[0]
                        PCc = ein[:, c * 64 + 63:c * 64 + 64]
                        SP = SPr.next()
                        tk.op("act", lambda: nc.scalar.activation(out=SP[:], in_=S[:], func=AF.Identity, scale=PCc), reads=[S.b, ein.b], writes=[SP.b])
                        p = self.psum()
                        mm(p[:, 0:128], p.b, bd(0), BD5.b, S[:], S.b, True, False)
                        mm(p[:, 0:128], p.b, GBK[:, 256:384], GBK.b, TM3[:, 256:384], TM3.b, False, True)
                        RH = RHr.next()
                        tk.op("act", lambda: nc.scalar.copy(RH[:], p[:, 0:128]), reads=[p.b], writes=[RH.b])
                        yield
                        p = self.psum()
                        mm(p[:, 0:128], p.b, TT[:], TT.b, RH[:], RH.b)
                        U = Ur.next()
                        tk.op("dve", lambda: nc.vector.tensor_copy(U[:], p[:, 0:128]), reads=[p.b], writes=[U.b])
                        yield
                        pS = self.psum()
                        mm(pS[:, 0:128], pS.b, TM3[:, 0:128], TM3.b, U[:], U.b, True, False)
                        mm(pS[:, 0:128], pS.b, TM3[:, 128:256], TM3.b, TM3[:, 256:384], TM3.b, False, True)
                        S2 = Sr.next()
                        tk.op("dve", lambda: nc.vector.scalar_tensor_tensor(out=S2[:], in0=pS[:, 0:128], scalar=PCc, in1=SP[:], op0=ALU.mult, op1=ALU.add),
                              reads=[pS.b, ein.b, SP.b], writes=[S2.b])
                        p = self.psum()
                        mm(p[:, 0:128], p.b, S[:], S.b, bd(1), BD5.b, True, False)
                        mm(p[:, 0:128], p.b, U[:], U.b, GBK[:, 128:256], GBK.b, False, False)
                        mm(p[:, 0:128], p.b, TM3[:, 256:384], TM3.b, GBK[:, 384:512], GBK.b, False, True)
                        tk.op("act", lambda: nc.scalar.copy(yfm[0:64, cs], p[0:64, 0:64]), reads=[p.b], writes=[yfm.b])
                        tk.op("act", lambda: nc.scalar.copy(yfm[64:128, cs], p[64:128, 64:128]), reads=[p.b], writes=[yfm.b])
                        Sh[0] = S2
                        yield

                nbt = TB // 64 // NBC
                batches = [list(range(k * NBC, (k + 1) * NBC)) for k in range(nbt)]
                for _ in batch_gen(batches[0]):
                    if epi_g is not None:
                        try:
                            next(epi_g)
                        except StopIteration:
                            epi_g = None
                if epi_g is not None:
                    for _ in epi_g:
                        pass
                    epi_g = None
                nxt = {}
                prep_g = prep(*blocks[bi_ + 1], nxt) if bi_ + 1 < len(blocks) else iter(())
                next(prep_g, None)
                for k in range(nbt):
                    cg = chain_gen(batches[k])
                    bg = batch_gen(batches[k + 1]) if k + 1 < nbt else iter(())
                    done_b = done_c = False
                    while not (done_b and done_c):
                        for _ in range(3):
                            if not done_b:
                                try:
                                    next(bg)
                                except StopIteration:
                                    done_b = True
                        if not done_c:
                            try:
                                next(cg)
                            except StopIteration:
                                done_c = True
                        next(prep_g, None)
                        next(prep_g, None)
                for _ in prep_g:
                    pass
                S = Sh[0]
                def epilogue(hp=hp, c0=c0, t0=t0, bonus=bonus, gg=gg):
                    p = self.psum()
                    mm(p[:], p.b, bdones[:], bdones.b, yfm[:], yfm.b)
                    tk.op("dve", lambda: nc.vector.scalar_tensor_tensor(out=dd[:], in0=p[:], scalar=-1.0 / 64, in1=yfm[:], op0=ALU.mult, op1=ALU.add),
                          reads=[p.b, yfm.b], writes=[dd.b])
                    yield
                    tk.op("act", lambda: nc.scalar.activation(out=sq[:], in_=dd[:], func=AF.Square), reads=[dd.b], writes=[sq.b])
                    p = self.psum()
                    mm(p[:], p.b, bdones[:], bdones.b, sq[:], sq.b)
                    tk.op("dve", lambda: nc.vector.tensor_scalar(tmp[:], p[:], 1.0 / 64, 64e-5, ALU.mult, ALU.add), reads=[p.b], writes=[tmp.b])
                    yield
                    self.rpow(tmp[:], tmp.b, -0.5)
                    yield
                    tk.op("dve", lambda: nc.vector.tensor_tensor(out=dd[:], in0=dd[:], in1=tmp[:], op=ALU.mult), reads=[dd.b, tmp.b], writes=[dd.b])
                    yield
                    tk.op("act", lambda: nc.scalar.activation(out=dd[:], in_=dd[:], func=AF.Identity, bias=pc["rwkv_lnx_b"][:, hp:hp + 1],
                                                              scale=pc["rwkv_lnx_w"][:, hp:hp + 1]),
                          reads=[dd.b, pc["rwkv_lnx_b"].b, pc["rwkv_lnx_w"].b], writes=[dd.b])
                    yield
                    tk.op("pool", lambda: nc.gpsimd.tensor_tensor(out=dd[:], in0=dd[:], in1=bonus[:], op=ALU.add), reads=[dd.b, bonus.b], writes=[dd.b])
                    o = ob.next()
                    yield
                    tk.op("dve", lambda: nc.vector.tensor_tensor(out=o[:], in0=dd[:], in1=gg[:], op=ALU.mult), reads=[dd.b, gg.b], writes=[o.b])
                    yield
                    tk.dma("pool", self.din["yr_fm"].ap()[c0:c0 + 128, t0:t0 + TB], o[:], reads=[o.b], writes=[self.dbuf["yr_fm"]])

                epi_g = epilogue()
        for _ in epi_g:
            pass


Prog.phase_rwkv = _rwkv


def _nsa_bias(self):
    nc, tk = self.nc, self.tk
    self.scratch("bvec_c", [16, LVEC], BF16)
    self.scratch("bvec_w", [16, LVEC], BF16)
    with self.scope() as es:
        tab = self.sb(es, "nb_tab", [33, 16], F32)
        tk.op("pool", lambda: nc.gpsimd.memset(tab[:], 1.0), writes=[tab.b])
        tk.dma("sp", tab[0:32, :], self.din["rel_bias"].ap()[:, :], reads=[self.dbuf["rel_bias"]], writes=[tab.b])
        e33 = self.sb(es, "nb_e33", [33, LVEC], F32)
        ob = self.ring(es, "nb_o", 2, [16, 512], BF16)
        for cname, dname in (("c_e33c", "bvec_c"), ("c_e33w", "bvec_w")):
            tk.dma("sp", e33[:], self.din[cname].ap()[:, :], reads=[self.dbuf[cname]], writes=[e33.b])
            for j in range(LVEC // 512):
                p = self.psum()
                tk.op("pe", lambda: nc.tensor.matmul(p[:16, :], lhsT=tab[:], rhs=e33[:, j * 512:(j + 1) * 512], start=True, stop=True),
                      reads=[tab.b, e33.b], writes=[p.b])
                o = ob.next()
                tk.op("act", lambda: nc.scalar.copy(o[:], p[:16, :]), reads=[p.b], writes=[o.b])
                tk.dma("pool", self.din[dname].ap()[:, j * 512:(j + 1) * 512], o[:], reads=[o.b], writes=[self.dbuf[dname]])


def _nsa(self):
    nc, tk = self.nc, self.tk
    self.scratch("yn_fm", [1024, T], BF16)
    ngen = getattr(self, "ns_ngen", 5)
    gen = Ring(self.psr.tiles[0:ngen])
    accp = Ring(self.psr.tiles[ngen:8])

    def mm(p_ap, pbuf, lhsT, lb, rhs, rb, start=True, stop=True):
        tk.op("pe", lambda: nc.tensor.matmul(p_ap, lhsT=lhsT, rhs=rhs, start=start, stop=stop), reads=[lb, rb], writes=[pbuf])

    with self.scope() as es:
        Jb = self.load_const(es, "c_J", BF16, tmp_es=es)
        c2s_f = self.sb(es, "ns_c2sf", [128, 2, 64], F32)
        tk.dma("sp", c2s_f[:, :, :], self.dap("c_c2s", 0, [[64, 128], [128 * 64, 2], [1, 64]]), reads=[self.dbuf["c_c2s"]], writes=[c2s_f.b])
        c2s = self.sb(es, "ns_c2s", [128, 2, 64], BF16)
        tk.op("dve", lambda: nc.vector.tensor_copy(c2s[:], c2s_f[:]), reads=[c2s_f.b], writes=[c2s.b])
        ksX = self.sb(es, "ns_ksX", [128, T], BF16, dj=True)
        kwX = self.sb(es, "ns_kwX", [128, T], BF16, dj=True)
        with self.scope() as es0:
            exf = self.sb(es0, "ns_exf", [128, 4096], F32)
            tk.dma("sp", exf[64:128, :], self.din["c_expand"].ap()[:, :], reads=[self.dbuf["c_expand"]], writes=[exf.b])
            tk.op("dve", lambda: nc.vector.tensor_scalar_mul(ksX[64:128, :], exf[64:128, :], BIG), reads=[exf.b], writes=[ksX.b])
        tk.op("pool", lambda: nc.gpsimd.memset(kwX[64:128, :], 0.0), writes=[kwX.b])
        ones = self.sb(es, "ns_ones", [128, 64], BF16)
        tk.op("pool", lambda: nc.gpsimd.memset(ones[:], 1.0), writes=[ones.b])
        kgain = self.col_vec(es, "nsa_k_gain", 0, 0, 1, "ns_kg", p=64)
        ident, bdones = self.ident, self.bdones
        kcmpT = [self.sb(es, f"ns_kcT{g}", [128, 256], BF16) for g in range(4)]
        for g in range(4):
            tk.op("pool", lambda: nc.gpsimd.memset(kcmpT[g][64:128, :], 0.0), writes=[kcmpT[g].b])
        vcmp = [self.sb(es, f"ns_vc{g}", [128, 2, 64], BF16) for g in range(4)]
        with self.scope() as es2:
            kc2 = self.sb(es2, "ns_kc2", [128, T], BF16)
            w1t = self.sb(es2, "ns_w1", [128, 16, 256], BF16)
            w2t = self.sb(es2, "ns_w2", [128, 2, 64], BF16)
            hg_ = self.sb(es2, "ns_hg", [128, 2, 256], BF16)
            xx = self.sb(es2, "ns_x", [128, 256], F32)
            x2 = self.sb(es2, "ns_x2", [128, 256], F32)
            pvb = self.sb(es2, "ns_pvb", [128, 2], F32)
            t64 = self.sb(es2, "ns_t64", [64, 256], F32)
            t64b = self.sb(es2, "ns_t64b", [64, 256], F32)
            for kind in range(2):
                sfx = "_k" if kind == 0 else "_v"
                self.load_w(w1t, "cmp_w1" + sfx + "_bf", 0, 256, kchunks=16)
                self.load_w(w2t, "cmp_w2" + sfx + "_bf", 0, 64, kchunks=2)
                pe_f = self.col_vec(es2, "cmp_pe" + sfx, 0, 0, 16, "ns_pe" + sfx)
                pe_b = self.sb(es2, "ns_peb" + sfx, [128, 16], BF16)
                tk.op("dve", lambda: nc.vector.tensor_copy(pe_b[:], pe_f[:]), reads=[pe_f.b], writes=[pe_b.b])
                for ct in range(2):
                    p = gen.next()
                    for l2 in range(16):
                        mm(p[:, 0:1], p.b, w1t[:, l2, ct * 128:(ct + 1) * 128], w1t.b, pe_b[:, l2:l2 + 1], pe_b.b, l2 == 0, l2 == 15)
                    tk.op("dve", lambda: nc.vector.tensor_copy(pvb[:, ct:ct + 1], p[:, 0:1]), reads=[p.b], writes=[pvb.b])
                for g in range(4):
                    r0 = 256 * kind + 64 * g
                    tk.op("pool", lambda: nc.gpsimd.memset(kc2[64:128, T - 1:T], 0.0), writes=[kc2.b])
                    tk.dma("sp", kc2[0:64, :], self.din["kcvc_fm"].ap()[r0:r0 + 64, :], reads=[self.dbuf["kcvc_fm"]], writes=[kc2.b])
                    tk.dma("sp", kc2[64:128, 0:T - 1], self.din["kcvc_fm"].ap()[r0:r0 + 64, 1:T], reads=[self.dbuf["kcvc_fm"]], writes=[kc2.b])
                    tk.op("pool", lambda: nc.gpsimd.memset(hg_[:], 0.0), writes=[hg_.b])
                    for ct in range(2):
                        p = gen.next()
                        for l2 in range(16):
                            rhs = kc2[:, 2 * l2: 2 * l2 + 16 * 254 + 1: 16]
                            mm(p[:, 0:255], p.b, w1t[:, l2, ct * 128:(ct + 1) * 128], w1t.b, rhs, kc2.b, l2 == 0, l2 == 15)
                        tk.op("act", lambda: nc.scalar.activation(out=xx[:, 0:255], in_=p[:, 0:255], func=AF.Identity, bias=pvb[:, ct:ct + 1]),
                              reads=[p.b, pvb.b], writes=[xx.b])
                        tk.op("act", lambda: nc.scalar.activation(out=x2[:, 0:255], in_=xx[:, 0:255], func=AF.Square), reads=[xx.b], writes=[x2.b])
                        tk.op("dve", lambda: nc.vector.tensor_scalar(x2[:, 0:255], x2[:, 0:255], 0.044715, 1.0, ALU.mult, ALU.add), reads=[x2.b], writes=[x2.b])
                        tk.op("dve", lambda: nc.vector.tensor_tensor(out=x2[:, 0:255], in0=x2[:, 0:255], in1=xx[:, 0:255], op=ALU.mult), reads=[x2.b, xx.b], writes=[x2.b])
                        tk.op("act", lambda: nc.scalar.activation(out=x2[:, 0:255], in_=x2[:, 0:255], func=AF.Sigmoid, scale=1.5957691216057308),
                              reads=[x2.b], writes=[x2.b])
                        tk.op("dve", lambda: nc.vector.tensor_tensor(out=hg_[:, ct, 0:255], in0=x2[:, 0:255], in1=xx[:, 0:255], op=ALU.mult),
                              reads=[x2.b, xx.b], writes=[hg_.b])
                    if kind == 0:
                        p = gen.next()
                        for ct in range(2):
                            mm(p[0:64, 0:256], p.b, w2t[:, ct, :], w2t.b, hg_[:, ct, :], hg_.b, ct == 0, ct == 1)
                        tk.op("act", lambda: nc.scalar.activation(out=t64[:], in_=p[0:64, 0:256], func=AF.Square), reads=[p.b], writes=[t64.b])
                        p2 = gen.next()
                        mm(p2[0:64, 0:256], p2.b, bdones[0:64, 0:64], bdones.b, t64[:], t64.b)
                        tk.op("dve", lambda: nc.vector.tensor_scalar(t64[:], p2[0:64, 0:256], 1.0 / 64, 1e-6, ALU.mult, ALU.add), reads=[p2.b], writes=[t64.b])
                        self.rpow(t64[:], t64.b, -0.5)
                        tk.op("dve", lambda: nc.vector.tensor_tensor(out=t64b[:], in0=p[0:64, 0:256], in1=t64[:], op=ALU.mult), reads=[p.b, t64.b], writes=[t64b.b])
                        tk.op("dve", lambda: nc.vector.tensor_scalar_mul(kcmpT[g][0:64, :], t64b[:], kgain[:, 0:1]), reads=[t64b.b, kgain.b], writes=[kcmpT[g].b])
                    else:
                        for nt in range(2):
                            p = gen.next()
                            for ct in range(2):
                                mm(p[:, 0:64], p.b, hg_[:, ct, nt * 128:(nt + 1) * 128], hg_.b, w2t[:, ct, :], w2t.b, ct == 0, ct == 1)
                            tk.op("act", lambda: nc.scalar.copy(vcmp[g][:, nt, :], p[:, 0:64]), reads=[p.b], writes=[vcmp[g].b])
        if getattr(self, "ns_lim", 99) <= 1:
            return
        Vs = self.sb(es, "ns_Vs", [128, 32, 128], BF16)
        Vw = self.sb(es, "ns_Vw", [128, 32, 128], BF16)
        tk.op("pool", lambda: nc.gpsimd.memset(Vs[:], 1.0), writes=[Vs.b])
        tk.op("pool", lambda: nc.gpsimd.memset(Vw[:], 1.0), writes=[Vw.b])
        vco = [self.sb(es, f"ns_vco{g}", [128, 2, 128], BF16) for g in range(4)]
        for g in range(4):
            tk.op("pool", lambda: nc.gpsimd.memset(vco[g][:], 1.0), writes=[vco[g].b])
            tk.op("dve", lambda: nc.vector.tensor_copy(vco[g][:, :, 0:64], vcmp[g][:]), reads=[vcmp[g].b], writes=[vco[g].b])
        bfar = self.sb(es, "ns_bfar", [128, 16], F32)
        tk.dma("sp", bfar[:], self.din["rel_bias"].ap()[31:32, :].partition_broadcast(128), reads=[self.dbuf["rel_bias"]], writes=[bfar.b])
        qTr = self.ring(es, "ns_qT", 2, [128, 4, 512], BF16)
        for t_ in qTr.tiles:
            tk.op("pool", lambda: nc.gpsimd.memset(t_[64:128, :, :], 0.0), writes=[t_.b])
        gbr = self.ring(es, "ns_gb", 3, [64, 4, 512], F32)
        Hr = self.ring(es, "ns_H", 4, [128, 512], BF16)
        Er = self.ring(es, "ns_E", 4, [128, 512], BF16)
        E2r = self.ring(es, "ns_E2", 4, [128, 512], BF16)
        Ec = self.sb(es, "ns_Ec", [128, 8, 512], BF16, dj=True)
        accC = self.sb(es, "ns_accC", [64, 4, 8, 512], BF16, dj=True)
        QSr = self.ring(es, "ns_QS", 2, [128, T], BF16)
        for t_ in QSr.tiles:
            t_.b.dj = True
        acc = self.ring(es, "ns_acc", 2, [64, 512], F32)
        impa = self.sb(es, "ns_impa", [64, 512], F32)
        frc = self.ring(es, "ns_frc", 2, [64, 512], F32)
        rdr = self.ring(es, "ns_rd", 3, [64, 512], F32)
        t1r = self.ring(es, "ns_t1", 3, [64, 512], F32)
        impq = self.sb(es, "ns_impq", [128, 4, 64], F32)
        selq = self.sb(es, "ns_selq", [128, 4, 64], F32)
        wk = self.sb(es, "ns_wk", [128, 64], F32)
        m8 = self.sb(es, "ns_m8", [128, 16], F32)
        obr = self.ring(es, "ns_ob", 2, [64, 512], BF16)
        XBr = [[self.sb(es, f"ns_xb{k}_{i}", [128, 512], BF16) for i in range(13)] for k in range(1)]
        pend = []
        eng_alt = [0]

        def flush():
            while pend:
                pend.pop(0)()

        def hankel(vname, h, c, pstep):
            H = Hr.next()
            src = self.dap(vname, h * LVEC + c, [[pstep, 128], [1, 512]])
            tk.dma("sp", H[:], src, reads=[self.dbuf[vname]], writes=[H.b])
            return H

        def ratio(pn):
            rd = rdr.next()
            tk.op("dve", lambda: nc.vector.tensor_scalar_max(rd[:], pn[64:128, :], 1e-30), reads=[pn.b], writes=[rd.b])
            self.rpow(rd[:], rd.b, -1.0)
            t1 = t1r.next()
            tk.op("dve", lambda: nc.vector.tensor_tensor(out=t1[:], in0=pn[0:64, :], in1=rd[:], op=ALU.mult), reads=[pn.b, rd.b], writes=[t1.b])
            return rd, t1

        def key_tile(s_mms, e_ap, e_buf, act_bias, pv, mult=None, cols=(0, 512)):
            lo, hi = cols
            p = gen.next()
            for i, (lhsT, lb, rhs, rb) in enumerate(s_mms):
                mm(p[:, lo:hi], p.b, lhsT, lb, rhs, rb, i == 0, i == len(s_mms) - 1)
            if mult is None:
                if act_bias is None:
                    tk.op("act", lambda: nc.scalar.activation(out=e_ap, in_=p[:, lo:hi], func=AF.Exp), reads=[p.b], writes=[e_buf])
                else:
                    tk.op("act", lambda: nc.scalar.activation(out=e_ap, in_=p[:, lo:hi], func=AF.Exp, bias=act_bias), reads=[p.b, bfar.b], writes=[e_buf])
            else:
                E0 = E2r.next()
                tk.op("act", lambda: nc.scalar.activation(out=E0[:, lo:hi], in_=p[:, lo:hi], func=AF.Exp), reads=[p.b], writes=[E0.b])
                tk.op("dve", lambda: nc.vector.tensor_tensor(out=e_ap, in0=E0[:, lo:hi], in1=mult[:, lo:hi], op=ALU.mult), reads=[E0.b, mult.b], writes=[e_buf])
            while len(pend) >= SKEW:
                pend.pop(0)()
            pend.append(pv)

        SKEW = getattr(self, "ns_skew", 3)
        for g in range(4):
            flush()
            tk.dma("sp", ksX[0:64, :], self.din["ks_fm"].ap()[64 * g:64 * g + 64, :], reads=[self.dbuf["ks_fm"]], writes=[ksX.b])
            tk.dma("sp", kwX[0:64, :], self.din["kw_fm"].ap()[64 * g:64 * g + 64, :], reads=[self.dbuf["kw_fm"]], writes=[kwX.b])
            for k8 in range(4):
                tk.dma("sp", Vs[:, 8 * k8:8 * k8 + 8, 0:64], self.dap("vsw_tm", 64 * g + 8 * k8 * 128 * 512, [[512, 128], [128 * 512, 8], [1, 64]]),
                       reads=[self.dbuf["vsw_tm"]], writes=[Vs.b])
                tk.dma("sp", Vw[:, 8 * k8:8 * k8 + 8, 0:64], self.dap("vsw_tm", 256 + 64 * g + 8 * k8 * 128 * 512, [[512, 128], [128 * 512, 8], [1, 64]]),
                       reads=[self.dbuf["vsw_tm"]], writes=[Vw.b])
            for qt in range(T // 512):
                t0 = qt * 512
                qT = qTr.next()
                tk.dma("sp", qT[0:64, :, :], self.dap("q_fm", 256 * g * T + t0, [[T, 64], [64 * T, 4], [1, 512]]), reads=[self.dbuf["q_fm"]], writes=[qT.b])
                gb = gbr.next()
                for j in range(4):
                    row = 12 * g + 3 * j
                    tk.dma("sp", gb[:, j, :], self.din["gates_fm"].ap()[row:row + 1, t0:t0 + 512].partition_broadcast(64),
                           reads=[self.dbuf["gates_fm"]], writes=[gb.b])
                fr = frc.next()
                tk.dma("sp", fr[:], self.din["c_forced"].ap()[:, t0:t0 + 512], reads=[self.dbuf["c_forced"]], writes=[fr.b])
                nnt = 2 if t0 >= 2048 else 1
                for hg in range(4):
                    h = 4 * g + hg
                    pn, pi = accp.next(), accp.next()
                    for nt in range(nnt):
                        H = hankel("bvec_c", h, OFFC + t0 - 16 * 128 * nt - 2063, 16)
                        e_ap = Ec[:, hg * 2 + nt, :]

                        def pv(pn=pn, pi=pi, nt=nt, e_ap=e_ap, nnt=nnt):
                            mm(pn[:], pn.b, vco[g][:, nt, :], vco[g].b, e_ap, Ec.b, nt == 0, nt == nnt - 1)
                            mm(pi[0:64, :], pi.b, c2s[:, nt, :], c2s.b, e_ap, Ec.b, nt == 0, nt == nnt - 1)
                        key_tile([(kcmpT[g][:, nt * 128:(nt + 1) * 128], kcmpT[g].b, qT[:, hg, :], qT.b), (Jb[:], Jb.b, H[:], H.b)], e_ap, Ec.b, None, pv)

                    def fin(pn=pn, pi=pi, hg=hg, gb=gb, qt=qt):
                        rd, t1 = ratio(pn)
                        tk.op("pool", lambda: nc.gpsimd.tensor_tensor(out=accC[:, hg, qt, :], in0=t1[:], in1=gb[:, hg, :], op=ALU.mult),
                              reads=[t1.b, gb.b], writes=[accC.b])
                        if hg == 0:
                            tk.op("dve", lambda: nc.vector.tensor_tensor(out=impa[:], in0=pi[0:64, :], in1=rd[:], op=ALU.mult), reads=[pi.b, rd.b], writes=[impa.b])
                        else:
                            t2 = t1r.next()
                            tk.op("dve", lambda: nc.vector.tensor_tensor(out=t2[:], in0=pi[0:64, :], in1=rd[:], op=ALU.mult), reads=[pi.b, rd.b], writes=[t2.b])
                            tk.op("pool", lambda: nc.gpsimd.tensor_tensor(out=impa[:], in0=impa[:], in1=t2[:], op=ALU.add), reads=[impa.b, t2.b], writes=[impa.b])
                    pend.append(fin)
                flush()
                tk.op("dve", lambda: nc.vector.tensor_tensor(out=impa[:], in0=impa[:], in1=fr[:], op=ALU.max), reads=[impa.b, fr.b], writes=[impa.b])
                p = gen.next()
                for s4 in range(4):
                    tk.op("pe", lambda: nc.tensor.transpose(p[:, s4 * 64:(s4 + 1) * 64], impa[:, s4 * 128:(s4 + 1) * 128], ident[0:64, 0:64]),
                          reads=[impa.b, ident.b], writes=[p.b])
                tk.op("act", lambda: nc.scalar.copy(impq[:], p[:, 0:256].rearrange("p (a b) -> p a b", a=4)), reads=[p.b], writes=[impq.b])
                for s4 in range(4):
                    tk.op("dve", lambda: nc.vector.max(out=m8[:, 0:8], in_=impq[:, s4, :]), reads=[impq.b], writes=[m8.b])
                    tk.op("dve", lambda: nc.vector.match_replace(out=wk[:], in_to_replace=m8[:, 0:8], in_values=impq[:, s4, :], imm_value=-1e30),
                          reads=[impq.b, m8.b], writes=[wk.b])
                    tk.op("dve", lambda: nc.vector.max(out=m8[:, 8:16], in_=wk[:]), reads=[wk.b], writes=[m8.b])
                    tk.op("dve", lambda: nc.vector.tensor_scalar(selq[:, s4, :], impq[:, s4, :], m8[:, 15:16], 1.0, ALU.is_ge, ALU.subtract),
                          reads=[impq.b, m8.b], writes=[selq.b])
                p = gen.next()
                for s4 in range(4):
                    tk.op("pe", lambda: nc.tensor.transpose(p[0:64, s4 * 128:(s4 + 1) * 128], selq[:, s4, :], ident[:]),
                          reads=[selq.b, ident.b], writes=[p.b])
                tk.op("act", lambda: nc.scalar.copy(QSr.tiles[0][64:128, qt * 512:(qt + 1) * 512], p[0:64, :]), reads=[p.b], writes=[QSr.tiles[0].b])
                tk.op("dve", lambda: nc.vector.tensor_copy(QSr.tiles[1][64:128, qt * 512:(qt + 1) * 512], p[0:64, :]), reads=[p.b], writes=[QSr.tiles[1].b])
            for hg in range(4):
                h = 4 * g + hg
                flush()
                XB = XBr[0]
                xw, xs = {}, {}
                for i, (vname, d) in enumerate([("bvec_w", dd_) for dd_ in range(512, -385, -128)] + [("bvec_c", dd_) for dd_ in range(128, -385, -128)]):
                    H = hankel(vname, h, OFFC + d - 127, 1)
                    p = gen.next()
                    mm(p[:], p.b, Jb[:], Jb.b, H[:], H.b)
                    tk.op("act", lambda: nc.scalar.activation(out=XB[i][:], in_=p[:], func=AF.Exp), reads=[p.b], writes=[XB[i].b])
                    (xw if vname == "bvec_w" else xs)[d] = XB[i]
                q_ = QSr.next()
                tk.dma("sp", q_[0:64, :], self.din["q_fm"].ap()[64 * h:64 * h + 64, :], reads=[self.dbuf["q_fm"]], writes=[q_.b])
                for qt in range(T // 512):
                    t0 = qt * 512
                    qs = q_[:, t0:t0 + 512]
                    gb = gbr.next()
                    for j in range(2):
                        row = 3 * h + 1 + j
                        tk.dma("sp", gb[:, j, :], self.din["gates_fm"].ap()[row:row + 1, t0:t0 + 512].partition_broadcast(64),
                               reads=[self.dbuf["gates_fm"]], writes=[gb.b])
                    ac = acc.next()
                    kts = list(range(max(0, (t0 - 512) // 128), (t0 + 511) // 128 + 1))
                    kts.sort(key=lambda kt: (128 * kt != t0, kt))
                    pn = accp.next()
                    for i, kt in enumerate(kts):
                        E = Er.next()
                        dl = (128 * kt - t0) // 128
                        lo, hi = (0, min(512, 128 * (dl + 5))) if dl < 0 else (128 * dl, 512)

                        def pv(pn=pn, kt=kt, E=E, first=(i == 0), last=(i == len(kts) - 1), lo=lo, hi=hi):
                            mm(pn[:, lo:hi], pn.b, Vw[:, kt, :], Vw.b, E[:, lo:hi], E.b, first, last)
                        key_tile([(kwX[:, kt * 128:(kt + 1) * 128], kwX.b, qs[:, lo:hi], q_.b)], E[:, lo:hi], E.b, None, pv, mult=xw[t0 - 128 * kt], cols=(lo, hi))

                    def finw(pn=pn, gb=gb, ac=ac):
                        rd, t1 = ratio(pn)
                        tk.op("dve", lambda: nc.vector.tensor_tensor(out=ac[:], in0=t1[:], in1=gb[:, 1, :], op=ALU.mult), reads=[t1.b, gb.b], writes=[ac.b])
                    pend.append(finw)
                    kts = list(range(0, (t0 + 511) // 128 + 1))
                    pn = accp.next()
                    for i, kt in enumerate(kts):
                        far = (128 * kt <= t0 - 256)
                        E = Er.next()
                        s_mms = [(ksX[:, kt * 128:(kt + 1) * 128], ksX.b, qs, q_.b)]

                        dl = (128 * kt - t0) // 128
                        lo = 128 * dl if dl > 0 else 0
                        s_mms = [(ksX[:, kt * 128:(kt + 1) * 128], ksX.b, qs[:, lo:512], q_.b)]

                        def pv(pn=pn, kt=kt, E=E, first=(i == 0), last=(i == len(kts) - 1), lo=lo):
                            mm(pn[:, lo:512], pn.b, Vs[:, kt, :], Vs.b, E[:, lo:512], E.b, first, last)
                        key_tile(s_mms, E[:, lo:512], E.b, bfar[:, h:h + 1] if far else None, pv, mult=None if far else xs[t0 - 128 * kt], cols=(lo, 512))

                    def fins(pn=pn, gb=gb, ac=ac, hg=hg, h=h, t0=t0, qt=qt):
                        rd, t1 = ratio(pn)
                        tk.op("dve", lambda: nc.vector.tensor_tensor(out=t1[:], in0=t1[:], in1=gb[:, 0, :], op=ALU.mult), reads=[t1.b, gb.b], writes=[t1.b])
                        tk.op("dve", lambda: nc.vector.tensor_tensor(out=ac[:], in0=ac[:], in1=t1[:], op=ALU.add), reads=[t1.b, ac.b], writes=[ac.b])
                        o = obr.next()
                        tk.op("dve", lambda: nc.vector.tensor_tensor(out=o[:], in0=ac[:], in1=accC[:, hg, qt, :], op=ALU.add), reads=[ac.b, accC.b], writes=[o.b])
                        tk.dma("pool", self.din["yn_fm"].ap()[64 * h:64 * h + 64, t0:t0 + 512], o[:], reads=[o.b], writes=[self.dbuf["yn_fm"]])
                    pend.append(fins)
        flush()


Prog.nsa_bias = _nsa_bias
Prog.phase_nsa = _nsa


def _proj_tm_res(self, actT, tok0, ntok, kchunks, w, res_name, res_row0, dst_name, dst_row0, es):
    nc, tk = self.nc, self.tk
    xr = self.ring(es, "pt_x", 2, [128, D], F32)
    orr = self.ring(es, "pt_o", 2, [128, D], F32)
    for i in range(ntok // 128):
        x = xr.next()
        o = orr.next()
        tk.dma("sp", x[:], self.din[res_name].ap()[res_row0 + i * 128:res_row0 + (i + 1) * 128, :], reads=[self.dbuf[res_name]], writes=[x.b])
        for half in range(2):
            p = self.psum()
            for kc in range(kchunks):
                tk.op("pe", lambda: nc.tensor.matmul(p[:], lhsT=actT[:, kc, tok0 + i * 128:tok0 + (i + 1) * 128], rhs=w[:, kc, half * 512:(half + 1) * 512],
                                                      start=(kc == 0), stop=(kc == kchunks - 1)), reads=[actT.b, w.b], writes=[p.b])
            tk.op("dve", lambda: nc.vector.tensor_tensor(out=o[:, half * 512:(half + 1) * 512], in0=p[:], in1=x[:, half * 512:(half + 1) * 512], op=ALU.add),
                  reads=[p.b, x.b], writes=[o.b])
        tk.dma("pool", self.din[dst_name].ap()[dst_row0 + i * 128:dst_row0 + (i + 1) * 128, :], o[:], reads=[o.b], writes=[self.dbuf[dst_name]])


def _merge(self, bi):
    nc, tk = self.nc, self.tk
    self.scratch("h1", [T, D], F32)
    with self.scope() as es:
        mT = self.sb(es, "mg_mT", [128, 8, T], BF16, dj=True)
        with self.scope() as es2:
            wr = self.sb(es2, "mg_wr", [128, 8, 1024], BF16)
            wn = self.sb(es2, "mg_wn", [128, 8, 1024], BF16)
            self.load_w(wr, "w_branch_rwkv_bf", 0, 1024)
            self.load_w(wn, "w_branch_nsa_bf", 0, 1024)
            yr = self.ring(es2, "mg_yr", 2, [128, 8, 512], BF16)
            yn = self.ring(es2, "mg_yn", 2, [128, 8, 512], BF16)
            gr = self.ring(es2, "mg_g", 4, [128, 512], F32)
            tr = self.ring(es2, "mg_t", 4, [128, 512], F32)
            for tt in range(T // 512):
                a, b = yr.next(), yn.next()
                tk.dma("sp", a[:], self.dap("yr_fm", tt * 512, [[T, 128], [128 * T, 8], [1, 512]]), reads=[self.dbuf["yr_fm"]], writes=[a.b])
                tk.dma("sp", b[:], self.dap("yn_fm", tt * 512, [[T, 128], [128 * T, 8], [1, 512]]), reads=[self.dbuf["yn_fm"]], writes=[b.b])
                for ci in range(8):
                    g0, g1 = gr.next(), gr.next()
                    tk.dma("sp", g0[:], self.din["gm_fm"].ap()[ci * 128:(ci + 1) * 128, tt * 512:(tt + 1) * 512], reads=[self.dbuf["gm_fm"]], writes=[g0.b])
                    tk.dma("sp", g1[:], self.din["gm_fm"].ap()[1024 + ci * 128:1024 + (ci + 1) * 128, tt * 512:(tt + 1) * 512], reads=[self.dbuf["gm_fm"]], writes=[g1.b])
                    pr, pn = self.psum(), self.psum()
                    for kc in range(8):
                        tk.op("pe", lambda: nc.tensor.matmul(pr[:], lhsT=wr[:, kc, ci * 128:(ci + 1) * 128], rhs=a[:, kc, :], start=(kc == 0), stop=(kc == 7)),
                              reads=[wr.b, a.b], writes=[pr.b])
                    for kc in range(8):
                        tk.op("pe", lambda: nc.tensor.matmul(pn[:], lhsT=wn[:, kc, ci * 128:(ci + 1) * 128], rhs=b[:, kc, :], start=(kc == 0), stop=(kc == 7)),
                              reads=[wn.b, b.b], writes=[pn.b])
                    t0_, t1_ = tr.next(), tr.next()
                    tk.op("dve", lambda: nc.vector.tensor_tensor(out=t0_[:], in0=pr[:], in1=g0[:], op=ALU.mult), reads=[pr.b, g0.b], writes=[t0_.b])
                    tk.op("dve", lambda: nc.vector.tensor_tensor(out=t1_[:], in0=pn[:], in1=g1[:], op=ALU.mult), reads=[pn.b, g1.b], writes=[t1_.b])
                    tk.op("pool", lambda: nc.gpsimd.tensor_tensor(out=mT[:, ci, tt * 512:(tt + 1) * 512], in0=t0_[:], in1=t1_[:], op=ALU.add),
                          reads=[t0_.b, t1_.b], writes=[mT.b])
        with self.scope() as es3:
            wm = self.sb(es3, "mg_wm", [128, 8, 1024], BF16)
            self.load_w(wm, "w_mix_out_bf", 0, 1024)
            self.proj_tm_res(mT, 0, T, 8, wm, "x", bi * T, "h1", 0, es3)


def _cross(self, bi):
    nc, tk = self.nc, self.tk
    self.scratch("h2", [T, D], F32)
    HT = 2048
    with self.scope() as es:
        wq = self.sb(es, "ca_wq", [128, 8, 1024], BF16)
        wo = self.sb(es, "ca_wo", [128, 8, 1024], BF16)
        self.load_w(wq, "ca_wq_bf", 0, 1024)
        self.load_w(wo, "ca_wo_bf", 0, 1024)
        kT = self.sb(es, "ca_kT", [128, 8, NMEM], BF16, dj=True)
        Vc = self.sb(es, "ca_V", [128, 2, 1024], BF16, dj=True)
        qgain = self.col_vec(es, "ca_q_gain", 0, 0, 2, "ca_qg")
        kgain = self.col_vec(es, "ca_k_gain", 0, 0, 2, "ca_kg")
        ones_f = self.load_const(es, "c_ones")
        ones_b = self.sb(es, "ca_1b", [128, 128], BF16)
        tk.op("dve", lambda: nc.vector.tensor_copy(ones_b[:], ones_f[:]), reads=[ones_f.b], writes=[ones_b.b])
        sqr = self.ring(es, "ca_sq", 2, [128, 2, 512], F32)
        rr = self.ring(es, "ca_r", 4, [128, 512], F32)
        tmpr = self.ring(es, "ca_tmp", 2, [128, 512], F32)
        qh = self.ring(es, "ca_qh", 3, [128, 2, 512], BF16)
        Er = self.ring(es, "ca_E", 2, [128, 2, 512], BF16)

        def qk_norm(p0, p1, n, gain, scale, out_aps, out_buf):
            s = sqr.next()
            tk.op("act", lambda: nc.scalar.activation(out=s[:, 0, 0:n], in_=p0[:, 0:n], func=AF.Square), reads=[p0.b], writes=[s.b])
            tk.op("act", lambda: nc.scalar.activation(out=s[:, 1, 0:n], in_=p1[:, 0:n], func=AF.Square), reads=[p1.b], writes=[s.b])
            p2 = self.psum()
            for j in range(2):
                tk.op("pe", lambda: nc.tensor.matmul(p2[:, 0:n], lhsT=ones_f[:], rhs=s[:, j, 0:n], start=(j == 0), stop=(j == 1)),
                      reads=[ones_f.b, s.b], writes=[p2.b])
            r = rr.next()
            tk.op("dve", lambda: nc.vector.tensor_scalar(r[:, 0:n], p2[:, 0:n], 1.0 / 256, 1e-6, ALU.mult, ALU.add), reads=[p2.b], writes=[r.b])
            self.rpow(r[:, 0:n], r.b, -0.5)
            for j, pj in enumerate((p0, p1)):
                t = tmpr.next()
                tk.op("dve", lambda: nc.vector.tensor_tensor(out=t[:, 0:n], in0=pj[:, 0:n], in1=r[:, 0:n], op=ALU.mult), reads=[pj.b, r.b], writes=[t.b])
                tk.op("dve", lambda: nc.vector.tensor_scalar(out_aps[j], t[:, 0:n], gain[:, j:j + 1], scale, ALU.mult, ALU.mult),
                      reads=[t.b, gain.b], writes=[out_buf])

        with self.scope() as es2:
            mnT = self.sb(es2, "ca_mnT", [128, 8, NMEM], BF16, dj=True)
            self.norm_T("mem", bi * NMEM, NMEM, "norm_mem", mnT)
            wk = self.sb(es2, "ca_wk", [128, 8, 1024], BF16)
            wv = self.sb(es2, "ca_wv", [128, 8, 1024], BF16)
            self.load_w(wk, "ca_wkv_bf", 0, 1024)
            self.load_w(wv, "ca_wkv_bf", 1024, 1024)
            for h in range(4):
                ps_ = []
                for j in range(2):
                    p = self.psum()
                    ci = 2 * h + j
                    for kc in range(8):
                        tk.op("pe", lambda: nc.tensor.matmul(p[:, 0:NMEM], lhsT=wk[:, kc, ci * 128:(ci + 1) * 128], rhs=mnT[:, kc, :], start=(kc == 0), stop=(kc == 7)),
                              reads=[wk.b, mnT.b], writes=[p.b])
                    ps_.append(p)
                qk_norm(ps_[0], ps_[1], NMEM, kgain, 1.0, [kT[:, 2 * h, :], kT[:, 2 * h + 1, :]], kT.b)
            for mt in range(2):
                for half in range(2):
                    p = self.psum()
                    for kc in range(8):
                        tk.op("pe", lambda: nc.tensor.matmul(p[:], lhsT=mnT[:, kc, mt * 128:(mt + 1) * 128], rhs=wv[:, kc, half * 512:(half + 1) * 512],
                                                              start=(kc == 0), stop=(kc == 7)), reads=[mnT.b, wv.b], writes=[p.b])
                    tk.op("act", lambda: nc.scalar.copy(Vc[:, mt, half * 512:(half + 1) * 512], p[:]), reads=[p.b], writes=[Vc.b])
        for hf in range(T // HT):
            with self.scope() as es2:
                hnT = self.sb(es2, "ca_hnT", [128, 8, HT], BF16, dj=True)
                oT = self.sb(es2, "ca_oT", [128, 8, HT], BF16, dj=True)
                self.norm_T("h1", hf * HT, HT, "norm_cross", hnT)
                def stage_q(h, tt):
                    ps_ = []
                    for j in range(2):
                        p = self.psum()
                        ci = 2 * h + j
                        for kc in range(8):
                            tk.op("pe", lambda: nc.tensor.matmul(p[:], lhsT=wq[:, kc, ci * 128:(ci + 1) * 128], rhs=hnT[:, kc, tt * 512:(tt + 1) * 512],
                                                                  start=(kc == 0), stop=(kc == 7)), reads=[wq.b, hnT.b], writes=[p.b])
                        ps_.append(p)
                    q = qh.next()
                    qk_norm(ps_[0], ps_[1], 512, qgain, 1.0 / 16, [q[:, 0, :], q[:, 1, :]], q.b)
                    return q

                def stage_att(h, tt, q):
                    E = Er.next()
                    for mt in range(2):
                        p = self.psum()
                        for j in range(2):
                            tk.op("pe", lambda: nc.tensor.matmul(p[:], lhsT=kT[:, 2 * h + j, mt * 128:(mt + 1) * 128], rhs=q[:, j, :], start=(j == 0), stop=(j == 1)),
                                  reads=[kT.b, q.b], writes=[p.b])
                        tk.op("act", lambda: nc.scalar.activation(out=E[:, mt, :], in_=p[:], func=AF.Exp), reads=[p.b], writes=[E.b])
                    pd = self.psum()
                    for mt in range(2):
                        tk.op("pe", lambda: nc.tensor.matmul(pd[:], lhsT=ones_b[:], rhs=E[:, mt, :], start=(mt == 0), stop=(mt == 1)),
                              reads=[ones_b.b, E.b], writes=[pd.b])
                    r = rr.next()
                    tk.op("act", lambda: nc.scalar.activation(out=r[:], in_=pd[:], func=AF.Ln), reads=[pd.b], writes=[r.b])
                    tk.op("act", lambda: nc.scalar.activation(out=r[:], in_=r[:], func=AF.Exp, scale=-1.0), reads=[r.b], writes=[r.b])
                    for j in range(2):
                        pn = self.psum()
                        for mt in range(2):
                            tk.op("pe", lambda: nc.tensor.matmul(pn[:], lhsT=Vc[:, mt, h * 256 + j * 128:h * 256 + (j + 1) * 128], rhs=E[:, mt, :],
                                                                  start=(mt == 0), stop=(mt == 1)), reads=[Vc.b, E.b], writes=[pn.b])
                        tk.op("dve", lambda: nc.vector.tensor_tensor(out=oT[:, 2 * h + j, tt * 512:(tt + 1) * 512], in0=pn[:], in1=r[:], op=ALU.mult),
                              reads=[pn.b, r.b], writes=[oT.b])

                its = [(h, tt) for h in range(4) for tt in range(HT // 512)]
                prev = None
                for (h, tt) in its:
                    q = stage_q(h, tt)
                    if prev is not None:
                        stage_att(*prev)
                    prev = (h, tt, q)
                stage_att(*prev)
                self.proj_tm_res(oT, 0, HT, 8, wo, "h1", hf * HT, "h2", hf * HT, es2)


def _ffn(self, bi):
    nc, tk = self.nc, self.tk
    self.scratch("ff_fm", [DFF, T], BF16)
    NCT = DFF // 128
    with self.scope() as es:
        hnT = self.sb(es, "ff_hnT", [128, 8, T], BF16, dj=True)
        self.norm_T("h2", 0, T, "norm_ffn", hnT)
        cw = [self.col_vec(es, "ffn_conv", j, 0, NCT, f"ff_cw{j}") for j in range(3)]
        cb = self.col_vec(es, "ffn_conv_b", 0, 0, NCT, "ff_cb")
        wring = self.ring(es, "ff_w", 4, [128, 8, 128], BF16)
        atr = self.ring(es, "ff_a", 2, [128, T + 2], F32)
        btr = self.ring(es, "ff_b", 2, [128, T], F32)
        acc = self.sb(es, "ff_acc", [128, T], F32)
        ob = self.ring(es, "ff_ob", 2, [128, T], BF16)
        for at in atr.tiles:
            tk.op("pool", lambda: nc.gpsimd.memset(at[:, 0:2], 0.0), writes=[at.b])
        for ci in range(NCT):
            at, bt = atr.next(), btr.next()
            wa, wb = wring.next(), wring.next()
            self.load_w(wa, "ffn_up_bf", ci * 128, 128)
            self.load_w(wb, "ffn_up_bf", DFF + ci * 128, 128)
            for tt in range(T // 512):
                pa, pb = self.psum(), self.psum()
                for kc in range(8):
                    tk.op("pe", lambda: nc.tensor.matmul(pa[:], lhsT=wa[:, kc, :], rhs=hnT[:, kc, tt * 512:(tt + 1) * 512], start=(kc == 0), stop=(kc == 7)),
                          reads=[wa.b, hnT.b], writes=[pa.b])
                for kc in range(8):
                    tk.op("pe", lambda: nc.tensor.matmul(pb[:], lhsT=wb[:, kc, :], rhs=hnT[:, kc, tt * 512:(tt + 1) * 512], start=(kc == 0), stop=(kc == 7)),
                          reads=[wb.b, hnT.b], writes=[pb.b])
                tk.op("act", lambda: nc.scalar.copy(at[:, 2 + tt * 512:2 + (tt + 1) * 512], pa[:]), reads=[pa.b], writes=[at.b])
                tk.op("dve", lambda: nc.vector.tensor_copy(bt[:, tt * 512:(tt + 1) * 512], pb[:]), reads=[pb.b], writes=[bt.b])
            tk.op("dve", lambda: nc.vector.tensor_scalar(acc[:], at[:, 2:T + 2], cw[2][:, ci:ci + 1], cb[:, ci:ci + 1], ALU.mult, ALU.add),
                  reads=[at.b, cw[2].b, cb.b], writes=[acc.b])
            tk.op("dve", lambda: nc.vector.scalar_tensor_tensor(out=acc[:], in0=at[:, 1:T + 1], scalar=cw[1][:, ci:ci + 1], in1=acc[:], op0=ALU.mult, op1=ALU.add),
                  reads=[at.b, cw[1].b, acc.b], writes=[acc.b])
            tk.op("dve", lambda: nc.vector.scalar_tensor_tensor(out=acc[:], in0=at[:, 0:T], scalar=cw[0][:, ci:ci + 1], in1=acc[:], op0=ALU.mult, op1=ALU.add),
                  reads=[at.b, cw[0].b, acc.b], writes=[acc.b])
            tk.op("act", lambda: nc.scalar.activation(out=acc[:], in_=acc[:], func=AF.Silu), reads=[acc.b], writes=[acc.b])
            o = ob.next()
            tk.op("dve", lambda: nc.vector.tensor_tensor(out=o[:], in0=acc[:], in1=bt[:], op=ALU.mult), reads=[acc.b, bt.b], writes=[o.b])
            tk.dma("pool", self.din["ff_fm"].ap()[ci * 128:(ci + 1) * 128, :], o[:], reads=[o.b], writes=[self.dbuf["ff_fm"]])
    with self.scope() as es:
        wd = self.sb(es, "ff_wd", [128, NCT, 1024], BF16)
        self.load_w(wd, "ffn_down_bf", 0, 1024, kchunks=NCT)
        TBK = 1024
        for blk in range(T // TBK):
            with self.scope() as es2:
                fT = self.sb(es2, "ff_fT", [128, NCT, TBK], BF16)
                tk.dma("sp", fT[:], self.dap("ff_fm", blk * TBK, [[T, 128], [128 * T, NCT], [1, TBK]]), reads=[self.dbuf["ff_fm"]], writes=[fT.b])
                self.proj_tm_res(fT, 0, TBK, NCT, wd, "h2", blk * TBK, "out", bi * T + blk * TBK, es2)


Prog.proj_tm_res = _proj_tm_res
Prog.phase_merge = _merge
Prog.phase_cross = _cross
Prog.phase_ffn = _ffn
```

```python
import contextlib
import math
import numpy as np
import concourse.bass as bass
import concourse.mybir as mybir
from concourse.bass_utils import run_bass_kernel_spmd

F32 = mybir.dt.float32
BF16 = mybir.dt.bfloat16
AF = mybir.ActivationFunctionType
ALU = mybir.AluOpType
AX = mybir.AxisListType

NCORES = 8
NB = 2
T = 4096
D = 1024
NMEM = 256
DFF = 2816
IN_COLS = 8016
BIG = 30000.0
OFFC = 2176
LVEC = 7680
SCALE_NSA = 0.125


class Buf:
    __slots__ = ("w", "r", "name", "dj", "xr")

    def __init__(self, name="", dj=False):
        self.w = {}
        self.r = {}
        self.name = name
        self.dj = dj
        self.xr = False


class Tile:
    def __init__(self, t, name, dj=False):
        self.t = t
        self.b = Buf(name, dj)

    def __getitem__(self, k):
        return self.t[k]


class Ring:
    def __init__(self, tiles):
        self.tiles = tiles
        self.i = 0

    def next(self):
        t = self.tiles[self.i]
        self.i = (self.i + 1) % len(self.tiles)
        return t


class TK:
    EPOCH = 20000
    NDSEM = 10

    def __init__(self, nc, es):
        self.nc = nc
        self.es = es
        self.eng = {"pe": nc.tensor, "act": nc.scalar, "dve": nc.vector,
                    "pool": nc.gpsimd, "sp": nc.sync}
        self.cnt = {e: 0 for e in self.eng}
        self.esem = {e: [] for e in self.eng}
        self.seen = {e: {} for e in self.eng}
        self.dsem = {}
        self.dptr = {}
        self.nwait = 0
        self.fence = {}

    def _newsem(self, name):
        return self.es.enter_context(self.nc.semaphore(name))

    def _engsem(self, e, epoch):
        while len(self.esem[e]) <= epoch:
            self.esem[e].append(self._newsem(f"s_{e}_{len(self.esem[e])}"))
        return self.esem[e][epoch]

    def _wait(self, e, ts):
        sem, val, src = ts
        if src == "pe" and e == "pe":
            return
        k = id(sem)
        if self.seen[e].get(k, 0) >= val:
            return
        self.seen[e][k] = val
        self.eng[e].wait_ge(sem, val)
        self.nwait += 1

    def deps(self, e, reads, writes):
        for b in reads:
            for ts in b.w.values():
                self._wait(e, ts)
            if b.xr:
                for ts in b.r.values():
                    if ts[2] != e:
                        self._wait(e, ts)
        for b in writes:
            if not (b.dj and not b.r):
                for ts in b.w.values():
                    self._wait(e, ts)
            for ts in b.r.values():
                self._wait(e, ts)

    def mark(self, ts, reads, writes):
        k = id(ts[0])
        for b in reads:
            b.r[k] = ts
        for b in writes:
            if b.dj and not b.r:
                b.w[k] = ts
            else:
                b.w = {k: ts}
                b.r = {}

    def op(self, e, ins_fn, reads=(), writes=()):
        self.deps(e, reads, writes)
        n = self.cnt[e]
        sem = self._engsem(e, n // self.EPOCH)
        val = n % self.EPOCH + 1
        ins_fn().then_inc(sem, 1)
        self.cnt[e] = n + 1
        ts = (sem, val, e)
        self.mark(ts, reads, writes)
        return ts

    def dma(self, q, out_ap, in_ap, reads=(), writes=(), **kw):
        if q not in self.dsem:
            self.dsem[q] = [[self._newsem(f"d_{q}_{i}"), 0] for i in range(self.NDSEM)]
            self.dptr[q] = 0
        slot = self.dsem[q][self.dptr[q]]
        self.dptr[q] = (self.dptr[q] + 1) % self.NDSEM
        sem, issued = slot
        if issued:
            self._wait(q, (sem, 16 * issued, None))
        self.deps(q, reads, writes)
        self.eng[q].dma_start(out=out_ap, in_=in_ap, **kw).then_inc(sem, 16)
        slot[1] = issued + 1
        ts = (sem, 16 * (issued + 1), None)
        self.mark(ts, reads, writes)
        return ts

    def update_fence(self):
        f = {}
        for e in self.eng:
            n = self.cnt[e]
            if n:
                sem = self.esem[e][(n - 1) // self.EPOCH]
                f[id(sem)] = (sem, (n - 1) % self.EPOCH + 1, e)
        for q in self.dsem:
            for sem, issued in self.dsem[q]:
                if issued:
                    f[id(sem)] = (sem, 16 * issued, None)
        self.fence = f

    def drain(self):
        for q in self.dsem:
            for sem, issued in self.dsem[q]:
                if issued:
                    self._wait(q, (sem, 16 * issued, None))


def _t5_bucket_np(dist):
    n = np.maximum(dist, 0)
    nf = np.maximum(n, 1).astype(np.float64)
    large = 16 + (np.log(nf / 16) / math.log(128 / 16) * 16).astype(np.int64)
    large = np.minimum(large, 31)
    return np.where(n < 16, n, large)


def host_consts():
    c = {}
    c["c_ident"] = np.eye(128, dtype=np.float32)
    c["c_J"] = np.ascontiguousarray(np.eye(128, dtype=np.float32)[::-1])
    hb = np.arange(128) // 64
    bd = (hb[:, None] == hb[None, :]).astype(np.float32)
    c["c_bdones"] = bd
    c["c_ones"] = np.ones((128, 128), np.float32)
    s = np.arange(128) % 64
    strict = bd * (s[:, None] < s[None, :])
    incl = bd * (s[:, None] <= s[None, :])
    c["c_mask2"] = np.concatenate([strict, incl], axis=1).astype(np.float32)
    bd5 = np.zeros((128, 5, 2, 64), np.float32)
    for h in range(2):
        bd5[64 * h:64 * h + 64, :, h, :] = 1.0
    c["c_bdmask5"] = bd5.reshape(128, 640)
    seg = np.ones((128, 1024), np.float32)
    seg[:, ::64] = 0.0
    c["c_segmask"] = seg
    dist = np.arange(LVEC) - OFFC
    bk = _t5_bucket_np(dist)
    oh = np.zeros((33, LVEC), np.float32)
    oh[bk, np.arange(LVEC)] = 1.0
    ec = oh.copy()
    ec[32] = np.where(dist >= 0, 0.0, -BIG)
    ec[:32, dist < 0] = 0.0
    ew = oh.copy()
    ok = (dist >= 0) & (dist < 512)
    ew[32] = np.where(ok, 0.0, -BIG)
    ew[:32, ~ok] = 0.0
    c["c_e33c"] = ec
    c["c_e33w"] = ew
    t = np.arange(T)
    cur = t // 64
    blk = np.arange(64)
    forced = (blk[:, None] == 0) | (blk[:, None] == cur[None, :]) | (blk[:, None] == cur[None, :] - 1)
    c["c_forced"] = np.where(forced, 1e4, 0.0).astype(np.float32)
    ex = np.zeros((64, 32, 128), np.float32)
    for kt in range(32):
        for p in range(128):
            ex[2 * kt + p // 64, kt, p] = 1.0
    c["c_expand"] = ex.reshape(64, 32 * 128)
    ncmp = 255
    ci = np.arange(256)[:, None] * 16
    sj = np.arange(64)[None, :] * 64
    c2s = ((ci <= sj + 63) & (ci + 31 >= sj)).astype(np.float32)
    c2s[ncmp:] = 0.0
    c["c_c2s"] = c2s
    return c


CONST_SHAPES = {k: v.shape for k, v in host_consts().items()}

W_SPECS = [
    ("w_in", 1024, IN_COLS), ("rwkv_w2", 64, 1024), ("rwkv_a2", 64, 1024), ("rwkv_g2", 160, 1024),
    ("cmp_w1_k", 2048, 256), ("cmp_w2_k", 256, 64), ("cmp_w1_v", 2048, 256), ("cmp_w2_v", 256, 64),
    ("w_branch_rwkv", 1024, 1024), ("w_branch_nsa", 1024, 1024), ("w_mix_out", 1024, 1024),
    ("ca_wq", 1024, 1024), ("ca_wkv", 1024, 2048), ("ca_wo", 1024, 1024),
    ("ffn_up", 1024, 2 * DFF), ("ffn_down", DFF, 1024),
]
V_SPECS = [
    ("rel_bias", (32, 16)), ("norm_mix", (1, 1024)), ("rwkv_mu", (1, 3360)), ("rwkv_w0", (1, 1024)),
    ("rwkv_a0", (1, 1024)), ("rwkv_kk", (1, 1024)), ("rwkv_ka", (1, 1024)), ("rwkv_rk", (1, 1024)),
    ("rwkv_lnx_w", (1, 1024)), ("rwkv_lnx_b", (1, 1024)), ("nsa_q_gain", (1, 64)), ("nsa_k_gain", (3, 64)),
    ("cmp_pe_k", (1, 2048)), ("cmp_pe_v", (1, 2048)), ("norm_cross", (1, 1024)), ("norm_mem", (1, 1024)),
    ("ca_q_gain", (1, 256)), ("ca_k_gain", (1, 256)), ("norm_ffn", (1, 1024)),
    ("ffn_conv", (3, DFF)), ("ffn_conv_b", (1, DFF)),
]


class Prog:
    def __init__(self, upto="all", dbg=()):
        self.upto = upto
        self.dbg = set(dbg)
        nc = self.nc = bass.Bass("TRN2", target_bir_lowering=False)
        self.es = contextlib.ExitStack()
        self.tk = TK(nc, self.es)
        self.din = {}
        self.dbuf = {}

    def dram_in(self, name, shape):
        self.din[name] = self.nc.dram_tensor(name, list(shape), F32, kind="ExternalInput")
        self.dbuf[name] = Buf(name, dj=True)
        return self.din[name]

    def scratch(self, name, shape, dt):
        if name in self.din:
            return self.din[name]
        kind = "ExternalOutput" if name in self.dbg else "Internal"
        self.din[name] = self.nc.dram_tensor(name, list(shape), dt, kind=kind)
        self.dbuf[name] = Buf(name, dj=True)
        return self.din[name]

    def sb(self, es, name, shape, dt, dj=False):
        self.uid = getattr(self, "uid", 0) + 1
        name = f"{name}_{self.uid}"
        t = Tile(es.enter_context(self.nc.sbuf_tensor(name, list(shape), dt)), name, dj)
        t.b.r = dict(self.tk.fence)
        return t

    @contextlib.contextmanager
    def scope(self):
        with contextlib.ExitStack() as es:
            yield es
        self.tk.update_fence()

    def ring(self, es, name, n, shape, dt):
        return Ring([self.sb(es, f"{name}{i}", shape, dt) for i in range(n)])

    def psum(self):
        return self.psr.next()

    def rpow(self, ap, buf, power):
        nc, tk = self.nc, self.tk
        tk.op("act", lambda: nc.scalar.activation(out=ap, in_=ap, func=AF.Ln), reads=[buf], writes=[buf])
        tk.op("act", lambda: nc.scalar.activation(out=ap, in_=ap, func=AF.Exp, scale=float(power)), reads=[buf], writes=[buf])

    def dap(self, name, offset, ap):
        return bass.AP(tensor=self.din[name], offset=offset, ap=[list(x) for x in ap])

    def load_const(self, es, name, dt=F32, tmp_es=None):
        nc, tk = self.nc, self.tk
        shp = CONST_SHAPES[name]
        t32 = self.sb(es if dt == F32 else tmp_es, name + "_f", shp, F32)
        tk.dma("sp", t32[:], self.din[name].ap()[:, :], reads=[self.dbuf[name]], writes=[t32.b])
        if dt == F32:
            return t32
        t16 = self.sb(es, name + "_h", shp, BF16)
        tk.op("dve", lambda: nc.vector.tensor_copy(t16[:], t32[:]), reads=[t32.b], writes=[t16.b])
        return t16

    def bcast_vec(self, es, name, row, c0, n, tname):
        t = self.sb(es, tname, [128, n], F32)
        src = self.din[name].ap()[row:row + 1, c0:c0 + n].partition_broadcast(128)
        self.tk.dma("sp", t[:], src, reads=[self.dbuf[name]], writes=[t.b])
        return t

    def col_vec(self, es, name, row, c0, nchunk, tname, p=128):
        nc, tk = self.nc, self.tk
        t = self.sb(es, tname, [p, nchunk], F32)
        ncols = self.din[name].shape[1]
        with self.scope() as es2:
            raw = self.sb(es2, tname + "_raw", [nchunk, p], F32)
            tk.dma("sp", raw[:], self.dap(name, row * ncols + c0, [[p, nchunk], [1, p]]), reads=[self.dbuf[name]], writes=[raw.b])
            ps_ = self.psum()
            tk.op("pe", lambda: nc.tensor.transpose(ps_[:p, 0:nchunk], raw[:], self.ident[:nchunk, :nchunk]),
                  reads=[raw.b, self.ident.b], writes=[ps_.b])
            tk.op("dve", lambda: nc.vector.tensor_copy(t[:], ps_[:p, 0:nchunk]), reads=[ps_.b], writes=[t.b])
        return t

    def phase_w(self):
        nc, tk = self.nc, self.tk
        with self.scope() as es:
            st = self.ring(es, "wst", 3, [128, 2048], F32)
            sh = self.ring(es, "wsh", 3, [128, 2048], BF16)
            k = 0
            for name, R, C in W_SPECS:
                dst = self.scratch(name + "_bf", [R, C], BF16)
                src = self.din[name].ap()
                for r0 in range(0, R, 128):
                    rr = min(128, R - r0)
                    for c0 in range(0, C, 2048):
                        cc = min(2048, C - c0)
                        a = st.next()
                        h = sh.next()
                        tk.dma("sp", a[:rr, :cc], src[r0:r0 + rr, c0:c0 + cc], reads=[self.dbuf[name]], writes=[a.b])
                        e = ("dve", "pool", "act")[k % 3]
                        k += 1
                        if e == "act":
                            tk.op(e, lambda: nc.scalar.copy(h[:rr, :cc], a[:rr, :cc]), reads=[a.b], writes=[h.b])
                        elif e == "dve":
                            tk.op(e, lambda: nc.vector.tensor_copy(h[:rr, :cc], a[:rr, :cc]), reads=[a.b], writes=[h.b])
                        else:
                            tk.op(e, lambda: nc.gpsimd.tensor_copy(h[:rr, :cc], a[:rr, :cc]), reads=[a.b], writes=[h.b])
                        tk.dma("pool", dst.ap()[r0:r0 + rr, c0:c0 + cc], h[:rr, :cc], reads=[h.b],
                               writes=[self.dbuf[name + "_bf"]])

    def norm_T(self, src_name, src_row0, ntok, gname, dstT):
        nc, tk = self.nc, self.tk
        with self.scope() as es:
            gbc = self.bcast_vec(es, gname, 0, 0, D, "nt_g")
            xr = self.ring(es, "nt_x", 2, [128, D], F32)
            xs = self.ring(es, "nt_xs", 2, [128, D], F32)
            junk = self.sb(es, "nt_junk", [128, D], BF16)
            st = self.ring(es, "nt_st", 2, [128, 4], F32)
            src = self.din[src_name].ap()
            for i in range(ntok // 128):
                x = xr.next()
                s = st.next()
                y = xs.next()
                tk.dma("sp", x[:], src[src_row0 + i * 128: src_row0 + (i + 1) * 128, :],
                       reads=[self.dbuf[src_name]], writes=[x.b])
                tk.op("act", lambda: nc.scalar.activation(out=junk[:], in_=x[:], func=AF.Square, accum_out=s[:, 0:1]),
                      reads=[x.b], writes=[junk.b, s.b])
                tk.op("dve", lambda: nc.vector.tensor_scalar(s[:, 1:2], s[:, 0:1], 1.0 / D, 1e-6, ALU.mult, ALU.add),
                      reads=[s.b], writes=[s.b])
                tk.op("act", lambda: nc.scalar.sqrt(s[:, 2:3], s[:, 1:2]), reads=[s.b], writes=[s.b])
                tk.op("dve", lambda: nc.vector.reciprocal(s[:, 3:4], s[:, 2:3]), reads=[s.b], writes=[s.b])
                tk.op("dve", lambda: nc.vector.scalar_tensor_tensor(out=y[:], in0=x[:], scalar=s[:, 3:4], in1=gbc[:],
                                                                    op0=ALU.mult, op1=ALU.mult),
                      reads=[x.b, s.b, gbc.b], writes=[y.b])
                for half in range(2):
                    p = self.psum()
                    for j in range(4):
                        kc = half * 4 + j
                        tk.op("pe", lambda: nc.tensor.transpose(p[:, j * 128:(j + 1) * 128], y[:, kc * 128:(kc + 1) * 128],
                                                                self.ident[:]),
                              reads=[y.b, self.ident.b], writes=[p.b])
                    o = dstT[:, half * 4:half * 4 + 4, i * 128:(i + 1) * 128]
                    pin = p[:, :].rearrange("p (a b) -> p a b", a=4)
                    if half == 0:
                        tk.op("act", lambda: nc.scalar.copy(o, pin), reads=[p.b], writes=[dstT.b])
                    else:
                        tk.op("dve", lambda: nc.vector.tensor_copy(o, pin), reads=[p.b], writes=[dstT.b])

    def load_w(self, tile, wname, c0, ncols, kchunks=8, r0=0):
        C = self.din[wname].shape[1]
        src = self.dap(wname, r0 * C + c0, [[C, 128], [128 * C, kchunks], [1, ncols]])
        self.tk.dma("sp", tile[:, 0:kchunks, 0:ncols], src, reads=[self.dbuf[wname]], writes=[tile.b])

    def proj_fm(self, wname, c0, ncols_total, actT, ntok, epi, kchunks=8, wring=None):
        nc, tk = self.nc, self.tk
        nct = (ncols_total + 127) // 128
        for ci in range(nct):
            cc = min(128, ncols_total - ci * 128)
            w = wring.next()
            self.load_w(w, wname, c0 + ci * 128, cc, kchunks)
            for tt in range(ntok // 512):
                p = self.psum()
                for kc in range(kchunks):
                    tk.op("pe", lambda: nc.tensor.matmul(p[:cc, :], lhsT=w[:, kc, 0:cc],
                                                          rhs=actT[:, kc, tt * 512:(tt + 1) * 512],
                                                          start=(kc == 0), stop=(kc == kchunks - 1)),
                          reads=[w.b, actT.b], writes=[p.b])
                epi(p, ci, tt, cc)

    def phase_b(self, xT):
        nc, tk = self.nc, self.tk
        self.scratch("zr_fm", [3360, T], F32)
        self.scratch("q_fm", [1024, T], BF16)
        self.scratch("kcvc_fm", [512, T], BF16)
        self.scratch("ks_fm", [256, T], BF16)
        self.scratch("kw_fm", [256, T], BF16)
        self.scratch("vsw_tm", [T, 512], BF16)
        self.scratch("gates_fm", [48, T], F32)
        self.scratch("gm_fm", [2048, T], F32)
        with self.scope() as es:
            wring = self.ring(es, "pb_w", 2, [128, 8, 128], BF16)
            o32 = self.ring(es, "pb_o32", 3, [128, 512], F32)
            o16 = self.ring(es, "pb_o16", 3, [128, 512], BF16)
            sq = self.ring(es, "pb_sq", 2, [128, 512], F32)
            qg = self.sb(es, "pb_qg", [128, 4], F32)
            eps = self.sb(es, "pb_eps", [128, 1], F32)
            tk.op("pool", lambda: nc.gpsimd.memset(eps[:], 1e-6), writes=[eps.b])
            for h in range(2):
                tk.dma("sp", qg[64 * h:64 * h + 64, 0:1], self.dap("nsa_q_gain", 0, [[1, 64], [1, 1]]),
                       reads=[self.dbuf["nsa_q_gain"]], writes=[qg.b])
                for j in (1, 2):
                    tk.dma("sp", qg[64 * h:64 * h + 64, j + 1:j + 2], self.dap("nsa_k_gain", 64 * j, [[1, 64], [1, 1]]),
                           reads=[self.dbuf["nsa_k_gain"]], writes=[qg.b])
            cnt = [0]

            def store(dname, row0, t0, tile, rows):
                tk.dma("pool", self.din[dname].ap()[row0:row0 + rows, t0:t0 + 512], tile[:rows, :], reads=[tile.b],
                       writes=[self.dbuf[dname]])

            def epi_copy(dname, row_base, dt):
                def f(p, ci, tt, cc):
                    o = (o32 if dt == F32 else o16).next()
                    cnt[0] += 1
                    if cnt[0] % 2:
                        tk.op("act", lambda: nc.scalar.copy(o[:cc, :], p[:cc, :]), reads=[p.b], writes=[o.b])
                    else:
                        tk.op("dve", lambda: nc.vector.tensor_copy(o[:cc, :], p[:cc, :]), reads=[p.b], writes=[o.b])
                    store(dname, row_base + ci * 128, tt * 512, o, cc)
                return f

            def epi_sig(dname, row_base):
                def f(p, ci, tt, cc):
                    o = o32.next()
                    tk.op("act", lambda: nc.scalar.activation(out=o[:cc, :], in_=p[:cc, :], func=AF.Sigmoid),
                          reads=[p.b], writes=[o.b])
                    store(dname, row_base + ci * 128, tt * 512, o, cc)
                return f

            def epi_norm(dname, row_base, gcol, scale):
                def f(p, ci, tt, cc):
                    s = sq.next()
                    tk.op("act", lambda: nc.scalar.activation(out=s[:], in_=p[:], func=AF.Square), reads=[p.b], writes=[s.b])
                    p2 = self.psum()
                    tk.op("pe", lambda: nc.tensor.matmul(p2[:], lhsT=self.bdones[:], rhs=s[:], start=True, stop=True),
                          reads=[self.bdones.b, s.b], writes=[p2.b])
                    r = o32.next()
                    tk.op("dve", lambda: nc.vector.tensor_scalar(r[:], p2[:], 1.0 / 64, 1e-6, ALU.mult, ALU.add),
                          reads=[p2.b], writes=[r.b])
                    self.rpow(r[:], r.b, -0.5)
                    tk.op("dve", lambda: nc.vector.tensor_tensor(out=r[:], in0=p[:], in1=r[:], op=ALU.mult),
                          reads=[p.b, r.b], writes=[r.b])
                    o = o16.next()
                    tk.op("dve", lambda: nc.vector.tensor_scalar(o[:], r[:], qg[:, gcol:gcol + 1], scale, ALU.mult, ALU.mult),
                          reads=[r.b, qg.b], writes=[o.b])
                    store(dname, row_base + ci * 128, tt * 512, o, cc)
                return f

            segs = [
                (0, 3360, epi_copy("zr_fm", 0, F32)),
                (3360, 1024, epi_norm("q_fm", 0, 0, SCALE_NSA)),
                (4384, 512, epi_copy("kcvc_fm", 0, BF16)),
                (4896, 256, epi_norm("ks_fm", 0, 2, 1.0)),
                (5408, 256, epi_norm("kw_fm", 0, 3, 1.0)),
                (5920, 48, epi_sig("gates_fm", 0)),
                (5968, 2048, epi_sig("gm_fm", 0)),
            ]
            for c0, n, epi in segs:
                self.proj_fm("w_in_bf", c0, n, xT, T, epi, wring=wring)
            wv = self.sb(es, "pb_wv", [128, 8, 512], BF16)
            self.load_w(wv, "w_in_bf", 5152, 256)
            C = IN_COLS
            tk.dma("sp", wv[:, :, 256:512], self.dap("w_in_bf", 5664, [[C, 128], [128 * C, 8], [1, 256]]),
                   reads=[self.dbuf["w_in_bf"]], writes=[wv.b])
            for i in range(T // 128):
                p = self.psum()
                for kc in range(8):
                    tk.op("pe", lambda: nc.tensor.matmul(p[:], lhsT=xT[:, kc, i * 128:(i + 1) * 128], rhs=wv[:, kc, :],
                                                          start=(kc == 0), stop=(kc == 7)), reads=[xT.b, wv.b], writes=[p.b])
                o = o16.next()
                tk.op("act", lambda: nc.scalar.copy(o[:], p[:]), reads=[p.b], writes=[o.b])
                tk.dma("pool", self.din["vsw_tm"].ap()[i * 128:(i + 1) * 128, :], o[:], reads=[o.b], writes=[self.dbuf["vsw_tm"]])

    def build(self):
        nc, tk = self.nc, self.tk
        self.dram_in("x", [NB * T, D])
        self.dram_in("mem", [NB * NMEM, D])
        for name, R, C in W_SPECS:
            self.dram_in(name, [R, C])
        for name, shp in V_SPECS:
            self.dram_in(name, shp)
        for name, shp in CONST_SHAPES.items():
            self.dram_in(name, shp)
        self.out = self.nc.dram_tensor("out", [NB * T, D], F32, kind="ExternalOutput")
        self.din["out"] = self.out
        self.dbuf["out"] = Buf("out", dj=True)
        es = self.es
        self.psr = Ring([Tile(es.enter_context(nc.psum_tensor(f"ps{i}", [128, 512], F32)), f"ps{i}") for i in range(8)])
        for t_ in self.psr.tiles:
            t_.b.xr = True
        self.ident = self.load_const(es, "c_ident")
        self.bdones = self.load_const(es, "c_bdones")
        self.phase_w()
        if self.upto == "w":
            return self.finish()
        self.nsa_bias()
        for bi in range(NB):
            self.seq(bi)
            if self.upto != "all":
                break
        return self.finish()

    def seq(self, bi):
        tk = self.tk
        with self.scope() as es1:
            xT = self.sb(es1, "xT", [128, 8, T], BF16, dj=True)
            self.norm_T("x", bi * T, T, "norm_mix", xT)
            if bi == 0 and "xT_dbg" in self.dbg:
                d = self.scratch("xT_dbg", [128, 8 * T], BF16)
                tk.dma("sp", d.ap()[:, :], xT[:, :, :].rearrange("p a b -> p (a b)"), reads=[xT.b], writes=[self.dbuf["xT_dbg"]])
            if self.upto == "a":
                return
            self.phase_b(xT)
        if self.upto == "b":
            return
        if not getattr(self, "skip_rwkv", False):
            self.phase_rwkv()
        if self.upto == "rwkv":
            return
        self.phase_nsa()
        if self.upto == "nsa":
            return
        self.phase_merge(bi)
        if self.upto == "merge":
            return
        self.phase_cross(bi)
        if self.upto == "cross":
            return
        self.phase_ffn(bi)

    def finish(self):
        self.tk.drain()
        self.es.close()
        return self.nc


def make_in_maps(inputs, cores=range(NCORES)):
    consts = host_consts()
    shared = {}
    for name, R, C in W_SPECS:
        shared[name] = np.ascontiguousarray(np.asarray(inputs[name], np.float32).reshape(R, C))
    for name, shp in V_SPECS:
        shared[name] = np.ascontiguousarray(np.asarray(inputs[name], np.float32).reshape(shp))
    shared.update(consts)
    x = np.asarray(inputs["x"], np.float32)
    mem = np.asarray(inputs["mem"], np.float32)
    maps = []
    for c in cores:
        m = dict(shared)
        m["x"] = np.ascontiguousarray(x[NB * c:NB * c + NB].reshape(NB * T, D))
        m["mem"] = np.ascontiguousarray(mem[NB * c:NB * c + NB].reshape(NB * NMEM, D))
        maps.append(m)
    return maps


def kernel(**inputs):
    prog = Prog()
    nc = prog.build()
    maps = make_in_maps(inputs)
    res = run_bass_kernel_spmd(nc, maps, core_ids=list(range(NCORES)))
    outs = [np.asarray(r["out"]).reshape(NB, T, D) for r in res.results]
    return np.concatenate(outs, axis=0).astype(np.float32)


def _rwkv(self):
    nc, tk = self.nc, self.tk
    TB = 512
    self.scratch("yr_fm", [1024, T], BF16)
    zr = self.din["zr_fm"].ap()
    zb = self.dbuf["zr_fm"]

    def shift_load(dst_ap, dst_buf, r0, nrows, t0, nt, mucol, X, dtile, mubuf):
        if t0 == 0:
            tk.op("pool", lambda: nc.gpsimd.memset(X[:nrows, 0:1], 0.0), writes=[X.b])
            tk.dma("sp", X[:nrows, 1:nt + 1], zr[r0:r0 + nrows, 0:nt], reads=[zb], writes=[X.b])
        else:
            tk.dma("sp", X[:nrows, 0:nt + 1], zr[r0:r0 + nrows, t0 - 1:t0 + nt], reads=[zb], writes=[X.b])
        tk.op("pool", lambda: nc.gpsimd.tensor_tensor(out=dtile[:nrows, :nt], in0=X[:nrows, 0:nt], in1=X[:nrows, 1:nt + 1],
                                                      op=ALU.subtract), reads=[X.b], writes=[dtile.b])
        tk.op("dve", lambda: nc.vector.scalar_tensor_tensor(out=dst_ap, in0=dtile[:nrows, :nt], scalar=mucol,
                                                             in1=X[:nrows, 1:nt + 1], op0=ALU.mult, op1=ALU.add),
              reads=[dtile.b, X.b, mubuf], writes=[dst_buf])

    with self.scope() as es:
        mask4 = self.sb(es, "rk_mask4", [128, 512], F32)
        for j in range(2):
            tk.dma("sp", mask4[:, j * 256:(j + 1) * 256], self.din["c_mask2"].ap()[:, :], reads=[self.dbuf["c_mask2"]], writes=[mask4.b])
        bdm5 = self.load_const(es, "c_bdmask5")
        segm = self.sb(es, "rk_seg", [128, TB], F32)
        tk.dma("sp", segm[:], self.din["c_segmask"].ap()[:, 0:TB], reads=[self.dbuf["c_segmask"]], writes=[segm.b])
        lw = self.sb(es, "rk_lw", [64, T], BF16)
        la = self.sb(es, "rk_la", [64, T], BF16)
        lg = self.sb(es, "rk_lg", [128, 2, T], BF16)
        w2 = self.sb(es, "rk_w2", [64, 1024], BF16)
        a2 = self.sb(es, "rk_a2", [64, 1024], BF16)
        g2 = self.sb(es, "rk_g2", [128, 2, 1024], BF16)
        tk.dma("sp", w2[:], self.din["rwkv_w2_bf"].ap()[:, :], reads=[self.dbuf["rwkv_w2_bf"]], writes=[w2.b])
        tk.dma("sp", a2[:], self.din["rwkv_a2_bf"].ap()[:, :], reads=[self.dbuf["rwkv_a2_bf"]], writes=[a2.b])
        tk.dma("sp", g2[:, 0, :], self.din["rwkv_g2_bf"].ap()[0:128, :], reads=[self.dbuf["rwkv_g2_bf"]], writes=[g2.b])
        tk.dma("sp", g2[0:32, 1, :], self.din["rwkv_g2_bf"].ap()[128:160, :], reads=[self.dbuf["rwkv_g2_bf"]], writes=[g2.b])
        pc = {}
        for nm in ("rwkv_w0", "rwkv_a0", "rwkv_kk", "rwkv_ka", "rwkv_rk", "rwkv_lnx_w", "rwkv_lnx_b"):
            pc[nm] = self.col_vec(es, nm, 0, 0, 8, "rk_" + nm)
        mu = self.col_vec(es, "rwkv_mu", 0, 0, 24, "rk_mu")
        omk = self.sb(es, "rk_omk", [128, 8], F32)
        tk.op("dve", lambda: nc.vector.tensor_scalar(omk[:], pc["rwkv_ka"][:], -1.0, 1.0, ALU.mult, ALU.add),
              reads=[pc["rwkv_ka"].b], writes=[omk.b])
        with self.scope() as es2:
            X = self.sb(es2, "rk_LX", [128, T + 1], F32)
            dt_ = self.sb(es2, "rk_Ld", [128, T], F32)
            zt = self.sb(es2, "rk_Lz", [128, T], F32)
            for (r0, nrows, kind) in ((3072, 64, "w"), (3136, 64, "a"), (3200, 128, "g0"), (3328, 32, "g1")):
                mucol = self.sb(es2, "rk_Lmu" + kind, [128, 1], F32)
                tk.dma("sp", mucol[:nrows, :], self.dap("rwkv_mu", r0, [[1, nrows], [1, 1]]), reads=[self.dbuf["rwkv_mu"]], writes=[mucol.b])
                shift_load(zt[:nrows, :], zt.b, r0, nrows, 0, T, mucol[:nrows, 0:1], X, dt_, mucol.b)
                if kind == "w":
                    tk.op("act", lambda: nc.scalar.activation(out=lw[:, :], in_=zt[:64, :], func=AF.Tanh), reads=[zt.b], writes=[lw.b])
                elif kind == "a":
                    tk.op("act", lambda: nc.scalar.copy(la[:, :], zt[:64, :]), reads=[zt.b], writes=[la.b])
                elif kind == "g0":
                    tk.op("act", lambda: nc.scalar.activation(out=lg[:, 0, :], in_=zt[:, :], func=AF.Sigmoid), reads=[zt.b], writes=[lg.b])
                else:
                    tk.op("act", lambda: nc.scalar.activation(out=lg[:32, 1, :], in_=zt[:32, :], func=AF.Sigmoid), reads=[zt.b], writes=[lg.b])
        LIM = getattr(self, "rk_lim", 99)
        if LIM <= 1:
            return
        f = lambda n: self.sb(es, n, [128, TB], F32)
        Xr = self.ring(es, "rk_X", 2, [128, TB + 1], F32)
        dtl = f("rk_d")
        rr, kp, logw, aa, gg, kkr, sq, kmod, kb, cum, cex, epv, eng, bonus, tmp = [f("rk_t%d" % i) for i in range(15)]
        einr = self.ring(es, "rk_ein", 2, [128, TB], F32)
        Q5r = self.ring(es, "rk_Q5", 2, [128, 5, TB], F32)
        yfm = f("rk_yfm")
        dd = f("rk_dd")
        ob = self.ring(es, "rk_ob", 2, [128, TB], BF16)
        BD5l = [self.sb(es, f"rk_BD5{i}", [128, 5, 2, 64], F32) for i in range(4)]
        GBKl = [self.sb(es, f"rk_GBK{i}", [128, 512], F32) for i in range(4)]
        NTl = [self.sb(es, f"rk_NT{i}", [128, 128], F32) for i in range(4)]
        MXl = [[self.sb(es, f"rk_MX{i}{k}", [128, 256], F32) for k in range(2)] for i in range(4)]
        XXl = [[self.sb(es, f"rk_XX{i}{k}", [128, 128], F32) for k in range(2)] for i in range(4)]
        TTl = [self.sb(es, f"rk_TT{i}", [128, 128], F32) for i in range(4)]
        TM3l = [self.sb(es, f"rk_TM3{i}", [128, 384], F32) for i in range(4)]
        RHr = self.ring(es, "rk_RH", 2, [128, 128], F32)
        Ur = self.ring(es, "rk_U", 2, [128, 128], F32)
        Sr = self.ring(es, "rk_S", 2, [128, 128], F32)
        SPr = self.ring(es, "rk_SP", 2, [128, 128], F32)
        ident, bdones = self.ident, self.bdones

        def mm(p_ap, pbuf, lhsT, lb, rhs, rb, start=True, stop=True):
            tk.op("pe", lambda: nc.tensor.matmul(p_ap, lhsT=lhsT, rhs=rhs, start=start, stop=stop), reads=[lb, rb], writes=[pbuf])

        bonr = self.ring(es, "rk_bon", 2, [128, TB], F32)
        ggr = self.ring(es, "rk_ggr", 2, [128, TB], F32)

        def prep(hp, tb, out):
            c0 = 128 * hp
            if True:
                t0 = tb * TB
                Q5 = Q5r.next()
                ein = einr.next()
                bonus = bonr.next()
                gg = ggr.next()
                out.update(Q5=Q5, ein=ein, bonus=bonus, gg=gg)
                shift_load(rr[:, :], rr.b, c0, 128, t0, TB, mu[:, hp:hp + 1], Xr.next(), dtl, mu.b)
                yield
                shift_load(kp[:, :], kp.b, 1024 + c0, 128, t0, TB, mu[:, 8 + hp:9 + hp], Xr.next(), dtl, mu.b)
                yield
                shift_load(Q5[:, 4, :], Q5.b, 2048 + c0, 128, t0, TB, mu[:, 16 + hp:17 + hp], Xr.next(), dtl, mu.b)
                p = self.psum()
                mm(p[:], p.b, w2[:, c0:c0 + 128], w2.b, lw[:, t0:t0 + TB], lw.b)
                tk.op("act", lambda: nc.scalar.activation(out=logw[:], in_=p[:], func=AF.Sigmoid, bias=pc["rwkv_w0"][:, hp:hp + 1]),
                      reads=[p.b, pc["rwkv_w0"].b], writes=[logw.b])
                yield
                tk.op("pool", lambda: nc.gpsimd.tensor_scalar_mul(logw[:], logw[:], -math.exp(-0.5)), reads=[logw.b], writes=[logw.b])
                p = self.psum()
                mm(p[:], p.b, a2[:, c0:c0 + 128], a2.b, la[:, t0:t0 + TB], la.b)
                tk.op("act", lambda: nc.scalar.activation(out=aa[:], in_=p[:], func=AF.Sigmoid, bias=pc["rwkv_a0"][:, hp:hp + 1]),
                      reads=[p.b, pc["rwkv_a0"].b], writes=[aa.b])
                p = self.psum()
                mm(p[:], p.b, g2[:, 0, c0:c0 + 128], g2.b, lg[:, 0, t0:t0 + TB], lg.b, True, False)
                mm(p[:], p.b, g2[:32, 1, c0:c0 + 128], g2.b, lg[:32, 1, t0:t0 + TB], lg.b, False, True)
                tk.op("act", lambda: nc.scalar.copy(gg[:], p[:]), reads=[p.b], writes=[gg.b])
                yield
                tk.op("dve", lambda: nc.vector.tensor_scalar_mul(kkr[:], kp[:], pc["rwkv_kk"][:, hp:hp + 1]),
                      reads=[kp.b, pc["rwkv_kk"].b], writes=[kkr.b])
                yield
                tk.op("act", lambda: nc.scalar.activation(out=sq[:], in_=kkr[:], func=AF.Square), reads=[kkr.b], writes=[sq.b])
                p = self.psum()
                mm(p[:], p.b, bdones[:], bdones.b, sq[:], sq.b)
                tk.op("dve", lambda: nc.vector.tensor_scalar_max(tmp[:], p[:], 1e-24), reads=[p.b], writes=[tmp.b])
                yield
                self.rpow(tmp[:], tmp.b, -0.5)
                yield
                tk.op("dve", lambda: nc.vector.tensor_tensor(out=kkr[:], in0=kkr[:], in1=tmp[:], op=ALU.mult), reads=[kkr.b, tmp.b], writes=[kkr.b])
                yield
                tk.op("dve", lambda: nc.vector.tensor_scalar(kmod[:], aa[:], pc["rwkv_ka"][:, hp:hp + 1], omk[:, hp:hp + 1], ALU.mult, ALU.add),
                      reads=[aa.b, pc["rwkv_ka"].b, omk.b], writes=[kmod.b])
                yield
                tk.op("pool", lambda: nc.gpsimd.tensor_tensor(out=kmod[:], in0=kmod[:], in1=kp[:], op=ALU.mult), reads=[kmod.b, kp.b], writes=[kmod.b])
                yield
                tk.op("pool", lambda: nc.gpsimd.tensor_tensor(out=kb[:], in0=kkr[:], in1=aa[:], op=ALU.mult), reads=[kkr.b, aa.b], writes=[kb.b])
                yield
                tk.op("dve", lambda: nc.vector.scalar_tensor_tensor(out=tmp[:], in0=rr[:], scalar=pc["rwkv_rk"][:, hp:hp + 1], in1=kmod[:],
                                                                    op0=ALU.mult, op1=ALU.mult), reads=[rr.b, kmod.b, pc["rwkv_rk"].b], writes=[tmp.b])
                p = self.psum()
                mm(p[:], p.b, bdones[:], bdones.b, tmp[:], tmp.b)
                tk.op("dve", lambda: nc.vector.tensor_tensor(out=bonus[:], in0=p[:], in1=Q5[:, 4, :], op=ALU.mult), reads=[p.b, Q5.b], writes=[bonus.b])
                yield
                tk.op("dve", lambda: nc.vector.tensor_tensor_scan(out=cum[:], data0=segm[:], data1=logw[:], initial=0.0, op0=ALU.mult, op1=ALU.add),
                      reads=[segm.b, logw.b], writes=[cum.b])
                yield
                tk.op("pool", lambda: nc.gpsimd.tensor_tensor(out=cex[:], in0=cum[:], in1=logw[:], op=ALU.subtract), reads=[cum.b, logw.b], writes=[cex.b])
                yield
                tk.op("act", lambda: nc.scalar.activation(out=epv[:], in_=cex[:], func=AF.Exp), reads=[cex.b], writes=[epv.b])
                yield
                tk.op("act", lambda: nc.scalar.activation(out=ein[:], in_=cum[:], func=AF.Exp), reads=[cum.b], writes=[ein.b])
                yield
                tk.op("act", lambda: nc.scalar.activation(out=eng[:], in_=cum[:], func=AF.Exp, scale=-1.0), reads=[cum.b], writes=[eng.b])
                yield
                tk.op("dve", lambda: nc.vector.scalar_tensor_tensor(out=Q5[:, 0, :], in0=kkr[:], scalar=-1.0, in1=epv[:], op0=ALU.mult, op1=ALU.mult),
                      reads=[kkr.b, epv.b], writes=[Q5.b])
                yield
                tk.op("pool", lambda: nc.gpsimd.tensor_tensor(out=Q5[:, 1, :], in0=rr[:], in1=ein[:], op=ALU.mult), reads=[rr.b, ein.b], writes=[Q5.b])
                yield
                tk.op("dve", lambda: nc.vector.tensor_tensor(out=Q5[:, 2, :], in0=kb[:], in1=eng[:], op=ALU.mult), reads=[kb.b, eng.b], writes=[Q5.b])
                yield
                tk.op("pool", lambda: nc.gpsimd.tensor_tensor(out=Q5[:, 3, :], in0=kmod[:], in1=eng[:], op=ALU.mult), reads=[kmod.b, eng.b], writes=[Q5.b])
                yield

        blocks = [(hp, tb) for hp in range(8) for tb in range(T // TB)]
        nxt = {}
        for _ in prep(*blocks[0], nxt):
            pass
        S = None
        epi_g = None
        for bi_, (hp, tb) in enumerate(blocks):
            c0 = 128 * hp
            if True:
                t0 = tb * TB
                Q5, ein, bonus, gg = nxt["Q5"], nxt["ein"], nxt["bonus"], nxt["gg"]
                if tb == 0:
                    S = Sr.next()
                    tk.op("pool", lambda: nc.gpsimd.memset(S[:], 0.0), writes=[S.b])
                NBC = 2
                st = {}

                def batch_gen(chunks):
                    for c in chunks:
                        cs = slice(c * 64, (c + 1) * 64)
                        BD5 = BD5l[c % 4]
                        src = Q5[:, :, cs].unsqueeze(2).to_broadcast([128, 5, 2, 64])
                        tk.op("dve", lambda: nc.vector.tensor_tensor(out=BD5[:], in0=src, in1=bdm5[:, :].rearrange("p (a h b) -> p a h b", a=5, h=2),
                                                                     op=ALU.mult), reads=[Q5.b, bdm5.b], writes=[BD5.b])
                        bd = lambda j, BD5=BD5: BD5[:, j, :, :].rearrange("p h b -> p (h b)")
                        p = self.psum()
                        ar = BD5[:, 0:2, :, :].rearrange("p a h b -> p (a h b)")
                        mm(p[:, 0:256], p.b, bd(2), BD5.b, ar, BD5.b)
                        mm(p[:, 256:512], p.b, bd(3), BD5.b, ar, BD5.b)
                        GBK = GBKl[c % 4]
                        tk.op("dve", lambda: nc.vector.tensor_tensor(out=GBK[:], in0=p[:], in1=mask4[:], op=ALU.mult), reads=[p.b, mask4.b], writes=[GBK.b])
                        yield
                        p3 = self.psum()
                        for j in range(3):
                            tk.op("pe", lambda: nc.tensor.transpose(p3[:, j * 128:(j + 1) * 128], bd(2 + j), ident[:]), reads=[BD5.b, ident.b], writes=[p3.b])
                        TM3 = TM3l[c % 4]
                        tk.op("act", lambda: nc.scalar.copy(TM3[:], p3[:, 0:384]), reads=[p3.b], writes=[TM3.b])
                        st[c] = dict(BD5=BD5, bd=bd, GBK=GBK, TM3=TM3, cs=cs)
                        yield
                    for c in chunks:
                        d = st[c]
                        GBK = d["GBK"]
                        NT = NTl[c % 4]
                        p = self.psum()
                        tk.op("pe", lambda: nc.tensor.transpose(p[:, 0:128], GBK[:, 0:128], ident[:]), reads=[GBK.b, ident.b], writes=[p.b])
                        tk.op("act", lambda: nc.scalar.copy(NT[:], p[:, 0:128]), reads=[p.b], writes=[NT.b])
                        X = XXl[c % 4][0]
                        tk.op("pool", lambda: nc.gpsimd.tensor_tensor(out=X[:], in0=ident[:], in1=GBK[:, 0:128], op=ALU.add),
                              reads=[ident.b, GBK.b], writes=[X.b])
                        d["NT"], d["X"] = NT, X
                        yield
                    for c in chunks:
                        d = st[c]
                        GBK, NT = d["GBK"], d["NT"]
                        MM = MXl[c % 4][0]
                        p = self.psum()
                        mm(p[:, 0:128], p.b, NT[:], NT.b, GBK[:, 0:128], GBK.b)
                        mm(p[:, 128:256], p.b, GBK[:, 0:128], GBK.b, NT[:], NT.b)
                        tk.op("act", lambda: nc.scalar.copy(MM[:], p[:, 0:256]), reads=[p.b], writes=[MM.b])
                        d["MM"], d["par"] = MM, 0
                        yield
                    for j in range(2, 6):
                        for c in chunks:
                            d = st[c]
                            MM, X = d["MM"], d["X"]
                            par = 1 - d["par"]
                            MM2, X2 = MXl[c % 4][par], XXl[c % 4][par]
                            px = self.psum()
                            mm(px[:, 0:128], px.b, MM[:, 128:256], MM.b, X[:], X.b)
                            tk.op("dve", lambda: nc.vector.tensor_tensor(out=X2[:], in0=px[:, 0:128], in1=X[:], op=ALU.add), reads=[px.b, X.b], writes=[X2.b])
                            pm = self.psum()
                            mm(pm[:, 0:128], pm.b, MM[:, 128:256], MM.b, MM[:, 0:128], MM.b)
                            mm(pm[:, 128:256], pm.b, MM[:, 0:128], MM.b, MM[:, 128:256], MM.b)
                            tk.op("act", lambda: nc.scalar.copy(MM2[:], pm[:, 0:256]), reads=[pm.b], writes=[MM2.b])
                            d["MM"], d["X"], d["par"] = MM2, X2, par
                            yield
                    for c in chunks:
                        d = st[c]
                        MM, X = d["MM"], d["X"]
                        p = self.psum()
                        mm(p[:, 0:128], p.b, MM[:, 128:256], MM.b, X[:], X.b)
                        TT = TTl[c % 4]
                        tk.op("dve", lambda: nc.vector.tensor_tensor(out=TT[:], in0=p[:, 0:128], in1=X[:], op=ALU.add), reads=[p.b, X.b], writes=[TT.b])
                        d["TT"] = TT
                        yield

                Sh = [S]

                def chain_gen(chunks):
                    for c in chunks:
                        d = st[c]
                        bd, GBK, TM3, TT, cs = d["bd"], d["GBK"], d["TM3"], d["TT"], d["cs"]
                        BD5 = d["BD5"]
                        S = Sh[0]
                        PCc = ein[:, c * 64 + 63:c * 64 + 64]
                        SP = SPr.next()
                        tk.op("act", lambda: nc.scalar.activation(out=SP[:], in_=S[:], func=AF.Identity, scale=PCc), reads=[S.b, ein.b], writes=[SP.b])
                        p = self.psum()
                        mm(p[:, 0:128], p.b, bd(0), BD5.b, S[:], S.b, True, False)
                        mm(p[:, 0:128], p.b, GBK[:, 256:384], GBK.b, TM3[:, 256:384], TM3.b, False, True)
                        RH = RHr.next()
                        tk.op("act", lambda: nc.scalar.copy(RH[:], p[:, 0:128]), reads=[p.b], writes=[RH.b])
                        yield
                        p = self.psum()
                        mm(p[:, 0:128], p.b, TT[:], TT.b, RH[:], RH.b)
                        U = Ur.next()
                        tk.op("dve", lambda: nc.vector.tensor_copy(U[:], p[:, 0:128]), reads=[p.b], writes=[U.b])
                        yield
                        pS = self.psum()
                        mm(pS[:, 0:128], pS.b, TM3[:, 0:128], TM3.b, U[:], U.b, True, False)
                        mm(pS[:, 0:128], pS.b, TM3[:, 128:256], TM3.b, TM3[:, 256:384], TM3.b, False, True)
                        S2 = Sr.next()
                        tk.op("dve", lambda: nc.vector.scalar_tensor_tensor(out=S2[:], in0=pS[:, 0:128], scalar=PCc, in1=SP[:], op0=ALU.mult, op1=ALU.add),
                              reads=[pS.b, ein.b, SP.b], writes=[S2.b])
                        p = self.psum()
                        mm(p[:, 0:128], p.b, S[:], S.b, bd(1), BD5.b, True, False)
                        mm(p[:, 0:128], p.b, U[:], U.b, GBK[:, 128:256], GBK.b, False, False)
                        mm(p[:, 0:128], p.b, TM3[:, 256:384], TM3.b, GBK[:, 384:512], GBK.b, False, True)
                        tk.op("act", lambda: nc.scalar.copy(yfm[0:64, cs], p[0:64, 0:64]), reads=[p.b], writes=[yfm.b])
                        tk.op("act", lambda: nc.scalar.copy(yfm[64:128, cs], p[64:128, 64:128]), reads=[p.b], writes=[yfm.b])
                        Sh[0] = S2
                        yield

                nbt = TB // 64 // NBC
                batches = [list(range(k * NBC, (k + 1) * NBC)) for k in range(nbt)]
                for _ in batch_gen(batches[0]):
                    if epi_g is not None:
                        try:
                            next(epi_g)
                        except StopIteration:
                            epi_g = None
                if epi_g is not None:
                    for _ in epi_g:
                        pass
                    epi_g = None
                nxt = {}
                prep_g = prep(*blocks[bi_ + 1], nxt) if bi_ + 1 < len(blocks) else iter(())
                next(prep_g, None)
                for k in range(nbt):
                    cg = chain_gen(batches[k])
                    bg = batch_gen(batches[k + 1]) if k + 1 < nbt else iter(())
                    done_b = done_c = False
                    while not (done_b and done_c):
                        for _ in range(3):
                            if not done_b:
                                try:
                                    next(bg)
                                except StopIteration:
                                    done_b = True
                        if not done_c:
                            try:
                                next(cg)
                            except StopIteration:
                                done_c = True
                        next(prep_g, None)
                        next(prep_g, None)
                for _ in prep_g:
                    pass
                S = Sh[0]
                def epilogue(hp=hp, c0=c0, t0=t0, bonus=bonus, gg=gg):
                    p = self.psum()
                    mm(p[:], p.b, bdones[:], bdones.b, yfm[:], yfm.b)
                    tk.op("dve", lambda: nc.vector.scalar_tensor_tensor(out=dd[:], in0=p[:], scalar=-1.0 / 64, in1=yfm[:], op0=ALU.mult, op1=ALU.add),
                          reads=[p.b, yfm.b], writes=[dd.b])
                    yield
                    tk.op("act", lambda: nc.scalar.activation(out=sq[:], in_=dd[:], func=AF.Square), reads=[dd.b], writes=[sq.b])
                    p = self.psum()
                    mm(p[:], p.b, bdones[:], bdones.b, sq[:], sq.b)
                    tk.op("dve", lambda: nc.vector.tensor_scalar(tmp[:], p[:], 1.0 / 64, 64e-5, ALU.mult, ALU.add), reads=[p.b], writes=[tmp.b])
                    yield
                    self.rpow(tmp[:], tmp.b, -0.5)
                    yield
                    tk.op("dve", lambda: nc.vector.tensor_tensor(out=dd[:], in0=dd[:], in1=tmp[:], op=ALU.mult), reads=[dd.b, tmp.b], writes=[dd.b])
                    yield
                    tk.op("act", lambda: nc.scalar.activation(out=dd[:], in_=dd[:], func=AF.Identity, bias=pc["rwkv_lnx_b"][:, hp:hp + 1],
                                                              scale=pc["rwkv_lnx_w"][:, hp:hp + 1]),
                          reads=[dd.b, pc["rwkv_lnx_b"].b, pc["rwkv_lnx_w"].b], writes=[dd.b])
                    yield
                    tk.op("pool", lambda: nc.gpsimd.tensor_tensor(out=dd[:], in0=dd[:], in1=bonus[:], op=ALU.add), reads=[dd.b, bonus.b], writes=[dd.b])
                    o = ob.next()
                    yield
                    tk.op("dve", lambda: nc.vector.tensor_tensor(out=o[:], in0=dd[:], in1=gg[:], op=ALU.mult), reads=[dd.b, gg.b], writes=[o.b])
                    yield
                    tk.dma("pool", self.din["yr_fm"].ap()[c0:c0 + 128, t0:t0 + TB], o[:], reads=[o.b], writes=[self.dbuf["yr_fm"]])

                epi_g = epilogue()
        for _ in epi_g:
            pass


Prog.phase_rwkv = _rwkv


def _nsa_bias(self):
    nc, tk = self.nc, self.tk
    self.scratch("bvec_c", [16, LVEC], BF16)
    self.scratch("bvec_w", [16, LVEC], BF16)
    with self.scope() as es:
        tab = self.sb(es, "nb_tab", [33, 16], F32)
        tk.op("pool", lambda: nc.gpsimd.memset(tab[:], 1.0), writes=[tab.b])
        tk.dma("sp", tab[0:32, :], self.din["rel_bias"].ap()[:, :], reads=[self.dbuf["rel_bias"]], writes=[tab.b])
        e33 = self.sb(es, "nb_e33", [33, LVEC], F32)
        ob = self.ring(es, "nb_o", 2, [16, 512], BF16)
        for cname, dname in (("c_e33c", "bvec_c"), ("c_e33w", "bvec_w")):
            tk.dma("sp", e33[:], self.din[cname].ap()[:, :], reads=[self.dbuf[cname]], writes=[e33.b])
            for j in range(LVEC // 512):
                p = self.psum()
                tk.op("pe", lambda: nc.tensor.matmul(p[:16, :], lhsT=tab[:], rhs=e33[:, j * 512:(j + 1) * 512], start=True, stop=True),
                      reads=[tab.b, e33.b], writes=[p.b])
                o = ob.next()
                tk.op("act", lambda: nc.scalar.copy(o[:], p[:16, :]), reads=[p.b], writes=[o.b])
                tk.dma("pool", self.din[dname].ap()[:, j * 512:(j + 1) * 512], o[:], reads=[o.b], writes=[self.dbuf[dname]])


def _nsa(self):
    nc, tk = self.nc, self.tk
    self.scratch("yn_fm", [1024, T], BF16)
    ngen = getattr(self, "ns_ngen", 5)
    gen = Ring(self.psr.tiles[0:ngen])
    accp = Ring(self.psr.tiles[ngen:8])

    def mm(p_ap, pbuf, lhsT, lb, rhs, rb, start=True, stop=True):
        tk.op("pe", lambda: nc.tensor.matmul(p_ap, lhsT=lhsT, rhs=rhs, start=start, stop=stop), reads=[lb, rb], writes=[pbuf])

    with self.scope() as es:
        Jb = self.load_const(es, "c_J", BF16, tmp_es=es)
        c2s_f = self.sb(es, "ns_c2sf", [128, 2, 64], F32)
        tk.dma("sp", c2s_f[:, :, :], self.dap("c_c2s", 0, [[64, 128], [128 * 64, 2], [1, 64]]), reads=[self.dbuf["c_c2s"]], writes=[c2s_f.b])
        c2s = self.sb(es, "ns_c2s", [128, 2, 64], BF16)
        tk.op("dve", lambda: nc.vector.tensor_copy(c2s[:], c2s_f[:]), reads=[c2s_f.b], writes=[c2s.b])
        ksX = self.sb(es, "ns_ksX", [128, T], BF16, dj=True)
        kwX = self.sb(es, "ns_kwX", [128, T], BF16, dj=True)
        with self.scope() as es0:
            exf = self.sb(es0, "ns_exf", [128, 4096], F32)
            tk.dma("sp", exf[64:128, :], self.din["c_expand"].ap()[:, :], reads=[self.dbuf["c_expand"]], writes=[exf.b])
            tk.op("dve", lambda: nc.vector.tensor_scalar_mul(ksX[64:128, :], exf[64:128, :], BIG), reads=[exf.b], writes=[ksX.b])
        tk.op("pool", lambda: nc.gpsimd.memset(kwX[64:128, :], 0.0), writes=[kwX.b])
        ones = self.sb(es, "ns_ones", [128, 64], BF16)
        tk.op("pool", lambda: nc.gpsimd.memset(ones[:], 1.0), writes=[ones.b])
        kgain = self.col_vec(es, "nsa_k_gain", 0, 0, 1, "ns_kg", p=64)
        ident, bdones = self.ident, self.bdones
        kcmpT = [self.sb(es, f"ns_kcT{g}", [128, 256], BF16) for g in range(4)]
        for g in range(4):
            tk.op("pool", lambda: nc.gpsimd.memset(kcmpT[g][64:128, :], 0.0), writes=[kcmpT[g].b])
        vcmp = [self.sb(es, f"ns_vc{g}", [128, 2, 64], BF16) for g in range(4)]
        with self.scope() as es2:
            kc2 = self.sb(es2, "ns_kc2", [128, T], BF16)
            w1t = self.sb(es2, "ns_w1", [128, 16, 256], BF16)
            w2t = self.sb(es2, "ns_w2", [128, 2, 64], BF16)
            hg_ = self.sb(es2, "ns_hg", [128, 2, 256], BF16)
            xx = self.sb(es2, "ns_x", [128, 256], F32)
            x2 = self.sb(es2, "ns_x2", [128, 256], F32)
            pvb = self.sb(es2, "ns_pvb", [128, 2], F32)
            t64 = self.sb(es2, "ns_t64", [64, 256], F32)
            t64b = self.sb(es2, "ns_t64b", [64, 256], F32)
            for kind in range(2):
                sfx = "_k" if kind == 0 else "_v"
                self.load_w(w1t, "cmp_w1" + sfx + "_bf", 0, 256, kchunks=16)
                self.load_w(w2t, "cmp_w2" + sfx + "_bf", 0, 64, kchunks=2)
                pe_f = self.col_vec(es2, "cmp_pe" + sfx, 0, 0, 16, "ns_pe" + sfx)
                pe_b = self.sb(es2, "ns_peb" + sfx, [128, 16], BF16)
                tk.op("dve", lambda: nc.vector.tensor_copy(pe_b[:], pe_f[:]), reads=[pe_f.b], writes=[pe_b.b])
                for ct in range(2):
                    p = gen.next()
                    for l2 in range(16):
                        mm(p[:, 0:1], p.b, w1t[:, l2, ct * 128:(ct + 1) * 128], w1t.b, pe_b[:, l2:l2 + 1], pe_b.b, l2 == 0, l2 == 15)
                    tk.op("dve", lambda: nc.vector.tensor_copy(pvb[:, ct:ct + 1], p[:, 0:1]), reads=[p.b], writes=[pvb.b])
                for g in range(4):
                    r0 = 256 * kind + 64 * g
                    tk.op("pool", lambda: nc.gpsimd.memset(kc2[64:128, T - 1:T], 0.0), writes=[kc2.b])
                    tk.dma("sp", kc2[0:64, :], self.din["kcvc_fm"].ap()[r0:r0 + 64, :], reads=[self.dbuf["kcvc_fm"]], writes=[kc2.b])
                    tk.dma("sp", kc2[64:128, 0:T - 1], self.din["kcvc_fm"].ap()[r0:r0 + 64, 1:T], reads=[self.dbuf["kcvc_fm"]], writes=[kc2.b])
                    tk.op("pool", lambda: nc.gpsimd.memset(hg_[:], 0.0), writes=[hg_.b])
                    for ct in range(2):
                        p = gen.next()
                        for l2 in range(16):
                            rhs = kc2[:, 2 * l2: 2 * l2 + 16 * 254 + 1: 16]
                            mm(p[:, 0:255], p.b, w1t[:, l2, ct * 128:(ct + 1) * 128], w1t.b, rhs, kc2.b, l2 == 0, l2 == 15)
                        tk.op("act", lambda: nc.scalar.activation(out=xx[:, 0:255], in_=p[:, 0:255], func=AF.Identity, bias=pvb[:, ct:ct + 1]),
                              reads=[p.b, pvb.b], writes=[xx.b])
                        tk.op("act", lambda: nc.scalar.activation(out=x2[:, 0:255], in_=xx[:, 0:255], func=AF.Square), reads=[xx.b], writes=[x2.b])
                        tk.op("dve", lambda: nc.vector.tensor_scalar(x2[:, 0:255], x2[:, 0:255], 0.044715, 1.0, ALU.mult, ALU.add), reads=[x2.b], writes=[x2.b])
                        tk.op("dve", lambda: nc.vector.tensor_tensor(out=x2[:, 0:255], in0=x2[:, 0:255], in1=xx[:, 0:255], op=ALU.mult), reads=[x2.b, xx.b], writes=[x2.b])
                        tk.op("act", lambda: nc.scalar.activation(out=x2[:, 0:255], in_=x2[:, 0:255], func=AF.Sigmoid, scale=1.5957691216057308),
                              reads=[x2.b], writes=[x2.b])
                        tk.op("dve", lambda: nc.vector.tensor_tensor(out=hg_[:, ct, 0:255], in0=x2[:, 0:255], in1=xx[:, 0:255], op=ALU.mult),
                              reads=[x2.b, xx.b], writes=[hg_.b])
                    if kind == 0:
                        p = gen.next()
                        for ct in range(2):
                            mm(p[0:64, 0:256], p.b, w2t[:, ct, :], w2t.b, hg_[:, ct, :], hg_.b, ct == 0, ct == 1)
                        tk.op("act", lambda: nc.scalar.activation(out=t64[:], in_=p[0:64, 0:256], func=AF.Square), reads=[p.b], writes=[t64.b])
                        p2 = gen.next()
                        mm(p2[0:64, 0:256], p2.b, bdones[0:64, 0:64], bdones.b, t64[:], t64.b)
                        tk.op("dve", lambda: nc.vector.tensor_scalar(t64[:], p2[0:64, 0:256], 1.0 / 64, 1e-6, ALU.mult, ALU.add), reads=[p2.b], writes=[t64.b])
                        self.rpow(t64[:], t64.b, -0.5)
                        tk.op("dve", lambda: nc.vector.tensor_tensor(out=t64b[:], in0=p[0:64, 0:256], in1=t64[:], op=ALU.mult), reads=[p.b, t64.b], writes=[t64b.b])
                        tk.op("dve", lambda: nc.vector.tensor_scalar_mul(kcmpT[g][0:64, :], t64b[:], kgain[:, 0:1]), reads=[t64b.b, kgain.b], writes=[kcmpT[g].b])
                    else:
                        for nt in range(2):
                            p = gen.next()
                            for ct in range(2):
                                mm(p[:, 0:64], p.b, hg_[:, ct, nt * 128:(nt + 1) * 128], hg_.b, w2t[:, ct, :], w2t.b, ct == 0, ct == 1)
                            tk.op("act", lambda: nc.scalar.copy(vcmp[g][:, nt, :], p[:, 0:64]), reads=[p.b], writes=[vcmp[g].b])
        if getattr(self, "ns_lim", 99) <= 1:
            return
        Vs = self.sb(es, "ns_Vs", [128, 32, 128], BF16)
        Vw = self.sb(es, "ns_Vw", [128, 32, 128], BF16)
        tk.op("pool", lambda: nc.gpsimd.memset(Vs[:], 1.0), writes=[Vs.b])
        tk.op("pool", lambda: nc.gpsimd.memset(Vw[:], 1.0), writes=[Vw.b])
        vco = [self.sb(es, f"ns_vco{g}", [128, 2, 128], BF16) for g in range(4)]
        for g in range(4):
            tk.op("pool", lambda: nc.gpsimd.memset(vco[g][:], 1.0), writes=[vco[g].b])
            tk.op("dve", lambda: nc.vector.tensor_copy(vco[g][:, :, 0:64], vcmp[g][:]), reads=[vcmp[g].b], writes=[vco[g].b])
        bfar = self.sb(es, "ns_bfar", [128, 16], F32)
        tk.dma("sp", bfar[:], self.din["rel_bias"].ap()[31:32, :].partition_broadcast(128), reads=[self.dbuf["rel_bias"]], writes=[bfar.b])
        qTr = self.ring(es, "ns_qT", 2, [128, 4, 512], BF16)
        for t_ in qTr.tiles:
            tk.op("pool", lambda: nc.gpsimd.memset(t_[64:128, :, :], 0.0), writes=[t_.b])
        gbr = self.ring(es, "ns_gb", 3, [64, 4, 512], F32)
        Hr = self.ring(es, "ns_H", 4, [128, 512], BF16)
        Er = self.ring(es, "ns_E", 4, [128, 512], BF16)
        E2r = self.ring(es, "ns_E2", 4, [128, 512], BF16)
        Ec = self.sb(es, "ns_Ec", [128, 8, 512], BF16, dj=True)
        accC = self.sb(es, "ns_accC", [64, 4, 8, 512], BF16, dj=True)
        QSr = self.ring(es, "ns_QS", 2, [128, T], BF16)
        for t_ in QSr.tiles:
            t_.b.dj = True
        acc = self.ring(es, "ns_acc", 2, [64, 512], F32)
        impa = self.sb(es, "ns_impa", [64, 512], F32)
        frc = self.ring(es, "ns_frc", 2, [64, 512], F32)
        rdr = self.ring(es, "ns_rd", 3, [64, 512], F32)
        t1r = self.ring(es, "ns_t1", 3, [64, 512], F32)
        impq = self.sb(es, "ns_impq", [128, 4, 64], F32)
        selq = self.sb(es, "ns_selq", [128, 4, 64], F32)
        wk = self.sb(es, "ns_wk", [128, 64], F32)
        m8 = self.sb(es, "ns_m8", [128, 16], F32)
        obr = self.ring(es, "ns_ob", 2, [64, 512], BF16)
        XBr = [[self.sb(es, f"ns_xb{k}_{i}", [128, 512], BF16) for i in range(13)] for k in range(1)]
        pend = []
        eng_alt = [0]

        def flush():
            while pend:
                pend.pop(0)()

        def hankel(vname, h, c, pstep):
            H = Hr.next()
            src = self.dap(vname, h * LVEC + c, [[pstep, 128], [1, 512]])
            tk.dma("sp", H[:], src, reads=[self.dbuf[vname]], writes=[H.b])
            return H

        def ratio(pn):
            rd = rdr.next()
            tk.op("dve", lambda: nc.vector.tensor_scalar_max(rd[:], pn[64:128, :], 1e-30), reads=[pn.b], writes=[rd.b])
            self.rpow(rd[:], rd.b, -1.0)
            t1 = t1r.next()
            tk.op("dve", lambda: nc.vector.tensor_tensor(out=t1[:], in0=pn[0:64, :], in1=rd[:], op=ALU.mult), reads=[pn.b, rd.b], writes=[t1.b])
            return rd, t1

        def key_tile(s_mms, e_ap, e_buf, act_bias, pv, mult=None, cols=(0, 512)):
            lo, hi = cols
            p = gen.next()
            for i, (lhsT, lb, rhs, rb) in enumerate(s_mms):
                mm(p[:, lo:hi], p.b, lhsT, lb, rhs, rb, i == 0, i == len(s_mms) - 1)
            if mult is None:
                if act_bias is None:
                    tk.op("act", lambda: nc.scalar.activation(out=e_ap, in_=p[:, lo:hi], func=AF.Exp), reads=[p.b], writes=[e_buf])
                else:
                    tk.op("act", lambda: nc.scalar.activation(out=e_ap, in_=p[:, lo:hi], func=AF.Exp, bias=act_bias), reads=[p.b, bfar.b], writes=[e_buf])
            else:
                E0 = E2r.next()
                tk.op("act", lambda: nc.scalar.activation(out=E0[:, lo:hi], in_=p[:, lo:hi], func=AF.Exp), reads=[p.b], writes=[E0.b])
                tk.op("dve", lambda: nc.vector.tensor_tensor(out=e_ap, in0=E0[:, lo:hi], in1=mult[:, lo:hi], op=ALU.mult), reads=[E0.b, mult.b], writes=[e_buf])
            while len(pend) >= SKEW:
                pend.pop(0)()
            pend.append(pv)

        SKEW = getattr(self, "ns_skew", 3)
        for g in range(4):
            flush()
            tk.dma("sp", ksX[0:64, :], self.din["ks_fm"].ap()[64 * g:64 * g + 64, :], reads=[self.dbuf["ks_fm"]], writes=[ksX.b])
            tk.dma("sp", kwX[0:64, :], self.din["kw_fm"].ap()[64 * g:64 * g + 64, :], reads=[self.dbuf["kw_fm"]], writes=[kwX.b])
            for k8 in range(4):
                tk.dma("sp", Vs[:, 8 * k8:8 * k8 + 8, 0:64], self.dap("vsw_tm", 64 * g + 8 * k8 * 128 * 512, [[512, 128], [128 * 512, 8], [1, 64]]),
                       reads=[self.dbuf["vsw_tm"]], writes=[Vs.b])
                tk.dma("sp", Vw[:, 8 * k8:8 * k8 + 8, 0:64], self.dap("vsw_tm", 256 + 64 * g + 8 * k8 * 128 * 512, [[512, 128], [128 * 512, 8], [1, 64]]),
                       reads=[self.dbuf["vsw_tm"]], writes=[Vw.b])
            for qt in range(T // 512):
                t0 = qt * 512
                qT = qTr.next()
                tk.dma("sp", qT[0:64, :, :], self.dap("q_fm", 256 * g * T + t0, [[T, 64], [64 * T, 4], [1, 512]]), reads=[self.dbuf["q_fm"]], writes=[qT.b])
                gb = gbr.next()
                for j in range(4):
                    row = 12 * g + 3 * j
                    tk.dma("sp", gb[:, j, :], self.din["gates_fm"].ap()[row:row + 1, t0:t0 + 512].partition_broadcast(64),
                           reads=[self.dbuf["gates_fm"]], writes=[gb.b])
                fr = frc.next()
                tk.dma("sp", fr[:], self.din["c_forced"].ap()[:, t0:t0 + 512], reads=[self.dbuf["c_forced"]], writes=[fr.b])
                nnt = 2 if t0 >= 2048 else 1
                for hg in range(4):
                    h = 4 * g + hg
                    pn, pi = accp.next(), accp.next()
                    for nt in range(nnt):
                        H = hankel("bvec_c", h, OFFC + t0 - 16 * 128 * nt - 2063, 16)
                        e_ap = Ec[:, hg * 2 + nt, :]

                        def pv(pn=pn, pi=pi, nt=nt, e_ap=e_ap, nnt=nnt):
                            mm(pn[:], pn.b, vco[g][:, nt, :], vco[g].b, e_ap, Ec.b, nt == 0, nt == nnt - 1)
                            mm(pi[0:64, :], pi.b, c2s[:, nt, :], c2s.b, e_ap, Ec.b, nt == 0, nt == nnt - 1)
                        key_tile([(kcmpT[g][:, nt * 128:(nt + 1) * 128], kcmpT[g].b, qT[:, hg, :], qT.b), (Jb[:], Jb.b, H[:], H.b)], e_ap, Ec.b, None, pv)

                    def fin(pn=pn, pi=pi, hg=hg, gb=gb, qt=qt):
                        rd, t1 = ratio(pn)
                        tk.op("pool", lambda: nc.gpsimd.tensor_tensor(out=accC[:, hg, qt, :], in0=t1[:], in1=gb[:, hg, :], op=ALU.mult),
                              reads=[t1.b, gb.b], writes=[accC.b])
                        if hg == 0:
                            tk.op("dve", lambda: nc.vector.tensor_tensor(out=impa[:], in0=pi[0:64, :], in1=rd[:], op=ALU.mult), reads=[pi.b, rd.b], writes=[impa.b])
                        else:
                            t2 = t1r.next()
                            tk.op("dve", lambda: nc.vector.tensor_tensor(out=t2[:], in0=pi[0:64, :], in1=rd[:], op=ALU.mult), reads=[pi.b, rd.b], writes=[t2.b])
                            tk.op("pool", lambda: nc.gpsimd.tensor_tensor(out=impa[:], in0=impa[:], in1=t2[:], op=ALU.add), reads=[impa.b, t2.b], writes=[impa.b])
                    pend.append(fin)
                flush()
                tk.op("dve", lambda: nc.vector.tensor_tensor(out=impa[:], in0=impa[:], in1=fr[:], op=ALU.max), reads=[impa.b, fr.b], writes=[impa.b])
                p = gen.next()
                for s4 in range(4):
                    tk.op("pe", lambda: nc.tensor.transpose(p[:, s4 * 64:(s4 + 1) * 64], impa[:, s4 * 128:(s4 + 1) * 128], ident[0:64, 0:64]),
                          reads=[impa.b, ident.b], writes=[p.b])
                tk.op("act", lambda: nc.scalar.copy(impq[:], p[:, 0:256].rearrange("p (a b) -> p a b", a=4)), reads=[p.b], writes=[impq.b])
                for s4 in range(4):
                    tk.op("dve", lambda: nc.vector.max(out=m8[:, 0:8], in_=impq[:, s4, :]), reads=[impq.b], writes=[m8.b])
                    tk.op("dve", lambda: nc.vector.match_replace(out=wk[:], in_to_replace=m8[:, 0:8], in_values=impq[:, s4, :], imm_value=-1e30),
                          reads=[impq.b, m8.b], writes=[wk.b])
                    tk.op("dve", lambda: nc.vector.max(out=m8[:, 8:16], in_=wk[:]), reads=[wk.b], writes=[m8.b])
                    tk.op("dve", lambda: nc.vector.tensor_scalar(selq[:, s4, :], impq[:, s4, :], m8[:, 15:16], 1.0, ALU.is_ge, ALU.subtract),
                          reads=[impq.b, m8.b], writes=[selq.b])
                p = gen.next()
                for s4 in range(4):
                    tk.op("pe", lambda: nc.tensor.transpose(p[0:64, s4 * 128:(s4 + 1) * 128], selq[:, s4, :], ident[:]),
                          reads=[selq.b, ident.b], writes=[p.b])
                tk.op("act", lambda: nc.scalar.copy(QSr.tiles[0][64:128, qt * 512:(qt + 1) * 512], p[0:64, :]), reads=[p.b], writes=[QSr.tiles[0].b])
                tk.op("dve", lambda: nc.vector.tensor_copy(QSr.tiles[1][64:128, qt * 512:(qt + 1) * 512], p[0:64, :]), reads=[p.b], writes=[QSr.tiles[1].b])
            for hg in range(4):
                h = 4 * g + hg
                flush()
                XB = XBr[0]
                xw, xs = {}, {}
                for i, (vname, d) in enumerate([("bvec_w", dd_) for dd_ in range(512, -385, -128)] + [("bvec_c", dd_) for dd_ in range(128, -385, -128)]):
                    H = hankel(vname, h, OFFC + d - 127, 1)
                    p = gen.next()
                    mm(p[:], p.b, Jb[:], Jb.b, H[:], H.b)
                    tk.op("act", lambda: nc.scalar.activation(out=XB[i][:], in_=p[:], func=AF.Exp), reads=[p.b], writes=[XB[i].b])
                    (xw if vname == "bvec_w" else xs)[d] = XB[i]
                q_ = QSr.next()
                tk.dma("sp", q_[0:64, :], self.din["q_fm"].ap()[64 * h:64 * h + 64, :], reads=[self.dbuf["q_fm"]], writes=[q_.b])
                for qt in range(T // 512):
                    t0 = qt * 512
                    qs = q_[:, t0:t0 + 512]
                    gb = gbr.next()
                    for j in range(2):
                        row = 3 * h + 1 + j
                        tk.dma("sp", gb[:, j, :], self.din["gates_fm"].ap()[row:row + 1, t0:t0 + 512].partition_broadcast(64),
                               reads=[self.dbuf["gates_fm"]], writes=[gb.b])
                    ac = acc.next()
                    kts = list(range(max(0, (t0 - 512) // 128), (t0 + 511) // 128 + 1))
                    kts.sort(key=lambda kt: (128 * kt != t0, kt))
                    pn = accp.next()
                    for i, kt in enumerate(kts):
                        E = Er.next()
                        dl = (128 * kt - t0) // 128
                        lo, hi = (0, min(512, 128 * (dl + 5))) if dl < 0 else (128 * dl, 512)

                        def pv(pn=pn, kt=kt, E=E, first=(i == 0), last=(i == len(kts) - 1), lo=lo, hi=hi):
                            mm(pn[:, lo:hi], pn.b, Vw[:, kt, :], Vw.b, E[:, lo:hi], E.b, first, last)
                        key_tile([(kwX[:, kt * 128:(kt + 1) * 128], kwX.b, qs[:, lo:hi], q_.b)], E[:, lo:hi], E.b, None, pv, mult=xw[t0 - 128 * kt], cols=(lo, hi))

                    def finw(pn=pn, gb=gb, ac=ac):
                        rd, t1 = ratio(pn)
                        tk.op("dve", lambda: nc.vector.tensor_tensor(out=ac[:], in0=t1[:], in1=gb[:, 1, :], op=ALU.mult), reads=[t1.b, gb.b], writes=[ac.b])
                    pend.append(finw)
                    kts = list(range(0, (t0 + 511) // 128 + 1))
                    pn = accp.next()
                    for i, kt in enumerate(kts):
                        far = (128 * kt <= t0 - 256)
                        E = Er.next()
                        s_mms = [(ksX[:, kt * 128:(kt + 1) * 128], ksX.b, qs, q_.b)]

                        dl = (128 * kt - t0) // 128
                        lo = 128 * dl if dl > 0 else 0
                        s_mms = [(ksX[:, kt * 128:(kt + 1) * 128], ksX.b, qs[:, lo:512], q_.b)]

                        def pv(pn=pn, kt=kt, E=E, first=(i == 0), last=(i == len(kts) - 1), lo=lo):
                            mm(pn[:, lo:512], pn.b, Vs[:, kt, :], Vs.b, E[:, lo:512], E.b, first, last)
                        key_tile(s_mms, E[:, lo:512], E.b, bfar[:, h:h + 1] if far else None, pv, mult=None if far else xs[t0 - 128 * kt], cols=(lo, 512))

                    def fins(pn=pn, gb=gb, ac=ac, hg=hg, h=h, t0=t0, qt=qt):
                        rd, t1 = ratio(pn)
                        tk.op("dve", lambda: nc.vector.tensor_tensor(out=t1[:], in0=t1[:], in1=gb[:, 0, :], op=ALU.mult), reads=[t1.b, gb.b], writes=[t1.b])
                        tk.op("dve", lambda: nc.vector.tensor_tensor(out=ac[:], in0=ac[:], in1=t1[:], op=ALU.add), reads=[t1.b, ac.b], writes=[ac.b])
                        o = obr.next()
                        tk.op("dve", lambda: nc.vector.tensor_tensor(out=o[:], in0=ac[:], in1=accC[:, hg, qt, :], op=ALU.add), reads=[ac.b, accC.b], writes=[o.b])
                        tk.dma("pool", self.din["yn_fm"].ap()[64 * h:64 * h + 64, t0:t0 + 512], o[:], reads=[o.b], writes=[self.dbuf["yn_fm"]])
                    pend.append(fins)
        flush()


Prog.nsa_bias = _nsa_bias
Prog.phase_nsa = _nsa


def _proj_tm_res(self, actT, tok0, ntok, kchunks, w, res_name, res_row0, dst_name, dst_row0, es):
    nc, tk = self.nc, self.tk
    xr = self.ring(es, "pt_x", 2, [128, D], F32)
    orr = self.ring(es, "pt_o", 2, [128, D], F32)
    for i in range(ntok // 128):
        x = xr.next()
        o = orr.next()
        tk.dma("sp", x[:], self.din[res_name].ap()[res_row0 + i * 128:res_row0 + (i + 1) * 128, :], reads=[self.dbuf[res_name]], writes=[x.b])
        for half in range(2):
            p = self.psum()
            for kc in range(kchunks):
                tk.op("pe", lambda: nc.tensor.matmul(p[:], lhsT=actT[:, kc, tok0 + i * 128:tok0 + (i + 1) * 128], rhs=w[:, kc, half * 512:(half + 1) * 512],
                                                      start=(kc == 0), stop=(kc == kchunks - 1)), reads=[actT.b, w.b], writes=[p.b])
            tk.op("dve", lambda: nc.vector.tensor_tensor(out=o[:, half * 512:(half + 1) * 512], in0=p[:], in1=x[:, half * 512:(half + 1) * 512], op=ALU.add),
                  reads=[p.b, x.b], writes=[o.b])
        tk.dma("pool", self.din[dst_name].ap()[dst_row0 + i * 128:dst_row0 + (i + 1) * 128, :], o[:], reads=[o.b], writes=[self.dbuf[dst_name]])


def _merge(self, bi):
    nc, tk = self.nc, self.tk
    self.scratch("h1", [T, D], F32)
    with self.scope() as es:
        mT = self.sb(es, "mg_mT", [128, 8, T], BF16, dj=True)
        with self.scope() as es2:
            wr = self.sb(es2, "mg_wr", [128, 8, 1024], BF16)
            wn = self.sb(es2, "mg_wn", [128, 8, 1024], BF16)
            self.load_w(wr, "w_branch_rwkv_bf", 0, 1024)
            self.load_w(wn, "w_branch_nsa_bf", 0, 1024)
            yr = self.ring(es2, "mg_yr", 2, [128, 8, 512], BF16)
            yn = self.ring(es2, "mg_yn", 2, [128, 8, 512], BF16)
            gr = self.ring(es2, "mg_g", 4, [128, 512], F32)
            tr = self.ring(es2, "mg_t", 4, [128, 512], F32)
            for tt in range(T // 512):
                a, b = yr.next(), yn.next()
                tk.dma("sp", a[:], self.dap("yr_fm", tt * 512, [[T, 128], [128 * T, 8], [1, 512]]), reads=[self.dbuf["yr_fm"]], writes=[a.b])
                tk.dma("sp", b[:], self.dap("yn_fm", tt * 512, [[T, 128], [128 * T, 8], [1, 512]]), reads=[self.dbuf["yn_fm"]], writes=[b.b])
                for ci in range(8):
                    g0, g1 = gr.next(), gr.next()
                    tk.dma("sp", g0[:], self.din["gm_fm"].ap()[ci * 128:(ci + 1) * 128, tt * 512:(tt + 1) * 512], reads=[self.dbuf["gm_fm"]], writes=[g0.b])
                    tk.dma("sp", g1[:], self.din["gm_fm"].ap()[1024 + ci * 128:1024 + (ci + 1) * 128, tt * 512:(tt + 1) * 512], reads=[self.dbuf["gm_fm"]], writes=[g1.b])
                    pr, pn = self.psum(), self.psum()
                    for kc in range(8):
                        tk.op("pe", lambda: nc.tensor.matmul(pr[:], lhsT=wr[:, kc, ci * 128:(ci + 1) * 128], rhs=a[:, kc, :], start=(kc == 0), stop=(kc == 7)),
                              reads=[wr.b, a.b], writes=[pr.b])
                    for kc in range(8):
                        tk.op("pe", lambda: nc.tensor.matmul(pn[:], lhsT=wn[:, kc, ci * 128:(ci + 1) * 128], rhs=b[:, kc, :], start=(kc == 0), stop=(kc == 7)),
                              reads=[wn.b, b.b], writes=[pn.b])
                    t0_, t1_ = tr.next(), tr.next()
                    tk.op("dve", lambda: nc.vector.tensor_tensor(out=t0_[:], in0=pr[:], in1=g0[:], op=ALU.mult), reads=[pr.b, g0.b], writes=[t0_.b])
                    tk.op("dve", lambda: nc.vector.tensor_tensor(out=t1_[:], in0=pn[:], in1=g1[:], op=ALU.mult), reads=[pn.b, g1.b], writes=[t1_.b])
                    tk.op("pool", lambda: nc.gpsimd.tensor_tensor(out=mT[:, ci, tt * 512:(tt + 1) * 512], in0=t0_[:], in1=t1_[:], op=ALU.add),
                          reads=[t0_.b, t1_.b], writes=[mT.b])
        with self.scope() as es3:
            wm = self.sb(es3, "mg_wm", [128, 8, 1024], BF16)
            self.load_w(wm, "w_mix_out_bf", 0, 1024)
            self.proj_tm_res(mT, 0, T, 8, wm, "x", bi * T, "h1", 0, es3)


def _cross(self, bi):
    nc, tk = self.nc, self.tk
    self.scratch("h2", [T, D], F32)
    HT = 2048
    with self.scope() as es:
        wq = self.sb(es, "ca_wq", [128, 8, 1024], BF16)
        wo = self.sb(es, "ca_wo", [128, 8, 1024], BF16)
        self.load_w(wq, "ca_wq_bf", 0, 1024)
        self.load_w(wo, "ca_wo_bf", 0, 1024)
        kT = self.sb(es, "ca_kT", [128, 8, NMEM], BF16, dj=True)
        Vc = self.sb(es, "ca_V", [128, 2, 1024], BF16, dj=True)
        qgain = self.col_vec(es, "ca_q_gain", 0, 0, 2, "ca_qg")
        kgain = self.col_vec(es, "ca_k_gain", 0, 0, 2, "ca_kg")
        ones_f = self.load_const(es, "c_ones")
        ones_b = self.sb(es, "ca_1b", [128, 128], BF16)
        tk.op("dve", lambda: nc.vector.tensor_copy(ones_b[:], ones_f[:]), reads=[ones_f.b], writes=[ones_b.b])
        sqr = self.ring(es, "ca_sq", 2, [128, 2, 512], F32)
        rr = self.ring(es, "ca_r", 4, [128, 512], F32)
        tmpr = self.ring(es, "ca_tmp", 2, [128, 512], F32)
        qh = self.ring(es, "ca_qh", 3, [128, 2, 512], BF16)
        Er = self.ring(es, "ca_E", 2, [128, 2, 512], BF16)

        def qk_norm(p0, p1, n, gain, scale, out_aps, out_buf):
            s = sqr.next()
            tk.op("act", lambda: nc.scalar.activation(out=s[:, 0, 0:n], in_=p0[:, 0:n], func=AF.Square), reads=[p0.b], writes=[s.b])
            tk.op("act", lambda: nc.scalar.activation(out=s[:, 1, 0:n], in_=p1[:, 0:n], func=AF.Square), reads=[p1.b], writes=[s.b])
            p2 = self.psum()
            for j in range(2):
                tk.op("pe", lambda: nc.tensor.matmul(p2[:, 0:n], lhsT=ones_f[:], rhs=s[:, j, 0:n], start=(j == 0), stop=(j == 1)),
                      reads=[ones_f.b, s.b], writes=[p2.b])
            r = rr.next()
            tk.op("dve", lambda: nc.vector.tensor_scalar(r[:, 0:n], p2[:, 0:n], 1.0 / 256, 1e-6, ALU.mult, ALU.add), reads=[p2.b], writes=[r.b])
            self.rpow(r[:, 0:n], r.b, -0.5)
            for j, pj in enumerate((p0, p1)):
                t = tmpr.next()
                tk.op("dve", lambda: nc.vector.tensor_tensor(out=t[:, 0:n], in0=pj[:, 0:n], in1=r[:, 0:n], op=ALU.mult), reads=[pj.b, r.b], writes=[t.b])
                tk.op("dve", lambda: nc.vector.tensor_scalar(out_aps[j], t[:, 0:n], gain[:, j:j + 1], scale, ALU.mult, ALU.mult),
                      reads=[t.b, gain.b], writes=[out_buf])

        with self.scope() as es2:
            mnT = self.sb(es2, "ca_mnT", [128, 8, NMEM], BF16, dj=True)
            self.norm_T("mem", bi * NMEM, NMEM, "norm_mem", mnT)
            wk = self.sb(es2, "ca_wk", [128, 8, 1024], BF16)
            wv = self.sb(es2, "ca_wv", [128, 8, 1024], BF16)
            self.load_w(wk, "ca_wkv_bf", 0, 1024)
            self.load_w(wv, "ca_wkv_bf", 1024, 1024)
            for h in range(4):
                ps_ = []
                for j in range(2):
                    p = self.psum()
                    ci = 2 * h + j
                    for kc in range(8):
                        tk.op("pe", lambda: nc.tensor.matmul(p[:, 0:NMEM], lhsT=wk[:, kc, ci * 128:(ci + 1) * 128], rhs=mnT[:, kc, :], start=(kc == 0), stop=(kc == 7)),
                              reads=[wk.b, mnT.b], writes=[p.b])
                    ps_.append(p)
                qk_norm(ps_[0], ps_[1], NMEM, kgain, 1.0, [kT[:, 2 * h, :], kT[:, 2 * h + 1, :]], kT.b)
            for mt in range(2):
                for half in range(2):
                    p = self.psum()
                    for kc in range(8):
                        tk.op("pe", lambda: nc.tensor.matmul(p[:], lhsT=mnT[:, kc, mt * 128:(mt + 1) * 128], rhs=wv[:, kc, half * 512:(half + 1) * 512],
                                                              start=(kc == 0), stop=(kc == 7)), reads=[mnT.b, wv.b], writes=[p.b])
                    tk.op("act", lambda: nc.scalar.copy(Vc[:, mt, half * 512:(half + 1) * 512], p[:]), reads=[p.b], writes=[Vc.b])
        for hf in range(T // HT):
            with self.scope() as es2:
                hnT = self.sb(es2, "ca_hnT", [128, 8, HT], BF16, dj=True)
                oT = self.sb(es2, "ca_oT", [128, 8, HT], BF16, dj=True)
                self.norm_T("h1", hf * HT, HT, "norm_cross", hnT)
                def stage_q(h, tt):
                    ps_ = []
                    for j in range(2):
                        p = self.psum()
                        ci = 2 * h + j
                        for kc in range(8):
                            tk.op("pe", lambda: nc.tensor.matmul(p[:], lhsT=wq[:, kc, ci * 128:(ci + 1) * 128], rhs=hnT[:, kc, tt * 512:(tt + 1) * 512],
                                                                  start=(kc == 0), stop=(kc == 7)), reads=[wq.b, hnT.b], writes=[p.b])
                        ps_.append(p)
                    q = qh.next()
                    qk_norm(ps_[0], ps_[1], 512, qgain, 1.0 / 16, [q[:, 0, :], q[:, 1, :]], q.b)
                    return q

                def stage_att(h, tt, q):
                    E = Er.next()
                    for mt in range(2):
                        p = self.psum()
                        for j in range(2):
                            tk.op("pe", lambda: nc.tensor.matmul(p[:], lhsT=kT[:, 2 * h + j, mt * 128:(mt + 1) * 128], rhs=q[:, j, :], start=(j == 0), stop=(j == 1)),
                                  reads=[kT.b, q.b], writes=[p.b])
                        tk.op("act", lambda: nc.scalar.activation(out=E[:, mt, :], in_=p[:], func=AF.Exp), reads=[p.b], writes=[E.b])
                    pd = self.psum()
                    for mt in range(2):
                        tk.op("pe", lambda: nc.tensor.matmul(pd[:], lhsT=ones_b[:], rhs=E[:, mt, :], start=(mt == 0), stop=(mt == 1)),
                              reads=[ones_b.b, E.b], writes=[pd.b])
                    r = rr.next()
                    tk.op("act", lambda: nc.scalar.activation(out=r[:], in_=pd[:], func=AF.Ln), reads=[pd.b], writes=[r.b])
                    tk.op("act", lambda: nc.scalar.activation(out=r[:], in_=r[:], func=AF.Exp, scale=-1.0), reads=[r.b], writes=[r.b])
                    for j in range(2):
                        pn = self.psum()
                        for mt in range(2):
                            tk.op("pe", lambda: nc.tensor.matmul(pn[:], lhsT=Vc[:, mt, h * 256 + j * 128:h * 256 + (j + 1) * 128], rhs=E[:, mt, :],
                                                                  start=(mt == 0), stop=(mt == 1)), reads=[Vc.b, E.b], writes=[pn.b])
                        tk.op("dve", lambda: nc.vector.tensor_tensor(out=oT[:, 2 * h + j, tt * 512:(tt + 1) * 512], in0=pn[:], in1=r[:], op=ALU.mult),
                              reads=[pn.b, r.b], writes=[oT.b])

                its = [(h, tt) for h in range(4) for tt in range(HT // 512)]
                prev = None
                for (h, tt) in its:
                    q = stage_q(h, tt)
                    if prev is not None:
                        stage_att(*prev)
                    prev = (h, tt, q)
                stage_att(*prev)
                self.proj_tm_res(oT, 0, HT, 8, wo, "h1", hf * HT, "h2", hf * HT, es2)


def _ffn(self, bi):
    nc, tk = self.nc, self.tk
    self.scratch("ff_fm", [DFF, T], BF16)
    NCT = DFF // 128
    with self.scope() as es:
        hnT = self.sb(es, "ff_hnT", [128, 8, T], BF16, dj=True)
        self.norm_T("h2", 0, T, "norm_ffn", hnT)
        cw = [self.col_vec(es, "ffn_conv", j, 0, NCT, f"ff_cw{j}") for j in range(3)]
        cb = self.col_vec(es, "ffn_conv_b", 0, 0, NCT, "ff_cb")
        wring = self.ring(es, "ff_w", 4, [128, 8, 128], BF16)
        atr = self.ring(es, "ff_a", 2, [128, T + 2], F32)
        btr = self.ring(es, "ff_b", 2, [128, T], F32)
        acc = self.sb(es, "ff_acc", [128, T], F32)
        ob = self.ring(es, "ff_ob", 2, [128, T], BF16)
        for at in atr.tiles:
            tk.op("pool", lambda: nc.gpsimd.memset(at[:, 0:2], 0.0), writes=[at.b])
        for ci in range(NCT):
            at, bt = atr.next(), btr.next()
            wa, wb = wring.next(), wring.next()
            self.load_w(wa, "ffn_up_bf", ci * 128, 128)
            self.load_w(wb, "ffn_up_bf", DFF + ci * 128, 128)
            for tt in range(T // 512):
                pa, pb = self.psum(), self.psum()
                for kc in range(8):
                    tk.op("pe", lambda: nc.tensor.matmul(pa[:], lhsT=wa[:, kc, :], rhs=hnT[:, kc, tt * 512:(tt + 1) * 512], start=(kc == 0), stop=(kc == 7)),
                          reads=[wa.b, hnT.b], writes=[pa.b])
                for kc in range(8):
                    tk.op("pe", lambda: nc.tensor.matmul(pb[:], lhsT=wb[:, kc, :], rhs=hnT[:, kc, tt * 512:(tt + 1) * 512], start=(kc == 0), stop=(kc == 7)),
                          reads=[wb.b, hnT.b], writes=[pb.b])
                tk.op("act", lambda: nc.scalar.copy(at[:, 2 + tt * 512:2 + (tt + 1) * 512], pa[:]), reads=[pa.b], writes=[at.b])
                tk.op("dve", lambda: nc.vector.tensor_copy(bt[:, tt * 512:(tt + 1) * 512], pb[:]), reads=[pb.b], writes=[bt.b])
            tk.op("dve", lambda: nc.vector.tensor_scalar(acc[:], at[:, 2:T + 2], cw[2][:, ci:ci + 1], cb[:, ci:ci + 1], ALU.mult, ALU.add),
                  reads=[at.b, cw[2].b, cb.b], writes=[acc.b])
            tk.op("dve", lambda: nc.vector.scalar_tensor_tensor(out=acc[:], in0=at[:, 1:T + 1], scalar=cw[1][:, ci:ci + 1], in1=acc[:], op0=ALU.mult, op1=ALU.add),
                  reads=[at.b, cw[1].b, acc.b], writes=[acc.b])
            tk.op("dve", lambda: nc.vector.scalar_tensor_tensor(out=acc[:], in0=at[:, 0:T], scalar=cw[0][:, ci:ci + 1], in1=acc[:], op0=ALU.mult, op1=ALU.add),
                  reads=[at.b, cw[0].b, acc.b], writes=[acc.b])
            tk.op("act", lambda: nc.scalar.activation(out=acc[:], in_=acc[:], func=AF.Silu), reads=[acc.b], writes=[acc.b])
            o = ob.next()
            tk.op("dve", lambda: nc.vector.tensor_tensor(out=o[:], in0=acc[:], in1=bt[:], op=ALU.mult), reads=[acc.b, bt.b], writes=[o.b])
            tk.dma("pool", self.din["ff_fm"].ap()[ci * 128:(ci + 1) * 128, :], o[:], reads=[o.b], writes=[self.dbuf["ff_fm"]])
    with self.scope() as es:
        wd = self.sb(es, "ff_wd", [128, NCT, 1024], BF16)
        self.load_w(wd, "ffn_down_bf", 0, 1024, kchunks=NCT)
        TBK = 1024
        for blk in range(T // TBK):
            with self.scope() as es2:
                fT = self.sb(es2, "ff_fT", [128, NCT, TBK], BF16)
                tk.dma("sp", fT[:], self.dap("ff_fm", blk * TBK, [[T, 128], [128 * T, NCT], [1, TBK]]), reads=[self.dbuf["ff_fm"]], writes=[fT.b])
                self.proj_tm_res(fT, 0, TBK, NCT, wd, "h2", blk * TBK, "out", bi * T + blk * TBK, es2)


Prog.proj_tm_res = _proj_tm_res
Prog.phase_merge = _merge
Prog.phase_cross = _cross
Prog.phase_ffn = _ffn
```
